# Optimizing a Trainium2 kernel written in Bass

```python
import jax, jax.numpy as jnp
from jax import lax
import numpy as np

D_MODEL = 2048
BATCH = 8
SEQ = 2048
DEPTH = 4
DEC_BATCH = 8
DEC_SEQ = 64
PAST_LEN = 1024

CHUNK = 64
N_MIXERS = 3
N_A = (DEPTH + 2) // 3
N_B = (DEPTH + 1) // 3
N_C = DEPTH // 3
DN_ALPHA = (2.0 * DEPTH) ** 0.25
DN_BETA = (8.0 * DEPTH) ** -0.25
LN_EPS = 1e-5
RMS_EPS = 1e-6
NEG_INF = -1e30
Q_BLOCK = 128

A_HEADS = 16
A_HEAD_DIM = D_MODEL // A_HEADS
A_PAST_CHUNKS = 8
A_WIN = A_PAST_CHUNKS * CHUNK
A_BAND = A_WIN + CHUNK
A_MAX_REL = 128

B_HEADS = 16
B_Q_LORA = 512
B_KV_LORA = 512
B_NOPE = 128
B_ROPE = 64
B_VDIM = 128
ROPE_THETA = 10000.0

C_HEADS = 16
C_HEAD_DIM = D_MODEL // C_HEADS

PEER_HEADS = 8
PEER_NKEYS = 128
PEER_EXPERTS = PEER_NKEYS * PEER_NKEYS
PEER_TOPK = 16
PEER_QDIM = 256
PEER_BLOCK = 128

kernel_name = 'streaming_hybrid_band_mla_stickbreak_peer'


def _layernorm(x, g, b):
    xf = x.astype(jnp.float32)
    mu = jnp.mean(xf, -1, keepdims=True)
    xc = xf - mu
    var = jnp.mean(xc * xc, -1, keepdims=True)
    y = xc * lax.rsqrt(var + LN_EPS) * g.astype(jnp.float32) + b.astype(jnp.float32)
    return y.astype(x.dtype)


def _rmsnorm(x, g):
    xf = x.astype(jnp.float32)
    y = xf * lax.rsqrt(jnp.mean(xf * xf, -1, keepdims=True) + RMS_EPS) * g.astype(jnp.float32)
    return y.astype(x.dtype)


def _rope(x, pos):
    half = x.shape[-1] // 2
    inv = ROPE_THETA ** (-jnp.arange(half, dtype=jnp.float32) / half)
    ang = pos.astype(jnp.float32)[:, None] * inv[None, :]
    ang = ang.reshape((1, ang.shape[0]) + (1,) * (x.ndim - 3) + (half,))
    cos, sin = jnp.cos(ang), jnp.sin(ang)
    xf = x.astype(jnp.float32)
    x1, x2 = xf[..., :half], xf[..., half:]
    return jnp.concatenate([x1 * cos - x2 * sin, x1 * sin + x2 * cos], -1).astype(x.dtype)


def _split_qkv(x, w, heads, hd):
    B, S, _ = x.shape
    qkv = (x @ w).reshape(B, S, 3, heads, hd)
    return qkv[:, :, 0], qkv[:, :, 1], qkv[:, :, 2]


def _blocked_queries(fn, qpos, *qs):
    S = qpos.shape[0]
    nb = S // Q_BLOCK

    def split(a):
        return jnp.moveaxis(a.reshape((a.shape[0], nb, Q_BLOCK) + a.shape[2:]), 1, 0)

    out = lax.map(lambda args: fn(*args), (qpos.reshape(nb, Q_BLOCK),) + tuple(split(a) for a in qs))
    out = jnp.moveaxis(out, 0, 1)
    return out.reshape((out.shape[0], S) + out.shape[3:])


def _band_attend(q, k, v, qpos, kpos, relb):
    s = jnp.einsum('bqhd,bkhd->bhqk', q, k).astype(jnp.float32) * (A_HEAD_DIM ** -0.5)
    rel = jnp.clip(qpos[:, None] - kpos[None, :], -A_MAX_REL, A_MAX_REL) + A_MAX_REL
    s = s + relb.astype(jnp.float32)[:, rel][None]
    s = jnp.where((kpos >= 0)[None, None, None, :], s, NEG_INF)
    p = jax.nn.softmax(s, axis=-1).astype(v.dtype)
    return jnp.einsum('bhqk,bkhd->bqhd', p, v)


def _mixer_a(xp, xs, ck, cv, wqkv, wo, relb):
    B, S, _ = xp.shape
    nC = S // CHUNK
    q, k, v = _split_qkv(xp, wqkv, A_HEADS, A_HEAD_DIM)
    kp = jnp.pad(k, ((0, 0), (A_WIN, 0), (0, 0), (0, 0)))
    vp = jnp.pad(v, ((0, 0), (A_WIN, 0), (0, 0), (0, 0)))
    qc = jnp.moveaxis(q.reshape(B, nC, CHUNK, A_HEADS, A_HEAD_DIM), 1, 0)

    def one_chunk(args):
        c, qb = args
        start = c * CHUNK
        kb = lax.dynamic_slice_in_dim(kp, start, A_BAND, axis=1)
        vb = lax.dynamic_slice_in_dim(vp, start, A_BAND, axis=1)
        qpos = start + jnp.arange(CHUNK)
        kpos = start - A_WIN + jnp.arange(A_BAND)
        return _band_attend(qb, kb, vb, qpos, kpos, relb)

    oc = lax.map(one_chunk, (jnp.arange(nC), qc))
    op = jnp.moveaxis(oc, 0, 1).reshape(B, S, A_HEADS * A_HEAD_DIM) @ wo
    keep = min(A_WIN, S)
    Bs, Ss, _ = xs.shape
    L = ck.shape[1]
    qs, ks, vs = _split_qkv(xs, wqkv, A_HEADS, A_HEAD_DIM)
    kcat = jnp.concatenate([ck, ks], 1)
    vcat = jnp.concatenate([cv, vs], 1)
    os_ = _band_attend(qs, kcat, vcat, L + jnp.arange(Ss), jnp.arange(L + Ss), relb)
    os_ = os_.reshape(Bs, Ss, A_HEADS * A_HEAD_DIM) @ wo
    return op, os_, k[:, S - keep:], v[:, S - keep:], kcat[:, Ss:], vcat[:, Ss:]


def _mla_project(x, pos, win, qnorm, kvnorm, wqb):
    B, S, _ = x.shape
    h = x @ win
    cq = _rmsnorm(h[..., :B_Q_LORA], qnorm)
    ckv = _rmsnorm(h[..., B_Q_LORA:B_Q_LORA + B_KV_LORA], kvnorm)
    kr = _rope(h[..., B_Q_LORA + B_KV_LORA:], pos)
    q = (cq @ wqb).reshape(B, S, B_HEADS, B_NOPE + B_ROPE)
    return q[..., :B_NOPE], _rope(q[..., B_NOPE:], pos), ckv, kr


def _mla_expand(ckv, wkvb):
    B, T, _ = ckv.shape
    kv = (ckv @ wkvb).reshape(B, T, B_HEADS, B_NOPE + B_VDIM)
    return kv[..., :B_NOPE], kv[..., B_NOPE:]


def _mla_attend(qn, qr, kn, kr, v, qpos, kpos):
    s = (jnp.einsum('bqhd,bkhd->bhqk', qn, kn) + jnp.einsum('bqhr,bkr->bhqk', qr, kr)).astype(jnp.float32)
    s = s * ((B_NOPE + B_ROPE) ** -0.5)
    visible = (kpos[None, :] // CHUNK) <= (qpos[:, None] // CHUNK)
    s = jnp.where(visible[None, None], s, NEG_INF)
    p = jax.nn.softmax(s, axis=-1).astype(v.dtype)
    return jnp.einsum('bhqk,bkhd->bqhd', p, v)


def _mixer_b(xp, xs, c_ckv, c_kr, win, qnorm, kvnorm, wqb, wkvb, wo):
    B, S, _ = xp.shape
    pos = jnp.arange(S)
    qn, qr, ckv, kr = _mla_project(xp, pos, win, qnorm, kvnorm, wqb)
    kn, v = _mla_expand(ckv, wkvb)
    o = _blocked_queries(lambda qp, a, b: _mla_attend(a, b, kn, kr, v, qp, pos), pos, qn, qr)
    op = o.reshape(B, S, B_HEADS * B_VDIM) @ wo
    Bs, Ss, _ = xs.shape
    P = c_ckv.shape[1]
    spos = P + jnp.arange(Ss)
    qn_s, qr_s, ckv_s, kr_s = _mla_project(xs, spos, win, qnorm, kvnorm, wqb)
    ckv_all = jnp.concatenate([c_ckv, ckv_s], 1)
    kr_all = jnp.concatenate([c_kr, kr_s], 1)
    kn_s, v_s = _mla_expand(ckv_all, wkvb)
    os_ = _mla_attend(qn_s, qr_s, kn_s, kr_all, v_s, spos, jnp.arange(P + Ss))
    os_ = os_.reshape(Bs, Ss, B_HEADS * B_VDIM) @ wo
    return op, os_, ckv, kr, ckv_s, kr_s


def _sb_attend(q, k, v, qpos, kpos):
    z = jnp.einsum('bqhd,bkhd->bhqk', q, k).astype(jnp.float32) * (C_HEAD_DIM ** -0.5)
    earlier = (kpos[None, :] < qpos[:, None])[None, None]
    log_beta = jax.nn.log_sigmoid(z)
    log_keep = jnp.where(earlier, jax.nn.log_sigmoid(-z), 0.0)
    between = lax.cumsum(log_keep, axis=3, reverse=True) - log_keep
    w = jnp.where(earlier, jnp.exp(log_beta + between), 0.0)
    return jnp.einsum('bhqk,bkhd->bqhd', w.astype(v.dtype), v)


def _mixer_c(xp, xs, ck, cv, wqkv, wo):
    B, S, _ = xp.shape
    pos = jnp.arange(S)
    q, k, v = _split_qkv(xp, wqkv, C_HEADS, C_HEAD_DIM)
    o = _blocked_queries(lambda qp, qb: _sb_attend(qb, k, v, qp, pos), pos, q)
    op = o.reshape(B, S, C_HEADS * C_HEAD_DIM) @ wo
    Bs, Ss, _ = xs.shape
    P = ck.shape[1]
    qs, ks, vs = _split_qkv(xs, wqkv, C_HEADS, C_HEAD_DIM)
    kcat = jnp.concatenate([ck, ks], 1)
    vcat = jnp.concatenate([cv, vs], 1)
    os_ = _sb_attend(qs, kcat, vcat, P + jnp.arange(Ss), jnp.arange(P + Ss))
    os_ = os_.reshape(Bs, Ss, C_HEADS * C_HEAD_DIM) @ wo
    return op, os_, k, v, ks, vs


def _peer(x, wq, keys, u, v):
    shp = x.shape
    xt = x.reshape(-1, shp[-1])
    T = xt.shape[0]
    nb = -(-T // PEER_BLOCK)
    xt = jnp.pad(xt, ((0, nb * PEER_BLOCK - T), (0, 0)))

    def block(xb):
        q = (xb @ wq).reshape(PEER_BLOCK, PEER_HEADS, 2, PEER_QDIM // 2)
        sc = jnp.einsum('thcd,hcnd->thcn', q, keys).astype(jnp.float32)
        s1, i1 = lax.top_k(sc[:, :, 0], PEER_TOPK)
        s2, i2 = lax.top_k(sc[:, :, 1], PEER_TOPK)
        cand = (s1[..., :, None] + s2[..., None, :]).reshape(PEER_BLOCK, PEER_HEADS, PEER_TOPK * PEER_TOPK)
        sv, ci = lax.top_k(cand, PEER_TOPK)
        e1 = jnp.take_along_axis(i1, ci // PEER_TOPK, axis=-1)
        e2 = jnp.take_along_axis(i2, ci % PEER_TOPK, axis=-1)
        eid = e1 * PEER_NKEYS + e2
        g = jax.nn.softmax(sv, axis=-1)
        h = jax.nn.gelu(jnp.einsum('td,thkd->thk', xb, u[eid]).astype(jnp.float32))
        return jnp.einsum('thk,thkd->td', (g * h).astype(xb.dtype), v[eid])

    y = lax.map(block, xt.reshape(nb, PEER_BLOCK, shp[-1]))
    return y.reshape(nb * PEER_BLOCK, shp[-1])[:T].reshape(shp)


def setup_inputs(seed: int = 0) -> dict:
    key = jax.random.key(seed)
    ks = iter(jax.random.split(key, 32))
    D = D_MODEL
    f32 = jnp.float32
    la = min(A_WIN, PAST_LEN)

    def nrm(shape, scale):
        return jax.random.normal(next(ks), shape, f32) * scale

    return {
        'x_prompt': nrm((BATCH, SEQ, D), 1.0),
        'x_sample': nrm((DEC_BATCH, DEC_SEQ, D), 1.0),
        'cache_a_k': nrm((N_A, DEC_BATCH, la, A_HEADS, A_HEAD_DIM), 1.0),
        'cache_a_v': nrm((N_A, DEC_BATCH, la, A_HEADS, A_HEAD_DIM), 1.0),
        'cache_b_ckv': nrm((N_B, DEC_BATCH, PAST_LEN, B_KV_LORA), 1.0),
        'cache_b_krope': nrm((N_B, DEC_BATCH, PAST_LEN, B_ROPE), 1.0),
        'cache_c_k': nrm((N_C, DEC_BATCH, PAST_LEN, C_HEADS, C_HEAD_DIM), 1.0),
        'cache_c_v': nrm((N_C, DEC_BATCH, PAST_LEN, C_HEADS, C_HEAD_DIM), 1.0),
        'a_wqkv': nrm((N_A, D, 3 * A_HEADS * A_HEAD_DIM), D ** -0.5),
        'a_wo': nrm((N_A, A_HEADS * A_HEAD_DIM, D), DN_BETA * (A_HEADS * A_HEAD_DIM) ** -0.5),
        'a_relbias': nrm((N_A, A_HEADS, 2 * A_MAX_REL + 1), 0.2),
        'b_win': nrm((N_B, D, B_Q_LORA + B_KV_LORA + B_ROPE), D ** -0.5),
        'b_qnorm': 1.0 + nrm((N_B, B_Q_LORA), 0.05),
        'b_kvnorm': 1.0 + nrm((N_B, B_KV_LORA), 0.05),
        'b_wqb': nrm((N_B, B_Q_LORA, B_HEADS * (B_NOPE + B_ROPE)), B_Q_LORA ** -0.5),
        'b_wkvb': nrm((N_B, B_KV_LORA, B_HEADS * (B_NOPE + B_VDIM)), B_KV_LORA ** -0.5),
        'b_wo': nrm((N_B, B_HEADS * B_VDIM, D), DN_BETA * (B_HEADS * B_VDIM) ** -0.5),
        'c_wqkv': nrm((N_C, D, 3 * C_HEADS * C_HEAD_DIM), D ** -0.5),
        'c_wo': nrm((N_C, C_HEADS * C_HEAD_DIM, D), DN_BETA * (C_HEADS * C_HEAD_DIM) ** -0.5),
        'peer_wq': nrm((DEPTH, D, PEER_HEADS * PEER_QDIM), D ** -0.5),
        'peer_keys': nrm((DEPTH, PEER_HEADS, 2, PEER_NKEYS, PEER_QDIM // 2), (PEER_QDIM // 2) ** -0.5),
        'peer_u': nrm((DEPTH, PEER_EXPERTS, D), D ** -0.5),
        'peer_v': nrm((DEPTH, PEER_EXPERTS, D), DN_BETA * PEER_HEADS ** -0.5),
        'ln_g': 1.0 + nrm((DEPTH, 2, D), 0.05),
        'ln_b': nrm((DEPTH, 2, D), 0.02),
    }


def reference(x_prompt, x_sample, cache_a_k, cache_a_v, cache_b_ckv, cache_b_krope, cache_c_k, cache_c_v,
              a_wqkv, a_wo, a_relbias, b_win, b_qnorm, b_kvnorm, b_wqb, b_wkvb, b_wo, c_wqkv, c_wo,
              peer_wq, peer_keys, peer_u, peer_v, ln_g, ln_b):
    xp, xs = x_prompt, x_sample
    akp, avp, aks, avs = [], [], [], []
    bcp, brp, bcs, brs = [], [], [], []
    ckp, cvp, cks, cvs = [], [], [], []
    for i in range(DEPTH):
        kind, j = i % N_MIXERS, i // N_MIXERS
        if kind == 0:
            mp, ms, s0, s1, s2, s3 = _mixer_a(xp, xs, cache_a_k[j], cache_a_v[j], a_wqkv[j], a_wo[j], a_relbias[j])
            akp.append(s0); avp.append(s1); aks.append(s2); avs.append(s3)
        elif kind == 1:
            mp, ms, s0, s1, s2, s3 = _mixer_b(xp, xs, cache_b_ckv[j], cache_b_krope[j], b_win[j], b_qnorm[j],
                                              b_kvnorm[j], b_wqb[j], b_wkvb[j], b_wo[j])
            bcp.append(s0); brp.append(s1); bcs.append(s2); brs.append(s3)
        else:
            mp, ms, s0, s1, s2, s3 = _mixer_c(xp, xs, cache_c_k[j], cache_c_v[j], c_wqkv[j], c_wo[j])
            ckp.append(s0); cvp.append(s1); cks.append(s2); cvs.append(s3)
        xp = _layernorm(DN_ALPHA * xp + mp, ln_g[i, 0], ln_b[i, 0])
        xs = _layernorm(DN_ALPHA * xs + ms, ln_g[i, 0], ln_b[i, 0])
        xp = _layernorm(DN_ALPHA * xp + _peer(xp, peer_wq[i], peer_keys[i], peer_u[i], peer_v[i]), ln_g[i, 1], ln_b[i, 1])
        xs = _layernorm(DN_ALPHA * xs + _peer(xs, peer_wq[i], peer_keys[i], peer_u[i], peer_v[i]), ln_g[i, 1], ln_b[i, 1])
    return (xp, xs,
            jnp.stack(akp), jnp.stack(avp), jnp.stack(bcp), jnp.stack(brp), jnp.stack(ckp), jnp.stack(cvp),
            jnp.stack(aks), jnp.stack(avs), jnp.stack(bcs), jnp.stack(brs), jnp.stack(cks), jnp.stack(cvs))
```

```python
import os
import numpy as np
from contextlib import ExitStack
import concourse.bass as bass
import concourse.mybir as mybir
from concourse.bass_utils import run_bass_kernel_spmd

F32 = mybir.dt.float32
BF16 = mybir.dt.bfloat16
AF = mybir.ActivationFunctionType
ALU = mybir.AluOpType
AX = mybir.AxisListType

NSLOT = 20
COMPUTE = ("pe", "act", "dve", "pool")
DMAQ = ("sp", "dq_pool")
ALLQ = COMPUTE + DMAQ


class Buf:
    __slots__ = ("name", "last_w", "readers")

    def __init__(self, name):
        self.name = name
        self.last_w = []
        self.readers = []


class Op:
    __slots__ = ("eng", "fn", "deps", "signal", "sigval", "eidx", "isdma", "slot", "idx")


class Prog:
    def __init__(self, nc):
        self.nc = nc
        self.ops = []
        self.ecount = {}
        self.last = {}
        self.recent_dma = {q: [] for q in DMAQ}

    def eng_obj(self, eng):
        nc = self.nc
        return {"pe": nc.tensor, "act": nc.scalar, "dve": nc.vector, "pool": nc.gpsimd,
                "sp": nc.sync, "dq_pool": nc.gpsimd}[eng]

    @staticmethod
    def phys(eng):
        return "pool" if eng == "dq_pool" else eng

    def op(self, eng, fn, reads=(), writes=(), nowaw=False):
        o = Op()
        o.eng = eng
        o.fn = fn
        o.isdma = eng in DMAQ
        o.signal = False
        o.sigval = 0
        o.slot = 0
        o.idx = len(self.ops)
        pe = self.phys(eng)
        o.eidx = self.ecount.get(pe, 0)
        self.ecount[pe] = o.eidx + 1
        deps = set()
        for b in reads:
            deps.update(b.last_w)
        for b in writes:
            if not nowaw:
                deps.update(b.last_w)
            deps.update(b.readers)
        deps.discard(o)
        o.deps = deps
        for b in reads:
            b.readers.append(o)
        for b in writes:
            if nowaw:
                b.last_w.append(o)
            else:
                b.last_w = [o]
                b.readers = []
        self.ops.append(o)
        if fn is not None:
            if o.isdma:
                r = self.recent_dma[eng]
                r.append(o)
                if len(r) > NSLOT:
                    r.pop(0)
            else:
                self.last[eng] = o
        return o

    def barrier(self):
        lasts = [o for o in self.last.values()]
        for q in DMAQ:
            lasts += self.recent_dma[q]
        for eng in ALLQ:
            o = self.op(eng, None)
            o.deps = set(lasts)

    def emit(self):
        nc = self.nc
        need = []
        for o in self.ops:
            ws = []
            for d in o.deps:
                if d.fn is None:
                    continue
                if (not d.isdma) and (not o.isdma) and d.eng == o.eng:
                    if o.eng == "pe" or o.fn is None:
                        continue
                    if o.eidx - d.eidx > 3:
                        continue
                ws.append(d)
                d.signal = True
            need.append(ws)
        sems = {e: nc.alloc_semaphore("s_" + e) for e in COMPUTE}
        dsems = {q: [nc.alloc_semaphore("d_%s_%d" % (q, i)) for i in range(NSLOT)] for q in DMAQ}
        sigcount = {e: 0 for e in COMPUTE}
        dcount = {q: 0 for q in DMAQ}
        waited = {}
        nw = [0]

        def do_wait(eng, semkey, sem, val):
            k = (self.phys(eng), semkey)
            if waited.get(k, 0) >= val:
                return
            waited[k] = val
            nw[0] += 1
            self.eng_obj(eng).wait_ge(sem, val)

        for o, ws in zip(self.ops, need):
            e = self.eng_obj(o.eng)
            for d in sorted(ws, key=lambda d: d.idx):
                if d.isdma:
                    do_wait(o.eng, ("d", d.eng, d.slot), dsems[d.eng][d.slot], d.sigval)
                else:
                    do_wait(o.eng, ("c", d.eng), sems[d.eng], d.sigval)
            if o.fn is None:
                continue
            if o.isdma:
                i = dcount[o.eng]
                dcount[o.eng] = i + 1
                o.slot = i % NSLOT
                o.sigval = 16 * (i // NSLOT + 1)
                if i >= NSLOT:
                    do_wait(o.eng, ("d", o.eng, o.slot), dsems[o.eng][o.slot], 16 * (i // NSLOT))
                ins = o.fn(e)
                ins.then_inc(dsems[o.eng][o.slot], 16)
            else:
                ins = o.fn(e)
                if o.signal:
                    sigcount[o.eng] += 1
                    o.sigval = sigcount[o.eng]
                    ins.then_inc(sems[o.eng], 1)
        for q in DMAQ:
            n = dcount[q]
            for s in range(min(n, NSLOT)):
                last_i = ((n - 1 - s) // NSLOT) * NSLOT + s
                nc.sync.wait_ge(dsems[q][s], 16 * (last_i // NSLOT + 1))
        for en in COMPUTE:
            if sigcount[en] > 0:
                nc.sync.wait_ge(sems[en], sigcount[en])
        return dict(n_ops=len(self.ops), sig=sigcount, dma=dcount, waits=nw[0])


def I(name, *a, **k):
    return lambda e: getattr(e, name)(*a, **k)


class Tile:
    def __init__(self, t, name):
        self.t = t
        self.b = Buf(name)

    def __getitem__(self, k):
        return self.t[k]


class Rot:
    def __init__(self, tiles):
        self.tiles = tiles
        self.i = 0

    def next(self):
        t = self.tiles[self.i % len(self.tiles)]
        self.i += 1
        return t


NT = 17
T = NT * 128
D = 2048
KC = 16
TOKB = [(0, 512), (512, 512), (1024, 512), (1536, 512), (2048, 128)]
ALPHA = (2.0 * 4) ** 0.25
NEG = -30000.0
KEXT = T + 1024


class K:
    def __init__(self, nc):
        self.nc = nc
        self.P = Prog(nc)
        self.es = None
        self.uid = 0

    def sb(self, shape, dt, name=None):
        self.uid += 1
        name = "%s_%d" % (name or "t", self.uid)
        t = self.es.enter_context(self.nc.sbuf_tensor(name, list(shape), dt))
        return Tile(t, name)

    def ps(self, shape, dt, name=None):
        self.uid += 1
        name = "%s_%d" % (name or "p", self.uid)
        t = self.es.enter_context(self.nc.psum_tensor(name, list(shape), dt))
        return Tile(t, name)

    def dram(self, name, shape, dt, kind="Internal"):
        t = self.nc.dram_tensor(name, list(shape), dt, kind=kind)
        tl = Tile(t.ap(), name)
        return tl

    def phase(self):
        k = self

        class _Ph:
            def __enter__(s):
                k.es = ExitStack()
                k.es.__enter__()
                return s

            def __exit__(s, *a):
                k.P.barrier()
                k.es.__exit__(*a)
                k.es = None
                return False
        return _Ph()

    def op(self, eng, fn, reads=(), writes=(), nowaw=False):
        if eng == "dq_pool":
            eng = "sp"
        return self.P.op(eng, fn, [r.b if isinstance(r, Tile) else r for r in reads],
                         [w.b if isinstance(w, Tile) else w for w in writes], nowaw)


def sap(tile_or_t, offset, dims):
    t = tile_or_t.t if isinstance(tile_or_t, Tile) else tile_or_t
    full = t[:]
    pstride = full.ap[0][0]
    return bass.AP(t, offset, [[pstride, 128]] + [list(d) for d in dims])


def dap(ap, offset, dims):
    return bass.AP(ap.tensor, offset, [list(d) for d in dims])


class _Stop(Exception):
    pass


def build(n_layers=4, dbg=False, stop_after=None, kinds=None):
    nc = bass.Bass("TRN2", target_bir_lowering=False)
    k = K(nc)
    P = k.P
    stage_ctr = [0]

    def stage(name):
        stage_ctr[0] += 1
        if stop_after is not None and stage_ctr[0] > stop_after:
            raise _Stop(name)

    def din(name, shape):
        return Tile(nc.dram_tensor(name, list(shape), F32, kind="ExternalInput").ap(), name)

    def dout(name, shape):
        return Tile(nc.dram_tensor(name, list(shape), F32, kind="ExternalOutput").ap(), name)

    x0 = din("x0", [T, D])
    ca_k = din("ca_k", [2, 512, 2048]); ca_v = din("ca_v", [2, 512, 2048])
    cb_c = din("cb_c", [1024, 512]); cb_r = din("cb_r", [1024, 64])
    cc_k = din("cc_k", [1024, 2048]); cc_v = din("cc_v", [1024, 2048])
    a_wqkv = din("a_wqkv", [2, 2048, 6144]); a_wo = din("a_wo", [2, 2048, 2048]); a_rel = din("a_rel", [2, 16, 257])
    b_win = din("b_win", [2048, 1088]); b_qn = din("b_qn", [1, 512]); b_kvn = din("b_kvn", [1, 512])
    b_wqb = din("b_wqb", [512, 3072]); b_wkvb = din("b_wkvb", [512, 4096]); b_wo = din("b_wo", [2048, 2048])
    c_wqkv = din("c_wqkv", [2048, 6144]); c_wo = din("c_wo", [2048, 2048])
    p_wq = din("p_wq", [4, 2048, 2048]); p_keys = din("p_keys", [4, 16, 128, 128])
    p_u = din("p_u", [n_layers, 16384, 2048]); p_v = din("p_v", [n_layers, 16384, 2048])
    ln_g = din("ln_g", [8, 2048]); ln_b = din("ln_b", [8, 2048])
    ropec = din("ropec", [T, 32]); ropes = din("ropes", [T, 32])

    y_out = dout("y", [T, D])
    oak = dout("oak", [2, 512, 2048]); oav = dout("oav", [2, 512, 2048])
    obc = dout("obc", [2048, 512]); obr = dout("obr", [2048, 64])
    ock = dout("ock", [2048, 2048]); ocv = dout("ocv", [2048, 2048])
    sak = dout("sak", [2, 512, 2048]); sav = dout("sav", [2, 512, 2048])
    sbc = dout("sbc", [64, 512]); sbr = dout("sbr", [64, 64])
    sck = dout("sck", [64, 2048]); scv = dout("scv", [64, 2048])

    xkind = "ExternalOutput" if dbg else "Internal"
    XA = k.dram("XA", [T, D], F32, kind=xkind); XB = k.dram("XB", [T, D], F32, kind=xkind)
    XT = Tile(nc.dram_tensor("XT", [NT, 128, KC, 128], BF16, kind="Internal").ap(), "XT")
    QT = Tile(nc.dram_tensor("QT", [16, 128, T], BF16, kind="Internal").ap(), "QT")
    KTX = Tile(nc.dram_tensor("KTX", [16, 128, KEXT], BF16, kind="Internal").ap(), "KTX")
    QR = Tile(nc.dram_tensor("QR", [8, 128, T], BF16, kind="Internal").ap(), "QR")
    KR = Tile(nc.dram_tensor("KR", [128, KEXT], BF16, kind="Internal").ap(), "KR")
    VX = Tile(nc.dram_tensor("VX", [KEXT, 2048], BF16, kind="Internal").ap(), "VX")
    SS = Tile(nc.dram_tensor("SS", [NT, 128, 2048], F32, kind="Internal").ap(), "SS")
    AUX = Tile(nc.dram_tensor("AUX", [NT, 128, 2048 + 16], F32, kind="Internal").ap(), "AUX")
    YAC = Tile(nc.dram_tensor("YAC", [NT, 128, 2048], F32, kind="Internal").ap(), "YAC")
    EXT = Tile(nc.dram_tensor("EXT", [16, 384], F32, kind="Internal").ap(), "EXT")
    CQT = Tile(nc.dram_tensor("CQT", [4, 128, T], BF16, kind="Internal").ap(), "CQT")
    CKVT = Tile(nc.dram_tensor("CKVT", [4, 128, KEXT], BF16, kind="Internal").ap(), "CKVT")
    ATd = Tile(nc.dram_tensor("ATd", [16, 128, T], BF16, kind="Internal").ap(), "ATd")
    XTb = [Buf("XT%d" % i) for i in range(NT)]
    AUXb = [Buf("AUX%d" % i) for i in range(NT)]
    YACb = [Buf("YAC%d" % i) for i in range(NT)]
    SSb = [Buf("SS%d" % i) for i in range(NT)]

    def palloc(shape, dt, name):
        return Tile(nc.alloc_sbuf_tensor(name, list(shape), dt), name)

    identf = palloc([128, 128], F32, "identf")
    ident = palloc([128, 128], BF16, "ident")
    ones_b = palloc([128, 128], BF16, "ones_b")
    ones_f = palloc([128, 128], F32, "ones_f")
    zeros_b = palloc([128, 128], BF16, "zeros_b")
    Uf = palloc([128, 128], F32, "Uf")
    mcaus = palloc([128, 128], F32, "mcaus")
    m0b = palloc([128, 128], BF16, "m0b")
    m4b = palloc([128, 128], BF16, "m4b")
    k.op("pool", I("memset", identf[:], 0.0), writes=[identf])
    k.op("pool", I("affine_select", out=identf[:], in_=identf[:], pattern=[[-1, 128]], compare_op=ALU.not_equal,
                   fill=1.0, base=0, channel_multiplier=1), reads=[identf], writes=[identf])
    k.op("pool", I("tensor_copy", out=ident[:], in_=identf[:]), reads=[identf], writes=[ident])
    k.op("pool", I("memset", ones_b[:], 1.0), writes=[ones_b])
    k.op("pool", I("memset", ones_f[:], 1.0), writes=[ones_f])
    k.op("pool", I("memset", zeros_b[:], 0.0), writes=[zeros_b])
    k.op("pool", I("affine_select", out=Uf[:], in_=ones_f[:], pattern=[[-1, 128]], compare_op=ALU.is_gt,
                   fill=0.0, base=0, channel_multiplier=1), reads=[ones_f], writes=[Uf])
    k.op("pool", I("affine_select", out=mcaus[:], in_=ones_f[:], pattern=[[1, 128]], compare_op=ALU.is_gt,
                   fill=0.0, base=0, channel_multiplier=-1), reads=[ones_f], writes=[mcaus])
    k.op("pool", I("memset", m0b[:], 0.0), writes=[m0b])
    k.op("pool", I("memset", m0b[0:64, 0:64], NEG), writes=[m0b])
    k.op("pool", I("memset", m4b[:], 0.0), writes=[m4b])
    k.op("pool", I("memset", m4b[64:128, 64:128], NEG), writes=[m4b])
    m0n = palloc([128, 128], BF16, "m0n")
    k.op("pool", I("memset", m0n[:], 0.0), writes=[m0n])
    k.op("pool", I("memset", m0n[64:128, 0:64], NEG), writes=[m0n])
    mcausb = palloc([128, 128], BF16, "mcausb")
    k.op("pool", I("tensor_copy", out=mcausb[:], in_=mcaus[:]), reads=[mcaus], writes=[mcausb])
    Jf = palloc([128, 128], F32, "Jf")
    Jb = palloc([128, 128], BF16, "Jb")
    k.op("pool", I("memset", Jf[:], 0.0), writes=[Jf])
    k.op("pool", I("affine_select", out=Jf[:], in_=Jf[:], pattern=[[1, 128]], compare_op=ALU.not_equal,
                   fill=1.0, base=-127, channel_multiplier=1), reads=[Jf], writes=[Jf])
    k.op("pool", I("tensor_copy", out=Jb[:], in_=Jf[:]), reads=[Jf], writes=[Jb])

    alt = [0]

    def evac_eng():
        alt[0] += 1
        return "act" if alt[0] % 2 else "dve"

    def copy_op(eng, out, in_, reads, writes, scale=None):
        if eng == "act":
            if scale is None:
                k.op("act", I("copy", out=out, in_=in_), reads, writes)
            else:
                k.op("act", I("activation", out=out, in_=in_, func=AF.Copy, scale=float(scale)), reads, writes)
        else:
            if scale is None:
                k.op(eng, I("tensor_copy", out=out, in_=in_), reads, writes)
            else:
                k.op(eng, I("tensor_scalar", out=out, in0=in_, scalar1=float(scale), scalar2=None, op0=ALU.mult), reads, writes)

    def transpose_tile_to_XT(src_bf, tt, pst_rot, xts_rot):
        xts = xts_rot.next()
        for half in range(2):
            pst = pst_rot.next()
            for j in range(8):
                kc = half * 8 + j
                k.op("pe", I("transpose", out=pst[:, j * 128:(j + 1) * 128], in_=src_bf[:, kc * 128:(kc + 1) * 128],
                             identity=ident[:]), reads=[src_bf, ident], writes=[pst])
            copy_op(evac_eng(), xts[:, half * 8:(half + 1) * 8, :], pst[:].rearrange("p (a b) -> p a b", a=8),
                    [pst], [xts])
        k.op("dq_pool", I("dma_start", out=XT[tt], in_=xts[:]), reads=[xts], writes=[XTb[tt]])

    def load_xT(xT):
        for tt in range(NT):
            k.op("sp", I("dma_start", out=xT[:, :, tt * 128:(tt + 1) * 128], in_=XT[tt]), reads=[XTb[tt]], writes=[xT],
                 nowaw=(tt > 0))

    class WLoader:
        def __init__(self, kc, ncols, nbuf=2, cast_eng="pool"):
            self.kc = kc
            self.ncols = ncols
            self.stg = Rot([k.sb([128, kc, ncols], F32, "wstg") for _ in range(nbuf)])
            self.wb = Rot([k.sb([128, kc, ncols], BF16, "wbf") for _ in range(nbuf)])
            self.cast_eng = cast_eng

        def load(self, Wt, W2d, col_list):
            st = self.stg.next()
            wb = self.wb.next()
            pos = 0
            first = True
            for (c0, n) in col_list:
                src = W2d[:, c0:c0 + n].rearrange("(kc p) n -> p kc n", p=128)
                k.op("sp", I("dma_start", out=st[:, :, pos:pos + n], in_=src), reads=[Wt], writes=[st], nowaw=not first)
                first = False
                pos += n
            if self.cast_eng == "act":
                k.op("act", I("copy", out=wb[:, :, 0:pos], in_=st[:, :, 0:pos]), reads=[st], writes=[wb])
            else:
                k.op(self.cast_eng, I("tensor_copy", out=wb[:, :, 0:pos], in_=st[:, :, 0:pos]), reads=[st], writes=[wb])
            return wb

    def layernorm_tile(z, idx, gt, bt, st6, mv, xn):
        for c in range(4):
            k.op("dve", I("bn_stats", out=st6[:, c, :], in_=z[:, c * 512:(c + 1) * 512]), reads=[z], writes=[st6])
        k.op("dve", I("bn_aggr", out=mv[:, 0:2], in_=st6[:].rearrange("p a b -> p (a b)")), reads=[st6], writes=[mv])
        k.op("dve", I("tensor_scalar", out=mv[:, 3:4], in0=mv[:, 1:2], scalar1=1e-5, scalar2=None, op0=ALU.add), reads=[mv], writes=[mv])
        k.op("act", I("activation", out=mv[:, 3:4], in_=mv[:, 3:4], func=AF.Sqrt), reads=[mv], writes=[mv])
        k.op("dve", I("reciprocal", out=mv[:, 2:3], in_=mv[:, 3:4]), reads=[mv], writes=[mv])
        k.op("dve", I("tensor_scalar", out=xn[:], in0=z[:], scalar1=mv[:, 0:1], scalar2=mv[:, 2:3], op0=ALU.subtract,
                      op1=ALU.mult), reads=[z, mv], writes=[xn])
        k.op("pool", I("tensor_tensor", out=xn[:], in0=xn[:], in1=gt[:], op=ALU.mult), reads=[xn, gt], writes=[xn])
        k.op("pool", I("tensor_tensor", out=xn[:], in0=xn[:], in1=bt[:], op=ALU.add), reads=[xn, bt], writes=[xn])

    class LNState:
        def __init__(self, lnrow):
            self.gt = k.sb([128, 2048], F32, "lng")
            self.bt = k.sb([128, 2048], F32, "lnb")
            k.op("sp", I("dma_start", out=self.gt[:], in_=dap(ln_g.t, lnrow * 2048, [[0, 128], [1, 2048]])), reads=[ln_g], writes=[self.gt])
            k.op("sp", I("dma_start", out=self.bt[:], in_=dap(ln_b.t, lnrow * 2048, [[0, 128], [1, 2048]])), reads=[ln_b], writes=[self.bt])
            self.st6 = k.sb([128, 4, 6], F32, "st6")
            self.mv = k.sb([128, 4], F32, "mv")
            self.xin = Rot([k.sb([128, 2048], F32, "lnx") for _ in range(2)])
            self.z = Rot([k.sb([128, 2048], F32, "lnz") for _ in range(2)])
            self.xbf = Rot([k.sb([128, 2048], BF16, "lnxb") for _ in range(2)])
            self.xts = Rot([k.sb([128, 16, 128], BF16, "xts") for _ in range(2)])

        def prefetch(self, Xsrc, tt):
            xin = self.xin.next()
            k.op("sp", I("dma_start", out=xin[:], in_=Xsrc[tt * 128:(tt + 1) * 128, :]), reads=[Xsrc], writes=[xin])
            return xin

        def run(self, xin, sub_aps, sub_tiles, Xdst, tt, pst_rot, final_out=None):
            z = self.z.next()
            for c in range(4):
                k.op("dve", I("scalar_tensor_tensor", out=z[:, c * 512:(c + 1) * 512], in0=xin[:, c * 512:(c + 1) * 512],
                              scalar=float(ALPHA), in1=sub_aps[c], op0=ALU.mult, op1=ALU.add),
                     reads=[xin] + sub_tiles, writes=[z])
            layernorm_tile(z, 0, self.gt, self.bt, self.st6, self.mv, z)
            k.op("dq_pool", I("dma_start", out=Xdst[tt * 128:(tt + 1) * 128, :], in_=z[:]), reads=[z], writes=[Xdst], nowaw=True)
            if final_out is not None:
                k.op("dq_pool", I("dma_start", out=final_out[tt * 128:(tt + 1) * 128, :], in_=z[:]), reads=[z], writes=[])
            else:
                xbf = self.xbf.next()
                k.op("act", I("copy", out=xbf[:], in_=z[:]), reads=[z], writes=[xbf])
                transpose_tile_to_XT(xbf, tt, pst_rot, self.xts)

    def phase_init():
        stage('phase_init')
        with k.phase():
            xin = Rot([k.sb([128, 2048], F32, "ix") for _ in range(2)])
            xbf = Rot([k.sb([128, 2048], BF16, "ixb") for _ in range(2)])
            xts = Rot([k.sb([128, 16, 128], BF16, "xts") for _ in range(2)])
            pst = Rot([k.ps([128, 1024], BF16, "pst") for _ in range(2)])
            for tt in range(NT):
                xi = xin.next()
                k.op("sp", I("dma_start", out=xi[:], in_=x0[tt * 128:(tt + 1) * 128, :]), reads=[x0], writes=[xi])
                xb = xbf.next()
                k.op("pool", I("tensor_copy", out=xb[:], in_=xi[:]), reads=[xi], writes=[xb])
                transpose_tile_to_XT(xb, tt, pst, xts)

    def phase_qkv(Wt, W2d, cacheK, cacheV, ncache, k_out_rows, v_out_rows, ks_out, vs_out, roll_k=None, roll_v=None):
        stage('phase_qkv')
        scale = 128 ** -0.5
        with k.phase():
            xT = k.sb([128, KC, T], BF16, "xT")
            load_xT(xT)
            wl = WLoader(KC, 256, nbuf=2, cast_eng="pool")
            pp = Rot([k.ps([128, 512], F32, "pp") for _ in range(4)])
            qts = Rot([k.sb([128, 512], BF16, "qts") for _ in range(3)])
            for which, dst in ((0, QT), (1, KTX)):
                for hp in range(8):
                    wb = wl.load(Wt, W2d, [(which * 2048 + hp * 256, 256)])
                    for hh in range(2):
                        h = hp * 2 + hh
                        for (t0, tn) in TOKB:
                            ps = pp.next()
                            for kc in range(KC):
                                k.op("pe", I("matmul", ps[:, 0:tn], lhsT=wb[:, kc, hh * 128:(hh + 1) * 128], rhs=xT[:, kc, t0:t0 + tn],
                                             start=(kc == 0), stop=(kc == KC - 1)), reads=[wb, xT], writes=[ps])
                            qs = qts.next()
                            copy_op(evac_eng(), qs[:, 0:tn], ps[:, 0:tn], [ps], [qs], scale=(scale if which == 0 else None))
                            k.op("dq_pool", I("dma_start", out=dst[h, :, t0:t0 + tn], in_=qs[:, 0:tn]), reads=[qs], writes=[dst])
            stage('qkv_tokmajor')
            vts = Rot([k.sb([128, 512], BF16, "vts") for _ in range(3)])
            vfs = Rot([k.sb([128, 512], F32, "vfs") for _ in range(3)])
            kdict = dict(k_out_rows)
            vdict = dict(v_out_rows)
            for which in (2, 1):
                for ns in range(4):
                    wb0 = wl.load(Wt, W2d, [(which * 2048 + ns * 512, 256)])
                    wb1 = wl.load(Wt, W2d, [(which * 2048 + ns * 512 + 256, 256)])
                    for tt in range(NT):
                        outd = (vdict if which == 2 else kdict)
                        need_f32 = (tt in outd) or tt == 16
                        if which == 1 and not need_f32:
                            continue
                        ps = pp.next()
                        for hf, wb in ((0, wb0), (1, wb1)):
                            for kc in range(KC):
                                k.op("pe", I("matmul", ps[:, hf * 256:(hf + 1) * 256], lhsT=xT[:, kc, tt * 128:(tt + 1) * 128], rhs=wb[:, kc, :],
                                             start=(kc == 0), stop=(kc == KC - 1)), reads=[wb, xT], writes=[ps])
                        src_t = ps
                        if need_f32:
                            vf = vfs.next()
                            copy_op("dve", vf[:], ps[:], [ps], [vf])
                            src_t = vf
                        if which == 2:
                            vt = vts.next()
                            copy_op("act", vt[:], src_t[:], [src_t], [vt])
                            k.op("dq_pool", I("dma_start", out=VX[tt * 128:(tt + 1) * 128, ns * 512:(ns + 1) * 512], in_=vt[:]), reads=[vt], writes=[VX], nowaw=True)
                        if need_f32:
                            if tt in outd:
                                dd = outd[tt]
                                k.op("dq_pool", I("dma_start", out=dd[1][:, ns * 512:(ns + 1) * 512], in_=vf[0:dd[0], :]), reads=[vf], writes=[])
                            if tt == 16:
                                so = vs_out if which == 2 else ks_out
                                k.op("dq_pool", I("dma_start", out=so[:, ns * 512:(ns + 1) * 512], in_=vf[0:64, :]), reads=[vf], writes=[])
        stage('qkv_caches')
        with k.phase():
            cin = Rot([k.sb([128, 2048], F32, "cin") for _ in range(2)])
            cbf = Rot([k.sb([128, 2048], BF16, "cbf") for _ in range(2)])
            kts = Rot([k.sb([128, 16, 128], BF16, "kts") for _ in range(2)])
            pst = Rot([k.ps([128, 1024], BF16, "pst") for _ in range(2)])
            for c in range(ncache // 128):
                ci = cin.next()
                k.op("sp", I("dma_start", out=ci[:], in_=cacheK[c * 128:(c + 1) * 128, :]), reads=[], writes=[ci])
                if roll_k is not None:
                    k.op("sp", I("dma_start", out=roll_k[c * 128:c * 128 + 64, :], in_=ci[64:128, :]), reads=[ci], writes=[])
                    if c > 0:
                        k.op("sp", I("dma_start", out=roll_k[c * 128 - 64:c * 128, :], in_=ci[0:64, :]), reads=[ci], writes=[])
                cb = cbf.next()
                k.op("pool", I("tensor_copy", out=cb[:], in_=ci[:]), reads=[ci], writes=[cb])
                kt = kts.next()
                for half in range(2):
                    ps = pst.next()
                    for j in range(8):
                        h = half * 8 + j
                        k.op("pe", I("transpose", out=ps[:, j * 128:(j + 1) * 128], in_=cb[:, h * 128:(h + 1) * 128], identity=ident[:]),
                             reads=[cb, ident], writes=[ps])
                    copy_op(evac_eng(), kt[:, half * 8:(half + 1) * 8, :], ps[:].rearrange("p (a b) -> p a b", a=8), [ps], [kt])
                k.op("dq_pool", I("dma_start", out=KTX[:, :, T + c * 128:T + (c + 1) * 128].rearrange("h p n -> p h n"), in_=kt[:]),
                     reads=[kt], writes=[KTX])
                ci = cin.next()
                k.op("sp", I("dma_start", out=ci[:], in_=cacheV[c * 128:(c + 1) * 128, :]), reads=[], writes=[ci])
                if roll_v is not None:
                    k.op("sp", I("dma_start", out=roll_v[c * 128:c * 128 + 64, :], in_=ci[64:128, :]), reads=[ci], writes=[])
                    if c > 0:
                        k.op("sp", I("dma_start", out=roll_v[c * 128 - 64:c * 128, :], in_=ci[0:64, :]), reads=[ci], writes=[])
                cb = cbf.next()
                k.op("pool", I("tensor_copy", out=cb[:], in_=ci[:]), reads=[ci], writes=[cb])
                k.op("dq_pool", I("dma_start", out=VX[T + c * 128:T + (c + 1) * 128, :], in_=cb[:]), reads=[cb], writes=[VX])

    def phase_wo_ln(Wt, W2d, Xsrc, Xdst, lnrow, final_out=None):
        stage('phase_wo_ln')
        with k.phase():
            wo = k.sb([128, KC, 2048], BF16, "wo")
            stg = Rot([k.sb([128, KC, 128], F32, "wostg") for _ in range(2)])
            for c in range(16):
                st = stg.next()
                k.op("sp", I("dma_start", out=st[:], in_=W2d[:, c * 128:(c + 1) * 128].rearrange("(kc p) n -> p kc n", p=128)), reads=[Wt], writes=[st])
                k.op("pool", I("tensor_copy", out=wo[:, :, c * 128:(c + 1) * 128], in_=st[:]), reads=[st], writes=[wo], nowaw=(c > 0))
            ln = LNState(lnrow)
            att = Rot([k.sb([128, KC, 128], BF16, "att") for _ in range(2)])
            pp = Rot([k.ps([128, 2048], F32, "pwo") for _ in range(1)])
            pst = Rot([k.ps([128, 1024], BF16, "pst") for _ in range(2)])
            for tt in range(NT):
                xin = ln.prefetch(Xsrc, tt)
                at = att.next()
                k.op("sp", I("dma_start", out=at[:], in_=ATd[:, :, tt * 128:(tt + 1) * 128].rearrange("h p n -> p h n")), reads=[ATd], writes=[at])
                ps = pp.next()
                for ns in range(4):
                    for kc in range(KC):
                        k.op("pe", I("matmul", ps[:, ns * 512:(ns + 1) * 512], lhsT=at[:, kc, :], rhs=wo[:, kc, ns * 512:(ns + 1) * 512],
                                     start=(kc == 0), stop=(kc == KC - 1)), reads=[at, wo], writes=[ps])
                ln.run(xin, [ps[:, c * 512:(c + 1) * 512] for c in range(4)], [ps], Xdst, tt, pst, final_out)

    def phase_ln_from_dram(Ysrc_tiles, Ybufs, Xsrc, Xdst, lnrow, final_out=None):
        stage('phase_ln_from_dram')
        with k.phase():
            ln = LNState(lnrow)
            yr = Rot([k.sb([128, 2048], F32, "lny") for _ in range(2)])
            pst = Rot([k.ps([128, 1024], BF16, "pst") for _ in range(2)])
            for tt in range(NT):
                xin = ln.prefetch(Xsrc, tt)
                y = yr.next()
                k.op("sp", I("dma_start", out=y[:], in_=Ysrc_tiles[tt]), reads=[Ybufs[tt]], writes=[y])
                ln.run(xin, [y[:, c * 512:(c + 1) * 512] for c in range(4)], [y], Xdst, tt, pst, final_out)

    def phase_attn_a(j, Xsrc, Xdst, lnrow):
        stage('phase_attn_a')
        with k.phase():
            ext_s = k.sb([16, 384], F32, "ext_s")
            k.op("sp", I("dma_start", out=ext_s[:, 0:257], in_=a_rel[j]), reads=[a_rel], writes=[ext_s])
            k.op("dve", I("tensor_copy", out=ext_s[:, 257:384], in_=ext_s[:, 256:257].to_broadcast([16, 127])), reads=[ext_s], writes=[ext_s])
            k.op("sp", I("dma_start", out=EXT[:, :], in_=ext_s[:]), reads=[ext_s], writes=[EXT])
            cst = k.sb([128, 16], F32, "cst")
            k.op("sp", I("dma_start", out=cst[:], in_=dap(a_rel.t, j * 16 * 257 + 256, [[0, 128], [257, 16]]), allow_slow_non_contiguous=True), reads=[a_rel], writes=[cst])
            aor = Rot([k.sb([128, 512], BF16, "ao") for _ in range(3)])
            qh = Rot([k.sb([128, T], BF16, "qh") for _ in range(2)])
            kh = Rot([k.sb([128, KEXT], BF16, "kh") for _ in range(2)])
            vh = Rot([k.sb([128, 25, 128], BF16, "vh") for _ in range(2)])
            bf32 = Rot([k.sb([128, 2, 128], F32, "bf32") for _ in range(2)])
            bt = Rot([k.sb([128, 4, 128], BF16, "bt") for _ in range(2)])
            ptr = Rot([k.sb([128, 5, 128], BF16, "pt") for _ in range(3)])
            ps_s = Rot([k.ps([128, 1024], F32, "ps_s") for _ in range(2)])
            ps_o = Rot([k.ps([128, 512], F32, "ps_o") for _ in range(2)])
            ps_z = Rot([k.ps([128, 512], F32, "ps_z") for _ in range(2)])
            rz = Rot([k.sb([128, 512], F32, "rz") for _ in range(2)])
            for h in range(16):
                q = qh.next(); kk = kh.next(); v = vh.next()
                k.op("sp", I("dma_start", out=q[:], in_=QT[h]), reads=[QT], writes=[q])
                k.op("sp", I("dma_start", out=kk[:, 0:T + 512], in_=KTX[h, :, 0:T + 512]), reads=[KTX], writes=[kk])
                k.op("sp", I("dma_start", out=v[:, 0:21, :], in_=VX[0:21 * 128, h * 128:(h + 1) * 128].rearrange("(t p) d -> p t d", p=128)),
                     reads=[VX], writes=[v])
                bf = bf32.next()
                k.op("sp", I("dma_start", out=bf[:, 0, :], in_=dap(EXT.t, h * 384 + 1, [[1, 128], [1, 128]])), reads=[EXT], writes=[bf])
                k.op("sp", I("dma_start", out=bf[:, 1, :], in_=dap(EXT.t, h * 384 + 129, [[1, 128], [1, 128]])), reads=[EXT], writes=[bf], nowaw=True)
                b = bt.next()
                k.op("dve", I("tensor_tensor", out=b[:, 0, :], in0=bf[:, 0, :], in1=m0b[:], op=ALU.add), reads=[bf, m0b], writes=[b])
                k.op("dve", I("tensor_copy", out=b[:, 1, :], in_=bf[:, 1, :]), reads=[bf], writes=[b])
                k.op("dve", I("tensor_scalar", out=b[:, 2, :], in0=zeros_b[:], scalar1=cst[:, h:h + 1], scalar2=None, op0=ALU.add), reads=[zeros_b, cst], writes=[b])
                k.op("dve", I("tensor_scalar", out=b[:, 3, :], in0=m4b[:], scalar1=cst[:, h:h + 1], scalar2=None, op0=ALU.add), reads=[m4b, cst], writes=[b])
                btype = {0: 0, 1: 1, 2: 2, 3: 2, 4: 3}
                for sbk in range(5):
                    po = ps_o.next(); pz = ps_z.next()
                    if sbk < 4:
                        blocks = [(sbk * 4 + i, 128) for i in range(4)]
                    else:
                        blocks = [(16, 64)]
                    for bi, (bq, nq) in enumerate(blocks):
                        kts = []
                        if bq < 16:
                            for ty in range(4, -1, -1):
                                kt_ = bq - ty
                                if kt_ < 0:
                                    continue
                                kts.append((kk[:, kt_ * 128:(kt_ + 1) * 128], v[:, kt_, :], 128, b[:, btype[ty], 0:nq]))
                        else:
                            for c in range(4):
                                ty = 1 if c == 3 else 2
                                kts.append((kk[:, T + c * 128:T + (c + 1) * 128], v[:, 17 + c, :], 128, b[:, ty, 0:nq]))
                            kts.append((kk[:, 2048:2112], v[0:64, 16, :], 64, b[64:128, 0, 0:nq]))
                        pss = ps_s.next()
                        pt = ptr.next()
                        qa = q[:, bq * 128:bq * 128 + nq]
                        for s, (ka, va, nk, ba) in enumerate(kts):
                            k.op("pe", I("matmul", pss[0:nk, s * 128:s * 128 + nq], lhsT=ka, rhs=qa, start=True, stop=False), reads=[kk, q], writes=[pss])
                            k.op("pe", I("matmul", pss[0:nk, s * 128:s * 128 + nq], lhsT=(Jb[:] if nk == 128 else Jb[64:128, 0:64]), rhs=ba, start=False, stop=True), reads=[Jb, b], writes=[pss])
                        for s, (ka, va, nk, ba) in enumerate(kts):
                            k.op("act", I("activation", out=pt[0:nk, s, 0:nq], in_=pss[0:nk, s * 128:s * 128 + nq], func=AF.Exp), reads=[pss], writes=[pt])
                        for s, (ka, va, nk, ba) in enumerate(kts):
                            k.op("pe", I("matmul", po[:, bi * 128:bi * 128 + nq], lhsT=va, rhs=pt[0:nk, s, 0:nq], start=(s == 0), stop=(s == len(kts) - 1)),
                                 reads=[v, pt], writes=[po])
                        for s, (ka, va, nk, ba) in enumerate(kts):
                            k.op("pe", I("matmul", pz[:, bi * 128:bi * 128 + nq], lhsT=ones_b[0:nk, :], rhs=pt[0:nk, s, 0:nq], start=(s == 0), stop=(s == len(kts) - 1)),
                                 reads=[ones_b, pt], writes=[pz])
                    ncol = 512 if sbk < 4 else 64
                    r = rz.next()
                    k.op("dve", I("reciprocal", out=r[:, 0:ncol], in_=pz[:, 0:ncol]), reads=[pz], writes=[r])
                    ao = aor.next()
                    if sbk == 4:
                        k.op("pool", I("memset", ao[:, 64:128], 0.0), writes=[ao])
                    k.op("dve", I("tensor_tensor", out=ao[:, 0:ncol], in0=po[:, 0:ncol], in1=r[:, 0:ncol], op=ALU.mult),
                         reads=[po, r], writes=[ao], nowaw=True)
                    nst = 512 if sbk < 4 else 128
                    k.op("dq_pool", I("dma_start", out=ATd[h, :, sbk * 512:sbk * 512 + nst], in_=ao[:, 0:nst]), reads=[ao], writes=[ATd], nowaw=True)
        phase_wo_ln(a_wo, a_wo[j], Xsrc, Xdst, lnrow)

    def phase_attn_c(Xsrc, Xdst, lnrow):
        stage('phase_attn_c')
        with k.phase():
            aor = Rot([k.sb([128, 512], BF16, "ao") for _ in range(2)])
            qh = Rot([k.sb([128, T], BF16, "qh") for _ in range(2)])
            kh = Rot([k.sb([128, KEXT], BF16, "kh") for _ in range(2)])
            vh = Rot([k.sb([128, 25, 128], BF16, "vh") for _ in range(2)])
            e1r = Rot([k.sb([128, 512], F32, "e1") for _ in range(2)])
            lspr = Rot([k.sb([128, 512], F32, "lsp") for _ in range(2)])
            lsnr = Rot([k.sb([128, 512], F32, "lsn") for _ in range(2)])
            t1r = Rot([k.sb([128, 512], F32, "t1") for _ in range(2)])
            wr = Rot([k.sb([128, 512], BF16, "w") for _ in range(2)])
            carry = k.sb([128, 512], F32, "carry")
            ps_z = Rot([k.ps([128, 512], F32, "ps_z") for _ in range(2)])
            ps_a = Rot([k.ps([128, 512], F32, "ps_a") for _ in range(2)])
            ps_c = Rot([k.ps([128, 512], F32, "ps_c") for _ in range(2)])
            ps_o = Rot([k.ps([128, 512], F32, "ps_o") for _ in range(2)])
            for h in range(16):
                q = qh.next(); kk = kh.next(); v = vh.next()
                k.op("sp", I("dma_start", out=q[:], in_=QT[h]), reads=[QT], writes=[q])
                k.op("sp", I("dma_start", out=kk[:], in_=KTX[h]), reads=[KTX], writes=[kk])
                k.op("sp", I("dma_start", out=v[:], in_=VX[:, h * 128:(h + 1) * 128].rearrange("(t p) d -> p t d", p=128)), reads=[VX], writes=[v])
                for sbk in range(5):
                    if sbk < 4:
                        qc0, nq = sbk * 512, 512
                        kts = [(128, kk[:, kt * 128:(kt + 1) * 128], v[:, kt, :], max(0, kt - 4 * sbk) * 128, kt >= 4 * sbk)
                               for kt in range(4 * sbk + 3, -1, -1)]
                    else:
                        qc0, nq = 2048, 64
                        kts = [(64, kk[:, 2048:2112], v[0:64, 16, :], 0, True)]
                        kts += [(128, kk[:, T + c * 128:T + (c + 1) * 128], v[:, 17 + c, :], 0, False) for c in range(7, -1, -1)]
                    po = ps_o.next()
                    k.op("pe", I("matmul", po[:, 0:nq], lhsT=zeros_b[:], rhs=q[:, qc0:qc0 + nq], start=True, stop=False), reads=[zeros_b, q], writes=[po])
                    k.op("pool", I("memset", carry[:], 0.0), writes=[carry])
                    for si, (nk, ka, va, c0, diag) in enumerate(kts):
                        last = (si == len(kts) - 1)
                        dn = min(128, nq - c0)
                        pz = ps_z.next()
                        k.op("pe", I("matmul", pz[0:nk, c0:nq], lhsT=ka, rhs=q[:, qc0 + c0:qc0 + nq], start=True, stop=True), reads=[kk, q], writes=[pz])
                        e1 = e1r.next()
                        k.op("act", I("activation", out=e1[0:nk, c0:nq], in_=pz[0:nk, c0:nq], func=AF.Exp, scale=-1.0), reads=[pz], writes=[e1])
                        lsp = lspr.next()
                        k.op("act", I("activation", out=lsp[0:nk, c0:nq], in_=e1[0:nk, c0:nq], func=AF.Ln, bias=ones_f[0:nk, 0:1], scale=1.0), reads=[e1, ones_f], writes=[lsp])
                        lsn = lsnr.next()
                        k.op("dve", I("tensor_tensor", out=lsn[0:nk, c0:nq], in0=pz[0:nk, c0:nq], in1=lsp[0:nk, c0:nq], op=ALU.add), reads=[pz, lsp], writes=[lsn])
                        if diag:
                            k.op("pool", I("tensor_tensor", out=lsn[0:nk, c0:c0 + dn], in0=lsn[0:nk, c0:c0 + dn], in1=mcaus[0:nk, 0:dn], op=ALU.mult),
                                 reads=[lsn, mcaus], writes=[lsn])
                        pa = ps_a.next(); pc = ps_c.next()
                        k.op("pe", I("matmul", pa[0:nk, c0:nq], lhsT=Uf[0:nk, 0:nk], rhs=lsn[0:nk, c0:nq], start=True, stop=True), reads=[Uf, lsn], writes=[pa])
                        k.op("pe", I("matmul", pc[:, c0:nq], lhsT=ones_f[0:nk, :], rhs=lsn[0:nk, c0:nq], start=True, stop=True), reads=[ones_f, lsn], writes=[pc])
                        t1 = t1r.next()
                        k.op("dve", I("tensor_tensor", out=t1[0:nk, c0:nq], in0=pa[0:nk, c0:nq], in1=lsp[0:nk, c0:nq], op=ALU.add), reads=[pa, lsp], writes=[t1])
                        k.op("pool", I("tensor_tensor", out=t1[0:nk, c0:nq], in0=t1[0:nk, c0:nq], in1=carry[0:nk, c0:nq], op=ALU.add), reads=[t1, carry], writes=[t1])
                        w = wr.next()
                        k.op("act", I("activation", out=w[0:nk, c0:nq], in_=t1[0:nk, c0:nq], func=AF.Exp, scale=-1.0), reads=[t1], writes=[w])
                        if diag:
                            k.op("pool", I("tensor_tensor", out=w[0:nk, c0:c0 + dn], in0=w[0:nk, c0:c0 + dn], in1=mcausb[0:nk, 0:dn], op=ALU.mult),
                                 reads=[w, mcausb], writes=[w])
                        if not last:
                            k.op("dve", I("tensor_tensor", out=carry[:, c0:nq], in0=pc[:, c0:nq], in1=carry[:, c0:nq], op=ALU.add), reads=[pc, carry], writes=[carry])
                        k.op("pe", I("matmul", po[:, c0:nq], lhsT=va, rhs=w[0:nk, c0:nq], start=False, stop=last), reads=[v, w], writes=[po])
                    ao = aor.next()
                    if sbk == 4:
                        k.op("pool", I("memset", ao[:, 64:128], 0.0), writes=[ao])
                    copy_op("act", ao[:, 0:nq], po[:, 0:nq], [po], [ao])
                    nst = 512 if sbk < 4 else 128
                    k.op("sp", I("dma_start", out=ATd[h, :, sbk * 512:sbk * 512 + nst], in_=ao[:, 0:nst]), reads=[ao], writes=[ATd], nowaw=True)
        phase_wo_ln(c_wo, c_wo[:, :], Xsrc, Xdst, lnrow)

    def rmsnorm_psum(ps, gtile, out_ap, st6, mv, reads_extra, out_tile):
        k.op("dve", I("bn_stats", out=st6[:, 0, :], in_=ps[:]), reads=[ps], writes=[st6])
        k.op("dve", I("bn_aggr", out=mv[:, 0:2], in_=st6[:, 0, :]), reads=[st6], writes=[mv])
        k.op("dve", I("scalar_tensor_tensor", out=mv[:, 2:3], in0=mv[:, 0:1], scalar=mv[:, 0:1], in1=mv[:, 1:2], op0=ALU.mult, op1=ALU.add), reads=[mv], writes=[mv])
        k.op("dve", I("tensor_scalar", out=mv[:, 2:3], in0=mv[:, 2:3], scalar1=1e-6, scalar2=None, op0=ALU.add), reads=[mv], writes=[mv])
        k.op("act", I("activation", out=mv[:, 2:3], in_=mv[:, 2:3], func=AF.Sqrt), reads=[mv], writes=[mv])
        k.op("dve", I("reciprocal", out=mv[:, 3:4], in_=mv[:, 2:3]), reads=[mv], writes=[mv])
        k.op("dve", I("scalar_tensor_tensor", out=out_ap, in0=ps[:], scalar=mv[:, 3:4], in1=gtile[:], op0=ALU.mult, op1=ALU.mult), reads=[ps, mv, gtile], writes=[out_tile])

    def rope_ops(src3, cs, sn, dst3, nh, tmp_a, tmp_b, reads, dst_tile, scale=None):
        cb = sap(cs, 0, [[0, nh], [1, 32]])
        sb_ = sap(sn, 0, [[0, nh], [1, 32]])
        x1 = src3[:, :, 0:32]; x2 = src3[:, :, 32:64]
        ta = tmp_a[:, 0:nh * 32].rearrange("p (h r) -> p h r", r=32)
        tb = tmp_b[:, 0:nh * 32].rearrange("p (h r) -> p h r", r=32)
        k.op("dve", I("tensor_tensor", out=ta, in0=x1, in1=cb, op=ALU.mult), reads=reads + [cs], writes=[tmp_a])
        k.op("dve", I("tensor_tensor", out=tb, in0=x2, in1=sb_, op=ALU.mult), reads=reads + [sn], writes=[tmp_b])
        k.op("dve", I("tensor_tensor", out=dst3[:, :, 0:32], in0=ta, in1=tb, op=ALU.subtract), reads=[tmp_a, tmp_b], writes=[dst_tile])
        k.op("dve", I("tensor_tensor", out=ta, in0=x1, in1=sb_, op=ALU.mult), reads=reads + [sn], writes=[tmp_a])
        k.op("dve", I("tensor_tensor", out=tb, in0=x2, in1=cb, op=ALU.mult), reads=reads + [cs], writes=[tmp_b])
        k.op("dve", I("tensor_tensor", out=dst3[:, :, 32:64], in0=ta, in1=tb, op=ALU.add), reads=[tmp_a, tmp_b], writes=[dst_tile], nowaw=True)

    def phase_mla(Xsrc, Xdst, lnrow):
        stage('phase_mla_in')
        sc = 192 ** -0.5
        with k.phase():
            xT = k.sb([128, KC, T], BF16, "xT")
            load_xT(xT)
            winb = k.sb([128, KC, 1088], BF16, "winb")
            stg = Rot([k.sb([128, KC, 128], F32, "wstg") for _ in range(2)])
            for c in range(9):
                n = 128 if c < 8 else 64
                st = stg.next()
                k.op("sp", I("dma_start", out=st[:, :, 0:n], in_=b_win[:, c * 128:c * 128 + n].rearrange("(kc p) n -> p kc n", p=128)), reads=[b_win], writes=[st])
                k.op("pool", I("tensor_copy", out=winb[:, :, c * 128:c * 128 + n], in_=st[:, :, 0:n]), reads=[st], writes=[winb], nowaw=(c > 0))
            qn_t = k.sb([128, 512], F32, "qn_t"); kvn_t = k.sb([128, 512], F32, "kvn_t")
            k.op("sp", I("dma_start", out=qn_t[:], in_=dap(b_qn.t, 0, [[0, 128], [1, 512]])), reads=[b_qn], writes=[qn_t])
            k.op("sp", I("dma_start", out=kvn_t[:], in_=dap(b_kvn.t, 0, [[0, 128], [1, 512]])), reads=[b_kvn], writes=[kvn_t])
            st6 = k.sb([128, 1, 6], F32, "st6"); mv = k.sb([128, 4], F32, "mv")
            csr = Rot([k.sb([128, 32], F32, "cs") for _ in range(2)]); snr = Rot([k.sb([128, 32], F32, "sn") for _ in range(2)])
            p0r = Rot([k.ps([128, 512], F32, "p0") for _ in range(2)])
            p1r = Rot([k.ps([128, 512], F32, "p1") for _ in range(2)])
            p2r = Rot([k.ps([128, 64], F32, "p2") for _ in range(1)])
            pst = Rot([k.ps([128, 1024], BF16, "pst") for _ in range(2)])
            cqb = Rot([k.sb([128, 512], BF16, "cqb") for _ in range(2)])
            ckf = Rot([k.sb([128, 512], F32, "ckf") for _ in range(2)])
            ckb = Rot([k.sb([128, 512], BF16, "ckb") for _ in range(2)])
            krf = Rot([k.sb([128, 1, 64], F32, "krf") for _ in range(2)])
            krb = Rot([k.sb([128, 128], BF16, "krb") for _ in range(2)])
            ta = k.sb([128, 512], F32, "ta"); tb = k.sb([128, 512], F32, "tb")
            tsb = Rot([k.sb([128, 9, 128], BF16, "tsb") for _ in range(2)])

            def transposes_out(cq_bf, ck_bf, kr_bf, col0):
                ps = pst.next()
                ts = tsb.next()
                n = 0
                srcs = []
                if cq_bf is not None:
                    srcs += [(cq_bf, c) for c in range(4)]
                srcs += [(ck_bf, c) for c in range(4)]
                for (tl, c) in srcs:
                    k.op("pe", I("transpose", out=ps[:, n * 128:(n + 1) * 128], in_=tl[:, c * 128:(c + 1) * 128], identity=ident[:]), reads=[tl, ident], writes=[ps])
                    n += 1
                copy_op(evac_eng(), ts[:, 0:n, :], ps[:, 0:n * 128].rearrange("p (a b) -> p a b", b=128), [ps], [ts])
                ps2 = pst.next()
                k.op("pe", I("transpose", out=ps2[:, 0:128], in_=kr_bf[:], identity=ident[:]), reads=[kr_bf, ident], writes=[ps2])
                copy_op(evac_eng(), ts[:, 8, :], ps2[:, 0:128], [ps2], [ts])
                o = 0
                if cq_bf is not None:
                    k.op("sp", I("dma_start", out=CQT[:, :, col0:col0 + 128].rearrange("c p n -> p c n"), in_=ts[:, 0:4, :]), reads=[ts], writes=[CQT], nowaw=True)
                    o = 4
                k.op("sp", I("dma_start", out=CKVT[:, :, col0:col0 + 128].rearrange("c p n -> p c n"), in_=ts[:, o:o + 4, :]), reads=[ts], writes=[CKVT], nowaw=True)
                k.op("sp", I("dma_start", out=KR[:, col0:col0 + 128], in_=ts[:, 8, :]), reads=[ts], writes=[KR], nowaw=True)

            for tt in range(NT):
                cs = csr.next(); sn = snr.next()
                k.op("sp", I("dma_start", out=cs[:], in_=ropec[tt * 128:(tt + 1) * 128, :]), reads=[ropec], writes=[cs])
                k.op("sp", I("dma_start", out=sn[:], in_=ropes[tt * 128:(tt + 1) * 128, :]), reads=[ropes], writes=[sn])
                p0 = p0r.next(); p1 = p1r.next(); p2 = p2r.next()
                for (pt_, c0, n) in ((p0, 0, 512), (p1, 512, 512), (p2, 1024, 64)):
                    for kc in range(KC):
                        k.op("pe", I("matmul", pt_[:, 0:n], lhsT=xT[:, kc, tt * 128:(tt + 1) * 128], rhs=winb[:, kc, c0:c0 + n], start=(kc == 0), stop=(kc == KC - 1)),
                             reads=[xT, winb], writes=[pt_])
                cq = cqb.next()
                rmsnorm_psum(p0, qn_t, cq[:], st6, mv, [], cq)
                cf = ckf.next()
                rmsnorm_psum(p1, kvn_t, cf[:], st6, mv, [], cf)
                if tt < 16:
                    k.op("sp", I("dma_start", out=obc[tt * 128:(tt + 1) * 128, :], in_=cf[:]), reads=[cf], writes=[])
                else:
                    k.op("sp", I("dma_start", out=sbc[:, :], in_=cf[0:64, :]), reads=[cf], writes=[])
                cb_ = ckb.next()
                k.op("act", I("copy", out=cb_[:], in_=cf[:]), reads=[cf], writes=[cb_])
                kf = krf.next()
                rope_ops(p2[:].rearrange("p (h c) -> p h c", c=64), cs, sn, kf[:], 1, ta, tb, [p2], kf)
                if tt < 16:
                    k.op("sp", I("dma_start", out=obr[tt * 128:(tt + 1) * 128, :], in_=kf[:, 0, :]), reads=[kf], writes=[])
                else:
                    k.op("sp", I("dma_start", out=sbr[:, :], in_=kf[0:64, 0, :]), reads=[kf], writes=[])
                kb = krb.next()
                k.op("act", I("copy", out=kb[:, 0:64], in_=kf[:, 0, :]), reads=[kf], writes=[kb])
                k.op("act", I("copy", out=kb[:, 64:128], in_=kf[:, 0, :]), reads=[kf], writes=[kb], nowaw=True)
                transposes_out(cq, cb_, kb, tt * 128)
            for c in range(8):
                cf = ckf.next()
                k.op("sp", I("dma_start", out=cf[:], in_=cb_c[c * 128:(c + 1) * 128, :]), reads=[], writes=[cf])
                cb_ = ckb.next()
                k.op("act", I("copy", out=cb_[:], in_=cf[:]), reads=[cf], writes=[cb_])
                kf = krf.next()
                k.op("sp", I("dma_start", out=kf[:, 0, :], in_=cb_r[c * 128:(c + 1) * 128, :]), reads=[], writes=[kf])
                kb = krb.next()
                k.op("act", I("copy", out=kb[:, 0:64], in_=kf[:, 0, :]), reads=[kf], writes=[kb])
                k.op("act", I("copy", out=kb[:, 64:128], in_=kf[:, 0, :]), reads=[kf], writes=[kb], nowaw=True)
                transposes_out(None, cb_, kb, T + c * 128)
        stage('phase_mla_q')
        with k.phase():
            cqT = k.sb([128, 4, T], BF16, "cqT")
            k.op("sp", I("dma_start", out=cqT[:], in_=CQT[:, :, :].rearrange("c p n -> p c n")), reads=[CQT], writes=[cqT])
            wqn = k.sb([128, 4, 2048], BF16, "wqn"); wqr = k.sb([128, 4, 1024], BF16, "wqr")
            stg = Rot([k.sb([128, 4, 768], F32, "wstg") for _ in range(2)])
            for c in range(4):
                st = stg.next()
                k.op("sp", I("dma_start", out=st[:], in_=b_wqb[:, c * 768:(c + 1) * 768].rearrange("(kc p) n -> p kc n", p=128)), reads=[b_wqb], writes=[st])
                for kc in range(4):
                    sv = st[:, kc, :].rearrange("p (h c) -> p h c", c=192)
                    k.op("pool", I("tensor_copy", out=wqn[:, kc, c * 512:(c + 1) * 512].rearrange("p (h c) -> p h c", c=128), in_=sv[:, :, 0:128]), reads=[st], writes=[wqn], nowaw=True)
                    k.op("pool", I("tensor_copy", out=wqr[:, kc, c * 256:(c + 1) * 256].rearrange("p (h c) -> p h c", c=64), in_=sv[:, :, 128:192]), reads=[st], writes=[wqr], nowaw=True)
            pp = Rot([k.ps([128, 512], F32, "pp") for _ in range(3)])
            qts = Rot([k.sb([128, 512], BF16, "qts") for _ in range(3)])
            for h in range(16):
                for (t0, tn) in TOKB:
                    ps = pp.next()
                    for kc in range(4):
                        k.op("pe", I("matmul", ps[:, 0:tn], lhsT=wqn[:, kc, h * 128:(h + 1) * 128], rhs=cqT[:, kc, t0:t0 + tn], start=(kc == 0), stop=(kc == 3)),
                             reads=[wqn, cqT], writes=[ps])
                    qs = qts.next()
                    copy_op(evac_eng(), qs[:, 0:tn], ps[:, 0:tn], [ps], [qs], scale=sc)
                    k.op("sp", I("dma_start", out=QT[h, :, t0:t0 + tn], in_=qs[:, 0:tn]), reads=[qs], writes=[QT], nowaw=True)
            csr = Rot([k.sb([128, 32], F32, "cs") for _ in range(2)]); snr = Rot([k.sb([128, 32], F32, "sn") for _ in range(2)])
            ta = k.sb([128, 512], F32, "ta"); tb = k.sb([128, 512], F32, "tb")
            qrf = Rot([k.sb([128, 16, 64], F32, "qrf") for _ in range(2)])
            qrb = Rot([k.sb([128, 1024], BF16, "qrb") for _ in range(2)])
            pst = Rot([k.ps([128, 1024], BF16, "pst") for _ in range(2)])
            qrt = Rot([k.sb([128, 8, 128], BF16, "qrt") for _ in range(2)])
            for tt in range(NT):
                cs = csr.next(); sn = snr.next()
                k.op("sp", I("dma_start", out=cs[:], in_=ropec[tt * 128:(tt + 1) * 128, :]), reads=[ropec], writes=[cs])
                k.op("sp", I("dma_start", out=sn[:], in_=ropes[tt * 128:(tt + 1) * 128, :]), reads=[ropes], writes=[sn])
                qf = qrf.next()
                for hg in range(2):
                    ps = pp.next()
                    for kc in range(4):
                        k.op("pe", I("matmul", ps[:, :], lhsT=cqT[:, kc, tt * 128:(tt + 1) * 128], rhs=wqr[:, kc, hg * 512:(hg + 1) * 512], start=(kc == 0), stop=(kc == 3)),
                             reads=[wqr, cqT], writes=[ps])
                    rope_ops(ps[:].rearrange("p (h c) -> p h c", c=64), cs, sn, qf[:, hg * 8:(hg + 1) * 8, :], 8, ta, tb, [ps], qf)
                qb = qrb.next()
                k.op("act", I("activation", out=qb[:], in_=qf[:].rearrange("p h c -> p (h c)"), func=AF.Copy, scale=float(sc)), reads=[qf], writes=[qb])
                ps2 = pst.next()
                for j8 in range(8):
                    k.op("pe", I("transpose", out=ps2[:, j8 * 128:(j8 + 1) * 128], in_=qb[:, j8 * 128:(j8 + 1) * 128], identity=ident[:]), reads=[qb, ident], writes=[ps2])
                qt_ = qrt.next()
                copy_op(evac_eng(), qt_[:], ps2[:].rearrange("p (a b) -> p a b", a=8), [ps2], [qt_])
                k.op("sp", I("dma_start", out=QR[:, :, tt * 128:(tt + 1) * 128].rearrange("a p n -> p a n"), in_=qt_[:]), reads=[qt_], writes=[QR], nowaw=True)
        stage('phase_mla_kv')
        with k.phase():
            ckvT = k.sb([128, 4, KEXT], BF16, "ckvT")
            k.op("sp", I("dma_start", out=ckvT[:], in_=CKVT[:, :, :].rearrange("c p n -> p c n")), reads=[CKVT], writes=[ckvT])
            wkn = k.sb([128, 4, 2048], BF16, "wkn"); wkv = k.sb([128, 4, 2048], BF16, "wkv")
            stg = Rot([k.sb([128, 4, 1024], F32, "wstg") for _ in range(2)])
            for c in range(4):
                st = stg.next()
                k.op("sp", I("dma_start", out=st[:], in_=b_wkvb[:, c * 1024:(c + 1) * 1024].rearrange("(kc p) n -> p kc n", p=128)), reads=[b_wkvb], writes=[st])
                for kc in range(4):
                    sv = st[:, kc, :].rearrange("p (h c) -> p h c", c=256)
                    k.op("pool", I("tensor_copy", out=wkn[:, kc, c * 512:(c + 1) * 512].rearrange("p (h c) -> p h c", c=128), in_=sv[:, :, 0:128]), reads=[st], writes=[wkn], nowaw=True)
                    k.op("pool", I("tensor_copy", out=wkv[:, kc, c * 512:(c + 1) * 512].rearrange("p (h c) -> p h c", c=128), in_=sv[:, :, 128:256]), reads=[st], writes=[wkv], nowaw=True)
            pp = Rot([k.ps([128, 512], F32, "pp") for _ in range(4)])
            qts = Rot([k.sb([128, 512], BF16, "qts") for _ in range(3)])
            KB = [(i * 512, 512) for i in range(6)] + [(3072, 128)]
            for h in range(16):
                for (t0, tn) in KB:
                    ps = pp.next()
                    for kc in range(4):
                        k.op("pe", I("matmul", ps[:, 0:tn], lhsT=wkn[:, kc, h * 128:(h + 1) * 128], rhs=ckvT[:, kc, t0:t0 + tn], start=(kc == 0), stop=(kc == 3)),
                             reads=[wkn, ckvT], writes=[ps])
                    qs = qts.next()
                    copy_op(evac_eng(), qs[:, 0:tn], ps[:, 0:tn], [ps], [qs])
                    k.op("sp", I("dma_start", out=KTX[h, :, t0:t0 + tn], in_=qs[:, 0:tn]), reads=[qs], writes=[KTX], nowaw=True)
            for kt in range(25):
                for ns in range(4):
                    ps = pp.next()
                    for kc in range(4):
                        k.op("pe", I("matmul", ps[:, :], lhsT=ckvT[:, kc, kt * 128:(kt + 1) * 128], rhs=wkv[:, kc, ns * 512:(ns + 1) * 512], start=(kc == 0), stop=(kc == 3)),
                             reads=[wkv, ckvT], writes=[ps])
                    qs = qts.next()
                    copy_op(evac_eng(), qs[:], ps[:], [ps], [qs])
                    k.op("sp", I("dma_start", out=VX[kt * 128:(kt + 1) * 128, ns * 512:(ns + 1) * 512], in_=qs[:]), reads=[qs], writes=[VX], nowaw=True)
        stage('phase_mla_attn')
        with k.phase():
            aor = Rot([k.sb([128, 512], BF16, "ao") for _ in range(2)])
            qh = Rot([k.sb([128, T], BF16, "qh") for _ in range(2)])
            qrh = Rot([k.sb([128, T], BF16, "qrh") for _ in range(2)])
            kh = Rot([k.sb([128, KEXT], BF16, "kh") for _ in range(2)])
            vh = Rot([k.sb([128, 25, 128], BF16, "vh") for _ in range(2)])
            krs = k.sb([128, KEXT], BF16, "krs")
            k.op("sp", I("dma_start", out=krs[:], in_=KR[:, :]), reads=[KR], writes=[krs])
            ptr = Rot([k.sb([128, 512], BF16, "pt") for _ in range(3)])
            rz = Rot([k.sb([128, 512], F32, "rz") for _ in range(2)])
            ps_s = Rot([k.ps([128, 512], F32, "ps_s") for _ in range(3)])
            ps_o = Rot([k.ps([128, 512], F32, "ps_o") for _ in range(2)])
            ps_zz = Rot([k.ps([128, 512], F32, "ps_zz") for _ in range(2)])
            for h in range(16):
                q = qh.next(); kk = kh.next(); v = vh.next()
                hb = (h % 2) * 64
                k.op("sp", I("dma_start", out=q[:], in_=QT[h]), reads=[QT], writes=[q])
                if h % 2 == 0:
                    qr = qrh.next()
                    k.op("sp", I("dma_start", out=qr[:], in_=QR[h // 2]), reads=[QR], writes=[qr])
                k.op("sp", I("dma_start", out=kk[:], in_=KTX[h]), reads=[KTX], writes=[kk])
                k.op("sp", I("dma_start", out=v[:], in_=VX[:, h * 128:(h + 1) * 128].rearrange("(t p) d -> p t d", p=128)), reads=[VX], writes=[v])
                for sbk in range(5):
                    if sbk < 4:
                        qc0, nq = sbk * 512, 512
                        kts = [(128, kt * 128, v[:, kt, :], max(0, kt - 4 * sbk) * 128, kt >= 4 * sbk) for kt in range(0, 4 * sbk + 4)]
                    else:
                        qc0, nq = 2048, 64
                        kts = [(128, T + c * 128, v[:, 17 + c, :], 0, False) for c in range(8)] + [(64, 2048, v[0:64, 16, :], 0, False)]
                    po = ps_o.next(); pz = ps_zz.next()
                    for si, (nk, kc0, va, c0, diag) in enumerate(kts):
                        last = (si == len(kts) - 1)
                        ps = ps_s.next()
                        k.op("pe", I("matmul", ps[0:nk, c0:nq], lhsT=kk[:, kc0:kc0 + nk], rhs=q[:, qc0 + c0:qc0 + nq], start=True, stop=False), reads=[kk, q], writes=[ps])
                        if diag:
                            k.op("pe", I("matmul", ps[0:nk, c0:c0 + 128], lhsT=ident[:], rhs=m0n[:], start=False, stop=False), reads=[ident, m0n], writes=[ps])
                        k.op("pe", I("matmul", ps[0:nk, c0:nq], lhsT=krs[hb:hb + 64, kc0:kc0 + nk], rhs=qr[hb:hb + 64, qc0 + c0:qc0 + nq], start=False, stop=True),
                             reads=[krs, qr], writes=[ps])
                        pt = ptr.next()
                        k.op("act", I("activation", out=pt[0:nk, c0:nq], in_=ps[0:nk, c0:nq], func=AF.Exp), reads=[ps], writes=[pt])
                        k.op("pe", I("matmul", po[:, c0:nq], lhsT=va, rhs=pt[0:nk, c0:nq], start=(si == 0), stop=last), reads=[v, pt], writes=[po])
                        k.op("pe", I("matmul", pz[:, c0:nq], lhsT=ones_b[0:nk, :], rhs=pt[0:nk, c0:nq], start=(si == 0), stop=last), reads=[ones_b, pt], writes=[pz])
                    r = rz.next()
                    k.op("dve", I("reciprocal", out=r[:, 0:nq], in_=pz[:, 0:nq]), reads=[pz], writes=[r])
                    ao = aor.next()
                    if sbk == 4:
                        k.op("pool", I("memset", ao[:, 64:128], 0.0), writes=[ao])
                    k.op("dve", I("tensor_tensor", out=ao[:, 0:nq], in0=po[:, 0:nq], in1=r[:, 0:nq], op=ALU.mult), reads=[po, r], writes=[ao], nowaw=True)
                    nst = 512 if sbk < 4 else 128
                    k.op("sp", I("dma_start", out=ATd[h, :, sbk * 512:sbk * 512 + nst], in_=ao[:, 0:nst]), reads=[ao], writes=[ATd], nowaw=True)
        phase_wo_ln(b_wo, b_wo[:, :], Xsrc, Xdst, lnrow)


    def phase_peer(i, Xsrc, Xdst, lnrow, final_out=None):
        keys2d = p_keys[i]
        stage('peer_scores')
        with k.phase():
            xT = k.sb([128, KC, T], BF16, "xT")
            load_xT(xT)
            kin = Rot([k.sb([128, 128], F32, "kin") for _ in range(2)])
            kbf = Rot([k.sb([128, 128], BF16, "kbf") for _ in range(2)])
            keysT = k.sb([128, 16, 128], BF16, "keysT")
            pst = Rot([k.ps([128, 1024], BF16, "pst") for _ in range(1)])
            for half in range(2):
                ps = pst.next()
                for jx in range(8):
                    hc = half * 8 + jx
                    ki = kin.next(); kb = kbf.next()
                    k.op("sp", I("dma_start", out=ki[:], in_=keys2d[hc]), reads=[p_keys], writes=[ki])
                    k.op("pool", I("tensor_copy", out=kb[:], in_=ki[:]), reads=[ki], writes=[kb])
                    k.op("pe", I("transpose", out=ps[:, jx * 128:(jx + 1) * 128], in_=kb[:], identity=ident[:]), reads=[kb, ident], writes=[ps])
                copy_op("dve", keysT[:, half * 8:(half + 1) * 8, :], ps[:].rearrange("p (a b) -> p a b", a=8), [ps], [keysT])
            wl = WLoader(KC, 256, nbuf=2, cast_eng="pool")
            pq = Rot([k.ps([128, 512], F32, "pq") for _ in range(3)])
            psc = Rot([k.ps([128, 512], F32, "psc") for _ in range(2)])
            qtb = Rot([k.sb([128, 512], BF16, "qtb") for _ in range(3)])
            scs = Rot([k.sb([128, 4, 128], F32, "scs") for _ in range(3)])
            for hp in range(8):
                wb = wl.load(p_wq, p_wq[i], [(hp * 256, 256)])
                for hh in range(2):
                    hc = hp * 2 + hh
                    for (t0, tn) in TOKB:
                        ps = pq.next()
                        for kc in range(KC):
                            k.op("pe", I("matmul", ps[:, 0:tn], lhsT=wb[:, kc, hh * 128:(hh + 1) * 128], rhs=xT[:, kc, t0:t0 + tn],
                                         start=(kc == 0), stop=(kc == KC - 1)), reads=[wb, xT], writes=[ps])
                        qb = qtb.next()
                        copy_op("act", qb[:, 0:tn], ps[:, 0:tn], [ps], [qb])
                        p2 = psc.next()
                        nt_ = tn // 128
                        for ti in range(nt_):
                            k.op("pe", I("matmul", p2[:, ti * 128:(ti + 1) * 128], lhsT=qb[:, ti * 128:(ti + 1) * 128], rhs=keysT[:, hc, :],
                                         start=True, stop=True), reads=[qb, keysT], writes=[p2])
                        sc = scs.next()
                        copy_op("dve", sc[:, 0:nt_, :], p2[:, 0:nt_ * 128].rearrange("p (a b) -> p a b", b=128), [p2], [sc])
                        tt0 = t0 // 128
                        k.op("dq_pool", I("dma_start", out=SS[tt0:tt0 + nt_, :, hc * 128:(hc + 1) * 128].rearrange("t p n -> p t n"), in_=sc[:, 0:nt_, :]),
                             reads=[sc], writes=[SSb[tt0 + ti_] for ti_ in range(nt_)], nowaw=True)
        stage('peer_topk')
        with k.phase():
            Sr = Rot([k.sb([128, 16, 128], F32, "S") for _ in range(2)])
            T16 = k.sb([128, 16, 16], F32, "T16")
            tmpS = k.sb([128, 128], F32, "tmpS")
            pen = k.sb([128, 16, 128], F32, "pen")
            Ar = Rot([k.sb([128, 2048 + 16], F32, "A12") for _ in range(2)])
            cand = k.sb([128, 8, 256], F32, "cand")
            ct1 = k.sb([128, 256], F32, "ct1")
            ct2 = k.sb([128, 256], F32, "ct2")
            C24 = k.sb([128, 8, 24], F32, "C24")
            dd = k.sb([128, 8, 16], F32, "dd")
            zz = k.sb([128, 16], F32, "zz")
            for tt in range(NT):
                S = Sr.next()
                k.op("sp", I("dma_start", out=S[:], in_=SS[tt].rearrange("p (a b) -> p a b", b=128)), reads=[SSb[tt]], writes=[S])
                for hc in range(16):
                    k.op("dve", I("max", out=T16[:, hc, 0:8], in_=S[:, hc, :]), reads=[S], writes=[T16])
                    k.op("dve", I("match_replace", out=tmpS[:], in_to_replace=T16[:, hc, 0:8], in_values=S[:, hc, :], imm_value=-1e30),
                         reads=[T16, S], writes=[tmpS])
                    k.op("dve", I("max", out=T16[:, hc, 8:16], in_=tmpS[:]), reads=[tmpS], writes=[T16])
                A = Ar.next()
                A3 = A[:, 0:2048].rearrange("p (a b) -> p a b", b=128)
                k.op("dve", I("tensor_tensor", out=pen[:], in0=S[:], in1=sap(T16, 15, [[16, 16], [0, 128]]), op=ALU.is_lt), reads=[S, T16], writes=[pen])
                k.op("dve", I("scalar_tensor_tensor", out=A3, in0=pen[:], scalar=-1e4, in1=S[:], op0=ALU.mult, op1=ALU.add), reads=[pen, S], writes=[A])
                k.op("dve", I("tensor_tensor", out=cand[:].rearrange("p h (i j) -> p h i j", j=16),
                              in0=sap(T16, 0, [[32, 8], [1, 16], [0, 16]]), in1=sap(T16, 16, [[32, 8], [0, 16], [1, 16]]), op=ALU.add),
                     reads=[T16], writes=[cand])
                for h in range(8):
                    k.op("dve", I("max", out=C24[:, h, 0:8], in_=cand[:, h, :]), reads=[cand], writes=[C24])
                    k.op("dve", I("match_replace", out=ct1[:], in_to_replace=C24[:, h, 0:8], in_values=cand[:, h, :], imm_value=-1e30), reads=[C24, cand], writes=[ct1])
                    k.op("dve", I("max", out=C24[:, h, 8:16], in_=ct1[:]), reads=[ct1], writes=[C24])
                    k.op("dve", I("match_replace", out=ct2[:], in_to_replace=C24[:, h, 8:16], in_values=ct1[:], imm_value=-1e30), reads=[C24, ct1], writes=[ct2])
                    k.op("dve", I("max", out=C24[:, h, 16:24], in_=ct2[:]), reads=[ct2], writes=[C24])
                k.op("dve", I("tensor_tensor", out=A[:, 2048:2056], in0=C24[:, :, 15], in1=C24[:, :, 16], op=ALU.add), reads=[C24], writes=[A])
                k.op("dve", I("tensor_scalar", out=A[:, 2048:2056], in0=A[:, 2048:2056], scalar1=0.5, scalar2=None, op0=ALU.mult), reads=[A], writes=[A])
                k.op("dve", I("tensor_tensor", out=dd[:], in0=C24[:, :, 0:16], in1=sap(C24, 0, [[24, 8], [0, 16]]), op=ALU.subtract), reads=[C24], writes=[dd])
                k.op("act", I("activation", out=dd[:], in_=dd[:], func=AF.Exp), reads=[dd], writes=[dd])
                k.op("dve", I("tensor_reduce", out=zz[:, 0:8], in_=dd[:], axis=AX.X, op=ALU.add), reads=[dd], writes=[zz])
                k.op("act", I("activation", out=zz[:, 8:16], in_=zz[:, 0:8], func=AF.Ln), reads=[zz], writes=[zz])
                k.op("dve", I("tensor_tensor", out=zz[:, 8:16], in0=zz[:, 8:16], in1=C24[:, :, 0], op=ALU.add), reads=[zz, C24], writes=[zz])
                k.op("dve", I("tensor_scalar", out=A[:, 2056:2064], in0=zz[:, 8:16], scalar1=-1.0, scalar2=None, op0=ALU.mult), reads=[zz], writes=[A])
                k.op("dq_pool", I("dma_start", out=AUX[tt], in_=A[:]), reads=[A], writes=[AUXb[tt]])
        stage('peer_main')
        with k.phase():
            NG = 16
            ustg = Rot([k.sb([128, 2048], F32, "ustg") for _ in range(2)])
            ubf = Rot([k.sb([128, 2048], BF16, "ubf") for _ in range(1)])
            vstg = Rot([k.sb([128, 2048], F32, "vstg") for _ in range(1)])
            uT = Rot([k.sb([128, KC, 512], BF16, "uT") for _ in range(3)])
            vB = Rot([k.sb([128, 2048], BF16, "vB") for _ in range(10)])
            xts = Rot([k.sb([128, KC, 128], BF16, "xtl") for _ in range(2)])
            a2r = Rot([k.sb([128, 8, 128], F32, "a2") for _ in range(2)])
            a1r = Rot([k.sb([128, 8, 8], F32, "a1") for _ in range(2)])
            str_ = Rot([k.sb([128, 16], F32, "st") for _ in range(2)])
            tmp = Rot([k.sb([128, 2, 1024], F32, "tmp") for _ in range(2)])
            Eb = Rot([k.sb([128, 1024], BF16, "Eb") for _ in range(2)])
            Gh = Rot([k.sb([128, 8, 1024], BF16, "Gh") for _ in range(1)])
            Gacc = Rot([k.sb([128, 1024], F32, "Gacc") for _ in range(1)])
            gl = Rot([k.sb([128, 1024], BF16, "gl") for _ in range(1)])
            Wb = Rot([k.sb([128, 1024], BF16, "Wb") for _ in range(2)])
            WT = Rot([k.sb([128, 8, 128], BF16, "WT") for _ in range(2)])
            yt = Rot([k.sb([128, 2048], F32, "yt") for _ in range(2)])
            pH = Rot([k.ps([128, 1024], F32, "pH") for _ in range(1)])
            pU = Rot([k.ps([128, 1024], BF16, "pU") for _ in range(1)])
            pW = Rot([k.ps([128, 1024], BF16, "pW") for _ in range(1)])
            pY = Rot([k.ps([128, 2048], F32, "pY") for _ in range(1)])
            pend_store = [None]
            for g in range(NG):
                uts = []
                vbs = []
                for half in range(2):
                    ut = uT.next()
                    uts.append(ut)
                    for cc in range(4):
                        c = half * 4 + cc
                        e0 = g * 1024 + c * 128
                        us = ustg.next()
                        k.op("sp", I("dma_start", out=us[:], in_=p_u[i, e0:e0 + 128, :]), reads=[p_u], writes=[us])
                        ub = ubf.next()
                        k.op("pool", I("tensor_copy", out=ub[:], in_=us[:]), reads=[us], writes=[ub])
                        for hf in range(2):
                            pu = pU.next()
                            for jx in range(8):
                                kc = hf * 8 + jx
                                k.op("pe", I("transpose", out=pu[:, jx * 128:(jx + 1) * 128], in_=ub[:, kc * 128:(kc + 1) * 128], identity=ident[:]),
                                     reads=[ub, ident], writes=[pu])
                            copy_op("act", ut[:, hf * 8:(hf + 1) * 8, cc * 128:(cc + 1) * 128], pu[:].rearrange("p (a b) -> p a b", a=8), [pu], [ut])
                        vs = vstg.next()
                        k.op("sp", I("dma_start", out=vs[:], in_=p_v[i, e0:e0 + 128, :]), reads=[p_v], writes=[vs])
                        vb = vB.next()
                        k.op("pool", I("tensor_copy", out=vb[:], in_=vs[:]), reads=[vs], writes=[vb])
                        vbs.append(vb)
                for tt in range(NT):
                    xt = xts.next()
                    k.op("sp", I("dma_start", out=xt[:], in_=XT[tt]), reads=[XTb[tt]], writes=[xt])
                    if pend_store[0] is not None:
                        pend_store[0]()
                        pend_store[0] = None
                    a2 = a2r.next(); a1 = a1r.next(); st = str_.next()
                    auxv = AUX[tt, :, 0:2048].rearrange("p (h c n) -> p h c n", h=8, c=2)
                    k.op("sp", I("dma_start", out=a2[:], in_=auxv[:, :, 1, :]), reads=[AUXb[tt]], writes=[a2])
                    k.op("sp", I("dma_start", out=a1[:], in_=auxv[:, :, 0, g * 8:(g + 1) * 8]), reads=[AUXb[tt]], writes=[a1])
                    k.op("sp", I("dma_start", out=st[:], in_=AUX[tt, :, 2048:2064]), reads=[AUXb[tt]], writes=[st])
                    y = yt.next()
                    if g > 0:
                        k.op("sp", I("dma_start", out=y[:], in_=YAC[tt]), reads=[YACb[tt]], writes=[y])
                    ghs = Gh.next()
                    for hp in range(4):
                        tm = tmp.next()
                        k.op("pool", I("tensor_tensor", out=tm[:].rearrange("p h (r n) -> p h r n", n=128),
                                       in0=sap(a1, hp * 16, [[8, 2], [1, 8], [0, 128]]), in1=sap(a2, hp * 256, [[128, 2], [0, 8], [1, 128]]), op=ALU.add),
                             reads=[a1, a2], writes=[tm])
                        for hh in range(2):
                            h = hp * 2 + hh
                            eb = Eb.next()
                            k.op("act", I("activation", out=eb[:], in_=tm[:, hh, :], func=AF.Exp, bias=st[:, 8 + h:9 + h], scale=1.0), reads=[tm, st], writes=[eb])
                            k.op("dve", I("scalar_tensor_tensor", out=ghs[:, h, :], in0=tm[:, hh, :], scalar=st[:, h:h + 1], in1=eb[:], op0=ALU.is_ge, op1=ALU.mult),
                                 reads=[tm, st, eb], writes=[ghs], nowaw=(h > 0))
                    ga = Gacc.next()
                    k.op("dve", I("tensor_reduce", out=ga[:], in_=sap(ghs, 0, [[1, 1024], [1024, 8]]), axis=AX.X, op=ALU.add), reads=[ghs], writes=[ga])
                    ph = pH.next()
                    for half in range(2):
                        for kc in range(KC):
                            k.op("pe", I("matmul", ph[:, half * 512:(half + 1) * 512], lhsT=xt[:, kc, :], rhs=uts[half][:, kc, :], start=(kc == 0), stop=(kc == KC - 1)),
                                 reads=[xt, uts[half]], writes=[ph])
                    gg = gl.next()
                    k.op("act", I("activation", out=gg[:], in_=ph[:], func=AF.Gelu_apprx_tanh), reads=[ph], writes=[gg])
                    wb_ = Wb.next()
                    k.op("pool", I("tensor_tensor", out=wb_[:], in0=gg[:], in1=ga[:], op=ALU.mult), reads=[gg, ga], writes=[wb_])
                    pw = pW.next()
                    for c in range(8):
                        k.op("pe", I("transpose", out=pw[:, c * 128:(c + 1) * 128], in_=wb_[:, c * 128:(c + 1) * 128], identity=ident[:]), reads=[wb_, ident], writes=[pw])
                    wt = WT.next()
                    copy_op("act", wt[:], pw[:].rearrange("p (a b) -> p a b", a=8), [pw], [wt])
                    py = pY.next()
                    for ds in range(4):
                        for c in range(8):
                            k.op("pe", I("matmul", py[:, ds * 512:(ds + 1) * 512], lhsT=wt[:, c, :], rhs=vbs[c][:, ds * 512:(ds + 1) * 512], start=(c == 0), stop=(c == 7)),
                                 reads=[wt, vbs[c]], writes=[py])
                    if g == 0:
                        k.op("dve", I("tensor_copy", out=y[:], in_=py[:]), reads=[py], writes=[y])
                    else:
                        k.op("dve", I("tensor_tensor", out=y[:], in0=py[:], in1=y[:], op=ALU.add), reads=[py, y], writes=[y])
                    pend_store[0] = (lambda y=y, tt=tt: k.op("sp", I("dma_start", out=YAC[tt], in_=y[:]), reads=[y], writes=[YACb[tt]]))
            pend_store[0]()
        phase_ln_from_dram([YAC[tt] for tt in range(NT)], YACb, Xsrc, Xdst, lnrow, final_out)

    try:
      phase_init()
      Xcur = x0
      for li in range(n_layers):
          kind, j = li % 3, li // 3
          if kinds is not None:
              kind, j = kinds[li], 0
          Xmid, Xnext = XA, XB
          if kind == 0:
              kro = [(tt, (128, oak[j, (tt - 12) * 128:(tt - 11) * 128, :])) for tt in range(12, 16)]
              vro = [(tt, (128, oav[j, (tt - 12) * 128:(tt - 11) * 128, :])) for tt in range(12, 16)]
              phase_qkv(a_wqkv, a_wqkv[j], ca_k[j], ca_v[j], 512, kro, vro, sak[j, 448:512, :], sav[j, 448:512, :], roll_k=sak[j], roll_v=sav[j])
              phase_attn_a(j, Xcur, Xmid, 2 * li)
          elif kind == 1:
              phase_mla(Xcur, Xmid, 2 * li)
          else:
              kro = [(tt, (128, ock[tt * 128:(tt + 1) * 128, :])) for tt in range(16)]
              vro = [(tt, (128, ocv[tt * 128:(tt + 1) * 128, :])) for tt in range(16)]
              phase_qkv(c_wqkv, c_wqkv[:, :], cc_k, cc_v, 1024, kro, vro, sck[:, :], scv[:, :])
              phase_attn_c(Xcur, Xmid, 2 * li)
          phase_peer(li, Xmid, Xnext, 2 * li + 1, final_out=(y_out if li == n_layers - 1 else None))
          Xcur = Xnext
    except _Stop as e:
        print('stopped before', e)
        if k.es is not None:
            k.P.barrier(); k.es.close(); k.es = None
    info = P.emit()
    return nc, info


def _prep_inputs(inputs, b, n_layers=4):
    f = np.float32
    xp = inputs["x_prompt"][b]
    xs = inputs["x_sample"][b]
    x0 = np.zeros((T, D), f)
    x0[0:2048] = xp
    x0[2048:2112] = xs
    half = 32
    inv = (10000.0 ** (-np.arange(half, dtype=np.float32) / half)).astype(np.float32)
    pos = np.zeros((T,), np.float32)
    pos[0:2048] = np.arange(2048)
    pos[2048:2112] = 1024 + np.arange(64)
    ang = (pos[:, None] * inv[None, :]).astype(np.float32)
    m = {
        "x0": x0,
        "ca_k": np.ascontiguousarray(inputs["cache_a_k"][:, b]).reshape(2, 512, 2048),
        "ca_v": np.ascontiguousarray(inputs["cache_a_v"][:, b]).reshape(2, 512, 2048),
        "cb_c": np.ascontiguousarray(inputs["cache_b_ckv"][0, b]),
        "cb_r": np.ascontiguousarray(inputs["cache_b_krope"][0, b]),
        "cc_k": np.ascontiguousarray(inputs["cache_c_k"][0, b]).reshape(1024, 2048),
        "cc_v": np.ascontiguousarray(inputs["cache_c_v"][0, b]).reshape(1024, 2048),
        "a_wqkv": inputs["a_wqkv"], "a_wo": inputs["a_wo"], "a_rel": inputs["a_relbias"],
        "b_win": inputs["b_win"][0], "b_qn": inputs["b_qnorm"], "b_kvn": inputs["b_kvnorm"],
        "b_wqb": inputs["b_wqb"][0], "b_wkvb": inputs["b_wkvb"][0], "b_wo": inputs["b_wo"][0],
        "c_wqkv": inputs["c_wqkv"][0], "c_wo": inputs["c_wo"][0],
        "p_wq": inputs["peer_wq"], "p_keys": inputs["peer_keys"].reshape(4, 16, 128, 128),
        "p_u": inputs["peer_u"][:n_layers], "p_v": inputs["peer_v"][:n_layers],
        "ln_g": inputs["ln_g"].reshape(8, 2048), "ln_b": inputs["ln_b"].reshape(8, 2048),
        "ropec": np.cos(ang).astype(f), "ropes": np.sin(ang).astype(f),
    }
    return {kk: np.ascontiguousarray(np.asarray(v, dtype=f)) for kk, v in m.items()}


_CACHE = {}


def kernel(**inputs):
    inputs = {kk: np.asarray(v) for kk, v in inputs.items()}
    if "nc" not in _CACHE:
        _CACHE["nc"] = build()[0]
    nc = _CACHE["nc"]
    in_maps = [_prep_inputs(inputs, b) for b in range(8)]
    res = run_bass_kernel_spmd(nc, in_maps, core_ids=list(range(8)))
    R = res.results
    f = np.float32

    def st(fn):
        return np.stack([fn(R[b]) for b in range(8)])

    y_prompt = st(lambda r: r["y"][0:2048])
    y_sample = st(lambda r: r["y"][2048:2112])
    oak = np.stack([R[b]["oak"].reshape(2, 512, 16, 128) for b in range(8)], axis=1)
    oav = np.stack([R[b]["oav"].reshape(2, 512, 16, 128) for b in range(8)], axis=1)
    obc = st(lambda r: r["obc"])[None]
    obr = st(lambda r: r["obr"])[None]
    ock = st(lambda r: r["ock"].reshape(2048, 16, 128))[None]
    ocv = st(lambda r: r["ocv"].reshape(2048, 16, 128))[None]
    sak = np.stack([R[b]["sak"].reshape(2, 512, 16, 128) for b in range(8)], axis=1)
    sav = np.stack([R[b]["sav"].reshape(2, 512, 16, 128) for b in range(8)], axis=1)
    sbc = st(lambda r: r["sbc"])[None]
    sbr = st(lambda r: r["sbr"])[None]
    sck = st(lambda r: r["sck"].reshape(64, 16, 128))[None]
    scv = st(lambda r: r["scv"].reshape(64, 16, 128))[None]
    outs = (y_prompt, y_sample, oak, oav, obc, obr, ock, ocv, sak, sav, sbc, sbr, sck, scv)
    return tuple(np.ascontiguousarray(o.astype(f)) for o in outs)
```

```python
import os
import numpy as np
from contextlib import ExitStack
import concourse.bass as bass
import concourse.mybir as mybir
from concourse.bass_utils import run_bass_kernel_spmd

F32 = mybir.dt.float32
BF16 = mybir.dt.bfloat16
AF = mybir.ActivationFunctionType
ALU = mybir.AluOpType
AX = mybir.AxisListType

NSLOT = 20
COMPUTE = ("pe", "act", "dve", "pool")
DMAQ = ("sp", "dq_pool")
ALLQ = COMPUTE + DMAQ


class Buf:
    __slots__ = ("name", "last_w", "readers", "war")

    def __init__(self, name):
        self.name = name
        self.last_w = []
        self.readers = []
        self.war = []


class Op:
    __slots__ = ("eng", "fn", "deps", "signal", "sigval", "eidx", "isdma", "slot", "idx")


class Prog:
    def __init__(self, nc):
        self.nc = nc
        self.ops = []
        self.ecount = {}
        self.last = {}
        self.recent_dma = {q: [] for q in DMAQ}

    def eng_obj(self, eng):
        nc = self.nc
        return {"pe": nc.tensor, "act": nc.scalar, "dve": nc.vector, "pool": nc.gpsimd,
                "sp": nc.sync, "dq_pool": nc.gpsimd}[eng]

    @staticmethod
    def phys(eng):
        return "pool" if eng == "dq_pool" else eng

    def op(self, eng, fn, reads=(), writes=(), nowaw=False):
        o = Op()
        o.eng = eng
        o.fn = fn
        o.isdma = eng in DMAQ
        o.signal = False
        o.sigval = 0
        o.slot = 0
        o.idx = len(self.ops)
        pe = self.phys(eng)
        o.eidx = self.ecount.get(pe, 0)
        self.ecount[pe] = o.eidx + 1
        deps = set()
        for b in reads:
            deps.update(b.last_w)
        for b in writes:
            if not nowaw:
                deps.update(b.last_w)
            else:
                deps.update(b.war)
            deps.update(b.readers)
        deps.discard(o)
        o.deps = deps
        for b in reads:
            b.readers.append(o)
        for b in writes:
            if nowaw:
                b.last_w.append(o)
            else:
                b.war = list(b.readers) + [x for x in b.last_w if x.fn is not None][-4:]
                b.last_w = [o]
                b.readers = []
        self.ops.append(o)
        if fn is not None:
            if o.isdma:
                r = self.recent_dma[eng]
                r.append(o)
                if len(r) > NSLOT:
                    r.pop(0)
            else:
                self.last[eng] = o
        return o

    def barrier(self):
        lasts = [o for o in self.last.values()]
        for q in DMAQ:
            lasts += self.recent_dma[q]
        for eng in ALLQ:
            o = self.op(eng, None)
            o.deps = set(lasts)

    def emit(self):
        nc = self.nc
        need = []
        for o in self.ops:
            ws = []
            for d in o.deps:
                if d.fn is None:
                    continue
                if (not d.isdma) and (not o.isdma) and d.eng == o.eng:
                    if o.eng == "pe" or o.fn is None:
                        continue
                    if o.eidx - d.eidx > 3:
                        continue
                ws.append(d)
                d.signal = True
            need.append(ws)
        sems = {e: nc.alloc_semaphore("s_" + e) for e in COMPUTE}
        dsems = {q: [nc.alloc_semaphore("d_%s_%d" % (q, i)) for i in range(NSLOT)] for q in DMAQ}
        sigcount = {e: 0 for e in COMPUTE}
        dcount = {q: 0 for q in DMAQ}
        waited = {}
        nw = [0]

        def do_wait(eng, semkey, sem, val):
            k = (self.phys(eng), semkey)
            if waited.get(k, 0) >= val:
                return
            waited[k] = val
            nw[0] += 1
            self.eng_obj(eng).wait_ge(sem, val)

        for o, ws in zip(self.ops, need):
            e = self.eng_obj(o.eng)
            for d in sorted(ws, key=lambda d: d.idx):
                if d.isdma:
                    do_wait(o.eng, ("d", d.eng, d.slot), dsems[d.eng][d.slot], d.sigval)
                else:
                    do_wait(o.eng, ("c", d.eng), sems[d.eng], d.sigval)
            if o.fn is None:
                continue
            if o.isdma:
                i = dcount[o.eng]
                dcount[o.eng] = i + 1
                o.slot = i % NSLOT
                o.sigval = 16 * (i // NSLOT + 1)
                if i >= NSLOT:
                    do_wait(o.eng, ("d", o.eng, o.slot), dsems[o.eng][o.slot], 16 * (i // NSLOT))
                ins = o.fn(e)
                ins.then_inc(dsems[o.eng][o.slot], 16)
            else:
                ins = o.fn(e)
                if o.signal:
                    sigcount[o.eng] += 1
                    o.sigval = sigcount[o.eng]
                    ins.then_inc(sems[o.eng], 1)
        for q in DMAQ:
            n = dcount[q]
            for s in range(min(n, NSLOT)):
                last_i = ((n - 1 - s) // NSLOT) * NSLOT + s
                nc.sync.wait_ge(dsems[q][s], 16 * (last_i // NSLOT + 1))
        for en in COMPUTE:
            if sigcount[en] > 0:
                nc.sync.wait_ge(sems[en], sigcount[en])
        return dict(n_ops=len(self.ops), sig=sigcount, dma=dcount, waits=nw[0])


def I(name, *a, **k):
    return lambda e: getattr(e, name)(*a, **k)


class Tile:
    def __init__(self, t, name):
        self.t = t
        self.b = Buf(name)

    def __getitem__(self, k):
        return self.t[k]


class Rot:
    def __init__(self, tiles):
        self.tiles = tiles
        self.i = 0

    def next(self):
        t = self.tiles[self.i % len(self.tiles)]
        self.i += 1
        return t


NT = 17
T = NT * 128
D = 2048
KC = 16
TOKB = [(0, 512), (512, 512), (1024, 512), (1536, 512), (2048, 128)]
ALPHA = (2.0 * 4) ** 0.25
NEG = -30000.0
KEXT = T + 1024


class K:
    def __init__(self, nc):
        self.nc = nc
        self.P = Prog(nc)
        self.es = None
        self.uid = 0

    def sb(self, shape, dt, name=None):
        self.uid += 1
        name = "%s_%d" % (name or "t", self.uid)
        t = self.es.enter_context(self.nc.sbuf_tensor(name, list(shape), dt))
        return Tile(t, name)

    def ps(self, shape, dt, name=None):
        self.uid += 1
        name = "%s_%d" % (name or "p", self.uid)
        t = self.es.enter_context(self.nc.psum_tensor(name, list(shape), dt))
        return Tile(t, name)

    def dram(self, name, shape, dt, kind="Internal"):
        t = self.nc.dram_tensor(name, list(shape), dt, kind=kind)
        tl = Tile(t.ap(), name)
        return tl

    def phase(self):
        k = self

        class _Ph:
            def __enter__(s):
                k.es = ExitStack()
                k.es.__enter__()
                return s

            def __exit__(s, *a):
                k.P.barrier()
                k.es.__exit__(*a)
                k.es = None
                return False
        return _Ph()

    def op(self, eng, fn, reads=(), writes=(), nowaw=False):
        if eng == "dq_pool":
            eng = "sp"
        return self.P.op(eng, fn, [r.b if isinstance(r, Tile) else r for r in reads],
                         [w.b if isinstance(w, Tile) else w for w in writes], nowaw)


def sap(tile_or_t, offset, dims):
    t = tile_or_t.t if isinstance(tile_or_t, Tile) else tile_or_t
    full = t[:]
    pstride = full.ap[0][0]
    return bass.AP(t, offset, [[pstride, 128]] + [list(d) for d in dims])


def dap(ap, offset, dims):
    return bass.AP(ap.tensor, offset, [list(d) for d in dims])


class _Stop(Exception):
    pass


def build(n_layers=4, dbg=False, stop_after=None, kinds=None):
    nc = bass.Bass("TRN2", target_bir_lowering=False)
    k = K(nc)
    P = k.P
    stage_ctr = [0]

    def stage(name):
        stage_ctr[0] += 1
        if stop_after is not None and stage_ctr[0] > stop_after:
            raise _Stop(name)

    def din(name, shape):
        return Tile(nc.dram_tensor(name, list(shape), F32, kind="ExternalInput").ap(), name)

    def dout(name, shape):
        return Tile(nc.dram_tensor(name, list(shape), F32, kind="ExternalOutput").ap(), name)

    x0 = din("x0", [T, D])
    ca_k = din("ca_k", [2, 512, 2048]); ca_v = din("ca_v", [2, 512, 2048])
    cb_c = din("cb_c", [1024, 512]); cb_r = din("cb_r", [1024, 64])
    cc_k = din("cc_k", [1024, 2048]); cc_v = din("cc_v", [1024, 2048])
    a_wqkv = din("a_wqkv", [2, 2048, 6144]); a_wo = din("a_wo", [2, 2048, 2048]); a_rel = din("a_rel", [2, 16, 257])
    b_win = din("b_win", [2048, 1088]); b_qn = din("b_qn", [1, 512]); b_kvn = din("b_kvn", [1, 512])
    b_wqb = din("b_wqb", [512, 3072]); b_wkvb = din("b_wkvb", [512, 4096]); b_wo = din("b_wo", [2048, 2048])
    c_wqkv = din("c_wqkv", [2048, 6144]); c_wo = din("c_wo", [2048, 2048])
    p_wq = din("p_wq", [4, 2048, 2048]); p_keys = din("p_keys", [4, 16, 128, 128])
    p_u = din("p_u", [n_layers, 16384, 2048]); p_v = din("p_v", [n_layers, 16384, 2048])
    ln_g = din("ln_g", [8, 2048]); ln_b = din("ln_b", [8, 2048])
    ropec = din("ropec", [T, 32]); ropes = din("ropes", [T, 32])

    y_out = dout("y", [T, D])
    oak = dout("oak", [2, 512, 2048]); oav = dout("oav", [2, 512, 2048])
    obc = dout("obc", [2048, 512]); obr = dout("obr", [2048, 64])
    ock = dout("ock", [2048, 2048]); ocv = dout("ocv", [2048, 2048])
    sak = dout("sak", [2, 512, 2048]); sav = dout("sav", [2, 512, 2048])
    sbc = dout("sbc", [64, 512]); sbr = dout("sbr", [64, 64])
    sck = dout("sck", [64, 2048]); scv = dout("scv", [64, 2048])

    xkind = "ExternalOutput" if dbg else "Internal"
    XA = k.dram("XA", [T, D], F32, kind=xkind); XB = k.dram("XB", [T, D], F32, kind=xkind)
    XT = Tile(nc.dram_tensor("XT", [NT, 128, KC, 128], BF16, kind="Internal").ap(), "XT")
    QT = Tile(nc.dram_tensor("QT", [16, 128, T], BF16, kind="Internal").ap(), "QT")
    KTX = Tile(nc.dram_tensor("KTX", [16, 128, KEXT], BF16, kind="Internal").ap(), "KTX")
    QR = Tile(nc.dram_tensor("QR", [8, 128, T], BF16, kind="Internal").ap(), "QR")
    KR = Tile(nc.dram_tensor("KR", [128, KEXT], BF16, kind="Internal").ap(), "KR")
    VX = Tile(nc.dram_tensor("VX", [KEXT, 2048], BF16, kind="Internal").ap(), "VX")
    SS = Tile(nc.dram_tensor("SS", [NT, 128, 2048], F32, kind="Internal").ap(), "SS")
    AUX = Tile(nc.dram_tensor("AUX", [NT, 128, 2048 + 16], F32, kind="Internal").ap(), "AUX")
    YAC = Tile(nc.dram_tensor("YAC", [NT, 128, 2048], F32, kind="Internal").ap(), "YAC")
    EXT = Tile(nc.dram_tensor("EXT", [16, 384], F32, kind="Internal").ap(), "EXT")
    CQT = Tile(nc.dram_tensor("CQT", [4, 128, T], BF16, kind="Internal").ap(), "CQT")
    CKVT = Tile(nc.dram_tensor("CKVT", [4, 128, KEXT], BF16, kind="Internal").ap(), "CKVT")
    ATd = Tile(nc.dram_tensor("ATd", [16, 128, T], BF16, kind="Internal").ap(), "ATd")
    XTb = [Buf("XT%d" % i) for i in range(NT)]
    AUXb = [Buf("AUX%d" % i) for i in range(NT)]
    YACb = [Buf("YAC%d" % i) for i in range(NT)]
    SSb = [Buf("SS%d" % i) for i in range(NT)]

    def palloc(shape, dt, name):
        return Tile(nc.alloc_sbuf_tensor(name, list(shape), dt), name)

    identf = palloc([128, 128], F32, "identf")
    ident = palloc([128, 128], BF16, "ident")
    ones_b = palloc([128, 128], BF16, "ones_b")
    ones_f = palloc([128, 128], F32, "ones_f")
    zeros_b = palloc([128, 128], BF16, "zeros_b")
    Uf = palloc([128, 128], F32, "Uf")
    mcaus = palloc([128, 128], F32, "mcaus")
    m0b = palloc([128, 128], BF16, "m0b")
    m4b = palloc([128, 128], BF16, "m4b")
    k.op("pool", I("memset", identf[:], 0.0), writes=[identf])
    k.op("pool", I("affine_select", out=identf[:], in_=identf[:], pattern=[[-1, 128]], compare_op=ALU.not_equal,
                   fill=1.0, base=0, channel_multiplier=1), reads=[identf], writes=[identf])
    k.op("pool", I("tensor_copy", out=ident[:], in_=identf[:]), reads=[identf], writes=[ident])
    k.op("pool", I("memset", ones_b[:], 1.0), writes=[ones_b])
    k.op("pool", I("memset", ones_f[:], 1.0), writes=[ones_f])
    k.op("pool", I("memset", zeros_b[:], 0.0), writes=[zeros_b])
    k.op("pool", I("affine_select", out=Uf[:], in_=ones_f[:], pattern=[[-1, 128]], compare_op=ALU.is_gt,
                   fill=0.0, base=0, channel_multiplier=1), reads=[ones_f], writes=[Uf])
    k.op("pool", I("affine_select", out=mcaus[:], in_=ones_f[:], pattern=[[1, 128]], compare_op=ALU.is_gt,
                   fill=0.0, base=0, channel_multiplier=-1), reads=[ones_f], writes=[mcaus])
    k.op("pool", I("memset", m0b[:], 0.0), writes=[m0b])
    k.op("pool", I("memset", m0b[0:64, 0:64], NEG), writes=[m0b])
    k.op("pool", I("memset", m4b[:], 0.0), writes=[m4b])
    k.op("pool", I("memset", m4b[64:128, 64:128], NEG), writes=[m4b])
    m0n = palloc([128, 128], BF16, "m0n")
    k.op("pool", I("memset", m0n[:], 0.0), writes=[m0n])
    k.op("pool", I("memset", m0n[64:128, 0:64], NEG), writes=[m0n])
    mcausb = palloc([128, 128], BF16, "mcausb")
    k.op("pool", I("tensor_copy", out=mcausb[:], in_=mcaus[:]), reads=[mcaus], writes=[mcausb])
    Jf = palloc([128, 128], F32, "Jf")
    Jb = palloc([128, 128], BF16, "Jb")
    k.op("pool", I("memset", Jf[:], 0.0), writes=[Jf])
    k.op("pool", I("affine_select", out=Jf[:], in_=Jf[:], pattern=[[1, 128]], compare_op=ALU.not_equal,
                   fill=1.0, base=-127, channel_multiplier=1), reads=[Jf], writes=[Jf])
    k.op("pool", I("tensor_copy", out=Jb[:], in_=Jf[:]), reads=[Jf], writes=[Jb])

    alt = [0]

    def evac_eng():
        alt[0] += 1
        return "act" if alt[0] % 2 else "dve"

    def copy_op(eng, out, in_, reads, writes, scale=None):
        if eng == "act":
            if scale is None:
                k.op("act", I("copy", out=out, in_=in_), reads, writes)
            else:
                k.op("act", I("activation", out=out, in_=in_, func=AF.Copy, scale=float(scale)), reads, writes)
        else:
            if scale is None:
                k.op(eng, I("tensor_copy", out=out, in_=in_), reads, writes)
            else:
                k.op(eng, I("tensor_scalar", out=out, in0=in_, scalar1=float(scale), scalar2=None, op0=ALU.mult), reads, writes)

    def transpose_tile_to_XT(src_bf, tt, pst_rot, xts_rot):
        xts = xts_rot.next()
        for half in range(2):
            pst = pst_rot.next()
            for j in range(8):
                kc = half * 8 + j
                k.op("pe", I("transpose", out=pst[:, j * 128:(j + 1) * 128], in_=src_bf[:, kc * 128:(kc + 1) * 128],
                             identity=ident[:]), reads=[src_bf, ident], writes=[pst])
            copy_op(evac_eng(), xts[:, half * 8:(half + 1) * 8, :], pst[:].rearrange("p (a b) -> p a b", a=8),
                    [pst], [xts])
        k.op("dq_pool", I("dma_start", out=XT[tt], in_=xts[:]), reads=[xts], writes=[XTb[tt]])

    def load_xT(xT):
        for tt in range(NT):
            k.op("sp", I("dma_start", out=xT[:, :, tt * 128:(tt + 1) * 128], in_=XT[tt]), reads=[XTb[tt]], writes=[xT],
                 nowaw=(tt > 0))

    class WLoader:
        def __init__(self, kc, ncols, nbuf=2, cast_eng="pool"):
            self.kc = kc
            self.ncols = ncols
            self.stg = Rot([k.sb([128, kc, ncols], F32, "wstg") for _ in range(nbuf)])
            self.wb = Rot([k.sb([128, kc, ncols], BF16, "wbf") for _ in range(nbuf)])
            self.cast_eng = cast_eng

        def load(self, Wt, W2d, col_list):
            st = self.stg.next()
            wb = self.wb.next()
            pos = 0
            first = True
            for (c0, n) in col_list:
                src = W2d[:, c0:c0 + n].rearrange("(kc p) n -> p kc n", p=128)
                k.op("sp", I("dma_start", out=st[:, :, pos:pos + n], in_=src), reads=[Wt], writes=[st], nowaw=not first)
                first = False
                pos += n
            if self.cast_eng == "act":
                k.op("act", I("copy", out=wb[:, :, 0:pos], in_=st[:, :, 0:pos]), reads=[st], writes=[wb])
            else:
                k.op(self.cast_eng, I("tensor_copy", out=wb[:, :, 0:pos], in_=st[:, :, 0:pos]), reads=[st], writes=[wb])
            return wb

    def layernorm_tile(z, idx, gt, bt, st6, mv, xn):
        for c in range(4):
            k.op("dve", I("bn_stats", out=st6[:, c, :], in_=z[:, c * 512:(c + 1) * 512]), reads=[z], writes=[st6])
        k.op("dve", I("bn_aggr", out=mv[:, 0:2], in_=st6[:].rearrange("p a b -> p (a b)")), reads=[st6], writes=[mv])
        k.op("dve", I("tensor_scalar", out=mv[:, 3:4], in0=mv[:, 1:2], scalar1=1e-5, scalar2=None, op0=ALU.add), reads=[mv], writes=[mv])
        k.op("act", I("activation", out=mv[:, 3:4], in_=mv[:, 3:4], func=AF.Sqrt), reads=[mv], writes=[mv])
        k.op("dve", I("reciprocal", out=mv[:, 2:3], in_=mv[:, 3:4]), reads=[mv], writes=[mv])
        k.op("dve", I("tensor_scalar", out=xn[:], in0=z[:], scalar1=mv[:, 0:1], scalar2=mv[:, 2:3], op0=ALU.subtract,
                      op1=ALU.mult), reads=[z, mv], writes=[xn])
        k.op("pool", I("tensor_tensor", out=xn[:], in0=xn[:], in1=gt[:], op=ALU.mult), reads=[xn, gt], writes=[xn])
        k.op("pool", I("tensor_tensor", out=xn[:], in0=xn[:], in1=bt[:], op=ALU.add), reads=[xn, bt], writes=[xn])

    class LNState:
        def __init__(self, lnrow):
            self.gt = k.sb([128, 2048], F32, "lng")
            self.bt = k.sb([128, 2048], F32, "lnb")
            k.op("sp", I("dma_start", out=self.gt[:], in_=dap(ln_g.t, lnrow * 2048, [[0, 128], [1, 2048]])), reads=[ln_g], writes=[self.gt])
            k.op("sp", I("dma_start", out=self.bt[:], in_=dap(ln_b.t, lnrow * 2048, [[0, 128], [1, 2048]])), reads=[ln_b], writes=[self.bt])
            self.st6 = k.sb([128, 4, 6], F32, "st6")
            self.mv = k.sb([128, 4], F32, "mv")
            self.xin = Rot([k.sb([128, 2048], F32, "lnx") for _ in range(2)])
            self.z = Rot([k.sb([128, 2048], F32, "lnz") for _ in range(2)])
            self.xbf = Rot([k.sb([128, 2048], BF16, "lnxb") for _ in range(2)])
            self.xts = Rot([k.sb([128, 16, 128], BF16, "xts") for _ in range(2)])

        def prefetch(self, Xsrc, tt):
            xin = self.xin.next()
            k.op("sp", I("dma_start", out=xin[:], in_=Xsrc[tt * 128:(tt + 1) * 128, :]), reads=[Xsrc], writes=[xin])
            return xin

        def run(self, xin, sub_aps, sub_tiles, Xdst, tt, pst_rot, final_out=None):
            z = self.z.next()
            for c in range(4):
                k.op("dve", I("scalar_tensor_tensor", out=z[:, c * 512:(c + 1) * 512], in0=xin[:, c * 512:(c + 1) * 512],
                              scalar=float(ALPHA), in1=sub_aps[c], op0=ALU.mult, op1=ALU.add),
                     reads=[xin] + sub_tiles, writes=[z])
            layernorm_tile(z, 0, self.gt, self.bt, self.st6, self.mv, z)
            k.op("dq_pool", I("dma_start", out=Xdst[tt * 128:(tt + 1) * 128, :], in_=z[:]), reads=[z], writes=[Xdst], nowaw=True)
            if final_out is not None:
                k.op("dq_pool", I("dma_start", out=final_out[tt * 128:(tt + 1) * 128, :], in_=z[:]), reads=[z], writes=[])
            else:
                xbf = self.xbf.next()
                k.op("act", I("copy", out=xbf[:], in_=z[:]), reads=[z], writes=[xbf])
                transpose_tile_to_XT(xbf, tt, pst_rot, self.xts)

    def phase_init():
        stage('phase_init')
        with k.phase():
            xin = Rot([k.sb([128, 2048], F32, "ix") for _ in range(2)])
            xbf = Rot([k.sb([128, 2048], BF16, "ixb") for _ in range(2)])
            xts = Rot([k.sb([128, 16, 128], BF16, "xts") for _ in range(2)])
            pst = Rot([k.ps([128, 1024], BF16, "pst") for _ in range(2)])
            for tt in range(NT):
                xi = xin.next()
                k.op("sp", I("dma_start", out=xi[:], in_=x0[tt * 128:(tt + 1) * 128, :]), reads=[x0], writes=[xi])
                xb = xbf.next()
                k.op("pool", I("tensor_copy", out=xb[:], in_=xi[:]), reads=[xi], writes=[xb])
                transpose_tile_to_XT(xb, tt, pst, xts)

    def phase_qkv(Wt, W2d, cacheK, cacheV, ncache, k_out_rows, v_out_rows, ks_out, vs_out, roll_k=None, roll_v=None):
        stage('phase_qkv')
        scale = 128 ** -0.5
        with k.phase():
            xT = k.sb([128, KC, T], BF16, "xT")
            load_xT(xT)
            wl = WLoader(KC, 256, nbuf=2, cast_eng="pool")
            pp = Rot([k.ps([128, 512], F32, "pp") for _ in range(4)])
            qts = Rot([k.sb([128, 512], BF16, "qts") for _ in range(3)])
            for which, dst in ((0, QT), (1, KTX)):
                for hp in range(8):
                    wb = wl.load(Wt, W2d, [(which * 2048 + hp * 256, 256)])
                    for hh in range(2):
                        h = hp * 2 + hh
                        for (t0, tn) in TOKB:
                            ps = pp.next()
                            for kc in range(KC):
                                k.op("pe", I("matmul", ps[:, 0:tn], lhsT=wb[:, kc, hh * 128:(hh + 1) * 128], rhs=xT[:, kc, t0:t0 + tn],
                                             start=(kc == 0), stop=(kc == KC - 1)), reads=[wb, xT], writes=[ps])
                            qs = qts.next()
                            copy_op(evac_eng(), qs[:, 0:tn], ps[:, 0:tn], [ps], [qs], scale=(scale if which == 0 else None))
                            k.op("dq_pool", I("dma_start", out=dst[h, :, t0:t0 + tn], in_=qs[:, 0:tn]), reads=[qs], writes=[dst])
            stage('qkv_tokmajor')
            vts = Rot([k.sb([128, 512], BF16, "vts") for _ in range(3)])
            vfs = Rot([k.sb([128, 512], F32, "vfs") for _ in range(3)])
            kdict = dict(k_out_rows)
            vdict = dict(v_out_rows)
            for which in (2, 1):
                for ns in range(4):
                    wb0 = wl.load(Wt, W2d, [(which * 2048 + ns * 512, 256)])
                    wb1 = wl.load(Wt, W2d, [(which * 2048 + ns * 512 + 256, 256)])
                    for tt in range(NT):
                        outd = (vdict if which == 2 else kdict)
                        need_f32 = (tt in outd) or tt == 16
                        if which == 1 and not need_f32:
                            continue
                        ps = pp.next()
                        for hf, wb in ((0, wb0), (1, wb1)):
                            for kc in range(KC):
                                k.op("pe", I("matmul", ps[:, hf * 256:(hf + 1) * 256], lhsT=xT[:, kc, tt * 128:(tt + 1) * 128], rhs=wb[:, kc, :],
                                             start=(kc == 0), stop=(kc == KC - 1)), reads=[wb, xT], writes=[ps])
                        src_t = ps
                        if need_f32:
                            vf = vfs.next()
                            copy_op("dve", vf[:], ps[:], [ps], [vf])
                            src_t = vf
                        if which == 2:
                            vt = vts.next()
                            copy_op("act", vt[:], src_t[:], [src_t], [vt])
                            k.op("dq_pool", I("dma_start", out=VX[tt * 128:(tt + 1) * 128, ns * 512:(ns + 1) * 512], in_=vt[:]), reads=[vt], writes=[VX], nowaw=True)
                        if need_f32:
                            if tt in outd:
                                dd = outd[tt]
                                k.op("dq_pool", I("dma_start", out=dd[1][:, ns * 512:(ns + 1) * 512], in_=vf[0:dd[0], :]), reads=[vf], writes=[])
                            if tt == 16:
                                so = vs_out if which == 2 else ks_out
                                k.op("dq_pool", I("dma_start", out=so[:, ns * 512:(ns + 1) * 512], in_=vf[0:64, :]), reads=[vf], writes=[])
        stage('qkv_caches')
        with k.phase():
            cin = Rot([k.sb([128, 2048], F32, "cin") for _ in range(2)])
            cbf = Rot([k.sb([128, 2048], BF16, "cbf") for _ in range(2)])
            kts = Rot([k.sb([128, 16, 128], BF16, "kts") for _ in range(2)])
            pst = Rot([k.ps([128, 1024], BF16, "pst") for _ in range(2)])
            for c in range(ncache // 128):
                ci = cin.next()
                k.op("sp", I("dma_start", out=ci[:], in_=cacheK[c * 128:(c + 1) * 128, :]), reads=[], writes=[ci])
                if roll_k is not None:
                    k.op("sp", I("dma_start", out=roll_k[c * 128:c * 128 + 64, :], in_=ci[64:128, :]), reads=[ci], writes=[])
                    if c > 0:
                        k.op("sp", I("dma_start", out=roll_k[c * 128 - 64:c * 128, :], in_=ci[0:64, :]), reads=[ci], writes=[])
                cb = cbf.next()
                k.op("pool", I("tensor_copy", out=cb[:], in_=ci[:]), reads=[ci], writes=[cb])
                kt = kts.next()
                for half in range(2):
                    ps = pst.next()
                    for j in range(8):
                        h = half * 8 + j
                        k.op("pe", I("transpose", out=ps[:, j * 128:(j + 1) * 128], in_=cb[:, h * 128:(h + 1) * 128], identity=ident[:]),
                             reads=[cb, ident], writes=[ps])
                    copy_op(evac_eng(), kt[:, half * 8:(half + 1) * 8, :], ps[:].rearrange("p (a b) -> p a b", a=8), [ps], [kt])
                k.op("dq_pool", I("dma_start", out=KTX[:, :, T + c * 128:T + (c + 1) * 128].rearrange("h p n -> p h n"), in_=kt[:]),
                     reads=[kt], writes=[KTX])
                ci = cin.next()
                k.op("sp", I("dma_start", out=ci[:], in_=cacheV[c * 128:(c + 1) * 128, :]), reads=[], writes=[ci])
                if roll_v is not None:
                    k.op("sp", I("dma_start", out=roll_v[c * 128:c * 128 + 64, :], in_=ci[64:128, :]), reads=[ci], writes=[])
                    if c > 0:
                        k.op("sp", I("dma_start", out=roll_v[c * 128 - 64:c * 128, :], in_=ci[0:64, :]), reads=[ci], writes=[])
                cb = cbf.next()
                k.op("pool", I("tensor_copy", out=cb[:], in_=ci[:]), reads=[ci], writes=[cb])
                k.op("dq_pool", I("dma_start", out=VX[T + c * 128:T + (c + 1) * 128, :], in_=cb[:]), reads=[cb], writes=[VX])

    def phase_wo_ln(Wt, W2d, Xsrc, Xdst, lnrow, final_out=None):
        stage('phase_wo_ln')
        with k.phase():
            wo = k.sb([128, KC, 2048], BF16, "wo")
            stg = Rot([k.sb([128, KC, 128], F32, "wostg") for _ in range(2)])
            for c in range(16):
                st = stg.next()
                k.op("sp", I("dma_start", out=st[:], in_=W2d[:, c * 128:(c + 1) * 128].rearrange("(kc p) n -> p kc n", p=128)), reads=[Wt], writes=[st])
                k.op("pool", I("tensor_copy", out=wo[:, :, c * 128:(c + 1) * 128], in_=st[:]), reads=[st], writes=[wo], nowaw=(c > 0))
            ln = LNState(lnrow)
            att = Rot([k.sb([128, KC, 128], BF16, "att") for _ in range(2)])
            pp = Rot([k.ps([128, 2048], F32, "pwo") for _ in range(1)])
            pst = Rot([k.ps([128, 1024], BF16, "pst") for _ in range(2)])
            for tt in range(NT):
                xin = ln.prefetch(Xsrc, tt)
                at = att.next()
                k.op("sp", I("dma_start", out=at[:], in_=ATd[:, :, tt * 128:(tt + 1) * 128].rearrange("h p n -> p h n")), reads=[ATd], writes=[at])
                ps = pp.next()
                for ns in range(4):
                    for kc in range(KC):
                        k.op("pe", I("matmul", ps[:, ns * 512:(ns + 1) * 512], lhsT=at[:, kc, :], rhs=wo[:, kc, ns * 512:(ns + 1) * 512],
                                     start=(kc == 0), stop=(kc == KC - 1)), reads=[at, wo], writes=[ps])
                ln.run(xin, [ps[:, c * 512:(c + 1) * 512] for c in range(4)], [ps], Xdst, tt, pst, final_out)

    def phase_ln_from_dram(Ysrc_tiles, Ybufs, Xsrc, Xdst, lnrow, final_out=None):
        stage('phase_ln_from_dram')
        with k.phase():
            ln = LNState(lnrow)
            yr = Rot([k.sb([128, 2048], F32, "lny") for _ in range(2)])
            pst = Rot([k.ps([128, 1024], BF16, "pst") for _ in range(2)])
            for tt in range(NT):
                xin = ln.prefetch(Xsrc, tt)
                y = yr.next()
                k.op("sp", I("dma_start", out=y[:], in_=Ysrc_tiles[tt]), reads=[Ybufs[tt]], writes=[y])
                ln.run(xin, [y[:, c * 512:(c + 1) * 512] for c in range(4)], [y], Xdst, tt, pst, final_out)

    def phase_attn_a(j, Xsrc, Xdst, lnrow):
        stage('phase_attn_a')
        with k.phase():
            ext_s = k.sb([16, 384], F32, "ext_s")
            k.op("sp", I("dma_start", out=ext_s[:, 0:257], in_=a_rel[j]), reads=[a_rel], writes=[ext_s])
            k.op("dve", I("tensor_copy", out=ext_s[:, 257:384], in_=ext_s[:, 256:257].to_broadcast([16, 127])), reads=[ext_s], writes=[ext_s])
            k.op("sp", I("dma_start", out=EXT[:, :], in_=ext_s[:]), reads=[ext_s], writes=[EXT])
            cst = k.sb([128, 16], F32, "cst")
            k.op("sp", I("dma_start", out=cst[:], in_=dap(a_rel.t, j * 16 * 257 + 256, [[0, 128], [257, 16]]), allow_slow_non_contiguous=True), reads=[a_rel], writes=[cst])
            aor = Rot([k.sb([128, 512], BF16, "ao") for _ in range(3)])
            qh = Rot([k.sb([128, T], BF16, "qh") for _ in range(2)])
            kh = Rot([k.sb([128, KEXT], BF16, "kh") for _ in range(2)])
            vh = Rot([k.sb([128, 25, 128], BF16, "vh") for _ in range(2)])
            bf32 = Rot([k.sb([128, 2, 128], F32, "bf32") for _ in range(2)])
            bt = Rot([k.sb([128, 4, 128], BF16, "bt") for _ in range(2)])
            ptr = Rot([k.sb([128, 5, 128], BF16, "pt") for _ in range(3)])
            ps_s = Rot([k.ps([128, 1024], F32, "ps_s") for _ in range(2)])
            ps_o = Rot([k.ps([128, 512], F32, "ps_o") for _ in range(2)])
            ps_z = Rot([k.ps([128, 512], F32, "ps_z") for _ in range(2)])
            rz = Rot([k.sb([128, 512], F32, "rz") for _ in range(2)])
            for h in range(16):
                q = qh.next(); kk = kh.next(); v = vh.next()
                k.op("sp", I("dma_start", out=q[:], in_=QT[h]), reads=[QT], writes=[q])
                k.op("sp", I("dma_start", out=kk[:, 0:T + 512], in_=KTX[h, :, 0:T + 512]), reads=[KTX], writes=[kk])
                k.op("sp", I("dma_start", out=v[:, 0:21, :], in_=VX[0:21 * 128, h * 128:(h + 1) * 128].rearrange("(t p) d -> p t d", p=128)),
                     reads=[VX], writes=[v])
                bf = bf32.next()
                k.op("sp", I("dma_start", out=bf[:, 0, :], in_=dap(EXT.t, h * 384 + 1, [[1, 128], [1, 128]])), reads=[EXT], writes=[bf])
                k.op("sp", I("dma_start", out=bf[:, 1, :], in_=dap(EXT.t, h * 384 + 129, [[1, 128], [1, 128]])), reads=[EXT], writes=[bf], nowaw=True)
                b = bt.next()
                k.op("dve", I("tensor_tensor", out=b[:, 0, :], in0=bf[:, 0, :], in1=m0b[:], op=ALU.add), reads=[bf, m0b], writes=[b])
                k.op("dve", I("tensor_copy", out=b[:, 1, :], in_=bf[:, 1, :]), reads=[bf], writes=[b])
                k.op("dve", I("tensor_scalar", out=b[:, 2, :], in0=zeros_b[:], scalar1=cst[:, h:h + 1], scalar2=None, op0=ALU.add), reads=[zeros_b, cst], writes=[b])
                k.op("dve", I("tensor_scalar", out=b[:, 3, :], in0=m4b[:], scalar1=cst[:, h:h + 1], scalar2=None, op0=ALU.add), reads=[m4b, cst], writes=[b])
                btype = {0: 0, 1: 1, 2: 2, 3: 2, 4: 3}
                for sbk in range(5):
                    po = ps_o.next(); pz = ps_z.next()
                    if sbk < 4:
                        blocks = [(sbk * 4 + i, 128) for i in range(4)]
                    else:
                        blocks = [(16, 64)]
                    for bi, (bq, nq) in enumerate(blocks):
                        kts = []
                        if bq < 16:
                            for ty in range(4, -1, -1):
                                kt_ = bq - ty
                                if kt_ < 0:
                                    continue
                                kts.append((kk[:, kt_ * 128:(kt_ + 1) * 128], v[:, kt_, :], 128, b[:, btype[ty], 0:nq]))
                        else:
                            for c in range(4):
                                ty = 1 if c == 3 else 2
                                kts.append((kk[:, T + c * 128:T + (c + 1) * 128], v[:, 17 + c, :], 128, b[:, ty, 0:nq]))
                            kts.append((kk[:, 2048:2112], v[0:64, 16, :], 64, b[64:128, 0, 0:nq]))
                        pss = ps_s.next()
                        pt = ptr.next()
                        qa = q[:, bq * 128:bq * 128 + nq]
                        for s, (ka, va, nk, ba) in enumerate(kts):
                            k.op("pe", I("matmul", pss[0:nk, s * 128:s * 128 + nq], lhsT=ka, rhs=qa, start=True, stop=False), reads=[kk, q], writes=[pss])
                            k.op("pe", I("matmul", pss[0:nk, s * 128:s * 128 + nq], lhsT=(Jb[:] if nk == 128 else Jb[64:128, 0:64]), rhs=ba, start=False, stop=True), reads=[Jb, b], writes=[pss])
                        for s, (ka, va, nk, ba) in enumerate(kts):
                            k.op("act", I("activation", out=pt[0:nk, s, 0:nq], in_=pss[0:nk, s * 128:s * 128 + nq], func=AF.Exp), reads=[pss], writes=[pt])
                        for s, (ka, va, nk, ba) in enumerate(kts):
                            k.op("pe", I("matmul", po[:, bi * 128:bi * 128 + nq], lhsT=va, rhs=pt[0:nk, s, 0:nq], start=(s == 0), stop=(s == len(kts) - 1)),
                                 reads=[v, pt], writes=[po])
                        for s, (ka, va, nk, ba) in enumerate(kts):
                            k.op("pe", I("matmul", pz[:, bi * 128:bi * 128 + nq], lhsT=ones_b[0:nk, :], rhs=pt[0:nk, s, 0:nq], start=(s == 0), stop=(s == len(kts) - 1)),
                                 reads=[ones_b, pt], writes=[pz])
                    ncol = 512 if sbk < 4 else 64
                    r = rz.next()
                    k.op("dve", I("reciprocal", out=r[:, 0:ncol], in_=pz[:, 0:ncol]), reads=[pz], writes=[r])
                    ao = aor.next()
                    if sbk == 4:
                        k.op("pool", I("memset", ao[:, 64:128], 0.0), writes=[ao])
                    k.op("dve", I("tensor_tensor", out=ao[:, 0:ncol], in0=po[:, 0:ncol], in1=r[:, 0:ncol], op=ALU.mult),
                         reads=[po, r], writes=[ao], nowaw=True)
                    nst = 512 if sbk < 4 else 128
                    k.op("dq_pool", I("dma_start", out=ATd[h, :, sbk * 512:sbk * 512 + nst], in_=ao[:, 0:nst]), reads=[ao], writes=[ATd], nowaw=True)
        phase_wo_ln(a_wo, a_wo[j], Xsrc, Xdst, lnrow)

    def phase_attn_c(Xsrc, Xdst, lnrow):
        stage('phase_attn_c')
        with k.phase():
            aor = Rot([k.sb([128, 512], BF16, "ao") for _ in range(2)])
            qh = Rot([k.sb([128, T], BF16, "qh") for _ in range(2)])
            kh = Rot([k.sb([128, KEXT], BF16, "kh") for _ in range(2)])
            vh = Rot([k.sb([128, 25, 128], BF16, "vh") for _ in range(2)])
            e1r = Rot([k.sb([128, 512], F32, "e1") for _ in range(2)])
            lspr = Rot([k.sb([128, 512], F32, "lsp") for _ in range(2)])
            lsnr = Rot([k.sb([128, 512], F32, "lsn") for _ in range(2)])
            t1r = Rot([k.sb([128, 512], F32, "t1") for _ in range(2)])
            wr = Rot([k.sb([128, 512], BF16, "w") for _ in range(2)])
            carry = k.sb([128, 512], F32, "carry")
            ps_z = Rot([k.ps([128, 512], F32, "ps_z") for _ in range(2)])
            ps_a = Rot([k.ps([128, 512], F32, "ps_a") for _ in range(2)])
            ps_c = Rot([k.ps([128, 512], F32, "ps_c") for _ in range(2)])
            ps_o = Rot([k.ps([128, 512], F32, "ps_o") for _ in range(2)])
            for h in range(16):
                q = qh.next(); kk = kh.next(); v = vh.next()
                k.op("sp", I("dma_start", out=q[:], in_=QT[h]), reads=[QT], writes=[q])
                k.op("sp", I("dma_start", out=kk[:], in_=KTX[h]), reads=[KTX], writes=[kk])
                k.op("sp", I("dma_start", out=v[:], in_=VX[:, h * 128:(h + 1) * 128].rearrange("(t p) d -> p t d", p=128)), reads=[VX], writes=[v])
                for sbk in range(5):
                    if sbk < 4:
                        qc0, nq = sbk * 512, 512
                        kts = [(128, kk[:, kt * 128:(kt + 1) * 128], v[:, kt, :], max(0, kt - 4 * sbk) * 128, kt >= 4 * sbk)
                               for kt in range(4 * sbk + 3, -1, -1)]
                    else:
                        qc0, nq = 2048, 64
                        kts = [(64, kk[:, 2048:2112], v[0:64, 16, :], 0, True)]
                        kts += [(128, kk[:, T + c * 128:T + (c + 1) * 128], v[:, 17 + c, :], 0, False) for c in range(7, -1, -1)]
                    po = ps_o.next()
                    k.op("pe", I("matmul", po[:, 0:nq], lhsT=zeros_b[:], rhs=q[:, qc0:qc0 + nq], start=True, stop=False), reads=[zeros_b, q], writes=[po])
                    k.op("pool", I("memset", carry[:], 0.0), writes=[carry])
                    for si, (nk, ka, va, c0, diag) in enumerate(kts):
                        last = (si == len(kts) - 1)
                        dn = min(128, nq - c0)
                        pz = ps_z.next()
                        k.op("pe", I("matmul", pz[0:nk, c0:nq], lhsT=ka, rhs=q[:, qc0 + c0:qc0 + nq], start=True, stop=True), reads=[kk, q], writes=[pz])
                        e1 = e1r.next()
                        k.op("act", I("activation", out=e1[0:nk, c0:nq], in_=pz[0:nk, c0:nq], func=AF.Exp, scale=-1.0), reads=[pz], writes=[e1])
                        lsp = lspr.next()
                        k.op("act", I("activation", out=lsp[0:nk, c0:nq], in_=e1[0:nk, c0:nq], func=AF.Ln, bias=ones_f[0:nk, 0:1], scale=1.0), reads=[e1, ones_f], writes=[lsp])
                        lsn = lsnr.next()
                        k.op("dve", I("tensor_tensor", out=lsn[0:nk, c0:nq], in0=pz[0:nk, c0:nq], in1=lsp[0:nk, c0:nq], op=ALU.add), reads=[pz, lsp], writes=[lsn])
                        if diag:
                            k.op("pool", I("tensor_tensor", out=lsn[0:nk, c0:c0 + dn], in0=lsn[0:nk, c0:c0 + dn], in1=mcaus[0:nk, 0:dn], op=ALU.mult),
                                 reads=[lsn, mcaus], writes=[lsn])
                        pa = ps_a.next(); pc = ps_c.next()
                        k.op("pe", I("matmul", pa[0:nk, c0:nq], lhsT=Uf[0:nk, 0:nk], rhs=lsn[0:nk, c0:nq], start=True, stop=True), reads=[Uf, lsn], writes=[pa])
                        k.op("pe", I("matmul", pc[:, c0:nq], lhsT=ones_f[0:nk, :], rhs=lsn[0:nk, c0:nq], start=True, stop=True), reads=[ones_f, lsn], writes=[pc])
                        t1 = t1r.next()
                        k.op("dve", I("tensor_tensor", out=t1[0:nk, c0:nq], in0=pa[0:nk, c0:nq], in1=lsp[0:nk, c0:nq], op=ALU.add), reads=[pa, lsp], writes=[t1])
                        k.op("pool", I("tensor_tensor", out=t1[0:nk, c0:nq], in0=t1[0:nk, c0:nq], in1=carry[0:nk, c0:nq], op=ALU.add), reads=[t1, carry], writes=[t1])
                        w = wr.next()
                        k.op("act", I("activation", out=w[0:nk, c0:nq], in_=t1[0:nk, c0:nq], func=AF.Exp, scale=-1.0), reads=[t1], writes=[w])
                        if diag:
                            k.op("pool", I("tensor_tensor", out=w[0:nk, c0:c0 + dn], in0=w[0:nk, c0:c0 + dn], in1=mcausb[0:nk, 0:dn], op=ALU.mult),
                                 reads=[w, mcausb], writes=[w])
                        if not last:
                            k.op("dve", I("tensor_tensor", out=carry[:, c0:nq], in0=pc[:, c0:nq], in1=carry[:, c0:nq], op=ALU.add), reads=[pc, carry], writes=[carry])
                        k.op("pe", I("matmul", po[:, c0:nq], lhsT=va, rhs=w[0:nk, c0:nq], start=False, stop=last), reads=[v, w], writes=[po])
                    ao = aor.next()
                    if sbk == 4:
                        k.op("pool", I("memset", ao[:, 64:128], 0.0), writes=[ao])
                    copy_op("act", ao[:, 0:nq], po[:, 0:nq], [po], [ao])
                    nst = 512 if sbk < 4 else 128
                    k.op("sp", I("dma_start", out=ATd[h, :, sbk * 512:sbk * 512 + nst], in_=ao[:, 0:nst]), reads=[ao], writes=[ATd], nowaw=True)
        phase_wo_ln(c_wo, c_wo[:, :], Xsrc, Xdst, lnrow)

    def rmsnorm_psum(ps, gtile, out_ap, st6, mv, reads_extra, out_tile):
        k.op("dve", I("bn_stats", out=st6[:, 0, :], in_=ps[:]), reads=[ps], writes=[st6])
        k.op("dve", I("bn_aggr", out=mv[:, 0:2], in_=st6[:, 0, :]), reads=[st6], writes=[mv])
        k.op("dve", I("scalar_tensor_tensor", out=mv[:, 2:3], in0=mv[:, 0:1], scalar=mv[:, 0:1], in1=mv[:, 1:2], op0=ALU.mult, op1=ALU.add), reads=[mv], writes=[mv])
        k.op("dve", I("tensor_scalar", out=mv[:, 2:3], in0=mv[:, 2:3], scalar1=1e-6, scalar2=None, op0=ALU.add), reads=[mv], writes=[mv])
        k.op("act", I("activation", out=mv[:, 2:3], in_=mv[:, 2:3], func=AF.Sqrt), reads=[mv], writes=[mv])
        k.op("dve", I("reciprocal", out=mv[:, 3:4], in_=mv[:, 2:3]), reads=[mv], writes=[mv])
        k.op("dve", I("scalar_tensor_tensor", out=out_ap, in0=ps[:], scalar=mv[:, 3:4], in1=gtile[:], op0=ALU.mult, op1=ALU.mult), reads=[ps, mv, gtile], writes=[out_tile])

    def rope_ops(src3, cs, sn, dst3, nh, tmp_a, tmp_b, reads, dst_tile, scale=None):
        cb = sap(cs, 0, [[0, nh], [1, 32]])
        sb_ = sap(sn, 0, [[0, nh], [1, 32]])
        x1 = src3[:, :, 0:32]; x2 = src3[:, :, 32:64]
        ta = tmp_a[:, 0:nh * 32].rearrange("p (h r) -> p h r", r=32)
        tb = tmp_b[:, 0:nh * 32].rearrange("p (h r) -> p h r", r=32)
        k.op("dve", I("tensor_tensor", out=ta, in0=x1, in1=cb, op=ALU.mult), reads=reads + [cs], writes=[tmp_a])
        k.op("dve", I("tensor_tensor", out=tb, in0=x2, in1=sb_, op=ALU.mult), reads=reads + [sn], writes=[tmp_b])
        k.op("dve", I("tensor_tensor", out=dst3[:, :, 0:32], in0=ta, in1=tb, op=ALU.subtract), reads=[tmp_a, tmp_b], writes=[dst_tile])
        k.op("dve", I("tensor_tensor", out=ta, in0=x1, in1=sb_, op=ALU.mult), reads=reads + [sn], writes=[tmp_a])
        k.op("dve", I("tensor_tensor", out=tb, in0=x2, in1=cb, op=ALU.mult), reads=reads + [cs], writes=[tmp_b])
        k.op("dve", I("tensor_tensor", out=dst3[:, :, 32:64], in0=ta, in1=tb, op=ALU.add), reads=[tmp_a, tmp_b], writes=[dst_tile], nowaw=True)

    def phase_mla(Xsrc, Xdst, lnrow):
        stage('phase_mla_in')
        sc = 192 ** -0.5
        with k.phase():
            xT = k.sb([128, KC, T], BF16, "xT")
            load_xT(xT)
            winb = k.sb([128, KC, 1088], BF16, "winb")
            stg = Rot([k.sb([128, KC, 128], F32, "wstg") for _ in range(2)])
            for c in range(9):
                n = 128 if c < 8 else 64
                st = stg.next()
                k.op("sp", I("dma_start", out=st[:, :, 0:n], in_=b_win[:, c * 128:c * 128 + n].rearrange("(kc p) n -> p kc n", p=128)), reads=[b_win], writes=[st])
                k.op("pool", I("tensor_copy", out=winb[:, :, c * 128:c * 128 + n], in_=st[:, :, 0:n]), reads=[st], writes=[winb], nowaw=(c > 0))
            qn_t = k.sb([128, 512], F32, "qn_t"); kvn_t = k.sb([128, 512], F32, "kvn_t")
            k.op("sp", I("dma_start", out=qn_t[:], in_=dap(b_qn.t, 0, [[0, 128], [1, 512]])), reads=[b_qn], writes=[qn_t])
            k.op("sp", I("dma_start", out=kvn_t[:], in_=dap(b_kvn.t, 0, [[0, 128], [1, 512]])), reads=[b_kvn], writes=[kvn_t])
            st6 = k.sb([128, 1, 6], F32, "st6"); mv = k.sb([128, 4], F32, "mv")
            csr = Rot([k.sb([128, 32], F32, "cs") for _ in range(2)]); snr = Rot([k.sb([128, 32], F32, "sn") for _ in range(2)])
            p0r = Rot([k.ps([128, 512], F32, "p0") for _ in range(2)])
            p1r = Rot([k.ps([128, 512], F32, "p1") for _ in range(2)])
            p2r = Rot([k.ps([128, 64], F32, "p2") for _ in range(1)])
            pst = Rot([k.ps([128, 1024], BF16, "pst") for _ in range(2)])
            cqb = Rot([k.sb([128, 512], BF16, "cqb") for _ in range(2)])
            ckf = Rot([k.sb([128, 512], F32, "ckf") for _ in range(2)])
            ckb = Rot([k.sb([128, 512], BF16, "ckb") for _ in range(2)])
            krf = Rot([k.sb([128, 1, 64], F32, "krf") for _ in range(2)])
            krb = Rot([k.sb([128, 128], BF16, "krb") for _ in range(2)])
            ta = k.sb([128, 512], F32, "ta"); tb = k.sb([128, 512], F32, "tb")
            tsb = Rot([k.sb([128, 9, 128], BF16, "tsb") for _ in range(2)])

            def transposes_out(cq_bf, ck_bf, kr_bf, col0):
                ps = pst.next()
                ts = tsb.next()
                n = 0
                srcs = []
                if cq_bf is not None:
                    srcs += [(cq_bf, c) for c in range(4)]
                srcs += [(ck_bf, c) for c in range(4)]
                for (tl, c) in srcs:
                    k.op("pe", I("transpose", out=ps[:, n * 128:(n + 1) * 128], in_=tl[:, c * 128:(c + 1) * 128], identity=ident[:]), reads=[tl, ident], writes=[ps])
                    n += 1
                copy_op(evac_eng(), ts[:, 0:n, :], ps[:, 0:n * 128].rearrange("p (a b) -> p a b", b=128), [ps], [ts])
                ps2 = pst.next()
                k.op("pe", I("transpose", out=ps2[:, 0:128], in_=kr_bf[:], identity=ident[:]), reads=[kr_bf, ident], writes=[ps2])
                copy_op(evac_eng(), ts[:, 8, :], ps2[:, 0:128], [ps2], [ts])
                o = 0
                if cq_bf is not None:
                    k.op("sp", I("dma_start", out=CQT[:, :, col0:col0 + 128].rearrange("c p n -> p c n"), in_=ts[:, 0:4, :]), reads=[ts], writes=[CQT], nowaw=True)
                    o = 4
                k.op("sp", I("dma_start", out=CKVT[:, :, col0:col0 + 128].rearrange("c p n -> p c n"), in_=ts[:, o:o + 4, :]), reads=[ts], writes=[CKVT], nowaw=True)
                k.op("sp", I("dma_start", out=KR[:, col0:col0 + 128], in_=ts[:, 8, :]), reads=[ts], writes=[KR], nowaw=True)

            for tt in range(NT):
                cs = csr.next(); sn = snr.next()
                k.op("sp", I("dma_start", out=cs[:], in_=ropec[tt * 128:(tt + 1) * 128, :]), reads=[ropec], writes=[cs])
                k.op("sp", I("dma_start", out=sn[:], in_=ropes[tt * 128:(tt + 1) * 128, :]), reads=[ropes], writes=[sn])
                p0 = p0r.next(); p1 = p1r.next(); p2 = p2r.next()
                for (pt_, c0, n) in ((p0, 0, 512), (p1, 512, 512), (p2, 1024, 64)):
                    for kc in range(KC):
                        k.op("pe", I("matmul", pt_[:, 0:n], lhsT=xT[:, kc, tt * 128:(tt + 1) * 128], rhs=winb[:, kc, c0:c0 + n], start=(kc == 0), stop=(kc == KC - 1)),
                             reads=[xT, winb], writes=[pt_])
                cq = cqb.next()
                rmsnorm_psum(p0, qn_t, cq[:], st6, mv, [], cq)
                cf = ckf.next()
                rmsnorm_psum(p1, kvn_t, cf[:], st6, mv, [], cf)
                if tt < 16:
                    k.op("sp", I("dma_start", out=obc[tt * 128:(tt + 1) * 128, :], in_=cf[:]), reads=[cf], writes=[])
                else:
                    k.op("sp", I("dma_start", out=sbc[:, :], in_=cf[0:64, :]), reads=[cf], writes=[])
                cb_ = ckb.next()
                k.op("act", I("copy", out=cb_[:], in_=cf[:]), reads=[cf], writes=[cb_])
                kf = krf.next()
                rope_ops(p2[:].rearrange("p (h c) -> p h c", c=64), cs, sn, kf[:], 1, ta, tb, [p2], kf)
                if tt < 16:
                    k.op("sp", I("dma_start", out=obr[tt * 128:(tt + 1) * 128, :], in_=kf[:, 0, :]), reads=[kf], writes=[])
                else:
                    k.op("sp", I("dma_start", out=sbr[:, :], in_=kf[0:64, 0, :]), reads=[kf], writes=[])
                kb = krb.next()
                k.op("act", I("copy", out=kb[:, 0:64], in_=kf[:, 0, :]), reads=[kf], writes=[kb])
                k.op("act", I("copy", out=kb[:, 64:128], in_=kf[:, 0, :]), reads=[kf], writes=[kb], nowaw=True)
                transposes_out(cq, cb_, kb, tt * 128)
            for c in range(8):
                cf = ckf.next()
                k.op("sp", I("dma_start", out=cf[:], in_=cb_c[c * 128:(c + 1) * 128, :]), reads=[], writes=[cf])
                cb_ = ckb.next()
                k.op("act", I("copy", out=cb_[:], in_=cf[:]), reads=[cf], writes=[cb_])
                kf = krf.next()
                k.op("sp", I("dma_start", out=kf[:, 0, :], in_=cb_r[c * 128:(c + 1) * 128, :]), reads=[], writes=[kf])
                kb = krb.next()
                k.op("act", I("copy", out=kb[:, 0:64], in_=kf[:, 0, :]), reads=[kf], writes=[kb])
                k.op("act", I("copy", out=kb[:, 64:128], in_=kf[:, 0, :]), reads=[kf], writes=[kb], nowaw=True)
                transposes_out(None, cb_, kb, T + c * 128)
        stage('phase_mla_q')
        with k.phase():
            cqT = k.sb([128, 4, T], BF16, "cqT")
            k.op("sp", I("dma_start", out=cqT[:], in_=CQT[:, :, :].rearrange("c p n -> p c n")), reads=[CQT], writes=[cqT])
            wqn = k.sb([128, 4, 2048], BF16, "wqn"); wqr = k.sb([128, 4, 1024], BF16, "wqr")
            stg = Rot([k.sb([128, 4, 768], F32, "wstg") for _ in range(2)])
            for c in range(4):
                st = stg.next()
                k.op("sp", I("dma_start", out=st[:], in_=b_wqb[:, c * 768:(c + 1) * 768].rearrange("(kc p) n -> p kc n", p=128)), reads=[b_wqb], writes=[st])
                for kc in range(4):
                    sv = st[:, kc, :].rearrange("p (h c) -> p h c", c=192)
                    k.op("pool", I("tensor_copy", out=wqn[:, kc, c * 512:(c + 1) * 512].rearrange("p (h c) -> p h c", c=128), in_=sv[:, :, 0:128]), reads=[st], writes=[wqn], nowaw=True)
                    k.op("pool", I("tensor_copy", out=wqr[:, kc, c * 256:(c + 1) * 256].rearrange("p (h c) -> p h c", c=64), in_=sv[:, :, 128:192]), reads=[st], writes=[wqr], nowaw=True)
            pp = Rot([k.ps([128, 512], F32, "pp") for _ in range(3)])
            qts = Rot([k.sb([128, 512], BF16, "qts") for _ in range(3)])
            for h in range(16):
                for (t0, tn) in TOKB:
                    ps = pp.next()
                    for kc in range(4):
                        k.op("pe", I("matmul", ps[:, 0:tn], lhsT=wqn[:, kc, h * 128:(h + 1) * 128], rhs=cqT[:, kc, t0:t0 + tn], start=(kc == 0), stop=(kc == 3)),
                             reads=[wqn, cqT], writes=[ps])
                    qs = qts.next()
                    copy_op(evac_eng(), qs[:, 0:tn], ps[:, 0:tn], [ps], [qs], scale=sc)
                    k.op("sp", I("dma_start", out=QT[h, :, t0:t0 + tn], in_=qs[:, 0:tn]), reads=[qs], writes=[QT], nowaw=True)
            csr = Rot([k.sb([128, 32], F32, "cs") for _ in range(2)]); snr = Rot([k.sb([128, 32], F32, "sn") for _ in range(2)])
            ta = k.sb([128, 512], F32, "ta"); tb = k.sb([128, 512], F32, "tb")
            qrf = Rot([k.sb([128, 16, 64], F32, "qrf") for _ in range(2)])
            qrb = Rot([k.sb([128, 1024], BF16, "qrb") for _ in range(2)])
            pst = Rot([k.ps([128, 1024], BF16, "pst") for _ in range(2)])
            qrt = Rot([k.sb([128, 8, 128], BF16, "qrt") for _ in range(2)])
            for tt in range(NT):
                cs = csr.next(); sn = snr.next()
                k.op("sp", I("dma_start", out=cs[:], in_=ropec[tt * 128:(tt + 1) * 128, :]), reads=[ropec], writes=[cs])
                k.op("sp", I("dma_start", out=sn[:], in_=ropes[tt * 128:(tt + 1) * 128, :]), reads=[ropes], writes=[sn])
                qf = qrf.next()
                for hg in range(2):
                    ps = pp.next()
                    for kc in range(4):
                        k.op("pe", I("matmul", ps[:, :], lhsT=cqT[:, kc, tt * 128:(tt + 1) * 128], rhs=wqr[:, kc, hg * 512:(hg + 1) * 512], start=(kc == 0), stop=(kc == 3)),
                             reads=[wqr, cqT], writes=[ps])
                    rope_ops(ps[:].rearrange("p (h c) -> p h c", c=64), cs, sn, qf[:, hg * 8:(hg + 1) * 8, :], 8, ta, tb, [ps], qf)
                qb = qrb.next()
                k.op("act", I("activation", out=qb[:], in_=qf[:].rearrange("p h c -> p (h c)"), func=AF.Copy, scale=float(sc)), reads=[qf], writes=[qb])
                ps2 = pst.next()
                for j8 in range(8):
                    k.op("pe", I("transpose", out=ps2[:, j8 * 128:(j8 + 1) * 128], in_=qb[:, j8 * 128:(j8 + 1) * 128], identity=ident[:]), reads=[qb, ident], writes=[ps2])
                qt_ = qrt.next()
                copy_op(evac_eng(), qt_[:], ps2[:].rearrange("p (a b) -> p a b", a=8), [ps2], [qt_])
                k.op("sp", I("dma_start", out=QR[:, :, tt * 128:(tt + 1) * 128].rearrange("a p n -> p a n"), in_=qt_[:]), reads=[qt_], writes=[QR], nowaw=True)
        stage('phase_mla_kv')
        with k.phase():
            ckvT = k.sb([128, 4, KEXT], BF16, "ckvT")
            k.op("sp", I("dma_start", out=ckvT[:], in_=CKVT[:, :, :].rearrange("c p n -> p c n")), reads=[CKVT], writes=[ckvT])
            wkn = k.sb([128, 4, 2048], BF16, "wkn"); wkv = k.sb([128, 4, 2048], BF16, "wkv")
            stg = Rot([k.sb([128, 4, 1024], F32, "wstg") for _ in range(2)])
            for c in range(4):
                st = stg.next()
                k.op("sp", I("dma_start", out=st[:], in_=b_wkvb[:, c * 1024:(c + 1) * 1024].rearrange("(kc p) n -> p kc n", p=128)), reads=[b_wkvb], writes=[st])
                for kc in range(4):
                    sv = st[:, kc, :].rearrange("p (h c) -> p h c", c=256)
                    k.op("pool", I("tensor_copy", out=wkn[:, kc, c * 512:(c + 1) * 512].rearrange("p (h c) -> p h c", c=128), in_=sv[:, :, 0:128]), reads=[st], writes=[wkn], nowaw=True)
                    k.op("pool", I("tensor_copy", out=wkv[:, kc, c * 512:(c + 1) * 512].rearrange("p (h c) -> p h c", c=128), in_=sv[:, :, 128:256]), reads=[st], writes=[wkv], nowaw=True)
            pp = Rot([k.ps([128, 512], F32, "pp") for _ in range(4)])
            qts = Rot([k.sb([128, 512], BF16, "qts") for _ in range(3)])
            KB = [(i * 512, 512) for i in range(6)] + [(3072, 128)]
            for h in range(16):
                for (t0, tn) in KB:
                    ps = pp.next()
                    for kc in range(4):
                        k.op("pe", I("matmul", ps[:, 0:tn], lhsT=wkn[:, kc, h * 128:(h + 1) * 128], rhs=ckvT[:, kc, t0:t0 + tn], start=(kc == 0), stop=(kc == 3)),
                             reads=[wkn, ckvT], writes=[ps])
                    qs = qts.next()
                    copy_op(evac_eng(), qs[:, 0:tn], ps[:, 0:tn], [ps], [qs])
                    k.op("sp", I("dma_start", out=KTX[h, :, t0:t0 + tn], in_=qs[:, 0:tn]), reads=[qs], writes=[KTX], nowaw=True)
            for kt in range(25):
                for ns in range(4):
                    ps = pp.next()
                    for kc in range(4):
                        k.op("pe", I("matmul", ps[:, :], lhsT=ckvT[:, kc, kt * 128:(kt + 1) * 128], rhs=wkv[:, kc, ns * 512:(ns + 1) * 512], start=(kc == 0), stop=(kc == 3)),
                             reads=[wkv, ckvT], writes=[ps])
                    qs = qts.next()
                    copy_op(evac_eng(), qs[:], ps[:], [ps], [qs])
                    k.op("sp", I("dma_start", out=VX[kt * 128:(kt + 1) * 128, ns * 512:(ns + 1) * 512], in_=qs[:]), reads=[qs], writes=[VX], nowaw=True)
        stage('phase_mla_attn')
        with k.phase():
            aor = Rot([k.sb([128, 512], BF16, "ao") for _ in range(2)])
            qh = Rot([k.sb([128, T], BF16, "qh") for _ in range(2)])
            qrh = Rot([k.sb([128, T], BF16, "qrh") for _ in range(2)])
            kh = Rot([k.sb([128, KEXT], BF16, "kh") for _ in range(2)])
            vh = Rot([k.sb([128, 25, 128], BF16, "vh") for _ in range(2)])
            krs = k.sb([128, KEXT], BF16, "krs")
            k.op("sp", I("dma_start", out=krs[:], in_=KR[:, :]), reads=[KR], writes=[krs])
            ptr = Rot([k.sb([128, 512], BF16, "pt") for _ in range(3)])
            rz = Rot([k.sb([128, 512], F32, "rz") for _ in range(2)])
            ps_s = Rot([k.ps([128, 512], F32, "ps_s") for _ in range(3)])
            ps_o = Rot([k.ps([128, 512], F32, "ps_o") for _ in range(2)])
            ps_zz = Rot([k.ps([128, 512], F32, "ps_zz") for _ in range(2)])
            for h in range(16):
                q = qh.next(); kk = kh.next(); v = vh.next()
                hb = (h % 2) * 64
                k.op("sp", I("dma_start", out=q[:], in_=QT[h]), reads=[QT], writes=[q])
                if h % 2 == 0:
                    qr = qrh.next()
                    k.op("sp", I("dma_start", out=qr[:], in_=QR[h // 2]), reads=[QR], writes=[qr])
                k.op("sp", I("dma_start", out=kk[:], in_=KTX[h]), reads=[KTX], writes=[kk])
                k.op("sp", I("dma_start", out=v[:], in_=VX[:, h * 128:(h + 1) * 128].rearrange("(t p) d -> p t d", p=128)), reads=[VX], writes=[v])
                for sbk in range(5):
                    if sbk < 4:
                        qc0, nq = sbk * 512, 512
                        kts = [(128, kt * 128, v[:, kt, :], max(0, kt - 4 * sbk) * 128, kt >= 4 * sbk) for kt in range(0, 4 * sbk + 4)]
                    else:
                        qc0, nq = 2048, 64
                        kts = [(128, T + c * 128, v[:, 17 + c, :], 0, False) for c in range(8)] + [(64, 2048, v[0:64, 16, :], 0, False)]
                    po = ps_o.next(); pz = ps_zz.next()
                    for si, (nk, kc0, va, c0, diag) in enumerate(kts):
                        last = (si == len(kts) - 1)
                        ps = ps_s.next()
                        k.op("pe", I("matmul", ps[0:nk, c0:nq], lhsT=kk[:, kc0:kc0 + nk], rhs=q[:, qc0 + c0:qc0 + nq], start=True, stop=False), reads=[kk, q], writes=[ps])
                        if diag:
                            k.op("pe", I("matmul", ps[0:nk, c0:c0 + 128], lhsT=ident[:], rhs=m0n[:], start=False, stop=False), reads=[ident, m0n], writes=[ps])
                        k.op("pe", I("matmul", ps[0:nk, c0:nq], lhsT=krs[hb:hb + 64, kc0:kc0 + nk], rhs=qr[hb:hb + 64, qc0 + c0:qc0 + nq], start=False, stop=True),
                             reads=[krs, qr], writes=[ps])
                        pt = ptr.next()
                        k.op("act", I("activation", out=pt[0:nk, c0:nq], in_=ps[0:nk, c0:nq], func=AF.Exp), reads=[ps], writes=[pt])
                        k.op("pe", I("matmul", po[:, c0:nq], lhsT=va, rhs=pt[0:nk, c0:nq], start=(si == 0), stop=last), reads=[v, pt], writes=[po])
                        k.op("pe", I("matmul", pz[:, c0:nq], lhsT=ones_b[0:nk, :], rhs=pt[0:nk, c0:nq], start=(si == 0), stop=last), reads=[ones_b, pt], writes=[pz])
                    r = rz.next()
                    k.op("dve", I("reciprocal", out=r[:, 0:nq], in_=pz[:, 0:nq]), reads=[pz], writes=[r])
                    ao = aor.next()
                    if sbk == 4:
                        k.op("pool", I("memset", ao[:, 64:128], 0.0), writes=[ao])
                    k.op("dve", I("tensor_tensor", out=ao[:, 0:nq], in0=po[:, 0:nq], in1=r[:, 0:nq], op=ALU.mult), reads=[po, r], writes=[ao], nowaw=True)
                    nst = 512 if sbk < 4 else 128
                    k.op("sp", I("dma_start", out=ATd[h, :, sbk * 512:sbk * 512 + nst], in_=ao[:, 0:nst]), reads=[ao], writes=[ATd], nowaw=True)
        phase_wo_ln(b_wo, b_wo[:, :], Xsrc, Xdst, lnrow)


    def phase_peer(i, Xsrc, Xdst, lnrow, final_out=None):
        keys2d = p_keys[i]
        stage('peer_scores')
        with k.phase():
            xT = k.sb([128, KC, T], BF16, "xT")
            load_xT(xT)
            kin = Rot([k.sb([128, 128], F32, "kin") for _ in range(2)])
            kbf = Rot([k.sb([128, 128], BF16, "kbf") for _ in range(2)])
            keysT = k.sb([128, 16, 128], BF16, "keysT")
            pst = Rot([k.ps([128, 1024], BF16, "pst") for _ in range(1)])
            for half in range(2):
                ps = pst.next()
                for jx in range(8):
                    hc = half * 8 + jx
                    ki = kin.next(); kb = kbf.next()
                    k.op("sp", I("dma_start", out=ki[:], in_=keys2d[hc]), reads=[p_keys], writes=[ki])
                    k.op("pool", I("tensor_copy", out=kb[:], in_=ki[:]), reads=[ki], writes=[kb])
                    k.op("pe", I("transpose", out=ps[:, jx * 128:(jx + 1) * 128], in_=kb[:], identity=ident[:]), reads=[kb, ident], writes=[ps])
                copy_op("dve", keysT[:, half * 8:(half + 1) * 8, :], ps[:].rearrange("p (a b) -> p a b", a=8), [ps], [keysT])
            wl = WLoader(KC, 256, nbuf=2, cast_eng="pool")
            pq = Rot([k.ps([128, 512], F32, "pq") for _ in range(3)])
            psc = Rot([k.ps([128, 512], F32, "psc") for _ in range(2)])
            qtb = Rot([k.sb([128, 512], BF16, "qtb") for _ in range(3)])
            scs = Rot([k.sb([128, 4, 128], F32, "scs") for _ in range(3)])
            for hp in range(8):
                wb = wl.load(p_wq, p_wq[i], [(hp * 256, 256)])
                for hh in range(2):
                    hc = hp * 2 + hh
                    for (t0, tn) in TOKB:
                        ps = pq.next()
                        for kc in range(KC):
                            k.op("pe", I("matmul", ps[:, 0:tn], lhsT=wb[:, kc, hh * 128:(hh + 1) * 128], rhs=xT[:, kc, t0:t0 + tn],
                                         start=(kc == 0), stop=(kc == KC - 1)), reads=[wb, xT], writes=[ps])
                        qb = qtb.next()
                        copy_op("act", qb[:, 0:tn], ps[:, 0:tn], [ps], [qb])
                        p2 = psc.next()
                        nt_ = tn // 128
                        for ti in range(nt_):
                            k.op("pe", I("matmul", p2[:, ti * 128:(ti + 1) * 128], lhsT=qb[:, ti * 128:(ti + 1) * 128], rhs=keysT[:, hc, :],
                                         start=True, stop=True), reads=[qb, keysT], writes=[p2])
                        sc = scs.next()
                        copy_op("dve", sc[:, 0:nt_, :], p2[:, 0:nt_ * 128].rearrange("p (a b) -> p a b", b=128), [p2], [sc])
                        tt0 = t0 // 128
                        k.op("dq_pool", I("dma_start", out=SS[tt0:tt0 + nt_, :, hc * 128:(hc + 1) * 128].rearrange("t p n -> p t n"), in_=sc[:, 0:nt_, :]),
                             reads=[sc], writes=[SSb[tt0 + ti_] for ti_ in range(nt_)], nowaw=True)
        stage('peer_topk')
        with k.phase():
            Sr = Rot([k.sb([128, 16, 128], F32, "S") for _ in range(2)])
            T16 = k.sb([128, 16, 16], F32, "T16")
            tmpS = k.sb([128, 128], F32, "tmpS")
            pen = k.sb([128, 16, 128], F32, "pen")
            Ar = Rot([k.sb([128, 2048 + 16], F32, "A12") for _ in range(2)])
            cand = k.sb([128, 8, 256], F32, "cand")
            ct1 = k.sb([128, 256], F32, "ct1")
            ct2 = k.sb([128, 256], F32, "ct2")
            C24 = k.sb([128, 8, 24], F32, "C24")
            dd = k.sb([128, 8, 16], F32, "dd")
            zz = k.sb([128, 16], F32, "zz")
            for tt in range(NT):
                S = Sr.next()
                k.op("sp", I("dma_start", out=S[:], in_=SS[tt].rearrange("p (a b) -> p a b", b=128)), reads=[SSb[tt]], writes=[S])
                for hc in range(16):
                    k.op("dve", I("max", out=T16[:, hc, 0:8], in_=S[:, hc, :]), reads=[S], writes=[T16])
                    k.op("dve", I("match_replace", out=tmpS[:], in_to_replace=T16[:, hc, 0:8], in_values=S[:, hc, :], imm_value=-1e30),
                         reads=[T16, S], writes=[tmpS])
                    k.op("dve", I("max", out=T16[:, hc, 8:16], in_=tmpS[:]), reads=[tmpS], writes=[T16])
                A = Ar.next()
                A3 = A[:, 0:2048].rearrange("p (a b) -> p a b", b=128)
                k.op("dve", I("tensor_tensor", out=pen[:], in0=S[:], in1=sap(T16, 15, [[16, 16], [0, 128]]), op=ALU.is_lt), reads=[S, T16], writes=[pen])
                k.op("dve", I("scalar_tensor_tensor", out=A3, in0=pen[:], scalar=-1e4, in1=S[:], op0=ALU.mult, op1=ALU.add), reads=[pen, S], writes=[A])
                k.op("dve", I("tensor_tensor", out=cand[:].rearrange("p h (i j) -> p h i j", j=16),
                              in0=sap(T16, 0, [[32, 8], [1, 16], [0, 16]]), in1=sap(T16, 16, [[32, 8], [0, 16], [1, 16]]), op=ALU.add),
                     reads=[T16], writes=[cand])
                for h in range(8):
                    k.op("dve", I("max", out=C24[:, h, 0:8], in_=cand[:, h, :]), reads=[cand], writes=[C24])
                    k.op("dve", I("match_replace", out=ct1[:], in_to_replace=C24[:, h, 0:8], in_values=cand[:, h, :], imm_value=-1e30), reads=[C24, cand], writes=[ct1])
                    k.op("dve", I("max", out=C24[:, h, 8:16], in_=ct1[:]), reads=[ct1], writes=[C24])
                    k.op("dve", I("match_replace", out=ct2[:], in_to_replace=C24[:, h, 8:16], in_values=ct1[:], imm_value=-1e30), reads=[C24, ct1], writes=[ct2])
                    k.op("dve", I("max", out=C24[:, h, 16:24], in_=ct2[:]), reads=[ct2], writes=[C24])
                k.op("dve", I("tensor_tensor", out=A[:, 2048:2056], in0=C24[:, :, 15], in1=C24[:, :, 16], op=ALU.add), reads=[C24], writes=[A])
                k.op("dve", I("tensor_scalar", out=A[:, 2048:2056], in0=A[:, 2048:2056], scalar1=0.5, scalar2=None, op0=ALU.mult), reads=[A], writes=[A])
                k.op("dve", I("tensor_tensor", out=dd[:], in0=C24[:, :, 0:16], in1=sap(C24, 0, [[24, 8], [0, 16]]), op=ALU.subtract), reads=[C24], writes=[dd])
                k.op("act", I("activation", out=dd[:], in_=dd[:], func=AF.Exp), reads=[dd], writes=[dd])
                k.op("dve", I("tensor_reduce", out=zz[:, 0:8], in_=dd[:], axis=AX.X, op=ALU.add), reads=[dd], writes=[zz])
                k.op("act", I("activation", out=zz[:, 8:16], in_=zz[:, 0:8], func=AF.Ln), reads=[zz], writes=[zz])
                k.op("dve", I("tensor_tensor", out=zz[:, 8:16], in0=zz[:, 8:16], in1=C24[:, :, 0], op=ALU.add), reads=[zz, C24], writes=[zz])
                k.op("dve", I("tensor_scalar", out=A[:, 2056:2064], in0=zz[:, 8:16], scalar1=-1.0, scalar2=None, op0=ALU.mult), reads=[zz], writes=[A])
                k.op("dq_pool", I("dma_start", out=AUX[tt], in_=A[:]), reads=[A], writes=[AUXb[tt]])
        stage('peer_main')
        with k.phase():
            NG = 16
            ustg = Rot([k.sb([128, 2048], F32, "ustg") for _ in range(1)])
            ubf = Rot([k.sb([128, 2048], BF16, "ubf") for _ in range(2)])
            vstg = Rot([k.sb([128, 2048], F32, "vstg") for _ in range(1)])
            uT = Rot([k.sb([128, KC, 512], BF16, "uT") for _ in range(3)])
            vB = Rot([k.sb([128, 2048], BF16, "vB") for _ in range(10)])
            xts = Rot([k.sb([128, KC, 128], BF16, "xtl") for _ in range(2)])
            a2r = Rot([k.sb([128, 8, 128], F32, "a2") for _ in range(2)])
            a1r = Rot([k.sb([128, 8, 8], F32, "a1") for _ in range(2)])
            str_ = Rot([k.sb([128, 16], F32, "st") for _ in range(2)])
            tmpr = Rot([k.sb([128, 1024], F32, "tmp") for _ in range(3)])
            Ebr = Rot([k.sb([128, 1024], BF16, "Eb") for _ in range(3)])
            Ghr = Rot([k.sb([128, 1024], BF16, "Gh") for _ in range(4)])
            Gsr = Rot([k.sb([128, 1024], BF16, "Gs") for _ in range(2)])
            glr = Rot([k.sb([128, 1024], BF16, "gl") for _ in range(2)])
            Wbr = Rot([k.sb([128, 1024], BF16, "Wb") for _ in range(2)])
            WTr = Rot([k.sb([128, 8, 128], BF16, "WT") for _ in range(2)])
            ytr = Rot([k.sb([128, 2048], F32, "yt") for _ in range(2)])
            pH = k.ps([128, 1024], F32, "pH")
            pG = k.ps([128, 1024], F32, "pG")
            pUW = k.ps([128, 1024], BF16, "pUW")
            pYr = Rot([k.ps([128, 512], F32, "pY") for _ in range(3)])
            steps = [(g, tt) for g in range(NG) for tt in range(NT)]
            NS = len(steps)
            S = [dict() for _ in range(NS)]
            Wg = [dict(uts=[None, None], vbs=[None] * 8) for _ in range(NG)]

            def prep_u(g, half):
                ut = uT.next()
                Wg[g]["uts"][half] = ut
                for cc in range(4):
                    e0 = g * 1024 + (half * 4 + cc) * 128
                    us = ustg.next()
                    k.op("sp", I("dma_start", out=us[:], in_=p_u[i, e0:e0 + 128, :]), reads=[p_u], writes=[us])
                    ub = ubf.next()
                    k.op("act", I("copy", out=ub[:], in_=us[:]), reads=[us], writes=[ub])
                    for hf in range(2):
                        for jx in range(8):
                            kc = hf * 8 + jx
                            k.op("pe", I("transpose", out=pUW[:, jx * 128:(jx + 1) * 128], in_=ub[:, kc * 128:(kc + 1) * 128], identity=ident[:]),
                                 reads=[ub, ident], writes=[pUW])
                        copy_op("act", ut[:, hf * 8:(hf + 1) * 8, cc * 128:(cc + 1) * 128], pUW[:].rearrange("p (a b) -> p a b", a=8), [pUW], [ut])

            def prep_v(g, c):
                e0 = g * 1024 + c * 128
                vs = vstg.next()
                k.op("sp", I("dma_start", out=vs[:], in_=p_v[i, e0:e0 + 128, :]), reads=[p_v], writes=[vs])
                vb = vB.next()
                k.op("act", I("copy", out=vb[:], in_=vs[:]), reads=[vs], writes=[vb])
                Wg[g]["vbs"][c] = vb

            def emit_loads(s):
                g, tt = steps[s]
                d = S[s]
                d["xt"] = xts.next(); d["a2"] = a2r.next(); d["a1"] = a1r.next(); d["st"] = str_.next()
                k.op("sp", I("dma_start", out=d["xt"][:], in_=XT[tt]), reads=[XTb[tt]], writes=[d["xt"]])
                auxv = AUX[tt, :, 0:2048].rearrange("p (h c n) -> p h c n", h=8, c=2)
                k.op("sp", I("dma_start", out=d["a2"][:], in_=auxv[:, :, 1, :]), reads=[AUXb[tt]], writes=[d["a2"]])
                k.op("sp", I("dma_start", out=d["a1"][:], in_=auxv[:, :, 0, g * 8:(g + 1) * 8]), reads=[AUXb[tt]], writes=[d["a1"]])
                k.op("sp", I("dma_start", out=d["st"][:], in_=AUX[tt, :, 2048:2064]), reads=[AUXb[tt]], writes=[d["st"]])

            def emit_yload(s):
                g, tt = steps[s]
                d = S[s]
                d["y"] = ytr.next()
                if g > 0:
                    k.op("sp", I("dma_start", out=d["y"][:], in_=YAC[tt]), reads=[YACb[tt]], writes=[d["y"]])

            def emit_store(s):
                g, tt = steps[s]
                k.op("sp", I("dma_start", out=YAC[tt], in_=S[s]["y"][:]), reads=[S[s]["y"]], writes=[YACb[tt]])

            def g_head(s, h):
                d = S[s]
                a1, a2, st = d["a1"], d["a2"], d["st"]
                tm = tmpr.next()
                k.op("pool", I("tensor_tensor", out=tm[:].rearrange("p (r n) -> p r n", n=128),
                               in0=sap(a1, h * 8, [[1, 8], [0, 128]]), in1=sap(a2, h * 128, [[0, 8], [1, 128]]), op=ALU.add),
                     reads=[a1, a2], writes=[tm])
                eb = Ebr.next()
                k.op("act", I("activation", out=eb[:], in_=tm[:], func=AF.Exp, bias=st[:, 8 + h:9 + h], scale=1.0), reads=[tm, st], writes=[eb])
                gh = Ghr.next()
                k.op("dve", I("scalar_tensor_tensor", out=gh[:], in0=tm[:], scalar=st[:, h:h + 1], in1=eb[:], op0=ALU.is_ge, op1=ALU.mult),
                     reads=[tm, st, eb], writes=[gh])
                for half in range(2):
                    k.op("pe", I("matmul", pG[:, half * 512:(half + 1) * 512], lhsT=ident[:], rhs=gh[:, half * 512:(half + 1) * 512], start=(h == 0), stop=(h == 7)),
                         reads=[ident, gh], writes=[pG])

            def g_evac(s):
                gs = Gsr.next()
                S[s]["Gs"] = gs
                k.op("act", I("copy", out=gs[:], in_=pG[:]), reads=[pG], writes=[gs])

            prep_u(0, 0); prep_u(0, 1)
            for c in range(8):
                prep_v(0, c)
            emit_loads(0)
            for h in range(8):
                g_head(0, h)
            g_evac(0)
            def y_quarter(sp_, q4):
                gp, ttp = steps[sp_]
                dp = S[sp_]
                vbs_p = Wg[gp]["vbs"]
                py = pYr.next()
                for c in range(8):
                    k.op("pe", I("matmul", py[:], lhsT=dp["wt"][:, c, :], rhs=vbs_p[c][:, q4 * 512:(q4 + 1) * 512], start=(c == 0), stop=(c == 7)),
                         reads=[dp["wt"], vbs_p[c]], writes=[py])
                dp["py%d" % q4] = py

            def y_add(sp_, q4):
                gp, ttp = steps[sp_]
                dp = S[sp_]
                y = dp["y"]; py = dp["py%d" % q4]
                if gp == 0:
                    k.op("dve", I("tensor_copy", out=y[:, q4 * 512:(q4 + 1) * 512], in_=py[:]), reads=[py], writes=[y], nowaw=(q4 > 0))
                else:
                    k.op("dve", I("tensor_tensor", out=y[:, q4 * 512:(q4 + 1) * 512], in0=py[:], in1=y[:, q4 * 512:(q4 + 1) * 512], op=ALU.add), reads=[py, y], writes=[y], nowaw=(q4 > 0))

            for s in range(NS + 1):
                if s < NS:
                    g, tt = steps[s]
                    d = S[s]
                    uts = Wg[g]["uts"]
                    if s + 1 < NS:
                        emit_loads(s + 1)
                if s >= 2:
                    emit_store(s - 2)
                if s < NS:
                    emit_yload(s)
                    xt = d["xt"]
                    for half in range(2):
                        for kc in range(KC):
                            k.op("pe", I("matmul", pH[:, half * 512:(half + 1) * 512], lhsT=xt[:, kc, :], rhs=uts[half][:, kc, :], start=(kc == 0), stop=(kc == KC - 1)),
                                 reads=[xt, uts[half]], writes=[pH])
                for h in range(8):
                    if s + 1 < NS:
                        g_head(s + 1, h)
                    if s >= 1 and h <= 3:
                        y_quarter(s - 1, h)
                    if s >= 1 and 2 <= h <= 5:
                        y_add(s - 1, h - 2)
                    if s < NS:
                        if h == 3:
                            gg = glr.next()
                            k.op("act", I("activation", out=gg[:], in_=pH[:], func=AF.Gelu_apprx_tanh), reads=[pH], writes=[gg])
                        elif h == 5:
                            wb_ = Wbr.next()
                            k.op("pool", I("tensor_tensor", out=wb_[:], in0=gg[:], in1=d["Gs"][:], op=ALU.mult), reads=[gg, d["Gs"]], writes=[wb_])
                        elif h == 6:
                            for c in range(8):
                                k.op("pe", I("transpose", out=pUW[:, c * 128:(c + 1) * 128], in_=wb_[:, c * 128:(c + 1) * 128], identity=ident[:]), reads=[wb_, ident], writes=[pUW])
                        elif h == 7:
                            wt = WTr.next()
                            copy_op("act", wt[:], pUW[:].rearrange("p (a b) -> p a b", a=8), [pUW], [wt])
                            d["wt"] = wt
                if s + 1 < NS:
                    g_evac(s + 1)
                if s < NS:
                    if tt == 0 and g >= 1:
                        for c in range(2, 8):
                            prep_v(g, c)
                    if g + 1 < NG:
                        if tt == 4:
                            prep_u(g + 1, 0)
                        elif tt == 8:
                            prep_v(g + 1, 0)
                        elif tt == 10:
                            prep_v(g + 1, 1)
                        elif tt == NT - 1:
                            prep_u(g + 1, 1)
            emit_store(NS - 1)
        phase_ln_from_dram([YAC[tt] for tt in range(NT)], YACb, Xsrc, Xdst, lnrow, final_out)

    try:
      phase_init()
      Xcur = x0
      for li in range(n_layers):
          kind, j = li % 3, li // 3
          if kinds is not None:
              kind, j = kinds[li], 0
          Xmid, Xnext = XA, XB
          if kind == 0:
              kro = [(tt, (128, oak[j, (tt - 12) * 128:(tt - 11) * 128, :])) for tt in range(12, 16)]
              vro = [(tt, (128, oav[j, (tt - 12) * 128:(tt - 11) * 128, :])) for tt in range(12, 16)]
              phase_qkv(a_wqkv, a_wqkv[j], ca_k[j], ca_v[j], 512, kro, vro, sak[j, 448:512, :], sav[j, 448:512, :], roll_k=sak[j], roll_v=sav[j])
              phase_attn_a(j, Xcur, Xmid, 2 * li)
          elif kind == 1:
              phase_mla(Xcur, Xmid, 2 * li)
          else:
              kro = [(tt, (128, ock[tt * 128:(tt + 1) * 128, :])) for tt in range(16)]
              vro = [(tt, (128, ocv[tt * 128:(tt + 1) * 128, :])) for tt in range(16)]
              phase_qkv(c_wqkv, c_wqkv[:, :], cc_k, cc_v, 1024, kro, vro, sck[:, :], scv[:, :])
              phase_attn_c(Xcur, Xmid, 2 * li)
          phase_peer(li, Xmid, Xnext, 2 * li + 1, final_out=(y_out if li == n_layers - 1 else None))
          Xcur = Xnext
    except _Stop as e:
        print('stopped before', e)
        if k.es is not None:
            k.P.barrier(); k.es.close(); k.es = None
    info = P.emit()
    return nc, info


def _prep_inputs(inputs, b, n_layers=4):
    f = np.float32
    xp = inputs["x_prompt"][b]
    xs = inputs["x_sample"][b]
    x0 = np.zeros((T, D), f)
    x0[0:2048] = xp
    x0[2048:2112] = xs
    half = 32
    inv = (10000.0 ** (-np.arange(half, dtype=np.float32) / half)).astype(np.float32)
    pos = np.zeros((T,), np.float32)
    pos[0:2048] = np.arange(2048)
    pos[2048:2112] = 1024 + np.arange(64)
    ang = (pos[:, None] * inv[None, :]).astype(np.float32)
    m = {
        "x0": x0,
        "ca_k": np.ascontiguousarray(inputs["cache_a_k"][:, b]).reshape(2, 512, 2048),
        "ca_v": np.ascontiguousarray(inputs["cache_a_v"][:, b]).reshape(2, 512, 2048),
        "cb_c": np.ascontiguousarray(inputs["cache_b_ckv"][0, b]),
        "cb_r": np.ascontiguousarray(inputs["cache_b_krope"][0, b]),
        "cc_k": np.ascontiguousarray(inputs["cache_c_k"][0, b]).reshape(1024, 2048),
        "cc_v": np.ascontiguousarray(inputs["cache_c_v"][0, b]).reshape(1024, 2048),
        "a_wqkv": inputs["a_wqkv"], "a_wo": inputs["a_wo"], "a_rel": inputs["a_relbias"],
        "b_win": inputs["b_win"][0], "b_qn": inputs["b_qnorm"], "b_kvn": inputs["b_kvnorm"],
        "b_wqb": inputs["b_wqb"][0], "b_wkvb": inputs["b_wkvb"][0], "b_wo": inputs["b_wo"][0],
        "c_wqkv": inputs["c_wqkv"][0], "c_wo": inputs["c_wo"][0],
        "p_wq": inputs["peer_wq"], "p_keys": inputs["peer_keys"].reshape(4, 16, 128, 128),
        "p_u": inputs["peer_u"][:n_layers], "p_v": inputs["peer_v"][:n_layers],
        "ln_g": inputs["ln_g"].reshape(8, 2048), "ln_b": inputs["ln_b"].reshape(8, 2048),
        "ropec": np.cos(ang).astype(f), "ropes": np.sin(ang).astype(f),
    }
    return {kk: np.ascontiguousarray(np.asarray(v, dtype=f)) for kk, v in m.items()}


_CACHE = {}


def kernel(**inputs):
    inputs = {kk: np.asarray(v) for kk, v in inputs.items()}
    if "nc" not in _CACHE:
        _CACHE["nc"] = build()[0]
    nc = _CACHE["nc"]
    in_maps = [_prep_inputs(inputs, b) for b in range(8)]
    res = run_bass_kernel_spmd(nc, in_maps, core_ids=list(range(8)))
    R = res.results
    f = np.float32

    def st(fn):
        return np.stack([fn(R[b]) for b in range(8)])

    y_prompt = st(lambda r: r["y"][0:2048])
    y_sample = st(lambda r: r["y"][2048:2112])
    oak = np.stack([R[b]["oak"].reshape(2, 512, 16, 128) for b in range(8)], axis=1)
    oav = np.stack([R[b]["oav"].reshape(2, 512, 16, 128) for b in range(8)], axis=1)
    obc = st(lambda r: r["obc"])[None]
    obr = st(lambda r: r["obr"])[None]
    ock = st(lambda r: r["ock"].reshape(2048, 16, 128))[None]
    ocv = st(lambda r: r["ocv"].reshape(2048, 16, 128))[None]
    sak = np.stack([R[b]["sak"].reshape(2, 512, 16, 128) for b in range(8)], axis=1)
    sav = np.stack([R[b]["sav"].reshape(2, 512, 16, 128) for b in range(8)], axis=1)
    sbc = st(lambda r: r["sbc"])[None]
    sbr = st(lambda r: r["sbr"])[None]
    sck = st(lambda r: r["sck"].reshape(64, 16, 128))[None]
    scv = st(lambda r: r["scv"].reshape(64, 16, 128))[None]
    outs = (y_prompt, y_sample, oak, oav, obc, obr, ock, ocv, sak, sav, sbc, sbr, sck, scv)
    return tuple(np.ascontiguousarray(o.astype(f)) for o in outs)
```

```python
import os
import numpy as np
from contextlib import ExitStack
import concourse.bass as bass
import concourse.mybir as mybir
from concourse.bass_utils import run_bass_kernel_spmd

F32 = mybir.dt.float32
BF16 = mybir.dt.bfloat16
AF = mybir.ActivationFunctionType
ALU = mybir.AluOpType
AX = mybir.AxisListType

NSLOT = 20
COMPUTE = ("pe", "act", "dve", "pool")
DMAQ = ("sp", "dq_pool")
ALLQ = COMPUTE + DMAQ


class Buf:
    __slots__ = ("name", "last_w", "readers", "war")

    def __init__(self, name):
        self.name = name
        self.last_w = []
        self.readers = []
        self.war = []


class Op:
    __slots__ = ("eng", "fn", "deps", "signal", "sigval", "eidx", "isdma", "slot", "idx")


class Prog:
    def __init__(self, nc):
        self.nc = nc
        self.ops = []
        self.ecount = {}
        self.last = {}
        self.recent_dma = {q: [] for q in DMAQ}

    def eng_obj(self, eng):
        nc = self.nc
        return {"pe": nc.tensor, "act": nc.scalar, "dve": nc.vector, "pool": nc.gpsimd,
                "sp": nc.sync, "dq_pool": nc.gpsimd}[eng]

    @staticmethod
    def phys(eng):
        return "pool" if eng == "dq_pool" else eng

    def op(self, eng, fn, reads=(), writes=(), nowaw=False):
        o = Op()
        o.eng = eng
        o.fn = fn
        o.isdma = eng in DMAQ
        o.signal = False
        o.sigval = 0
        o.slot = 0
        o.idx = len(self.ops)
        pe = self.phys(eng)
        o.eidx = self.ecount.get(pe, 0)
        self.ecount[pe] = o.eidx + 1
        deps = set()
        for b in reads:
            deps.update(b.last_w)
        for b in writes:
            if not nowaw:
                deps.update(b.last_w)
            else:
                deps.update(b.war)
            deps.update(b.readers)
        deps.discard(o)
        o.deps = deps
        for b in reads:
            b.readers.append(o)
        for b in writes:
            if nowaw:
                b.last_w.append(o)
            else:
                b.war = list(b.readers) + [x for x in b.last_w if x.fn is not None][-4:]
                b.last_w = [o]
                b.readers = []
        self.ops.append(o)
        if fn is not None:
            if o.isdma:
                r = self.recent_dma[eng]
                r.append(o)
                if len(r) > NSLOT:
                    r.pop(0)
            else:
                self.last[eng] = o
        return o

    def barrier(self):
        lasts = [o for o in self.last.values()]
        for q in DMAQ:
            lasts += self.recent_dma[q]
        for eng in ALLQ:
            o = self.op(eng, None)
            o.deps = set(lasts)

    def emit(self):
        nc = self.nc
        need = []
        for o in self.ops:
            ws = []
            for d in o.deps:
                if d.fn is None:
                    continue
                if (not d.isdma) and (not o.isdma) and d.eng == o.eng:
                    if o.eng == "pe" or o.fn is None:
                        continue
                    if o.eidx - d.eidx > 3:
                        continue
                ws.append(d)
                d.signal = True
            need.append(ws)
        sems = {e: nc.alloc_semaphore("s_" + e) for e in COMPUTE}
        dsems = {q: [nc.alloc_semaphore("d_%s_%d" % (q, i)) for i in range(NSLOT)] for q in DMAQ}
        sigcount = {e: 0 for e in COMPUTE}
        dcount = {q: 0 for q in DMAQ}
        waited = {}
        nw = [0]

        def do_wait(eng, semkey, sem, val):
            k = (self.phys(eng), semkey)
            if waited.get(k, 0) >= val:
                return
            waited[k] = val
            nw[0] += 1
            self.eng_obj(eng).wait_ge(sem, val)

        for o, ws in zip(self.ops, need):
            e = self.eng_obj(o.eng)
            for d in sorted(ws, key=lambda d: d.idx):
                if d.isdma:
                    do_wait(o.eng, ("d", d.eng, d.slot), dsems[d.eng][d.slot], d.sigval)
                else:
                    do_wait(o.eng, ("c", d.eng), sems[d.eng], d.sigval)
            if o.fn is None:
                continue
            if o.isdma:
                i = dcount[o.eng]
                dcount[o.eng] = i + 1
                o.slot = i % NSLOT
                o.sigval = 16 * (i // NSLOT + 1)
                if i >= NSLOT:
                    do_wait(o.eng, ("d", o.eng, o.slot), dsems[o.eng][o.slot], 16 * (i // NSLOT))
                ins = o.fn(e)
                ins.then_inc(dsems[o.eng][o.slot], 16)
            else:
                ins = o.fn(e)
                if o.signal:
                    sigcount[o.eng] += 1
                    o.sigval = sigcount[o.eng]
                    ins.then_inc(sems[o.eng], 1)
        for q in DMAQ:
            n = dcount[q]
            for s in range(min(n, NSLOT)):
                last_i = ((n - 1 - s) // NSLOT) * NSLOT + s
                nc.sync.wait_ge(dsems[q][s], 16 * (last_i // NSLOT + 1))
        for en in COMPUTE:
            if sigcount[en] > 0:
                nc.sync.wait_ge(sems[en], sigcount[en])
        return dict(n_ops=len(self.ops), sig=sigcount, dma=dcount, waits=nw[0])


def I(name, *a, **k):
    return lambda e: getattr(e, name)(*a, **k)


class Tile:
    def __init__(self, t, name):
        self.t = t
        self.b = Buf(name)

    def __getitem__(self, k):
        return self.t[k]


class Rot:
    def __init__(self, tiles):
        self.tiles = tiles
        self.i = 0

    def next(self):
        t = self.tiles[self.i % len(self.tiles)]
        self.i += 1
        return t


NT = 17
T = NT * 128
D = 2048
KC = 16
TOKB = [(0, 512), (512, 512), (1024, 512), (1536, 512), (2048, 128)]
ALPHA = (2.0 * 4) ** 0.25
NEG = -30000.0
KEXT = T + 1024


class K:
    def __init__(self, nc):
        self.nc = nc
        self.P = Prog(nc)
        self.es = None
        self.uid = 0

    def sb(self, shape, dt, name=None):
        self.uid += 1
        name = "%s_%d" % (name or "t", self.uid)
        t = self.es.enter_context(self.nc.sbuf_tensor(name, list(shape), dt))
        return Tile(t, name)

    def ps(self, shape, dt, name=None):
        self.uid += 1
        name = "%s_%d" % (name or "p", self.uid)
        t = self.es.enter_context(self.nc.psum_tensor(name, list(shape), dt))
        return Tile(t, name)

    def dram(self, name, shape, dt, kind="Internal"):
        t = self.nc.dram_tensor(name, list(shape), dt, kind=kind)
        tl = Tile(t.ap(), name)
        return tl

    def phase(self):
        k = self

        class _Ph:
            def __enter__(s):
                k.es = ExitStack()
                k.es.__enter__()
                return s

            def __exit__(s, *a):
                k.P.barrier()
                k.es.__exit__(*a)
                k.es = None
                return False
        return _Ph()

    def op(self, eng, fn, reads=(), writes=(), nowaw=False):
        if eng == "dq_pool":
            eng = "sp"
        return self.P.op(eng, fn, [r.b if isinstance(r, Tile) else r for r in reads],
                         [w.b if isinstance(w, Tile) else w for w in writes], nowaw)


def sap(tile_or_t, offset, dims):
    t = tile_or_t.t if isinstance(tile_or_t, Tile) else tile_or_t
    full = t[:]
    pstride = full.ap[0][0]
    return bass.AP(t, offset, [[pstride, 128]] + [list(d) for d in dims])


def dap(ap, offset, dims):
    return bass.AP(ap.tensor, offset, [list(d) for d in dims])


class _Stop(Exception):
    pass


def build(n_layers=4, dbg=False, stop_after=None, kinds=None):
    nc = bass.Bass("TRN2", target_bir_lowering=False)
    k = K(nc)
    P = k.P
    stage_ctr = [0]

    def stage(name):
        stage_ctr[0] += 1
        if stop_after is not None and stage_ctr[0] > stop_after:
            raise _Stop(name)

    def din(name, shape):
        return Tile(nc.dram_tensor(name, list(shape), F32, kind="ExternalInput").ap(), name)

    def dout(name, shape):
        return Tile(nc.dram_tensor(name, list(shape), F32, kind="ExternalOutput").ap(), name)

    x0 = din("x0", [T, D])
    ca_k = din("ca_k", [2, 512, 2048]); ca_v = din("ca_v", [2, 512, 2048])
    cb_c = din("cb_c", [1024, 512]); cb_r = din("cb_r", [1024, 64])
    cc_k = din("cc_k", [1024, 2048]); cc_v = din("cc_v", [1024, 2048])
    a_wqkv = din("a_wqkv", [2, 2048, 6144]); a_wo = din("a_wo", [2, 2048, 2048]); a_rel = din("a_rel", [2, 16, 257])
    b_win = din("b_win", [2048, 1088]); b_qn = din("b_qn", [1, 512]); b_kvn = din("b_kvn", [1, 512])
    b_wqb = din("b_wqb", [512, 3072]); b_wkvb = din("b_wkvb", [512, 4096]); b_wo = din("b_wo", [2048, 2048])
    c_wqkv = din("c_wqkv", [2048, 6144]); c_wo = din("c_wo", [2048, 2048])
    p_wq = din("p_wq", [4, 2048, 2048]); p_keys = din("p_keys", [4, 16, 128, 128])
    p_u = din("p_u", [n_layers, 16384, 2048]); p_v = din("p_v", [n_layers, 16384, 2048])
    ln_g = din("ln_g", [8, 2048]); ln_b = din("ln_b", [8, 2048])
    ropec = din("ropec", [T, 32]); ropes = din("ropes", [T, 32])

    y_out = dout("y", [T, D])
    oak = dout("oak", [2, 512, 2048]); oav = dout("oav", [2, 512, 2048])
    obc = dout("obc", [2048, 512]); obr = dout("obr", [2048, 64])
    ock = dout("ock", [2048, 2048]); ocv = dout("ocv", [2048, 2048])
    sak = dout("sak", [2, 512, 2048]); sav = dout("sav", [2, 512, 2048])
    sbc = dout("sbc", [64, 512]); sbr = dout("sbr", [64, 64])
    sck = dout("sck", [64, 2048]); scv = dout("scv", [64, 2048])

    xkind = "ExternalOutput" if dbg else "Internal"
    XA = k.dram("XA", [T, D], F32, kind=xkind); XB = k.dram("XB", [T, D], F32, kind=xkind)
    XT = Tile(nc.dram_tensor("XT", [NT, 128, KC, 128], BF16, kind="Internal").ap(), "XT")
    QT = Tile(nc.dram_tensor("QT", [16, 128, T], BF16, kind="Internal").ap(), "QT")
    KTX = Tile(nc.dram_tensor("KTX", [16, 128, KEXT], BF16, kind="Internal").ap(), "KTX")
    QR = Tile(nc.dram_tensor("QR", [8, 128, T], BF16, kind="Internal").ap(), "QR")
    KR = Tile(nc.dram_tensor("KR", [128, KEXT], BF16, kind="Internal").ap(), "KR")
    VX = Tile(nc.dram_tensor("VX", [KEXT, 2048], BF16, kind="Internal").ap(), "VX")
    SS = Tile(nc.dram_tensor("SS", [NT, 128, 2048], F32, kind="Internal").ap(), "SS")
    AUX = Tile(nc.dram_tensor("AUX", [NT, 128, 2048 + 16], F32, kind="Internal").ap(), "AUX")
    YAC = Tile(nc.dram_tensor("YAC", [NT, 128, 2048], F32, kind="Internal").ap(), "YAC")
    EXT = Tile(nc.dram_tensor("EXT", [16, 384], F32, kind="Internal").ap(), "EXT")
    CQT = Tile(nc.dram_tensor("CQT", [4, 128, T], BF16, kind="Internal").ap(), "CQT")
    CKVT = Tile(nc.dram_tensor("CKVT", [4, 128, KEXT], BF16, kind="Internal").ap(), "CKVT")
    ATd = Tile(nc.dram_tensor("ATd", [16, 128, T], BF16, kind="Internal").ap(), "ATd")
    XTb = [Buf("XT%d" % i) for i in range(NT)]
    AUXb = [Buf("AUX%d" % i) for i in range(NT)]
    YACb = [Buf("YAC%d" % i) for i in range(NT)]
    SSb = [Buf("SS%d" % i) for i in range(NT)]

    def palloc(shape, dt, name):
        return Tile(nc.alloc_sbuf_tensor(name, list(shape), dt), name)

    identf = palloc([128, 128], F32, "identf")
    ident = palloc([128, 128], BF16, "ident")
    ones_b = palloc([128, 128], BF16, "ones_b")
    ones_f = palloc([128, 128], F32, "ones_f")
    zeros_b = palloc([128, 128], BF16, "zeros_b")
    Uf = palloc([128, 128], F32, "Uf")
    mcaus = palloc([128, 128], F32, "mcaus")
    m0b = palloc([128, 128], BF16, "m0b")
    m4b = palloc([128, 128], BF16, "m4b")
    k.op("pool", I("memset", identf[:], 0.0), writes=[identf])
    k.op("pool", I("affine_select", out=identf[:], in_=identf[:], pattern=[[-1, 128]], compare_op=ALU.not_equal,
                   fill=1.0, base=0, channel_multiplier=1), reads=[identf], writes=[identf])
    k.op("pool", I("tensor_copy", out=ident[:], in_=identf[:]), reads=[identf], writes=[ident])
    k.op("pool", I("memset", ones_b[:], 1.0), writes=[ones_b])
    k.op("pool", I("memset", ones_f[:], 1.0), writes=[ones_f])
    k.op("pool", I("memset", zeros_b[:], 0.0), writes=[zeros_b])
    k.op("pool", I("affine_select", out=Uf[:], in_=ones_f[:], pattern=[[-1, 128]], compare_op=ALU.is_gt,
                   fill=0.0, base=0, channel_multiplier=1), reads=[ones_f], writes=[Uf])
    k.op("pool", I("affine_select", out=mcaus[:], in_=ones_f[:], pattern=[[1, 128]], compare_op=ALU.is_gt,
                   fill=0.0, base=0, channel_multiplier=-1), reads=[ones_f], writes=[mcaus])
    k.op("pool", I("memset", m0b[:], 0.0), writes=[m0b])
    k.op("pool", I("memset", m0b[0:64, 0:64], NEG), writes=[m0b])
    k.op("pool", I("memset", m4b[:], 0.0), writes=[m4b])
    k.op("pool", I("memset", m4b[64:128, 64:128], NEG), writes=[m4b])
    m0n = palloc([128, 128], BF16, "m0n")
    k.op("pool", I("memset", m0n[:], 0.0), writes=[m0n])
    k.op("pool", I("memset", m0n[64:128, 0:64], NEG), writes=[m0n])
    mcausb = palloc([128, 128], BF16, "mcausb")
    k.op("pool", I("tensor_copy", out=mcausb[:], in_=mcaus[:]), reads=[mcaus], writes=[mcausb])
    Jf = palloc([128, 128], F32, "Jf")
    Jb = palloc([128, 128], BF16, "Jb")
    k.op("pool", I("memset", Jf[:], 0.0), writes=[Jf])
    k.op("pool", I("affine_select", out=Jf[:], in_=Jf[:], pattern=[[1, 128]], compare_op=ALU.not_equal,
                   fill=1.0, base=-127, channel_multiplier=1), reads=[Jf], writes=[Jf])
    k.op("pool", I("tensor_copy", out=Jb[:], in_=Jf[:]), reads=[Jf], writes=[Jb])

    alt = [0]

    def evac_eng():
        alt[0] += 1
        return "act" if alt[0] % 2 else "dve"

    def copy_op(eng, out, in_, reads, writes, scale=None):
        if eng == "act":
            if scale is None:
                k.op("act", I("copy", out=out, in_=in_), reads, writes)
            else:
                k.op("act", I("activation", out=out, in_=in_, func=AF.Copy, scale=float(scale)), reads, writes)
        else:
            if scale is None:
                k.op(eng, I("tensor_copy", out=out, in_=in_), reads, writes)
            else:
                k.op(eng, I("tensor_scalar", out=out, in0=in_, scalar1=float(scale), scalar2=None, op0=ALU.mult), reads, writes)

    def transpose_tile_to_XT(src_bf, tt, pst_rot, xts_rot):
        xts = xts_rot.next()
        for half in range(2):
            pst = pst_rot.next()
            for j in range(8):
                kc = half * 8 + j
                k.op("pe", I("transpose", out=pst[:, j * 128:(j + 1) * 128], in_=src_bf[:, kc * 128:(kc + 1) * 128],
                             identity=ident[:]), reads=[src_bf, ident], writes=[pst])
            copy_op(evac_eng(), xts[:, half * 8:(half + 1) * 8, :], pst[:].rearrange("p (a b) -> p a b", a=8),
                    [pst], [xts])
        k.op("dq_pool", I("dma_start", out=XT[tt], in_=xts[:]), reads=[xts], writes=[XTb[tt]])

    def load_xT(xT):
        for tt in range(NT):
            k.op("sp", I("dma_start", out=xT[:, :, tt * 128:(tt + 1) * 128], in_=XT[tt]), reads=[XTb[tt]], writes=[xT],
                 nowaw=(tt > 0))

    class WLoader:
        def __init__(self, kc, ncols, nbuf=2, cast_eng="pool"):
            self.kc = kc
            self.ncols = ncols
            self.stg = Rot([k.sb([128, kc, ncols], F32, "wstg") for _ in range(nbuf)])
            self.wb = Rot([k.sb([128, kc, ncols], BF16, "wbf") for _ in range(nbuf)])
            self.cast_eng = cast_eng

        def load(self, Wt, W2d, col_list):
            st = self.stg.next()
            wb = self.wb.next()
            pos = 0
            first = True
            for (c0, n) in col_list:
                src = W2d[:, c0:c0 + n].rearrange("(kc p) n -> p kc n", p=128)
                k.op("sp", I("dma_start", out=st[:, :, pos:pos + n], in_=src), reads=[Wt], writes=[st], nowaw=not first)
                first = False
                pos += n
            if self.cast_eng == "act":
                k.op("act", I("copy", out=wb[:, :, 0:pos], in_=st[:, :, 0:pos]), reads=[st], writes=[wb])
            else:
                k.op(self.cast_eng, I("tensor_copy", out=wb[:, :, 0:pos], in_=st[:, :, 0:pos]), reads=[st], writes=[wb])
            return wb

    def layernorm_tile(z, idx, gt, bt, st6, mv, xn):
        for c in range(4):
            k.op("dve", I("bn_stats", out=st6[:, c, :], in_=z[:, c * 512:(c + 1) * 512]), reads=[z], writes=[st6])
        k.op("dve", I("bn_aggr", out=mv[:, 0:2], in_=st6[:].rearrange("p a b -> p (a b)")), reads=[st6], writes=[mv])
        k.op("dve", I("tensor_scalar", out=mv[:, 3:4], in0=mv[:, 1:2], scalar1=1e-5, scalar2=None, op0=ALU.add), reads=[mv], writes=[mv])
        k.op("act", I("activation", out=mv[:, 3:4], in_=mv[:, 3:4], func=AF.Sqrt), reads=[mv], writes=[mv])
        k.op("dve", I("reciprocal", out=mv[:, 2:3], in_=mv[:, 3:4]), reads=[mv], writes=[mv])
        k.op("dve", I("tensor_scalar", out=xn[:], in0=z[:], scalar1=mv[:, 0:1], scalar2=mv[:, 2:3], op0=ALU.subtract,
                      op1=ALU.mult), reads=[z, mv], writes=[xn])
        k.op("pool", I("tensor_tensor", out=xn[:], in0=xn[:], in1=gt[:], op=ALU.mult), reads=[xn, gt], writes=[xn])
        k.op("pool", I("tensor_tensor", out=xn[:], in0=xn[:], in1=bt[:], op=ALU.add), reads=[xn, bt], writes=[xn])

    class LNState:
        def __init__(self, lnrow):
            self.gt = k.sb([128, 2048], F32, "lng")
            self.bt = k.sb([128, 2048], F32, "lnb")
            k.op("sp", I("dma_start", out=self.gt[:], in_=dap(ln_g.t, lnrow * 2048, [[0, 128], [1, 2048]])), reads=[ln_g], writes=[self.gt])
            k.op("sp", I("dma_start", out=self.bt[:], in_=dap(ln_b.t, lnrow * 2048, [[0, 128], [1, 2048]])), reads=[ln_b], writes=[self.bt])
            self.st6 = k.sb([128, 4, 6], F32, "st6")
            self.mv = k.sb([128, 4], F32, "mv")
            self.xin = Rot([k.sb([128, 2048], F32, "lnx") for _ in range(2)])
            self.z = Rot([k.sb([128, 2048], F32, "lnz") for _ in range(2)])
            self.xbf = Rot([k.sb([128, 2048], BF16, "lnxb") for _ in range(2)])
            self.xts = Rot([k.sb([128, 16, 128], BF16, "xts") for _ in range(2)])

        def prefetch(self, Xsrc, tt):
            xin = self.xin.next()
            k.op("sp", I("dma_start", out=xin[:], in_=Xsrc[tt * 128:(tt + 1) * 128, :]), reads=[Xsrc], writes=[xin])
            return xin

        def run(self, xin, sub_aps, sub_tiles, Xdst, tt, pst_rot, final_out=None):
            z = self.z.next()
            for c in range(4):
                k.op("dve", I("scalar_tensor_tensor", out=z[:, c * 512:(c + 1) * 512], in0=xin[:, c * 512:(c + 1) * 512],
                              scalar=float(ALPHA), in1=sub_aps[c], op0=ALU.mult, op1=ALU.add),
                     reads=[xin] + sub_tiles, writes=[z])
            layernorm_tile(z, 0, self.gt, self.bt, self.st6, self.mv, z)
            k.op("dq_pool", I("dma_start", out=Xdst[tt * 128:(tt + 1) * 128, :], in_=z[:]), reads=[z], writes=[Xdst], nowaw=True)
            if final_out is not None:
                k.op("dq_pool", I("dma_start", out=final_out[tt * 128:(tt + 1) * 128, :], in_=z[:]), reads=[z], writes=[])
            else:
                xbf = self.xbf.next()
                k.op("act", I("copy", out=xbf[:], in_=z[:]), reads=[z], writes=[xbf])
                transpose_tile_to_XT(xbf, tt, pst_rot, self.xts)

    def phase_init():
        stage('phase_init')
        with k.phase():
            xin = Rot([k.sb([128, 2048], F32, "ix") for _ in range(2)])
            xbf = Rot([k.sb([128, 2048], BF16, "ixb") for _ in range(2)])
            xts = Rot([k.sb([128, 16, 128], BF16, "xts") for _ in range(2)])
            pst = Rot([k.ps([128, 1024], BF16, "pst") for _ in range(2)])
            for tt in range(NT):
                xi = xin.next()
                k.op("sp", I("dma_start", out=xi[:], in_=x0[tt * 128:(tt + 1) * 128, :]), reads=[x0], writes=[xi])
                xb = xbf.next()
                k.op("pool", I("tensor_copy", out=xb[:], in_=xi[:]), reads=[xi], writes=[xb])
                transpose_tile_to_XT(xb, tt, pst, xts)

    def phase_qkv(Wt, W2d, cacheK, cacheV, ncache, k_out_rows, v_out_rows, ks_out, vs_out, roll_k=None, roll_v=None):
        stage('phase_qkv')
        scale = 128 ** -0.5
        with k.phase():
            xT = k.sb([128, KC, T], BF16, "xT")
            load_xT(xT)
            wl = WLoader(KC, 256, nbuf=2, cast_eng="act")
            pp = Rot([k.ps([128, 512], F32, "pp") for _ in range(4)])
            qts = Rot([k.sb([128, 512], BF16, "qts") for _ in range(3)])
            for which, dst in ((0, QT), (1, KTX)):
                for hp in range(8):
                    wb = wl.load(Wt, W2d, [(which * 2048 + hp * 256, 256)])
                    for hh in range(2):
                        h = hp * 2 + hh
                        for (t0, tn) in TOKB:
                            ps = pp.next()
                            for kc in range(KC):
                                k.op("pe", I("matmul", ps[:, 0:tn], lhsT=wb[:, kc, hh * 128:(hh + 1) * 128], rhs=xT[:, kc, t0:t0 + tn],
                                             start=(kc == 0), stop=(kc == KC - 1)), reads=[wb, xT], writes=[ps])
                            qs = qts.next()
                            copy_op(evac_eng(), qs[:, 0:tn], ps[:, 0:tn], [ps], [qs], scale=(scale if which == 0 else None))
                            k.op("dq_pool", I("dma_start", out=dst[h, :, t0:t0 + tn], in_=qs[:, 0:tn]), reads=[qs], writes=[dst])
            stage('qkv_tokmajor')
            vts = Rot([k.sb([128, 512], BF16, "vts") for _ in range(3)])
            vfs = Rot([k.sb([128, 512], F32, "vfs") for _ in range(3)])
            kdict = dict(k_out_rows)
            vdict = dict(v_out_rows)
            for which in (2, 1):
                for ns in range(4):
                    wb0 = wl.load(Wt, W2d, [(which * 2048 + ns * 512, 256)])
                    wb1 = wl.load(Wt, W2d, [(which * 2048 + ns * 512 + 256, 256)])
                    for tt in range(NT):
                        outd = (vdict if which == 2 else kdict)
                        need_f32 = (tt in outd) or tt == 16
                        if which == 1 and not need_f32:
                            continue
                        ps = pp.next()
                        for hf, wb in ((0, wb0), (1, wb1)):
                            for kc in range(KC):
                                k.op("pe", I("matmul", ps[:, hf * 256:(hf + 1) * 256], lhsT=xT[:, kc, tt * 128:(tt + 1) * 128], rhs=wb[:, kc, :],
                                             start=(kc == 0), stop=(kc == KC - 1)), reads=[wb, xT], writes=[ps])
                        src_t = ps
                        if need_f32:
                            vf = vfs.next()
                            copy_op("dve", vf[:], ps[:], [ps], [vf])
                            src_t = vf
                        if which == 2:
                            vt = vts.next()
                            copy_op("act", vt[:], src_t[:], [src_t], [vt])
                            k.op("dq_pool", I("dma_start", out=VX[tt * 128:(tt + 1) * 128, ns * 512:(ns + 1) * 512], in_=vt[:]), reads=[vt], writes=[VX], nowaw=True)
                        if need_f32:
                            if tt in outd:
                                dd = outd[tt]
                                k.op("dq_pool", I("dma_start", out=dd[1][:, ns * 512:(ns + 1) * 512], in_=vf[0:dd[0], :]), reads=[vf], writes=[])
                            if tt == 16:
                                so = vs_out if which == 2 else ks_out
                                k.op("dq_pool", I("dma_start", out=so[:, ns * 512:(ns + 1) * 512], in_=vf[0:64, :]), reads=[vf], writes=[])
        stage('qkv_caches')
        with k.phase():
            cin = Rot([k.sb([128, 2048], F32, "cin") for _ in range(2)])
            cbf = Rot([k.sb([128, 2048], BF16, "cbf") for _ in range(2)])
            kts = Rot([k.sb([128, 16, 128], BF16, "kts") for _ in range(2)])
            pst = Rot([k.ps([128, 1024], BF16, "pst") for _ in range(2)])
            for c in range(ncache // 128):
                ci = cin.next()
                k.op("sp", I("dma_start", out=ci[:], in_=cacheK[c * 128:(c + 1) * 128, :]), reads=[], writes=[ci])
                if roll_k is not None:
                    k.op("sp", I("dma_start", out=roll_k[c * 128:c * 128 + 64, :], in_=ci[64:128, :]), reads=[ci], writes=[])
                    if c > 0:
                        k.op("sp", I("dma_start", out=roll_k[c * 128 - 64:c * 128, :], in_=ci[0:64, :]), reads=[ci], writes=[])
                cb = cbf.next()
                k.op("pool", I("tensor_copy", out=cb[:], in_=ci[:]), reads=[ci], writes=[cb])
                kt = kts.next()
                for half in range(2):
                    ps = pst.next()
                    for j in range(8):
                        h = half * 8 + j
                        k.op("pe", I("transpose", out=ps[:, j * 128:(j + 1) * 128], in_=cb[:, h * 128:(h + 1) * 128], identity=ident[:]),
                             reads=[cb, ident], writes=[ps])
                    copy_op(evac_eng(), kt[:, half * 8:(half + 1) * 8, :], ps[:].rearrange("p (a b) -> p a b", a=8), [ps], [kt])
                k.op("dq_pool", I("dma_start", out=KTX[:, :, T + c * 128:T + (c + 1) * 128].rearrange("h p n -> p h n"), in_=kt[:]),
                     reads=[kt], writes=[KTX])
                ci = cin.next()
                k.op("sp", I("dma_start", out=ci[:], in_=cacheV[c * 128:(c + 1) * 128, :]), reads=[], writes=[ci])
                if roll_v is not None:
                    k.op("sp", I("dma_start", out=roll_v[c * 128:c * 128 + 64, :], in_=ci[64:128, :]), reads=[ci], writes=[])
                    if c > 0:
                        k.op("sp", I("dma_start", out=roll_v[c * 128 - 64:c * 128, :], in_=ci[0:64, :]), reads=[ci], writes=[])
                cb = cbf.next()
                k.op("pool", I("tensor_copy", out=cb[:], in_=ci[:]), reads=[ci], writes=[cb])
                k.op("dq_pool", I("dma_start", out=VX[T + c * 128:T + (c + 1) * 128, :], in_=cb[:]), reads=[cb], writes=[VX])

    def phase_wo_ln(Wt, W2d, Xsrc, Xdst, lnrow, final_out=None):
        stage('phase_wo_ln')
        with k.phase():
            wo = k.sb([128, KC, 2048], BF16, "wo")
            stg = Rot([k.sb([128, KC, 128], F32, "wostg") for _ in range(2)])
            for c in range(16):
                st = stg.next()
                k.op("sp", I("dma_start", out=st[:], in_=W2d[:, c * 128:(c + 1) * 128].rearrange("(kc p) n -> p kc n", p=128)), reads=[Wt], writes=[st])
                k.op("act", I("copy", out=wo[:, :, c * 128:(c + 1) * 128], in_=st[:]), reads=[st], writes=[wo], nowaw=(c > 0))
            ln = LNState(lnrow)
            att = Rot([k.sb([128, KC, 128], BF16, "att") for _ in range(2)])
            pp = Rot([k.ps([128, 2048], F32, "pwo") for _ in range(1)])
            pst = Rot([k.ps([128, 1024], BF16, "pst") for _ in range(2)])
            for tt in range(NT):
                xin = ln.prefetch(Xsrc, tt)
                at = att.next()
                k.op("sp", I("dma_start", out=at[:], in_=ATd[:, :, tt * 128:(tt + 1) * 128].rearrange("h p n -> p h n")), reads=[ATd], writes=[at])
                ps = pp.next()
                for ns in range(4):
                    for kc in range(KC):
                        k.op("pe", I("matmul", ps[:, ns * 512:(ns + 1) * 512], lhsT=at[:, kc, :], rhs=wo[:, kc, ns * 512:(ns + 1) * 512],
                                     start=(kc == 0), stop=(kc == KC - 1)), reads=[at, wo], writes=[ps])
                ln.run(xin, [ps[:, c * 512:(c + 1) * 512] for c in range(4)], [ps], Xdst, tt, pst, final_out)

    def phase_ln_from_dram(Ysrc_tiles, Ybufs, Xsrc, Xdst, lnrow, final_out=None):
        stage('phase_ln_from_dram')
        with k.phase():
            ln = LNState(lnrow)
            yr = Rot([k.sb([128, 2048], F32, "lny") for _ in range(2)])
            pst = Rot([k.ps([128, 1024], BF16, "pst") for _ in range(2)])
            for tt in range(NT):
                xin = ln.prefetch(Xsrc, tt)
                y = yr.next()
                k.op("sp", I("dma_start", out=y[:], in_=Ysrc_tiles[tt]), reads=[Ybufs[tt]], writes=[y])
                ln.run(xin, [y[:, c * 512:(c + 1) * 512] for c in range(4)], [y], Xdst, tt, pst, final_out)

    def phase_attn_a(j, Xsrc, Xdst, lnrow):
        stage('phase_attn_a')
        with k.phase():
            ext_s = k.sb([16, 384], F32, "ext_s")
            k.op("sp", I("dma_start", out=ext_s[:, 0:257], in_=a_rel[j]), reads=[a_rel], writes=[ext_s])
            k.op("dve", I("tensor_copy", out=ext_s[:, 257:384], in_=ext_s[:, 256:257].to_broadcast([16, 127])), reads=[ext_s], writes=[ext_s])
            k.op("sp", I("dma_start", out=EXT[:, :], in_=ext_s[:]), reads=[ext_s], writes=[EXT])
            cst = k.sb([128, 16], F32, "cst")
            k.op("sp", I("dma_start", out=cst[:], in_=dap(a_rel.t, j * 16 * 257 + 256, [[0, 128], [257, 16]]), allow_slow_non_contiguous=True), reads=[a_rel], writes=[cst])
            aor = Rot([k.sb([128, 512], BF16, "ao") for _ in range(3)])
            qh = Rot([k.sb([128, T], BF16, "qh") for _ in range(2)])
            kh = Rot([k.sb([128, KEXT], BF16, "kh") for _ in range(2)])
            vh = Rot([k.sb([128, 25, 128], BF16, "vh") for _ in range(2)])
            bf32 = Rot([k.sb([128, 2, 128], F32, "bf32") for _ in range(2)])
            bt = Rot([k.sb([128, 4, 128], BF16, "bt") for _ in range(2)])
            ptr = Rot([k.sb([128, 5, 128], BF16, "pt") for _ in range(3)])
            ps_s = Rot([k.ps([128, 1024], F32, "ps_s") for _ in range(2)])
            ps_o = Rot([k.ps([128, 512], F32, "ps_o") for _ in range(2)])
            ps_z = Rot([k.ps([128, 512], F32, "ps_z") for _ in range(2)])
            rz = Rot([k.sb([128, 512], F32, "rz") for _ in range(2)])
            for h in range(16):
                q = qh.next(); kk = kh.next(); v = vh.next()
                k.op("sp", I("dma_start", out=q[:], in_=QT[h]), reads=[QT], writes=[q])
                k.op("sp", I("dma_start", out=kk[:, 0:T + 512], in_=KTX[h, :, 0:T + 512]), reads=[KTX], writes=[kk])
                k.op("sp", I("dma_start", out=v[:, 0:21, :], in_=VX[0:21 * 128, h * 128:(h + 1) * 128].rearrange("(t p) d -> p t d", p=128)),
                     reads=[VX], writes=[v])
                bf = bf32.next()
                k.op("sp", I("dma_start", out=bf[:, 0, :], in_=dap(EXT.t, h * 384 + 1, [[1, 128], [1, 128]])), reads=[EXT], writes=[bf])
                k.op("sp", I("dma_start", out=bf[:, 1, :], in_=dap(EXT.t, h * 384 + 129, [[1, 128], [1, 128]])), reads=[EXT], writes=[bf], nowaw=True)
                b = bt.next()
                k.op("dve", I("tensor_tensor", out=b[:, 0, :], in0=bf[:, 0, :], in1=m0b[:], op=ALU.add), reads=[bf, m0b], writes=[b])
                k.op("dve", I("tensor_copy", out=b[:, 1, :], in_=bf[:, 1, :]), reads=[bf], writes=[b])
                k.op("dve", I("tensor_scalar", out=b[:, 2, :], in0=zeros_b[:], scalar1=cst[:, h:h + 1], scalar2=None, op0=ALU.add), reads=[zeros_b, cst], writes=[b])
                k.op("dve", I("tensor_scalar", out=b[:, 3, :], in0=m4b[:], scalar1=cst[:, h:h + 1], scalar2=None, op0=ALU.add), reads=[m4b, cst], writes=[b])
                btype = {0: 0, 1: 1, 2: 2, 3: 2, 4: 3}
                for sbk in range(5):
                    po = ps_o.next(); pz = ps_z.next()
                    if sbk < 4:
                        blocks = [(sbk * 4 + i, 128) for i in range(4)]
                    else:
                        blocks = [(16, 64)]
                    for bi, (bq, nq) in enumerate(blocks):
                        kts = []
                        if bq < 16:
                            for ty in range(4, -1, -1):
                                kt_ = bq - ty
                                if kt_ < 0:
                                    continue
                                kts.append((kk[:, kt_ * 128:(kt_ + 1) * 128], v[:, kt_, :], 128, b[:, btype[ty], 0:nq]))
                        else:
                            for c in range(4):
                                ty = 1 if c == 3 else 2
                                kts.append((kk[:, T + c * 128:T + (c + 1) * 128], v[:, 17 + c, :], 128, b[:, ty, 0:nq]))
                            kts.append((kk[:, 2048:2112], v[0:64, 16, :], 64, b[64:128, 0, 0:nq]))
                        pss = ps_s.next()
                        pt = ptr.next()
                        qa = q[:, bq * 128:bq * 128 + nq]
                        for s, (ka, va, nk, ba) in enumerate(kts):
                            k.op("pe", I("matmul", pss[0:nk, s * 128:s * 128 + nq], lhsT=ka, rhs=qa, start=True, stop=False), reads=[kk, q], writes=[pss])
                            k.op("pe", I("matmul", pss[0:nk, s * 128:s * 128 + nq], lhsT=(Jb[:] if nk == 128 else Jb[64:128, 0:64]), rhs=ba, start=False, stop=True), reads=[Jb, b], writes=[pss])
                        for s, (ka, va, nk, ba) in enumerate(kts):
                            k.op("act", I("activation", out=pt[0:nk, s, 0:nq], in_=pss[0:nk, s * 128:s * 128 + nq], func=AF.Exp), reads=[pss], writes=[pt])
                        for s, (ka, va, nk, ba) in enumerate(kts):
                            k.op("pe", I("matmul", po[:, bi * 128:bi * 128 + nq], lhsT=va, rhs=pt[0:nk, s, 0:nq], start=(s == 0), stop=(s == len(kts) - 1)),
                                 reads=[v, pt], writes=[po])
                        for s, (ka, va, nk, ba) in enumerate(kts):
                            k.op("pe", I("matmul", pz[:, bi * 128:bi * 128 + nq], lhsT=ones_b[0:nk, :], rhs=pt[0:nk, s, 0:nq], start=(s == 0), stop=(s == len(kts) - 1)),
                                 reads=[ones_b, pt], writes=[pz])
                    ncol = 512 if sbk < 4 else 64
                    r = rz.next()
                    k.op("dve", I("reciprocal", out=r[:, 0:ncol], in_=pz[:, 0:ncol]), reads=[pz], writes=[r])
                    ao = aor.next()
                    if sbk == 4:
                        k.op("pool", I("memset", ao[:, 64:128], 0.0), writes=[ao])
                    k.op("dve", I("tensor_tensor", out=ao[:, 0:ncol], in0=po[:, 0:ncol], in1=r[:, 0:ncol], op=ALU.mult),
                         reads=[po, r], writes=[ao], nowaw=True)
                    nst = 512 if sbk < 4 else 128
                    k.op("dq_pool", I("dma_start", out=ATd[h, :, sbk * 512:sbk * 512 + nst], in_=ao[:, 0:nst]), reads=[ao], writes=[ATd], nowaw=True)
        phase_wo_ln(a_wo, a_wo[j], Xsrc, Xdst, lnrow)

    def phase_attn_c(Xsrc, Xdst, lnrow):
        stage('phase_attn_c')
        with k.phase():
            aor = Rot([k.sb([128, 512], BF16, "ao") for _ in range(2)])
            qh = Rot([k.sb([128, T], BF16, "qh") for _ in range(2)])
            kh = Rot([k.sb([128, KEXT], BF16, "kh") for _ in range(2)])
            vh = Rot([k.sb([128, 25, 128], BF16, "vh") for _ in range(2)])
            e1r = Rot([k.sb([128, 512], F32, "e1") for _ in range(2)])
            lspr = Rot([k.sb([128, 512], F32, "lsp") for _ in range(2)])
            lsnr = Rot([k.sb([128, 512], F32, "lsn") for _ in range(2)])
            t1r = Rot([k.sb([128, 512], F32, "t1") for _ in range(2)])
            wr = Rot([k.sb([128, 512], BF16, "w") for _ in range(2)])
            carry = k.sb([128, 512], F32, "carry")
            ps_z = Rot([k.ps([128, 512], F32, "ps_z") for _ in range(2)])
            ps_a = Rot([k.ps([128, 512], F32, "ps_a") for _ in range(2)])
            ps_c = Rot([k.ps([128, 512], F32, "ps_c") for _ in range(2)])
            ps_o = Rot([k.ps([128, 512], F32, "ps_o") for _ in range(2)])
            for h in range(16):
                q = qh.next(); kk = kh.next(); v = vh.next()
                k.op("sp", I("dma_start", out=q[:], in_=QT[h]), reads=[QT], writes=[q])
                k.op("sp", I("dma_start", out=kk[:], in_=KTX[h]), reads=[KTX], writes=[kk])
                k.op("sp", I("dma_start", out=v[:], in_=VX[:, h * 128:(h + 1) * 128].rearrange("(t p) d -> p t d", p=128)), reads=[VX], writes=[v])
                for sbk in range(5):
                    if sbk < 4:
                        qc0, nq = sbk * 512, 512
                        kts = [(128, kk[:, kt * 128:(kt + 1) * 128], v[:, kt, :], max(0, kt - 4 * sbk) * 128, kt >= 4 * sbk)
                               for kt in range(4 * sbk + 3, -1, -1)]
                    else:
                        qc0, nq = 2048, 64
                        kts = [(64, kk[:, 2048:2112], v[0:64, 16, :], 0, True)]
                        kts += [(128, kk[:, T + c * 128:T + (c + 1) * 128], v[:, 17 + c, :], 0, False) for c in range(7, -1, -1)]
                    po = ps_o.next()
                    k.op("pe", I("matmul", po[:, 0:nq], lhsT=zeros_b[:], rhs=q[:, qc0:qc0 + nq], start=True, stop=False), reads=[zeros_b, q], writes=[po])
                    k.op("pool", I("memset", carry[:], 0.0), writes=[carry])
                    for si, (nk, ka, va, c0, diag) in enumerate(kts):
                        last = (si == len(kts) - 1)
                        dn = min(128, nq - c0)
                        pz = ps_z.next()
                        k.op("pe", I("matmul", pz[0:nk, c0:nq], lhsT=ka, rhs=q[:, qc0 + c0:qc0 + nq], start=True, stop=True), reads=[kk, q], writes=[pz])
                        e1 = e1r.next()
                        k.op("act", I("activation", out=e1[0:nk, c0:nq], in_=pz[0:nk, c0:nq], func=AF.Exp, scale=-1.0), reads=[pz], writes=[e1])
                        lsp = lspr.next()
                        k.op("act", I("activation", out=lsp[0:nk, c0:nq], in_=e1[0:nk, c0:nq], func=AF.Ln, bias=ones_f[0:nk, 0:1], scale=1.0), reads=[e1, ones_f], writes=[lsp])
                        lsn = lsnr.next()
                        k.op("dve", I("tensor_tensor", out=lsn[0:nk, c0:nq], in0=pz[0:nk, c0:nq], in1=lsp[0:nk, c0:nq], op=ALU.add), reads=[pz, lsp], writes=[lsn])
                        if diag:
                            k.op("pool", I("tensor_tensor", out=lsn[0:nk, c0:c0 + dn], in0=lsn[0:nk, c0:c0 + dn], in1=mcaus[0:nk, 0:dn], op=ALU.mult),
                                 reads=[lsn, mcaus], writes=[lsn])
                        pa = ps_a.next(); pc = ps_c.next()
                        k.op("pe", I("matmul", pa[0:nk, c0:nq], lhsT=Uf[0:nk, 0:nk], rhs=lsn[0:nk, c0:nq], start=True, stop=True), reads=[Uf, lsn], writes=[pa])
                        k.op("pe", I("matmul", pc[:, c0:nq], lhsT=ones_f[0:nk, :], rhs=lsn[0:nk, c0:nq], start=True, stop=True), reads=[ones_f, lsn], writes=[pc])
                        t1 = t1r.next()
                        k.op("dve", I("tensor_tensor", out=t1[0:nk, c0:nq], in0=pa[0:nk, c0:nq], in1=lsp[0:nk, c0:nq], op=ALU.add), reads=[pa, lsp], writes=[t1])
                        k.op("pool", I("tensor_tensor", out=t1[0:nk, c0:nq], in0=t1[0:nk, c0:nq], in1=carry[0:nk, c0:nq], op=ALU.add), reads=[t1, carry], writes=[t1])
                        w = wr.next()
                        k.op("act", I("activation", out=w[0:nk, c0:nq], in_=t1[0:nk, c0:nq], func=AF.Exp, scale=-1.0), reads=[t1], writes=[w])
                        if diag:
                            k.op("pool", I("tensor_tensor", out=w[0:nk, c0:c0 + dn], in0=w[0:nk, c0:c0 + dn], in1=mcausb[0:nk, 0:dn], op=ALU.mult),
                                 reads=[w, mcausb], writes=[w])
                        if not last:
                            k.op("dve", I("tensor_tensor", out=carry[:, c0:nq], in0=pc[:, c0:nq], in1=carry[:, c0:nq], op=ALU.add), reads=[pc, carry], writes=[carry])
                        k.op("pe", I("matmul", po[:, c0:nq], lhsT=va, rhs=w[0:nk, c0:nq], start=False, stop=last), reads=[v, w], writes=[po])
                    ao = aor.next()
                    if sbk == 4:
                        k.op("pool", I("memset", ao[:, 64:128], 0.0), writes=[ao])
                    copy_op("act", ao[:, 0:nq], po[:, 0:nq], [po], [ao])
                    nst = 512 if sbk < 4 else 128
                    k.op("sp", I("dma_start", out=ATd[h, :, sbk * 512:sbk * 512 + nst], in_=ao[:, 0:nst]), reads=[ao], writes=[ATd], nowaw=True)
        phase_wo_ln(c_wo, c_wo[:, :], Xsrc, Xdst, lnrow)

    def rmsnorm_psum(ps, gtile, out_ap, st6, mv, reads_extra, out_tile):
        k.op("dve", I("bn_stats", out=st6[:, 0, :], in_=ps[:]), reads=[ps], writes=[st6])
        k.op("dve", I("bn_aggr", out=mv[:, 0:2], in_=st6[:, 0, :]), reads=[st6], writes=[mv])
        k.op("dve", I("scalar_tensor_tensor", out=mv[:, 2:3], in0=mv[:, 0:1], scalar=mv[:, 0:1], in1=mv[:, 1:2], op0=ALU.mult, op1=ALU.add), reads=[mv], writes=[mv])
        k.op("dve", I("tensor_scalar", out=mv[:, 2:3], in0=mv[:, 2:3], scalar1=1e-6, scalar2=None, op0=ALU.add), reads=[mv], writes=[mv])
        k.op("act", I("activation", out=mv[:, 2:3], in_=mv[:, 2:3], func=AF.Sqrt), reads=[mv], writes=[mv])
        k.op("dve", I("reciprocal", out=mv[:, 3:4], in_=mv[:, 2:3]), reads=[mv], writes=[mv])
        k.op("dve", I("scalar_tensor_tensor", out=out_ap, in0=ps[:], scalar=mv[:, 3:4], in1=gtile[:], op0=ALU.mult, op1=ALU.mult), reads=[ps, mv, gtile], writes=[out_tile])

    def rope_ops(src3, cs, sn, dst3, nh, tmp_a, tmp_b, reads, dst_tile, scale=None):
        cb = sap(cs, 0, [[0, nh], [1, 32]])
        sb_ = sap(sn, 0, [[0, nh], [1, 32]])
        x1 = src3[:, :, 0:32]; x2 = src3[:, :, 32:64]
        ta = tmp_a[:, 0:nh * 32].rearrange("p (h r) -> p h r", r=32)
        tb = tmp_b[:, 0:nh * 32].rearrange("p (h r) -> p h r", r=32)
        k.op("dve", I("tensor_tensor", out=ta, in0=x1, in1=cb, op=ALU.mult), reads=reads + [cs], writes=[tmp_a])
        k.op("dve", I("tensor_tensor", out=tb, in0=x2, in1=sb_, op=ALU.mult), reads=reads + [sn], writes=[tmp_b])
        k.op("dve", I("tensor_tensor", out=dst3[:, :, 0:32], in0=ta, in1=tb, op=ALU.subtract), reads=[tmp_a, tmp_b], writes=[dst_tile])
        k.op("dve", I("tensor_tensor", out=ta, in0=x1, in1=sb_, op=ALU.mult), reads=reads + [sn], writes=[tmp_a])
        k.op("dve", I("tensor_tensor", out=tb, in0=x2, in1=cb, op=ALU.mult), reads=reads + [cs], writes=[tmp_b])
        k.op("dve", I("tensor_tensor", out=dst3[:, :, 32:64], in0=ta, in1=tb, op=ALU.add), reads=[tmp_a, tmp_b], writes=[dst_tile], nowaw=True)

    def phase_mla(Xsrc, Xdst, lnrow):
        stage('phase_mla_in')
        sc = 192 ** -0.5
        with k.phase():
            xT = k.sb([128, KC, T], BF16, "xT")
            load_xT(xT)
            winb = k.sb([128, KC, 1088], BF16, "winb")
            stg = Rot([k.sb([128, KC, 128], F32, "wstg") for _ in range(2)])
            for c in range(9):
                n = 128 if c < 8 else 64
                st = stg.next()
                k.op("sp", I("dma_start", out=st[:, :, 0:n], in_=b_win[:, c * 128:c * 128 + n].rearrange("(kc p) n -> p kc n", p=128)), reads=[b_win], writes=[st])
                k.op("pool", I("tensor_copy", out=winb[:, :, c * 128:c * 128 + n], in_=st[:, :, 0:n]), reads=[st], writes=[winb], nowaw=(c > 0))
            qn_t = k.sb([128, 512], F32, "qn_t"); kvn_t = k.sb([128, 512], F32, "kvn_t")
            k.op("sp", I("dma_start", out=qn_t[:], in_=dap(b_qn.t, 0, [[0, 128], [1, 512]])), reads=[b_qn], writes=[qn_t])
            k.op("sp", I("dma_start", out=kvn_t[:], in_=dap(b_kvn.t, 0, [[0, 128], [1, 512]])), reads=[b_kvn], writes=[kvn_t])
            st6 = k.sb([128, 1, 6], F32, "st6"); mv = k.sb([128, 4], F32, "mv")
            csr = Rot([k.sb([128, 32], F32, "cs") for _ in range(2)]); snr = Rot([k.sb([128, 32], F32, "sn") for _ in range(2)])
            p0r = Rot([k.ps([128, 512], F32, "p0") for _ in range(2)])
            p1r = Rot([k.ps([128, 512], F32, "p1") for _ in range(2)])
            p2r = Rot([k.ps([128, 64], F32, "p2") for _ in range(1)])
            pst = Rot([k.ps([128, 1024], BF16, "pst") for _ in range(2)])
            cqb = Rot([k.sb([128, 512], BF16, "cqb") for _ in range(2)])
            ckf = Rot([k.sb([128, 512], F32, "ckf") for _ in range(2)])
            ckb = Rot([k.sb([128, 512], BF16, "ckb") for _ in range(2)])
            krf = Rot([k.sb([128, 1, 64], F32, "krf") for _ in range(2)])
            krb = Rot([k.sb([128, 128], BF16, "krb") for _ in range(2)])
            ta = k.sb([128, 512], F32, "ta"); tb = k.sb([128, 512], F32, "tb")
            tsb = Rot([k.sb([128, 9, 128], BF16, "tsb") for _ in range(2)])

            def transposes_out(cq_bf, ck_bf, kr_bf, col0):
                ps = pst.next()
                ts = tsb.next()
                n = 0
                srcs = []
                if cq_bf is not None:
                    srcs += [(cq_bf, c) for c in range(4)]
                srcs += [(ck_bf, c) for c in range(4)]
                for (tl, c) in srcs:
                    k.op("pe", I("transpose", out=ps[:, n * 128:(n + 1) * 128], in_=tl[:, c * 128:(c + 1) * 128], identity=ident[:]), reads=[tl, ident], writes=[ps])
                    n += 1
                copy_op(evac_eng(), ts[:, 0:n, :], ps[:, 0:n * 128].rearrange("p (a b) -> p a b", b=128), [ps], [ts])
                ps2 = pst.next()
                k.op("pe", I("transpose", out=ps2[:, 0:128], in_=kr_bf[:], identity=ident[:]), reads=[kr_bf, ident], writes=[ps2])
                copy_op(evac_eng(), ts[:, 8, :], ps2[:, 0:128], [ps2], [ts])
                o = 0
                if cq_bf is not None:
                    k.op("sp", I("dma_start", out=CQT[:, :, col0:col0 + 128].rearrange("c p n -> p c n"), in_=ts[:, 0:4, :]), reads=[ts], writes=[CQT], nowaw=True)
                    o = 4
                k.op("sp", I("dma_start", out=CKVT[:, :, col0:col0 + 128].rearrange("c p n -> p c n"), in_=ts[:, o:o + 4, :]), reads=[ts], writes=[CKVT], nowaw=True)
                k.op("sp", I("dma_start", out=KR[:, col0:col0 + 128], in_=ts[:, 8, :]), reads=[ts], writes=[KR], nowaw=True)

            for tt in range(NT):
                cs = csr.next(); sn = snr.next()
                k.op("sp", I("dma_start", out=cs[:], in_=ropec[tt * 128:(tt + 1) * 128, :]), reads=[ropec], writes=[cs])
                k.op("sp", I("dma_start", out=sn[:], in_=ropes[tt * 128:(tt + 1) * 128, :]), reads=[ropes], writes=[sn])
                p0 = p0r.next(); p1 = p1r.next(); p2 = p2r.next()
                for (pt_, c0, n) in ((p0, 0, 512), (p1, 512, 512), (p2, 1024, 64)):
                    for kc in range(KC):
                        k.op("pe", I("matmul", pt_[:, 0:n], lhsT=xT[:, kc, tt * 128:(tt + 1) * 128], rhs=winb[:, kc, c0:c0 + n], start=(kc == 0), stop=(kc == KC - 1)),
                             reads=[xT, winb], writes=[pt_])
                cq = cqb.next()
                rmsnorm_psum(p0, qn_t, cq[:], st6, mv, [], cq)
                cf = ckf.next()
                rmsnorm_psum(p1, kvn_t, cf[:], st6, mv, [], cf)
                if tt < 16:
                    k.op("sp", I("dma_start", out=obc[tt * 128:(tt + 1) * 128, :], in_=cf[:]), reads=[cf], writes=[])
                else:
                    k.op("sp", I("dma_start", out=sbc[:, :], in_=cf[0:64, :]), reads=[cf], writes=[])
                cb_ = ckb.next()
                k.op("act", I("copy", out=cb_[:], in_=cf[:]), reads=[cf], writes=[cb_])
                kf = krf.next()
                rope_ops(p2[:].rearrange("p (h c) -> p h c", c=64), cs, sn, kf[:], 1, ta, tb, [p2], kf)
                if tt < 16:
                    k.op("sp", I("dma_start", out=obr[tt * 128:(tt + 1) * 128, :], in_=kf[:, 0, :]), reads=[kf], writes=[])
                else:
                    k.op("sp", I("dma_start", out=sbr[:, :], in_=kf[0:64, 0, :]), reads=[kf], writes=[])
                kb = krb.next()
                k.op("act", I("copy", out=kb[:, 0:64], in_=kf[:, 0, :]), reads=[kf], writes=[kb])
                k.op("act", I("copy", out=kb[:, 64:128], in_=kf[:, 0, :]), reads=[kf], writes=[kb], nowaw=True)
                transposes_out(cq, cb_, kb, tt * 128)
            for c in range(8):
                cf = ckf.next()
                k.op("sp", I("dma_start", out=cf[:], in_=cb_c[c * 128:(c + 1) * 128, :]), reads=[], writes=[cf])
                cb_ = ckb.next()
                k.op("act", I("copy", out=cb_[:], in_=cf[:]), reads=[cf], writes=[cb_])
                kf = krf.next()
                k.op("sp", I("dma_start", out=kf[:, 0, :], in_=cb_r[c * 128:(c + 1) * 128, :]), reads=[], writes=[kf])
                kb = krb.next()
                k.op("act", I("copy", out=kb[:, 0:64], in_=kf[:, 0, :]), reads=[kf], writes=[kb])
                k.op("act", I("copy", out=kb[:, 64:128], in_=kf[:, 0, :]), reads=[kf], writes=[kb], nowaw=True)
                transposes_out(None, cb_, kb, T + c * 128)
        stage('phase_mla_q')
        with k.phase():
            cqT = k.sb([128, 4, T], BF16, "cqT")
            k.op("sp", I("dma_start", out=cqT[:], in_=CQT[:, :, :].rearrange("c p n -> p c n")), reads=[CQT], writes=[cqT])
            wqn = k.sb([128, 4, 2048], BF16, "wqn"); wqr = k.sb([128, 4, 1024], BF16, "wqr")
            stg = Rot([k.sb([128, 4, 768], F32, "wstg") for _ in range(2)])
            for c in range(4):
                st = stg.next()
                k.op("sp", I("dma_start", out=st[:], in_=b_wqb[:, c * 768:(c + 1) * 768].rearrange("(kc p) n -> p kc n", p=128)), reads=[b_wqb], writes=[st])
                for kc in range(4):
                    sv = st[:, kc, :].rearrange("p (h c) -> p h c", c=192)
                    k.op("pool", I("tensor_copy", out=wqn[:, kc, c * 512:(c + 1) * 512].rearrange("p (h c) -> p h c", c=128), in_=sv[:, :, 0:128]), reads=[st], writes=[wqn], nowaw=True)
                    k.op("pool", I("tensor_copy", out=wqr[:, kc, c * 256:(c + 1) * 256].rearrange("p (h c) -> p h c", c=64), in_=sv[:, :, 128:192]), reads=[st], writes=[wqr], nowaw=True)
            pp = Rot([k.ps([128, 512], F32, "pp") for _ in range(3)])
            qts = Rot([k.sb([128, 512], BF16, "qts") for _ in range(3)])
            for h in range(16):
                for (t0, tn) in TOKB:
                    ps = pp.next()
                    for kc in range(4):
                        k.op("pe", I("matmul", ps[:, 0:tn], lhsT=wqn[:, kc, h * 128:(h + 1) * 128], rhs=cqT[:, kc, t0:t0 + tn], start=(kc == 0), stop=(kc == 3)),
                             reads=[wqn, cqT], writes=[ps])
                    qs = qts.next()
                    copy_op(evac_eng(), qs[:, 0:tn], ps[:, 0:tn], [ps], [qs], scale=sc)
                    k.op("sp", I("dma_start", out=QT[h, :, t0:t0 + tn], in_=qs[:, 0:tn]), reads=[qs], writes=[QT], nowaw=True)
            csr = Rot([k.sb([128, 32], F32, "cs") for _ in range(2)]); snr = Rot([k.sb([128, 32], F32, "sn") for _ in range(2)])
            ta = k.sb([128, 512], F32, "ta"); tb = k.sb([128, 512], F32, "tb")
            qrf = Rot([k.sb([128, 16, 64], F32, "qrf") for _ in range(2)])
            qrb = Rot([k.sb([128, 1024], BF16, "qrb") for _ in range(2)])
            pst = Rot([k.ps([128, 1024], BF16, "pst") for _ in range(2)])
            qrt = Rot([k.sb([128, 8, 128], BF16, "qrt") for _ in range(2)])
            for tt in range(NT):
                cs = csr.next(); sn = snr.next()
                k.op("sp", I("dma_start", out=cs[:], in_=ropec[tt * 128:(tt + 1) * 128, :]), reads=[ropec], writes=[cs])
                k.op("sp", I("dma_start", out=sn[:], in_=ropes[tt * 128:(tt + 1) * 128, :]), reads=[ropes], writes=[sn])
                qf = qrf.next()
                for hg in range(2):
                    ps = pp.next()
                    for kc in range(4):
                        k.op("pe", I("matmul", ps[:, :], lhsT=cqT[:, kc, tt * 128:(tt + 1) * 128], rhs=wqr[:, kc, hg * 512:(hg + 1) * 512], start=(kc == 0), stop=(kc == 3)),
                             reads=[wqr, cqT], writes=[ps])
                    rope_ops(ps[:].rearrange("p (h c) -> p h c", c=64), cs, sn, qf[:, hg * 8:(hg + 1) * 8, :], 8, ta, tb, [ps], qf)
                qb = qrb.next()
                k.op("act", I("activation", out=qb[:], in_=qf[:].rearrange("p h c -> p (h c)"), func=AF.Copy, scale=float(sc)), reads=[qf], writes=[qb])
                ps2 = pst.next()
                for j8 in range(8):
                    k.op("pe", I("transpose", out=ps2[:, j8 * 128:(j8 + 1) * 128], in_=qb[:, j8 * 128:(j8 + 1) * 128], identity=ident[:]), reads=[qb, ident], writes=[ps2])
                qt_ = qrt.next()
                copy_op(evac_eng(), qt_[:], ps2[:].rearrange("p (a b) -> p a b", a=8), [ps2], [qt_])
                k.op("sp", I("dma_start", out=QR[:, :, tt * 128:(tt + 1) * 128].rearrange("a p n -> p a n"), in_=qt_[:]), reads=[qt_], writes=[QR], nowaw=True)
        stage('phase_mla_kv')
        with k.phase():
            ckvT = k.sb([128, 4, KEXT], BF16, "ckvT")
            k.op("sp", I("dma_start", out=ckvT[:], in_=CKVT[:, :, :].rearrange("c p n -> p c n")), reads=[CKVT], writes=[ckvT])
            wkn = k.sb([128, 4, 2048], BF16, "wkn"); wkv = k.sb([128, 4, 2048], BF16, "wkv")
            stg = Rot([k.sb([128, 4, 1024], F32, "wstg") for _ in range(2)])
            for c in range(4):
                st = stg.next()
                k.op("sp", I("dma_start", out=st[:], in_=b_wkvb[:, c * 1024:(c + 1) * 1024].rearrange("(kc p) n -> p kc n", p=128)), reads=[b_wkvb], writes=[st])
                for kc in range(4):
                    sv = st[:, kc, :].rearrange("p (h c) -> p h c", c=256)
                    k.op("pool", I("tensor_copy", out=wkn[:, kc, c * 512:(c + 1) * 512].rearrange("p (h c) -> p h c", c=128), in_=sv[:, :, 0:128]), reads=[st], writes=[wkn], nowaw=True)
                    k.op("pool", I("tensor_copy", out=wkv[:, kc, c * 512:(c + 1) * 512].rearrange("p (h c) -> p h c", c=128), in_=sv[:, :, 128:256]), reads=[st], writes=[wkv], nowaw=True)
            pp = Rot([k.ps([128, 512], F32, "pp") for _ in range(4)])
            qts = Rot([k.sb([128, 512], BF16, "qts") for _ in range(3)])
            KB = [(i * 512, 512) for i in range(6)] + [(3072, 128)]
            for h in range(16):
                for (t0, tn) in KB:
                    ps = pp.next()
                    for kc in range(4):
                        k.op("pe", I("matmul", ps[:, 0:tn], lhsT=wkn[:, kc, h * 128:(h + 1) * 128], rhs=ckvT[:, kc, t0:t0 + tn], start=(kc == 0), stop=(kc == 3)),
                             reads=[wkn, ckvT], writes=[ps])
                    qs = qts.next()
                    copy_op(evac_eng(), qs[:, 0:tn], ps[:, 0:tn], [ps], [qs])
                    k.op("sp", I("dma_start", out=KTX[h, :, t0:t0 + tn], in_=qs[:, 0:tn]), reads=[qs], writes=[KTX], nowaw=True)
            for kt in range(25):
                for ns in range(4):
                    ps = pp.next()
                    for kc in range(4):
                        k.op("pe", I("matmul", ps[:, :], lhsT=ckvT[:, kc, kt * 128:(kt + 1) * 128], rhs=wkv[:, kc, ns * 512:(ns + 1) * 512], start=(kc == 0), stop=(kc == 3)),
                             reads=[wkv, ckvT], writes=[ps])
                    qs = qts.next()
                    copy_op(evac_eng(), qs[:], ps[:], [ps], [qs])
                    k.op("sp", I("dma_start", out=VX[kt * 128:(kt + 1) * 128, ns * 512:(ns + 1) * 512], in_=qs[:]), reads=[qs], writes=[VX], nowaw=True)
        stage('phase_mla_attn')
        with k.phase():
            aor = Rot([k.sb([128, 512], BF16, "ao") for _ in range(2)])
            qh = Rot([k.sb([128, T], BF16, "qh") for _ in range(2)])
            qrh = Rot([k.sb([128, T], BF16, "qrh") for _ in range(2)])
            kh = Rot([k.sb([128, KEXT], BF16, "kh") for _ in range(2)])
            vh = Rot([k.sb([128, 25, 128], BF16, "vh") for _ in range(2)])
            krs = k.sb([128, KEXT], BF16, "krs")
            k.op("sp", I("dma_start", out=krs[:], in_=KR[:, :]), reads=[KR], writes=[krs])
            ptr = Rot([k.sb([128, 512], BF16, "pt") for _ in range(3)])
            rz = Rot([k.sb([128, 512], F32, "rz") for _ in range(2)])
            ps_s = Rot([k.ps([128, 512], F32, "ps_s") for _ in range(3)])
            ps_o = Rot([k.ps([128, 512], F32, "ps_o") for _ in range(2)])
            ps_zz = Rot([k.ps([128, 512], F32, "ps_zz") for _ in range(2)])
            for h in range(16):
                q = qh.next(); kk = kh.next(); v = vh.next()
                hb = (h % 2) * 64
                k.op("sp", I("dma_start", out=q[:], in_=QT[h]), reads=[QT], writes=[q])
                if h % 2 == 0:
                    qr = qrh.next()
                    k.op("sp", I("dma_start", out=qr[:], in_=QR[h // 2]), reads=[QR], writes=[qr])
                k.op("sp", I("dma_start", out=kk[:], in_=KTX[h]), reads=[KTX], writes=[kk])
                k.op("sp", I("dma_start", out=v[:], in_=VX[:, h * 128:(h + 1) * 128].rearrange("(t p) d -> p t d", p=128)), reads=[VX], writes=[v])
                for sbk in range(5):
                    if sbk < 4:
                        qc0, nq = sbk * 512, 512
                        kts = [(128, kt * 128, v[:, kt, :], max(0, kt - 4 * sbk) * 128, kt >= 4 * sbk) for kt in range(0, 4 * sbk + 4)]
                    else:
                        qc0, nq = 2048, 64
                        kts = [(128, T + c * 128, v[:, 17 + c, :], 0, False) for c in range(8)] + [(64, 2048, v[0:64, 16, :], 0, False)]
                    po = ps_o.next(); pz = ps_zz.next()
                    for si, (nk, kc0, va, c0, diag) in enumerate(kts):
                        last = (si == len(kts) - 1)
                        ps = ps_s.next()
                        k.op("pe", I("matmul", ps[0:nk, c0:nq], lhsT=kk[:, kc0:kc0 + nk], rhs=q[:, qc0 + c0:qc0 + nq], start=True, stop=False), reads=[kk, q], writes=[ps])
                        if diag:
                            k.op("pe", I("matmul", ps[0:nk, c0:c0 + 128], lhsT=ident[:], rhs=m0n[:], start=False, stop=False), reads=[ident, m0n], writes=[ps])
                        k.op("pe", I("matmul", ps[0:nk, c0:nq], lhsT=krs[hb:hb + 64, kc0:kc0 + nk], rhs=qr[hb:hb + 64, qc0 + c0:qc0 + nq], start=False, stop=True),
                             reads=[krs, qr], writes=[ps])
                        pt = ptr.next()
                        k.op("act", I("activation", out=pt[0:nk, c0:nq], in_=ps[0:nk, c0:nq], func=AF.Exp), reads=[ps], writes=[pt])
                        k.op("pe", I("matmul", po[:, c0:nq], lhsT=va, rhs=pt[0:nk, c0:nq], start=(si == 0), stop=last), reads=[v, pt], writes=[po])
                        k.op("pe", I("matmul", pz[:, c0:nq], lhsT=ones_b[0:nk, :], rhs=pt[0:nk, c0:nq], start=(si == 0), stop=last), reads=[ones_b, pt], writes=[pz])
                    r = rz.next()
                    k.op("dve", I("reciprocal", out=r[:, 0:nq], in_=pz[:, 0:nq]), reads=[pz], writes=[r])
                    ao = aor.next()
                    if sbk == 4:
                        k.op("pool", I("memset", ao[:, 64:128], 0.0), writes=[ao])
                    k.op("dve", I("tensor_tensor", out=ao[:, 0:nq], in0=po[:, 0:nq], in1=r[:, 0:nq], op=ALU.mult), reads=[po, r], writes=[ao], nowaw=True)
                    nst = 512 if sbk < 4 else 128
                    k.op("sp", I("dma_start", out=ATd[h, :, sbk * 512:sbk * 512 + nst], in_=ao[:, 0:nst]), reads=[ao], writes=[ATd], nowaw=True)
        phase_wo_ln(b_wo, b_wo[:, :], Xsrc, Xdst, lnrow)


    def phase_peer(i, Xsrc, Xdst, lnrow, final_out=None):
        keys2d = p_keys[i]
        stage('peer_scores')
        with k.phase():
            xT = k.sb([128, KC, T], BF16, "xT")
            load_xT(xT)
            kin = Rot([k.sb([128, 128], F32, "kin") for _ in range(2)])
            kbf = Rot([k.sb([128, 128], BF16, "kbf") for _ in range(2)])
            keysT = k.sb([128, 16, 128], BF16, "keysT")
            pst = Rot([k.ps([128, 1024], BF16, "pst") for _ in range(1)])
            for half in range(2):
                ps = pst.next()
                for jx in range(8):
                    hc = half * 8 + jx
                    ki = kin.next(); kb = kbf.next()
                    k.op("sp", I("dma_start", out=ki[:], in_=keys2d[hc]), reads=[p_keys], writes=[ki])
                    k.op("act", I("copy", out=kb[:], in_=ki[:]), reads=[ki], writes=[kb])
                    k.op("pe", I("transpose", out=ps[:, jx * 128:(jx + 1) * 128], in_=kb[:], identity=ident[:]), reads=[kb, ident], writes=[ps])
                copy_op("act", keysT[:, half * 8:(half + 1) * 8, :], ps[:].rearrange("p (a b) -> p a b", a=8), [ps], [keysT])
            wl = WLoader(KC, 128, nbuf=2, cast_eng="act")
            pq = Rot([k.ps([128, 512], F32, "pq") for _ in range(3)])
            psc = Rot([k.ps([128, 512], F32, "psc") for _ in range(2)])
            qtb = Rot([k.sb([128, 512], BF16, "qtb") for _ in range(3)])
            Sr = Rot([k.sb([128, 16, 128], F32, "S") for _ in range(5)])
            T16 = k.sb([128, 16, 16], F32, "T16")
            tmpS = k.sb([128, 16, 128], F32, "tmpS")
            pen = k.sb([128, 16, 128], F32, "pen")
            Ar = Rot([k.sb([128, 2048 + 16], F32, "A12") for _ in range(2)])
            cand = k.sb([128, 8, 256], F32, "cand")
            ct1 = k.sb([128, 8, 256], F32, "ct1")
            ct2 = k.sb([128, 8, 256], F32, "ct2")
            C24 = k.sb([128, 8, 24], F32, "C24")
            dd = k.sb([128, 8, 16], F32, "dd")
            zz = k.sb([128, 16], F32, "zz")

            def topk_tile(S, tt):
                for hc in range(16):
                    k.op("dve", I("max", out=T16[:, hc, 0:8], in_=S[:, hc, :]), reads=[S], writes=[T16], nowaw=True)
                for hc in range(16):
                    k.op("dve", I("match_replace", out=tmpS[:, hc, :], in_to_replace=T16[:, hc, 0:8], in_values=S[:, hc, :], imm_value=-1e30),
                         reads=[T16, S], writes=[tmpS], nowaw=True)
                for hc in range(16):
                    k.op("dve", I("max", out=T16[:, hc, 8:16], in_=tmpS[:, hc, :]), reads=[tmpS], writes=[T16], nowaw=True)
                A = Ar.next()
                A3 = A[:, 0:2048].rearrange("p (a b) -> p a b", b=128)
                k.op("dve", I("tensor_tensor", out=pen[:], in0=S[:], in1=sap(T16, 15, [[16, 16], [0, 128]]), op=ALU.is_lt), reads=[S, T16], writes=[pen])
                k.op("dve", I("scalar_tensor_tensor", out=A3, in0=pen[:], scalar=-1e4, in1=S[:], op0=ALU.mult, op1=ALU.add), reads=[pen, S], writes=[A])
                k.op("dve", I("tensor_tensor", out=cand[:].rearrange("p h (i j) -> p h i j", j=16),
                              in0=sap(T16, 0, [[32, 8], [1, 16], [0, 16]]), in1=sap(T16, 16, [[32, 8], [0, 16], [1, 16]]), op=ALU.add),
                     reads=[T16], writes=[cand])
                for h in range(8):
                    k.op("dve", I("max", out=C24[:, h, 0:8], in_=cand[:, h, :]), reads=[cand], writes=[C24], nowaw=True)
                for h in range(8):
                    k.op("dve", I("match_replace", out=ct1[:, h, :], in_to_replace=C24[:, h, 0:8], in_values=cand[:, h, :], imm_value=-1e30), reads=[C24, cand], writes=[ct1], nowaw=True)
                for h in range(8):
                    k.op("dve", I("max", out=C24[:, h, 8:16], in_=ct1[:, h, :]), reads=[ct1], writes=[C24], nowaw=True)
                for h in range(8):
                    k.op("dve", I("match_replace", out=ct2[:, h, :], in_to_replace=C24[:, h, 8:16], in_values=ct1[:, h, :], imm_value=-1e30), reads=[C24, ct1], writes=[ct2], nowaw=True)
                for h in range(8):
                    k.op("dve", I("max", out=C24[:, h, 16:24], in_=ct2[:, h, :]), reads=[ct2], writes=[C24], nowaw=True)
                k.op("dve", I("tensor_tensor", out=A[:, 2048:2056], in0=C24[:, :, 15], in1=C24[:, :, 16], op=ALU.add), reads=[C24], writes=[A])
                k.op("dve", I("tensor_scalar", out=A[:, 2048:2056], in0=A[:, 2048:2056], scalar1=0.5, scalar2=None, op0=ALU.mult), reads=[A], writes=[A])
                k.op("dve", I("tensor_tensor", out=dd[:], in0=C24[:, :, 0:16], in1=sap(C24, 0, [[24, 8], [0, 16]]), op=ALU.subtract), reads=[C24], writes=[dd])
                k.op("act", I("activation", out=dd[:], in_=dd[:], func=AF.Exp), reads=[dd], writes=[dd])
                k.op("dve", I("tensor_reduce", out=zz[:, 0:8], in_=dd[:], axis=AX.X, op=ALU.add), reads=[dd], writes=[zz])
                k.op("act", I("activation", out=zz[:, 8:16], in_=zz[:, 0:8], func=AF.Ln), reads=[zz], writes=[zz])
                k.op("dve", I("tensor_tensor", out=zz[:, 8:16], in0=zz[:, 8:16], in1=C24[:, :, 0], op=ALU.add), reads=[zz, C24], writes=[zz])
                k.op("dve", I("tensor_scalar", out=A[:, 2056:2064], in0=zz[:, 8:16], scalar1=-1.0, scalar2=None, op0=ALU.mult), reads=[zz], writes=[A])
                k.op("sp", I("dma_start", out=AUX[tt], in_=A[:]), reads=[A], writes=[AUXb[tt]])

            for (t0, tn) in TOKB:
                nt_ = tn // 128
                tt0 = t0 // 128
                Sts = [Sr.next() for _ in range(nt_)]
                for hc in range(16):
                    wb = wl.load(p_wq, p_wq[i], [(hc * 128, 128)])
                    for hh in range(1):
                        ps = pq.next()
                        for kc in range(KC):
                            k.op("pe", I("matmul", ps[:, 0:tn], lhsT=wb[:, kc, 0:128], rhs=xT[:, kc, t0:t0 + tn],
                                         start=(kc == 0), stop=(kc == KC - 1)), reads=[wb, xT], writes=[ps])
                        qb = qtb.next()
                        copy_op("act", qb[:, 0:tn], ps[:, 0:tn], [ps], [qb])
                        p2 = psc.next()
                        for ti in range(nt_):
                            k.op("pe", I("matmul", p2[:, ti * 128:(ti + 1) * 128], lhsT=qb[:, ti * 128:(ti + 1) * 128], rhs=keysT[:, hc, :],
                                         start=True, stop=True), reads=[qb, keysT], writes=[p2])
                        for ti in range(nt_):
                            k.op("act", I("copy", out=Sts[ti][:, hc, :], in_=p2[:, ti * 128:(ti + 1) * 128]), reads=[p2], writes=[Sts[ti]], nowaw=(hc > 0))
                for ti in range(nt_):
                    topk_tile(Sts[ti], tt0 + ti)
        stage('peer_main')
        with k.phase():
            NG = 16
            ustg = Rot([k.sb([128, 2048], F32, "ustg") for _ in range(1)])
            ubf = Rot([k.sb([128, 2048], BF16, "ubf") for _ in range(2)])
            vstg = Rot([k.sb([128, 2048], F32, "vstg") for _ in range(1)])
            uT = Rot([k.sb([128, KC, 512], BF16, "uT") for _ in range(4)])
            vB = Rot([k.sb([128, 2048], BF16, "vB") for _ in range(10)])
            xts = Rot([k.sb([128, KC, 128], BF16, "xtl") for _ in range(2)])
            a2r = Rot([k.sb([128, 8, 128], F32, "a2") for _ in range(2)])
            a1r = Rot([k.sb([128, 8, 8], F32, "a1") for _ in range(2)])
            str_ = Rot([k.sb([128, 16], F32, "st") for _ in range(2)])
            tmpr = Rot([k.sb([128, 1024], F32, "tmp") for _ in range(3)])
            Ebr = Rot([k.sb([128, 1024], BF16, "Eb") for _ in range(3)])
            Ghr = Rot([k.sb([128, 1024], BF16, "Gh") for _ in range(4)])
            Gsr = Rot([k.sb([128, 1024], BF16, "Gs") for _ in range(2)])
            glr = Rot([k.sb([128, 1024], BF16, "gl") for _ in range(2)])
            Wbr = Rot([k.sb([128, 1024], BF16, "Wb") for _ in range(2)])
            WTr = Rot([k.sb([128, 8, 128], BF16, "WT") for _ in range(2)])
            ytr = Rot([k.sb([128, 2048], F32, "yt") for _ in range(2)])
            pH = k.ps([128, 1024], F32, "pH")
            pG = k.ps([128, 1024], F32, "pG")
            pUW = k.ps([128, 1024], BF16, "pUW")
            pYr = Rot([k.ps([128, 512], F32, "pY") for _ in range(3)])
            steps = [(g, tt) for g in range(NG) for tt in range(NT)]
            NS = len(steps)
            S = [dict() for _ in range(NS)]
            Wg = [dict(uts=[None, None], vbs=[None] * 8) for _ in range(NG)]

            def prep_u(g, half):
                ut = uT.next()
                Wg[g]["uts"][half] = ut
                for cc in range(4):
                    e0 = g * 1024 + (half * 4 + cc) * 128
                    us = ustg.next()
                    k.op("sp", I("dma_start", out=us[:], in_=p_u[i, e0:e0 + 128, :]), reads=[p_u], writes=[us])
                    ub = ubf.next()
                    k.op("act", I("copy", out=ub[:], in_=us[:]), reads=[us], writes=[ub])
                    for hf in range(2):
                        for jx in range(8):
                            kc = hf * 8 + jx
                            k.op("pe", I("transpose", out=pUW[:, jx * 128:(jx + 1) * 128], in_=ub[:, kc * 128:(kc + 1) * 128], identity=ident[:]),
                                 reads=[ub, ident], writes=[pUW])
                        copy_op("act", ut[:, hf * 8:(hf + 1) * 8, cc * 128:(cc + 1) * 128], pUW[:].rearrange("p (a b) -> p a b", a=8), [pUW], [ut])

            def prep_v(g, c):
                e0 = g * 1024 + c * 128
                vs = vstg.next()
                k.op("sp", I("dma_start", out=vs[:], in_=p_v[i, e0:e0 + 128, :]), reads=[p_v], writes=[vs])
                vb = vB.next()
                k.op("act", I("copy", out=vb[:], in_=vs[:]), reads=[vs], writes=[vb])
                Wg[g]["vbs"][c] = vb

            def emit_loads(s):
                g, tt = steps[s]
                d = S[s]
                d["xt"] = xts.next(); d["a2"] = a2r.next(); d["a1"] = a1r.next(); d["st"] = str_.next()
                k.op("sp", I("dma_start", out=d["xt"][:], in_=XT[tt]), reads=[XTb[tt]], writes=[d["xt"]])
                auxv = AUX[tt, :, 0:2048].rearrange("p (h c n) -> p h c n", h=8, c=2)
                k.op("sp", I("dma_start", out=d["a2"][:], in_=auxv[:, :, 1, :]), reads=[AUXb[tt]], writes=[d["a2"]])
                k.op("sp", I("dma_start", out=d["a1"][:], in_=auxv[:, :, 0, g * 8:(g + 1) * 8]), reads=[AUXb[tt]], writes=[d["a1"]])
                k.op("sp", I("dma_start", out=d["st"][:], in_=AUX[tt, :, 2048:2064]), reads=[AUXb[tt]], writes=[d["st"]])

            def emit_yload(s):
                g, tt = steps[s]
                d = S[s]
                d["y"] = ytr.next()
                if g > 0:
                    k.op("sp", I("dma_start", out=d["y"][:], in_=YAC[tt]), reads=[YACb[tt]], writes=[d["y"]])

            def emit_store(s):
                g, tt = steps[s]
                k.op("sp", I("dma_start", out=YAC[tt], in_=S[s]["y"][:]), reads=[S[s]["y"]], writes=[YACb[tt]])

            def g_head(s, h):
                d = S[s]
                a1, a2, st = d["a1"], d["a2"], d["st"]
                tm = tmpr.next()
                k.op("pool", I("tensor_tensor", out=tm[:].rearrange("p (r n) -> p r n", n=128),
                               in0=sap(a1, h * 8, [[1, 8], [0, 128]]), in1=sap(a2, h * 128, [[0, 8], [1, 128]]), op=ALU.add),
                     reads=[a1, a2], writes=[tm])
                eb = Ebr.next()
                k.op("act", I("activation", out=eb[:], in_=tm[:], func=AF.Exp, bias=st[:, 8 + h:9 + h], scale=1.0), reads=[tm, st], writes=[eb])
                gh = Ghr.next()
                k.op("dve", I("scalar_tensor_tensor", out=gh[:], in0=tm[:], scalar=st[:, h:h + 1], in1=eb[:], op0=ALU.is_ge, op1=ALU.mult),
                     reads=[tm, st, eb], writes=[gh])
                for half in range(2):
                    k.op("pe", I("matmul", pG[:, half * 512:(half + 1) * 512], lhsT=ident[:], rhs=gh[:, half * 512:(half + 1) * 512], start=(h == 0), stop=(h == 7)),
                         reads=[ident, gh], writes=[pG])

            def g_evac(s):
                gs = Gsr.next()
                S[s]["Gs"] = gs
                k.op("act", I("copy", out=gs[:], in_=pG[:]), reads=[pG], writes=[gs])

            prep_u(0, 0); prep_u(0, 1)
            for c in range(8):
                prep_v(0, c)
            emit_loads(0)
            for h in range(8):
                g_head(0, h)
            g_evac(0)
            def y_quarter(sp_, q4):
                gp, ttp = steps[sp_]
                dp = S[sp_]
                vbs_p = Wg[gp]["vbs"]
                py = pYr.next()
                for c in range(8):
                    k.op("pe", I("matmul", py[:], lhsT=dp["wt"][:, c, :], rhs=vbs_p[c][:, q4 * 512:(q4 + 1) * 512], start=(c == 0), stop=(c == 7)),
                         reads=[dp["wt"], vbs_p[c]], writes=[py])
                dp["py%d" % q4] = py

            def y_add(sp_, q4):
                gp, ttp = steps[sp_]
                dp = S[sp_]
                y = dp["y"]; py = dp["py%d" % q4]
                if gp == 0:
                    k.op("dve", I("tensor_copy", out=y[:, q4 * 512:(q4 + 1) * 512], in_=py[:]), reads=[py], writes=[y], nowaw=(q4 > 0))
                else:
                    k.op("dve", I("tensor_tensor", out=y[:, q4 * 512:(q4 + 1) * 512], in0=py[:], in1=y[:, q4 * 512:(q4 + 1) * 512], op=ALU.add), reads=[py, y], writes=[y], nowaw=(q4 > 0))

            for s in range(NS + 1):
                if s < NS:
                    g, tt = steps[s]
                    d = S[s]
                    uts = Wg[g]["uts"]
                    if s + 1 < NS:
                        emit_loads(s + 1)
                if s >= 2:
                    emit_store(s - 2)
                if s < NS:
                    emit_yload(s)
                    xt = d["xt"]
                    for half in range(2):
                        for kc in range(KC):
                            k.op("pe", I("matmul", pH[:, half * 512:(half + 1) * 512], lhsT=xt[:, kc, :], rhs=uts[half][:, kc, :], start=(kc == 0), stop=(kc == KC - 1)),
                                 reads=[xt, uts[half]], writes=[pH])
                for h in range(8):
                    if s + 1 < NS:
                        g_head(s + 1, h)
                    if s >= 1 and h <= 3:
                        y_quarter(s - 1, h)
                    if s >= 1 and 2 <= h <= 5:
                        y_add(s - 1, h - 2)
                    if s < NS:
                        if h == 3:
                            gg = glr.next()
                            k.op("act", I("activation", out=gg[:], in_=pH[:], func=AF.Gelu_apprx_tanh), reads=[pH], writes=[gg])
                        elif h == 5:
                            wb_ = Wbr.next()
                            k.op("pool", I("tensor_tensor", out=wb_[:], in0=gg[:], in1=d["Gs"][:], op=ALU.mult), reads=[gg, d["Gs"]], writes=[wb_])
                        elif h == 6:
                            for c in range(8):
                                k.op("pe", I("transpose", out=pUW[:, c * 128:(c + 1) * 128], in_=wb_[:, c * 128:(c + 1) * 128], identity=ident[:]), reads=[wb_, ident], writes=[pUW])
                        elif h == 7:
                            wt = WTr.next()
                            copy_op("act", wt[:], pUW[:].rearrange("p (a b) -> p a b", a=8), [pUW], [wt])
                            d["wt"] = wt
                if s + 1 < NS:
                    g_evac(s + 1)
                if s < NS:
                    if tt == 0 and g >= 1:
                        for c in range(2, 8):
                            prep_v(g, c)
                    if g + 1 < NG:
                        if tt == 3:
                            prep_u(g + 1, 0)
                        elif tt == 9:
                            prep_u(g + 1, 1)
                        elif tt in (6, 12):
                            prep_v(g + 1, (6, 12).index(tt))
            emit_store(NS - 1)
        phase_ln_from_dram([YAC[tt] for tt in range(NT)], YACb, Xsrc, Xdst, lnrow, final_out)

    try:
      phase_init()
      Xcur = x0
      for li in range(n_layers):
          kind, j = li % 3, li // 3
          if kinds is not None:
              kind, j = kinds[li], 0
          Xmid, Xnext = XA, XB
          if kind == 0:
              kro = [(tt, (128, oak[j, (tt - 12) * 128:(tt - 11) * 128, :])) for tt in range(12, 16)]
              vro = [(tt, (128, oav[j, (tt - 12) * 128:(tt - 11) * 128, :])) for tt in range(12, 16)]
              phase_qkv(a_wqkv, a_wqkv[j], ca_k[j], ca_v[j], 512, kro, vro, sak[j, 448:512, :], sav[j, 448:512, :], roll_k=sak[j], roll_v=sav[j])
              phase_attn_a(j, Xcur, Xmid, 2 * li)
          elif kind == 1:
              phase_mla(Xcur, Xmid, 2 * li)
          else:
              kro = [(tt, (128, ock[tt * 128:(tt + 1) * 128, :])) for tt in range(16)]
              vro = [(tt, (128, ocv[tt * 128:(tt + 1) * 128, :])) for tt in range(16)]
              phase_qkv(c_wqkv, c_wqkv[:, :], cc_k, cc_v, 1024, kro, vro, sck[:, :], scv[:, :])
              phase_attn_c(Xcur, Xmid, 2 * li)
          phase_peer(li, Xmid, Xnext, 2 * li + 1, final_out=(y_out if li == n_layers - 1 else None))
          Xcur = Xnext
    except _Stop as e:
        print('stopped before', e)
        if k.es is not None:
            k.P.barrier(); k.es.close(); k.es = None
    info = P.emit()
    return nc, info


def _prep_inputs(inputs, b, n_layers=4):
    f = np.float32
    xp = inputs["x_prompt"][b]
    xs = inputs["x_sample"][b]
    x0 = np.zeros((T, D), f)
    x0[0:2048] = xp
    x0[2048:2112] = xs
    half = 32
    inv = (10000.0 ** (-np.arange(half, dtype=np.float32) / half)).astype(np.float32)
    pos = np.zeros((T,), np.float32)
    pos[0:2048] = np.arange(2048)
    pos[2048:2112] = 1024 + np.arange(64)
    ang = (pos[:, None] * inv[None, :]).astype(np.float32)
    m = {
        "x0": x0,
        "ca_k": np.ascontiguousarray(inputs["cache_a_k"][:, b]).reshape(2, 512, 2048),
        "ca_v": np.ascontiguousarray(inputs["cache_a_v"][:, b]).reshape(2, 512, 2048),
        "cb_c": np.ascontiguousarray(inputs["cache_b_ckv"][0, b]),
        "cb_r": np.ascontiguousarray(inputs["cache_b_krope"][0, b]),
        "cc_k": np.ascontiguousarray(inputs["cache_c_k"][0, b]).reshape(1024, 2048),
        "cc_v": np.ascontiguousarray(inputs["cache_c_v"][0, b]).reshape(1024, 2048),
        "a_wqkv": inputs["a_wqkv"], "a_wo": inputs["a_wo"], "a_rel": inputs["a_relbias"],
        "b_win": inputs["b_win"][0], "b_qn": inputs["b_qnorm"], "b_kvn": inputs["b_kvnorm"],
        "b_wqb": inputs["b_wqb"][0], "b_wkvb": inputs["b_wkvb"][0], "b_wo": inputs["b_wo"][0],
        "c_wqkv": inputs["c_wqkv"][0], "c_wo": inputs["c_wo"][0],
        "p_wq": inputs["peer_wq"], "p_keys": inputs["peer_keys"].reshape(4, 16, 128, 128),
        "p_u": inputs["peer_u"][:n_layers], "p_v": inputs["peer_v"][:n_layers],
        "ln_g": inputs["ln_g"].reshape(8, 2048), "ln_b": inputs["ln_b"].reshape(8, 2048),
        "ropec": np.cos(ang).astype(f), "ropes": np.sin(ang).astype(f),
    }
    return {kk: np.ascontiguousarray(np.asarray(v, dtype=f)) for kk, v in m.items()}


_CACHE = {}


def kernel(**inputs):
    inputs = {kk: np.asarray(v) for kk, v in inputs.items()}
    if "nc" not in _CACHE:
        _CACHE["nc"] = build()[0]
    nc = _CACHE["nc"]
    in_maps = [_prep_inputs(inputs, b) for b in range(8)]
    res = run_bass_kernel_spmd(nc, in_maps, core_ids=list(range(8)))
    R = res.results
    f = np.float32

    def st(fn):
        return np.stack([fn(R[b]) for b in range(8)])

    y_prompt = st(lambda r: r["y"][0:2048])
    y_sample = st(lambda r: r["y"][2048:2112])
    oak = np.stack([R[b]["oak"].reshape(2, 512, 16, 128) for b in range(8)], axis=1)
    oav = np.stack([R[b]["oav"].reshape(2, 512, 16, 128) for b in range(8)], axis=1)
    obc = st(lambda r: r["obc"])[None]
    obr = st(lambda r: r["obr"])[None]
    ock = st(lambda r: r["ock"].reshape(2048, 16, 128))[None]
    ocv = st(lambda r: r["ocv"].reshape(2048, 16, 128))[None]
    sak = np.stack([R[b]["sak"].reshape(2, 512, 16, 128) for b in range(8)], axis=1)
    sav = np.stack([R[b]["sav"].reshape(2, 512, 16, 128) for b in range(8)], axis=1)
    sbc = st(lambda r: r["sbc"])[None]
    sbr = st(lambda r: r["sbr"])[None]
    sck = st(lambda r: r["sck"].reshape(64, 16, 128))[None]
    scv = st(lambda r: r["scv"].reshape(64, 16, 128))[None]
    outs = (y_prompt, y_sample, oak, oav, obc, obr, ock, ocv, sak, sav, sbc, sbr, sck, scv)
    return tuple(np.ascontiguousarray(o.astype(f)) for o in outs)
```

```python
import os
import numpy as np
from contextlib import ExitStack
import concourse.bass as bass
import concourse.mybir as mybir
from concourse.bass_utils import run_bass_kernel_spmd

F32 = mybir.dt.float32
BF16 = mybir.dt.bfloat16
AF = mybir.ActivationFunctionType
ALU = mybir.AluOpType
AX = mybir.AxisListType

NSLOT = 20
COMPUTE = ("pe", "act", "dve", "pool")
DMAQ = ("sp", "dq_pool")
ALLQ = COMPUTE + DMAQ


class Buf:
    __slots__ = ("name", "last_w", "readers", "war")

    def __init__(self, name):
        self.name = name
        self.last_w = []
        self.readers = []
        self.war = []


class Op:
    __slots__ = ("eng", "fn", "deps", "signal", "sigval", "eidx", "isdma", "slot", "idx")


class Prog:
    def __init__(self, nc):
        self.nc = nc
        self.ops = []
        self.ecount = {}
        self.last = {}
        self.recent_dma = {q: [] for q in DMAQ}

    def eng_obj(self, eng):
        nc = self.nc
        return {"pe": nc.tensor, "act": nc.scalar, "dve": nc.vector, "pool": nc.gpsimd,
                "sp": nc.sync, "dq_pool": nc.gpsimd}[eng]

    @staticmethod
    def phys(eng):
        return "pool" if eng == "dq_pool" else eng

    def op(self, eng, fn, reads=(), writes=(), nowaw=False):
        o = Op()
        o.eng = eng
        o.fn = fn
        o.isdma = eng in DMAQ
        o.signal = False
        o.sigval = 0
        o.slot = 0
        o.idx = len(self.ops)
        pe = self.phys(eng)
        o.eidx = self.ecount.get(pe, 0)
        self.ecount[pe] = o.eidx + 1
        deps = set()
        for b in reads:
            deps.update(b.last_w)
        for b in writes:
            if not nowaw:
                deps.update(b.last_w)
            else:
                deps.update(b.war)
            deps.update(b.readers)
        deps.discard(o)
        o.deps = deps
        for b in reads:
            b.readers.append(o)
        for b in writes:
            if nowaw:
                b.last_w.append(o)
            else:
                b.war = list(b.readers) + [x for x in b.last_w if x.fn is not None][-4:]
                b.last_w = [o]
                b.readers = []
        self.ops.append(o)
        if fn is not None:
            if o.isdma:
                r = self.recent_dma[eng]
                r.append(o)
                if len(r) > NSLOT:
                    r.pop(0)
            else:
                self.last[eng] = o
        return o

    def barrier(self):
        lasts = [o for o in self.last.values()]
        for q in DMAQ:
            lasts += self.recent_dma[q]
        for eng in ALLQ:
            o = self.op(eng, None)
            o.deps = set(lasts)

    def emit(self):
        nc = self.nc
        need = []
        for o in self.ops:
            ws = []
            for d in o.deps:
                if d.fn is None:
                    continue
                if (not d.isdma) and (not o.isdma) and d.eng == o.eng:
                    if o.eng == "pe" or o.fn is None:
                        continue
                    if o.eidx - d.eidx > 3:
                        continue
                ws.append(d)
                d.signal = True
            need.append(ws)
        sems = {e: nc.alloc_semaphore("s_" + e) for e in COMPUTE}
        dsems = {q: [nc.alloc_semaphore("d_%s_%d" % (q, i)) for i in range(NSLOT)] for q in DMAQ}
        sigcount = {e: 0 for e in COMPUTE}
        dcount = {q: 0 for q in DMAQ}
        waited = {}
        nw = [0]

        def do_wait(eng, semkey, sem, val):
            k = (self.phys(eng), semkey)
            if waited.get(k, 0) >= val:
                return
            waited[k] = val
            nw[0] += 1
            self.eng_obj(eng).wait_ge(sem, val)

        for o, ws in zip(self.ops, need):
            e = self.eng_obj(o.eng)
            for d in sorted(ws, key=lambda d: d.idx):
                if d.isdma:
                    do_wait(o.eng, ("d", d.eng, d.slot), dsems[d.eng][d.slot], d.sigval)
                else:
                    do_wait(o.eng, ("c", d.eng), sems[d.eng], d.sigval)
            if o.fn is None:
                continue
            if o.isdma:
                i = dcount[o.eng]
                dcount[o.eng] = i + 1
                o.slot = i % NSLOT
                o.sigval = 16 * (i // NSLOT + 1)
                if i >= NSLOT:
                    do_wait(o.eng, ("d", o.eng, o.slot), dsems[o.eng][o.slot], 16 * (i // NSLOT))
                ins = o.fn(e)
                ins.then_inc(dsems[o.eng][o.slot], 16)
            else:
                ins = o.fn(e)
                if o.signal:
                    sigcount[o.eng] += 1
                    o.sigval = sigcount[o.eng]
                    ins.then_inc(sems[o.eng], 1)
        for q in DMAQ:
            n = dcount[q]
            for s in range(min(n, NSLOT)):
                last_i = ((n - 1 - s) // NSLOT) * NSLOT + s
                nc.sync.wait_ge(dsems[q][s], 16 * (last_i // NSLOT + 1))
        for en in COMPUTE:
            if sigcount[en] > 0:
                nc.sync.wait_ge(sems[en], sigcount[en])
        return dict(n_ops=len(self.ops), sig=sigcount, dma=dcount, waits=nw[0])


def I(name, *a, **k):
    return lambda e: getattr(e, name)(*a, **k)


class Tile:
    def __init__(self, t, name):
        self.t = t
        self.b = Buf(name)

    def __getitem__(self, k):
        return self.t[k]


class Rot:
    def __init__(self, tiles):
        self.tiles = tiles
        self.i = 0

    def next(self):
        t = self.tiles[self.i % len(self.tiles)]
        self.i += 1
        return t


NT = 17
T = NT * 128
D = 2048
KC = 16
TOKB = [(0, 512), (512, 512), (1024, 512), (1536, 512), (2048, 128)]
ALPHA = (2.0 * 4) ** 0.25
NEG = -30000.0
TMP_POOL_HEADS = tuple(int(x) for x in os.environ.get('TMP_POOL_HEADS', '0,1,2,3,4,5,6,7').split(',') if x != '')
KEXT = T + 1024


class K:
    def __init__(self, nc):
        self.nc = nc
        self.P = Prog(nc)
        self.es = None
        self.uid = 0

    def sb(self, shape, dt, name=None):
        self.uid += 1
        name = "%s_%d" % (name or "t", self.uid)
        t = self.es.enter_context(self.nc.sbuf_tensor(name, list(shape), dt))
        return Tile(t, name)

    def ps(self, shape, dt, name=None):
        self.uid += 1
        name = "%s_%d" % (name or "p", self.uid)
        t = self.es.enter_context(self.nc.psum_tensor(name, list(shape), dt))
        return Tile(t, name)

    def dram(self, name, shape, dt, kind="Internal"):
        t = self.nc.dram_tensor(name, list(shape), dt, kind=kind)
        tl = Tile(t.ap(), name)
        return tl

    def phase(self):
        k = self

        class _Ph:
            def __enter__(s):
                k.es = ExitStack()
                k.es.__enter__()
                return s

            def __exit__(s, *a):
                k.P.barrier()
                k.es.__exit__(*a)
                k.es = None
                return False
        return _Ph()

    def op(self, eng, fn, reads=(), writes=(), nowaw=False):
        if eng == "dq_pool":
            eng = "sp"
        return self.P.op(eng, fn, [r.b if isinstance(r, Tile) else r for r in reads],
                         [w.b if isinstance(w, Tile) else w for w in writes], nowaw)


def sap(tile_or_t, offset, dims):
    t = tile_or_t.t if isinstance(tile_or_t, Tile) else tile_or_t
    full = t[:]
    pstride = full.ap[0][0]
    return bass.AP(t, offset, [[pstride, 128]] + [list(d) for d in dims])


def dap(ap, offset, dims):
    return bass.AP(ap.tensor, offset, [list(d) for d in dims])


class _Stop(Exception):
    pass


def build(n_layers=4, dbg=False, stop_after=None, kinds=None):
    nc = bass.Bass("TRN2", target_bir_lowering=False)
    k = K(nc)
    P = k.P
    stage_ctr = [0]

    def stage(name):
        stage_ctr[0] += 1
        if stop_after is not None and stage_ctr[0] > stop_after:
            raise _Stop(name)

    def din(name, shape):
        return Tile(nc.dram_tensor(name, list(shape), F32, kind="ExternalInput").ap(), name)

    def dout(name, shape):
        return Tile(nc.dram_tensor(name, list(shape), F32, kind="ExternalOutput").ap(), name)

    x0 = din("x0", [T, D])
    ca_k = din("ca_k", [2, 512, 2048]); ca_v = din("ca_v", [2, 512, 2048])
    cb_c = din("cb_c", [1024, 512]); cb_r = din("cb_r", [1024, 64])
    cc_k = din("cc_k", [1024, 2048]); cc_v = din("cc_v", [1024, 2048])
    a_wqkv = din("a_wqkv", [2, 2048, 6144]); a_wo = din("a_wo", [2, 2048, 2048]); a_rel = din("a_rel", [2, 16, 257])
    b_win = din("b_win", [2048, 1088]); b_qn = din("b_qn", [1, 512]); b_kvn = din("b_kvn", [1, 512])
    b_wqb = din("b_wqb", [512, 3072]); b_wkvb = din("b_wkvb", [512, 4096]); b_wo = din("b_wo", [2048, 2048])
    c_wqkv = din("c_wqkv", [2048, 6144]); c_wo = din("c_wo", [2048, 2048])
    p_wq = din("p_wq", [4, 2048, 2048]); p_keys = din("p_keys", [4, 16, 128, 128])
    p_u = din("p_u", [n_layers, 16384, 2048]); p_v = din("p_v", [n_layers, 16384, 2048])
    ln_g = din("ln_g", [8, 2048]); ln_b = din("ln_b", [8, 2048])
    ropec = din("ropec", [T, 32]); ropes = din("ropes", [T, 32])

    y_out = dout("y", [T, D])
    oak = dout("oak", [2, 512, 2048]); oav = dout("oav", [2, 512, 2048])
    obc = dout("obc", [2048, 512]); obr = dout("obr", [2048, 64])
    ock = dout("ock", [2048, 2048]); ocv = dout("ocv", [2048, 2048])
    sak = dout("sak", [2, 512, 2048]); sav = dout("sav", [2, 512, 2048])
    sbc = dout("sbc", [64, 512]); sbr = dout("sbr", [64, 64])
    sck = dout("sck", [64, 2048]); scv = dout("scv", [64, 2048])

    xkind = "ExternalOutput" if dbg else "Internal"
    XA = k.dram("XA", [T, D], F32, kind=xkind); XB = k.dram("XB", [T, D], F32, kind=xkind)
    XT = Tile(nc.dram_tensor("XT", [NT, 128, KC, 128], BF16, kind="Internal").ap(), "XT")
    QT = Tile(nc.dram_tensor("QT", [16, 128, T], BF16, kind="Internal").ap(), "QT")
    KTX = Tile(nc.dram_tensor("KTX", [16, 128, KEXT], BF16, kind="Internal").ap(), "KTX")
    QR = Tile(nc.dram_tensor("QR", [8, 128, T], BF16, kind="Internal").ap(), "QR")
    KR = Tile(nc.dram_tensor("KR", [128, KEXT], BF16, kind="Internal").ap(), "KR")
    VX = Tile(nc.dram_tensor("VX", [KEXT, 2048], BF16, kind="Internal").ap(), "VX")
    SS = Tile(nc.dram_tensor("SS", [NT, 128, 2048], F32, kind="Internal").ap(), "SS")
    AUX = Tile(nc.dram_tensor("AUX", [NT, 128, 2048 + 16], F32, kind="Internal").ap(), "AUX")
    YAC = Tile(nc.dram_tensor("YAC", [NT, 128, 2048], F32, kind="Internal").ap(), "YAC")
    EXT = Tile(nc.dram_tensor("EXT", [16, 384], F32, kind="Internal").ap(), "EXT")
    CQT = Tile(nc.dram_tensor("CQT", [4, 128, T], BF16, kind="Internal").ap(), "CQT")
    CKVT = Tile(nc.dram_tensor("CKVT", [4, 128, KEXT], BF16, kind="Internal").ap(), "CKVT")
    ATd = Tile(nc.dram_tensor("ATd", [16, 128, T], BF16, kind="Internal").ap(), "ATd")
    XTb = [Buf("XT%d" % i) for i in range(NT)]
    AUXb = [Buf("AUX%d" % i) for i in range(NT)]
    YACb = [Buf("YAC%d" % i) for i in range(NT)]
    SSb = [Buf("SS%d" % i) for i in range(NT)]

    def palloc(shape, dt, name):
        return Tile(nc.alloc_sbuf_tensor(name, list(shape), dt), name)

    identf = palloc([128, 128], F32, "identf")
    ident = palloc([128, 128], BF16, "ident")
    ones_b = palloc([128, 128], BF16, "ones_b")
    ones_f = palloc([128, 128], F32, "ones_f")
    zeros_b = palloc([128, 128], BF16, "zeros_b")
    Uf = palloc([128, 128], F32, "Uf")
    mcaus = palloc([128, 128], F32, "mcaus")
    m0b = palloc([128, 128], BF16, "m0b")
    m4b = palloc([128, 128], BF16, "m4b")
    k.op("pool", I("memset", identf[:], 0.0), writes=[identf])
    k.op("pool", I("affine_select", out=identf[:], in_=identf[:], pattern=[[-1, 128]], compare_op=ALU.not_equal,
                   fill=1.0, base=0, channel_multiplier=1), reads=[identf], writes=[identf])
    k.op("pool", I("tensor_copy", out=ident[:], in_=identf[:]), reads=[identf], writes=[ident])
    k.op("pool", I("memset", ones_b[:], 1.0), writes=[ones_b])
    k.op("pool", I("memset", ones_f[:], 1.0), writes=[ones_f])
    k.op("pool", I("memset", zeros_b[:], 0.0), writes=[zeros_b])
    k.op("pool", I("affine_select", out=Uf[:], in_=ones_f[:], pattern=[[-1, 128]], compare_op=ALU.is_gt,
                   fill=0.0, base=0, channel_multiplier=1), reads=[ones_f], writes=[Uf])
    k.op("pool", I("affine_select", out=mcaus[:], in_=ones_f[:], pattern=[[1, 128]], compare_op=ALU.is_gt,
                   fill=0.0, base=0, channel_multiplier=-1), reads=[ones_f], writes=[mcaus])
    k.op("pool", I("memset", m0b[:], 0.0), writes=[m0b])
    k.op("pool", I("memset", m0b[0:64, 0:64], NEG), writes=[m0b])
    k.op("pool", I("memset", m4b[:], 0.0), writes=[m4b])
    k.op("pool", I("memset", m4b[64:128, 64:128], NEG), writes=[m4b])
    m0n = palloc([128, 128], BF16, "m0n")
    k.op("pool", I("memset", m0n[:], 0.0), writes=[m0n])
    k.op("pool", I("memset", m0n[64:128, 0:64], NEG), writes=[m0n])
    mcausb = palloc([128, 128], BF16, "mcausb")
    k.op("pool", I("tensor_copy", out=mcausb[:], in_=mcaus[:]), reads=[mcaus], writes=[mcausb])
    Jf = palloc([128, 128], F32, "Jf")
    Jb = palloc([128, 128], BF16, "Jb")
    k.op("pool", I("memset", Jf[:], 0.0), writes=[Jf])
    k.op("pool", I("affine_select", out=Jf[:], in_=Jf[:], pattern=[[1, 128]], compare_op=ALU.not_equal,
                   fill=1.0, base=-127, channel_multiplier=1), reads=[Jf], writes=[Jf])
    k.op("pool", I("tensor_copy", out=Jb[:], in_=Jf[:]), reads=[Jf], writes=[Jb])

    alt = [0]

    def evac_eng():
        alt[0] += 1
        return "act" if alt[0] % 2 else "dve"

    def copy_op(eng, out, in_, reads, writes, scale=None):
        if eng == "act":
            if scale is None:
                k.op("act", I("copy", out=out, in_=in_), reads, writes)
            else:
                k.op("act", I("activation", out=out, in_=in_, func=AF.Copy, scale=float(scale)), reads, writes)
        else:
            if scale is None:
                k.op(eng, I("tensor_copy", out=out, in_=in_), reads, writes)
            else:
                k.op(eng, I("tensor_scalar", out=out, in0=in_, scalar1=float(scale), scalar2=None, op0=ALU.mult), reads, writes)

    def transpose_tile_to_XT(src_bf, tt, pst_rot, xts_rot):
        xts = xts_rot.next()
        for half in range(2):
            pst = pst_rot.next()
            for j in range(8):
                kc = half * 8 + j
                k.op("pe", I("transpose", out=pst[:, j * 128:(j + 1) * 128], in_=src_bf[:, kc * 128:(kc + 1) * 128],
                             identity=ident[:]), reads=[src_bf, ident], writes=[pst])
            copy_op(evac_eng(), xts[:, half * 8:(half + 1) * 8, :], pst[:].rearrange("p (a b) -> p a b", a=8),
                    [pst], [xts])
        k.op("dq_pool", I("dma_start", out=XT[tt], in_=xts[:]), reads=[xts], writes=[XTb[tt]])

    def load_xT(xT):
        for tt in range(NT):
            k.op("sp", I("dma_start", out=xT[:, :, tt * 128:(tt + 1) * 128], in_=XT[tt]), reads=[XTb[tt]], writes=[xT],
                 nowaw=(tt > 0))

    class WLoader:
        def __init__(self, kc, ncols, nbuf=2, cast_eng="pool"):
            self.kc = kc
            self.ncols = ncols
            self.stg = Rot([k.sb([128, kc, ncols], F32, "wstg") for _ in range(nbuf)])
            self.wb = Rot([k.sb([128, kc, ncols], BF16, "wbf") for _ in range(nbuf)])
            self.cast_eng = cast_eng

        def load(self, Wt, W2d, col_list):
            st = self.stg.next()
            wb = self.wb.next()
            pos = 0
            first = True
            for (c0, n) in col_list:
                src = W2d[:, c0:c0 + n].rearrange("(kc p) n -> p kc n", p=128)
                k.op("sp", I("dma_start", out=st[:, :, pos:pos + n], in_=src), reads=[Wt], writes=[st], nowaw=not first)
                first = False
                pos += n
            if self.cast_eng == "act":
                k.op("act", I("copy", out=wb[:, :, 0:pos], in_=st[:, :, 0:pos]), reads=[st], writes=[wb])
            else:
                k.op(self.cast_eng, I("tensor_copy", out=wb[:, :, 0:pos], in_=st[:, :, 0:pos]), reads=[st], writes=[wb])
            return wb

    def layernorm_tile(z, idx, gt, bt, st6, mv, xn):
        for c in range(4):
            k.op("dve", I("bn_stats", out=st6[:, c, :], in_=z[:, c * 512:(c + 1) * 512]), reads=[z], writes=[st6])
        k.op("dve", I("bn_aggr", out=mv[:, 0:2], in_=st6[:].rearrange("p a b -> p (a b)")), reads=[st6], writes=[mv])
        k.op("dve", I("tensor_scalar", out=mv[:, 3:4], in0=mv[:, 1:2], scalar1=1e-5, scalar2=None, op0=ALU.add), reads=[mv], writes=[mv])
        k.op("act", I("activation", out=mv[:, 3:4], in_=mv[:, 3:4], func=AF.Sqrt), reads=[mv], writes=[mv])
        k.op("dve", I("reciprocal", out=mv[:, 2:3], in_=mv[:, 3:4]), reads=[mv], writes=[mv])
        k.op("dve", I("tensor_scalar", out=xn[:], in0=z[:], scalar1=mv[:, 0:1], scalar2=mv[:, 2:3], op0=ALU.subtract,
                      op1=ALU.mult), reads=[z, mv], writes=[xn])
        k.op("pool", I("tensor_tensor", out=xn[:], in0=xn[:], in1=gt[:], op=ALU.mult), reads=[xn, gt], writes=[xn])
        k.op("pool", I("tensor_tensor", out=xn[:], in0=xn[:], in1=bt[:], op=ALU.add), reads=[xn, bt], writes=[xn])

    class LNState:
        def __init__(self, lnrow):
            self.gt = k.sb([128, 2048], F32, "lng")
            self.bt = k.sb([128, 2048], F32, "lnb")
            k.op("sp", I("dma_start", out=self.gt[:], in_=dap(ln_g.t, lnrow * 2048, [[0, 128], [1, 2048]])), reads=[ln_g], writes=[self.gt])
            k.op("sp", I("dma_start", out=self.bt[:], in_=dap(ln_b.t, lnrow * 2048, [[0, 128], [1, 2048]])), reads=[ln_b], writes=[self.bt])
            self.st6 = k.sb([128, 4, 6], F32, "st6")
            self.mv = k.sb([128, 4], F32, "mv")
            self.xin = Rot([k.sb([128, 2048], F32, "lnx") for _ in range(2)])
            self.z = Rot([k.sb([128, 2048], F32, "lnz") for _ in range(2)])
            self.xbf = Rot([k.sb([128, 2048], BF16, "lnxb") for _ in range(2)])
            self.xts = Rot([k.sb([128, 16, 128], BF16, "xts") for _ in range(2)])

        def prefetch(self, Xsrc, tt):
            xin = self.xin.next()
            k.op("sp", I("dma_start", out=xin[:], in_=Xsrc[tt * 128:(tt + 1) * 128, :]), reads=[Xsrc], writes=[xin])
            return xin

        def run(self, xin, sub_aps, sub_tiles, Xdst, tt, pst_rot, final_out=None):
            z = self.z.next()
            for c in range(4):
                k.op("dve", I("scalar_tensor_tensor", out=z[:, c * 512:(c + 1) * 512], in0=xin[:, c * 512:(c + 1) * 512],
                              scalar=float(ALPHA), in1=sub_aps[c], op0=ALU.mult, op1=ALU.add),
                     reads=[xin] + sub_tiles, writes=[z])
            layernorm_tile(z, 0, self.gt, self.bt, self.st6, self.mv, z)
            k.op("dq_pool", I("dma_start", out=Xdst[tt * 128:(tt + 1) * 128, :], in_=z[:]), reads=[z], writes=[Xdst], nowaw=True)
            if final_out is not None:
                k.op("dq_pool", I("dma_start", out=final_out[tt * 128:(tt + 1) * 128, :], in_=z[:]), reads=[z], writes=[])
            else:
                xbf = self.xbf.next()
                k.op("act", I("copy", out=xbf[:], in_=z[:]), reads=[z], writes=[xbf])
                transpose_tile_to_XT(xbf, tt, pst_rot, self.xts)

    def phase_init():
        stage('phase_init')
        with k.phase():
            xin = Rot([k.sb([128, 2048], F32, "ix") for _ in range(2)])
            xbf = Rot([k.sb([128, 2048], BF16, "ixb") for _ in range(2)])
            xts = Rot([k.sb([128, 16, 128], BF16, "xts") for _ in range(2)])
            pst = Rot([k.ps([128, 1024], BF16, "pst") for _ in range(2)])
            for tt in range(NT):
                xi = xin.next()
                k.op("sp", I("dma_start", out=xi[:], in_=x0[tt * 128:(tt + 1) * 128, :]), reads=[x0], writes=[xi])
                xb = xbf.next()
                k.op("pool", I("tensor_copy", out=xb[:], in_=xi[:]), reads=[xi], writes=[xb])
                transpose_tile_to_XT(xb, tt, pst, xts)

    def phase_qkv(Wt, W2d, cacheK, cacheV, ncache, k_out_rows, v_out_rows, ks_out, vs_out, roll_k=None, roll_v=None):
        stage('phase_qkv')
        scale = 128 ** -0.5
        with k.phase():
            xT = k.sb([128, KC, T], BF16, "xT")
            load_xT(xT)
            wl = WLoader(KC, 256, nbuf=2, cast_eng="act")
            pp = Rot([k.ps([128, 512], F32, "pp") for _ in range(4)])
            qts = Rot([k.sb([128, 512], BF16, "qts") for _ in range(3)])
            for which, dst in ((0, QT), (1, KTX)):
                for hp in range(8):
                    wb = wl.load(Wt, W2d, [(which * 2048 + hp * 256, 256)])
                    for hh in range(2):
                        h = hp * 2 + hh
                        for (t0, tn) in TOKB:
                            ps = pp.next()
                            for kc in range(KC):
                                k.op("pe", I("matmul", ps[:, 0:tn], lhsT=wb[:, kc, hh * 128:(hh + 1) * 128], rhs=xT[:, kc, t0:t0 + tn],
                                             start=(kc == 0), stop=(kc == KC - 1)), reads=[wb, xT], writes=[ps])
                            qs = qts.next()
                            copy_op(evac_eng(), qs[:, 0:tn], ps[:, 0:tn], [ps], [qs], scale=(scale if which == 0 else None))
                            k.op("dq_pool", I("dma_start", out=dst[h, :, t0:t0 + tn], in_=qs[:, 0:tn]), reads=[qs], writes=[dst])
            stage('qkv_tokmajor')
            vts = Rot([k.sb([128, 512], BF16, "vts") for _ in range(3)])
            vfs = Rot([k.sb([128, 512], F32, "vfs") for _ in range(3)])
            kdict = dict(k_out_rows)
            vdict = dict(v_out_rows)
            for which in (2, 1):
                for ns in range(4):
                    wb0 = wl.load(Wt, W2d, [(which * 2048 + ns * 512, 256)])
                    wb1 = wl.load(Wt, W2d, [(which * 2048 + ns * 512 + 256, 256)])
                    for tt in range(NT):
                        outd = (vdict if which == 2 else kdict)
                        need_f32 = (tt in outd) or tt == 16
                        if which == 1 and not need_f32:
                            continue
                        ps = pp.next()
                        for hf, wb in ((0, wb0), (1, wb1)):
                            for kc in range(KC):
                                k.op("pe", I("matmul", ps[:, hf * 256:(hf + 1) * 256], lhsT=xT[:, kc, tt * 128:(tt + 1) * 128], rhs=wb[:, kc, :],
                                             start=(kc == 0), stop=(kc == KC - 1)), reads=[wb, xT], writes=[ps])
                        src_t = ps
                        if need_f32:
                            vf = vfs.next()
                            copy_op("dve", vf[:], ps[:], [ps], [vf])
                            src_t = vf
                        if which == 2:
                            vt = vts.next()
                            copy_op("act", vt[:], src_t[:], [src_t], [vt])
                            k.op("dq_pool", I("dma_start", out=VX[tt * 128:(tt + 1) * 128, ns * 512:(ns + 1) * 512], in_=vt[:]), reads=[vt], writes=[VX], nowaw=True)
                        if need_f32:
                            if tt in outd:
                                dd = outd[tt]
                                k.op("dq_pool", I("dma_start", out=dd[1][:, ns * 512:(ns + 1) * 512], in_=vf[0:dd[0], :]), reads=[vf], writes=[])
                            if tt == 16:
                                so = vs_out if which == 2 else ks_out
                                k.op("dq_pool", I("dma_start", out=so[:, ns * 512:(ns + 1) * 512], in_=vf[0:64, :]), reads=[vf], writes=[])
        stage('qkv_caches')
        with k.phase():
            cin = Rot([k.sb([128, 2048], F32, "cin") for _ in range(2)])
            cbf = Rot([k.sb([128, 2048], BF16, "cbf") for _ in range(2)])
            kts = Rot([k.sb([128, 16, 128], BF16, "kts") for _ in range(2)])
            pst = Rot([k.ps([128, 1024], BF16, "pst") for _ in range(2)])
            for c in range(ncache // 128):
                ci = cin.next()
                k.op("sp", I("dma_start", out=ci[:], in_=cacheK[c * 128:(c + 1) * 128, :]), reads=[], writes=[ci])
                if roll_k is not None:
                    k.op("sp", I("dma_start", out=roll_k[c * 128:c * 128 + 64, :], in_=ci[64:128, :]), reads=[ci], writes=[])
                    if c > 0:
                        k.op("sp", I("dma_start", out=roll_k[c * 128 - 64:c * 128, :], in_=ci[0:64, :]), reads=[ci], writes=[])
                cb = cbf.next()
                k.op("pool", I("tensor_copy", out=cb[:], in_=ci[:]), reads=[ci], writes=[cb])
                kt = kts.next()
                for half in range(2):
                    ps = pst.next()
                    for j in range(8):
                        h = half * 8 + j
                        k.op("pe", I("transpose", out=ps[:, j * 128:(j + 1) * 128], in_=cb[:, h * 128:(h + 1) * 128], identity=ident[:]),
                             reads=[cb, ident], writes=[ps])
                    copy_op(evac_eng(), kt[:, half * 8:(half + 1) * 8, :], ps[:].rearrange("p (a b) -> p a b", a=8), [ps], [kt])
                k.op("dq_pool", I("dma_start", out=KTX[:, :, T + c * 128:T + (c + 1) * 128].rearrange("h p n -> p h n"), in_=kt[:]),
                     reads=[kt], writes=[KTX])
                ci = cin.next()
                k.op("sp", I("dma_start", out=ci[:], in_=cacheV[c * 128:(c + 1) * 128, :]), reads=[], writes=[ci])
                if roll_v is not None:
                    k.op("sp", I("dma_start", out=roll_v[c * 128:c * 128 + 64, :], in_=ci[64:128, :]), reads=[ci], writes=[])
                    if c > 0:
                        k.op("sp", I("dma_start", out=roll_v[c * 128 - 64:c * 128, :], in_=ci[0:64, :]), reads=[ci], writes=[])
                cb = cbf.next()
                k.op("pool", I("tensor_copy", out=cb[:], in_=ci[:]), reads=[ci], writes=[cb])
                k.op("dq_pool", I("dma_start", out=VX[T + c * 128:T + (c + 1) * 128, :], in_=cb[:]), reads=[cb], writes=[VX])

    def phase_wo_ln(Wt, W2d, Xsrc, Xdst, lnrow, final_out=None):
        stage('phase_wo_ln')
        with k.phase():
            wo = k.sb([128, KC, 2048], BF16, "wo")
            stg = Rot([k.sb([128, KC, 128], F32, "wostg") for _ in range(2)])
            for c in range(16):
                st = stg.next()
                k.op("sp", I("dma_start", out=st[:], in_=W2d[:, c * 128:(c + 1) * 128].rearrange("(kc p) n -> p kc n", p=128)), reads=[Wt], writes=[st])
                k.op("act", I("copy", out=wo[:, :, c * 128:(c + 1) * 128], in_=st[:]), reads=[st], writes=[wo], nowaw=(c > 0))
            ln = LNState(lnrow)
            att = Rot([k.sb([128, KC, 128], BF16, "att") for _ in range(2)])
            pp = Rot([k.ps([128, 2048], F32, "pwo") for _ in range(1)])
            pst = Rot([k.ps([128, 1024], BF16, "pst") for _ in range(2)])
            for tt in range(NT):
                xin = ln.prefetch(Xsrc, tt)
                at = att.next()
                k.op("sp", I("dma_start", out=at[:], in_=ATd[:, :, tt * 128:(tt + 1) * 128].rearrange("h p n -> p h n")), reads=[ATd], writes=[at])
                ps = pp.next()
                for ns in range(4):
                    for kc in range(KC):
                        k.op("pe", I("matmul", ps[:, ns * 512:(ns + 1) * 512], lhsT=at[:, kc, :], rhs=wo[:, kc, ns * 512:(ns + 1) * 512],
                                     start=(kc == 0), stop=(kc == KC - 1)), reads=[at, wo], writes=[ps])
                ln.run(xin, [ps[:, c * 512:(c + 1) * 512] for c in range(4)], [ps], Xdst, tt, pst, final_out)

    def phase_ln_from_dram(Ysrc_tiles, Ybufs, Xsrc, Xdst, lnrow, final_out=None):
        stage('phase_ln_from_dram')
        with k.phase():
            ln = LNState(lnrow)
            yr = Rot([k.sb([128, 2048], F32, "lny") for _ in range(2)])
            pst = Rot([k.ps([128, 1024], BF16, "pst") for _ in range(2)])
            for tt in range(NT):
                xin = ln.prefetch(Xsrc, tt)
                y = yr.next()
                k.op("sp", I("dma_start", out=y[:], in_=Ysrc_tiles[tt]), reads=[Ybufs[tt]], writes=[y])
                ln.run(xin, [y[:, c * 512:(c + 1) * 512] for c in range(4)], [y], Xdst, tt, pst, final_out)

    def phase_attn_a(j, Xsrc, Xdst, lnrow):
        stage('phase_attn_a')
        with k.phase():
            ext_s = k.sb([16, 384], F32, "ext_s")
            k.op("sp", I("dma_start", out=ext_s[:, 0:257], in_=a_rel[j]), reads=[a_rel], writes=[ext_s])
            k.op("dve", I("tensor_copy", out=ext_s[:, 257:384], in_=ext_s[:, 256:257].to_broadcast([16, 127])), reads=[ext_s], writes=[ext_s])
            k.op("sp", I("dma_start", out=EXT[:, :], in_=ext_s[:]), reads=[ext_s], writes=[EXT])
            cst = k.sb([128, 16], F32, "cst")
            k.op("sp", I("dma_start", out=cst[:], in_=dap(a_rel.t, j * 16 * 257 + 256, [[0, 128], [257, 16]]), allow_slow_non_contiguous=True), reads=[a_rel], writes=[cst])
            aor = Rot([k.sb([128, 512], BF16, "ao") for _ in range(3)])
            qh = Rot([k.sb([128, T], BF16, "qh") for _ in range(2)])
            kh = Rot([k.sb([128, KEXT], BF16, "kh") for _ in range(2)])
            vh = Rot([k.sb([128, 25, 128], BF16, "vh") for _ in range(2)])
            bf32 = Rot([k.sb([128, 2, 128], F32, "bf32") for _ in range(2)])
            bt = Rot([k.sb([128, 4, 128], BF16, "bt") for _ in range(2)])
            ptr = Rot([k.sb([128, 5, 128], BF16, "pt") for _ in range(3)])
            ps_s = Rot([k.ps([128, 1024], F32, "ps_s") for _ in range(2)])
            ps_o = Rot([k.ps([128, 512], F32, "ps_o") for _ in range(2)])
            ps_z = Rot([k.ps([128, 512], F32, "ps_z") for _ in range(2)])
            rz = Rot([k.sb([128, 512], F32, "rz") for _ in range(2)])
            for h in range(16):
                q = qh.next(); kk = kh.next(); v = vh.next()
                k.op("sp", I("dma_start", out=q[:], in_=QT[h]), reads=[QT], writes=[q])
                k.op("sp", I("dma_start", out=kk[:, 0:T + 512], in_=KTX[h, :, 0:T + 512]), reads=[KTX], writes=[kk])
                k.op("sp", I("dma_start", out=v[:, 0:21, :], in_=VX[0:21 * 128, h * 128:(h + 1) * 128].rearrange("(t p) d -> p t d", p=128)),
                     reads=[VX], writes=[v])
                bf = bf32.next()
                k.op("sp", I("dma_start", out=bf[:, 0, :], in_=dap(EXT.t, h * 384 + 1, [[1, 128], [1, 128]])), reads=[EXT], writes=[bf])
                k.op("sp", I("dma_start", out=bf[:, 1, :], in_=dap(EXT.t, h * 384 + 129, [[1, 128], [1, 128]])), reads=[EXT], writes=[bf], nowaw=True)
                b = bt.next()
                k.op("dve", I("tensor_tensor", out=b[:, 0, :], in0=bf[:, 0, :], in1=m0b[:], op=ALU.add), reads=[bf, m0b], writes=[b])
                k.op("dve", I("tensor_copy", out=b[:, 1, :], in_=bf[:, 1, :]), reads=[bf], writes=[b])
                k.op("dve", I("tensor_scalar", out=b[:, 2, :], in0=zeros_b[:], scalar1=cst[:, h:h + 1], scalar2=None, op0=ALU.add), reads=[zeros_b, cst], writes=[b])
                k.op("dve", I("tensor_scalar", out=b[:, 3, :], in0=m4b[:], scalar1=cst[:, h:h + 1], scalar2=None, op0=ALU.add), reads=[m4b, cst], writes=[b])
                btype = {0: 0, 1: 1, 2: 2, 3: 2, 4: 3}
                for sbk in range(5):
                    po = ps_o.next(); pz = ps_z.next()
                    if sbk < 4:
                        blocks = [(sbk * 4 + i, 128) for i in range(4)]
                    else:
                        blocks = [(16, 64)]
                    for bi, (bq, nq) in enumerate(blocks):
                        kts = []
                        if bq < 16:
                            for ty in range(4, -1, -1):
                                kt_ = bq - ty
                                if kt_ < 0:
                                    continue
                                kts.append((kk[:, kt_ * 128:(kt_ + 1) * 128], v[:, kt_, :], 128, b[:, btype[ty], 0:nq]))
                        else:
                            for c in range(4):
                                ty = 1 if c == 3 else 2
                                kts.append((kk[:, T + c * 128:T + (c + 1) * 128], v[:, 17 + c, :], 128, b[:, ty, 0:nq]))
                            kts.append((kk[:, 2048:2112], v[0:64, 16, :], 64, b[64:128, 0, 0:nq]))
                        pss = ps_s.next()
                        pt = ptr.next()
                        qa = q[:, bq * 128:bq * 128 + nq]
                        for s, (ka, va, nk, ba) in enumerate(kts):
                            k.op("pe", I("matmul", pss[0:nk, s * 128:s * 128 + nq], lhsT=ka, rhs=qa, start=True, stop=False), reads=[kk, q], writes=[pss])
                            k.op("pe", I("matmul", pss[0:nk, s * 128:s * 128 + nq], lhsT=(Jb[:] if nk == 128 else Jb[64:128, 0:64]), rhs=ba, start=False, stop=True), reads=[Jb, b], writes=[pss])
                        for s, (ka, va, nk, ba) in enumerate(kts):
                            k.op("act", I("activation", out=pt[0:nk, s, 0:nq], in_=pss[0:nk, s * 128:s * 128 + nq], func=AF.Exp), reads=[pss], writes=[pt])
                        for s, (ka, va, nk, ba) in enumerate(kts):
                            k.op("pe", I("matmul", po[:, bi * 128:bi * 128 + nq], lhsT=va, rhs=pt[0:nk, s, 0:nq], start=(s == 0), stop=(s == len(kts) - 1)),
                                 reads=[v, pt], writes=[po])
                        for s, (ka, va, nk, ba) in enumerate(kts):
                            k.op("pe", I("matmul", pz[:, bi * 128:bi * 128 + nq], lhsT=ones_b[0:nk, :], rhs=pt[0:nk, s, 0:nq], start=(s == 0), stop=(s == len(kts) - 1)),
                                 reads=[ones_b, pt], writes=[pz])
                    ncol = 512 if sbk < 4 else 64
                    r = rz.next()
                    k.op("dve", I("reciprocal", out=r[:, 0:ncol], in_=pz[:, 0:ncol]), reads=[pz], writes=[r])
                    ao = aor.next()
                    if sbk == 4:
                        k.op("pool", I("memset", ao[:, 64:128], 0.0), writes=[ao])
                    k.op("dve", I("tensor_tensor", out=ao[:, 0:ncol], in0=po[:, 0:ncol], in1=r[:, 0:ncol], op=ALU.mult),
                         reads=[po, r], writes=[ao], nowaw=True)
                    nst = 512 if sbk < 4 else 128
                    k.op("dq_pool", I("dma_start", out=ATd[h, :, sbk * 512:sbk * 512 + nst], in_=ao[:, 0:nst]), reads=[ao], writes=[ATd], nowaw=True)
        phase_wo_ln(a_wo, a_wo[j], Xsrc, Xdst, lnrow)

    def phase_attn_c(Xsrc, Xdst, lnrow):
        stage('phase_attn_c')
        with k.phase():
            aor = Rot([k.sb([128, 512], BF16, "ao") for _ in range(3)])
            qh = Rot([k.sb([128, T], BF16, "qh") for _ in range(2)])
            kh = Rot([k.sb([128, KEXT], BF16, "kh") for _ in range(2)])
            vh = Rot([k.sb([128, 25, 128], BF16, "vh") for _ in range(2)])
            e1r = Rot([k.sb([128, 512], F32, "e1") for _ in range(4)])
            lspr = Rot([k.sb([128, 512], F32, "lsp") for _ in range(4)])
            lsnr = Rot([k.sb([128, 512], F32, "lsn") for _ in range(4)])
            t1r = Rot([k.sb([128, 512], F32, "t1") for _ in range(4)])
            wr = Rot([k.sb([128, 512], BF16, "w") for _ in range(4)])
            carries = [k.sb([128, 512], F32, "carry") for _ in range(2)]
            ps_z = Rot([k.ps([128, 512], F32, "ps_z") for _ in range(2)])
            ps_a = Rot([k.ps([128, 512], F32, "ps_a") for _ in range(2)])
            ps_c = Rot([k.ps([128, 512], F32, "ps_c") for _ in range(2)])
            ps_o = Rot([k.ps([128, 512], F32, "ps_o") for _ in range(2)])

            def stream(h, sbk, q, kk, v, carry, po):
                if sbk < 4:
                    qc0, nq = sbk * 512, 512
                    kts = [(128, kk[:, kt * 128:(kt + 1) * 128], v[:, kt, :], max(0, kt - 4 * sbk) * 128, kt >= 4 * sbk)
                           for kt in range(4 * sbk + 3, -1, -1)]
                else:
                    qc0, nq = 2048, 64
                    kts = [(64, kk[:, 2048:2112], v[0:64, 16, :], 0, True)]
                    kts += [(128, kk[:, T + c * 128:T + (c + 1) * 128], v[:, 17 + c, :], 0, False) for c in range(7, -1, -1)]
                k.op("pe", I("matmul", po[:, 0:nq], lhsT=zeros_b[:], rhs=q[:, qc0:qc0 + nq], start=True, stop=False), reads=[zeros_b, q], writes=[po])
                k.op("pool", I("memset", carry[:], 0.0), writes=[carry])
                for si, (nk, ka, va, c0, diag) in enumerate(kts):
                    last = (si == len(kts) - 1)
                    dn = min(128, nq - c0)
                    pz = ps_z.next()
                    k.op("pe", I("matmul", pz[0:nk, c0:nq], lhsT=ka, rhs=q[:, qc0 + c0:qc0 + nq], start=True, stop=True), reads=[kk, q], writes=[pz])
                    yield
                    e1 = e1r.next()
                    k.op("act", I("activation", out=e1[0:nk, c0:nq], in_=pz[0:nk, c0:nq], func=AF.Exp, scale=-1.0), reads=[pz], writes=[e1])
                    lsp = lspr.next()
                    k.op("act", I("activation", out=lsp[0:nk, c0:nq], in_=e1[0:nk, c0:nq], func=AF.Ln, bias=ones_f[0:nk, 0:1], scale=1.0), reads=[e1, ones_f], writes=[lsp])
                    yield
                    lsn = lsnr.next()
                    k.op("dve", I("tensor_tensor", out=lsn[0:nk, c0:nq], in0=pz[0:nk, c0:nq], in1=lsp[0:nk, c0:nq], op=ALU.add), reads=[pz, lsp], writes=[lsn])
                    if diag:
                        k.op("pool", I("tensor_tensor", out=lsn[0:nk, c0:c0 + dn], in0=lsn[0:nk, c0:c0 + dn], in1=mcaus[0:nk, 0:dn], op=ALU.mult),
                             reads=[lsn, mcaus], writes=[lsn])
                    yield
                    pa = ps_a.next(); pc = ps_c.next()
                    k.op("pe", I("matmul", pa[0:nk, c0:nq], lhsT=Uf[0:nk, 0:nk], rhs=lsn[0:nk, c0:nq], start=True, stop=True), reads=[Uf, lsn], writes=[pa])
                    k.op("pe", I("matmul", pc[:, c0:nq], lhsT=ones_f[0:nk, :], rhs=lsn[0:nk, c0:nq], start=True, stop=True), reads=[ones_f, lsn], writes=[pc])
                    yield
                    t1 = t1r.next()
                    k.op("dve", I("tensor_tensor", out=t1[0:nk, c0:nq], in0=pa[0:nk, c0:nq], in1=lsp[0:nk, c0:nq], op=ALU.add), reads=[pa, lsp], writes=[t1])
                    k.op("pool", I("tensor_tensor", out=t1[0:nk, c0:nq], in0=t1[0:nk, c0:nq], in1=carry[0:nk, c0:nq], op=ALU.add), reads=[t1, carry], writes=[t1])
                    yield
                    w = wr.next()
                    k.op("act", I("activation", out=w[0:nk, c0:nq], in_=t1[0:nk, c0:nq], func=AF.Exp, scale=-1.0), reads=[t1], writes=[w])
                    if diag:
                        k.op("pool", I("tensor_tensor", out=w[0:nk, c0:c0 + dn], in0=w[0:nk, c0:c0 + dn], in1=mcausb[0:nk, 0:dn], op=ALU.mult),
                             reads=[w, mcausb], writes=[w])
                    yield
                    if not last:
                        k.op("dve", I("tensor_tensor", out=carry[:, c0:nq], in0=pc[:, c0:nq], in1=carry[:, c0:nq], op=ALU.add), reads=[pc, carry], writes=[carry])
                    k.op("pe", I("matmul", po[:, c0:nq], lhsT=va, rhs=w[0:nk, c0:nq], start=False, stop=last), reads=[v, w], writes=[po])
                    yield
                ao = aor.next()
                if sbk == 4:
                    k.op("pool", I("memset", ao[:, 64:128], 0.0), writes=[ao])
                copy_op("act", ao[:, 0:nq], po[:, 0:nq], [po], [ao])
                nst = 512 if sbk < 4 else 128
                k.op("sp", I("dma_start", out=ATd[h, :, sbk * 512:sbk * 512 + nst], in_=ao[:, 0:nst]), reads=[ao], writes=[ATd], nowaw=True)

            pending = []
            for h in range(16):
                q = qh.next(); kk = kh.next(); v = vh.next()
                k.op("sp", I("dma_start", out=q[:], in_=QT[h]), reads=[QT], writes=[q])
                k.op("sp", I("dma_start", out=kk[:], in_=KTX[h]), reads=[KTX], writes=[kk])
                k.op("sp", I("dma_start", out=v[:], in_=VX[:, h * 128:(h + 1) * 128].rearrange("(t p) d -> p t d", p=128)), reads=[VX], writes=[v])
                order = [3, 0, 2, 1, 4]
                slots = [None, None]
                todo = list(order)
                while todo or any(sl is not None for sl in slots):
                    for si_ in range(2):
                        if slots[si_] is None and todo:
                            slots[si_] = stream(h, todo.pop(0), q, kk, v, carries[si_], ps_o.tiles[si_])
                        if slots[si_] is not None:
                            try:
                                next(slots[si_])
                            except StopIteration:
                                slots[si_] = None
        phase_wo_ln(c_wo, c_wo[:, :], Xsrc, Xdst, lnrow)

    def rmsnorm_psum(ps, gtile, out_ap, st6, mv, reads_extra, out_tile):
        k.op("dve", I("bn_stats", out=st6[:, 0, :], in_=ps[:]), reads=[ps], writes=[st6])
        k.op("dve", I("bn_aggr", out=mv[:, 0:2], in_=st6[:, 0, :]), reads=[st6], writes=[mv])
        k.op("dve", I("scalar_tensor_tensor", out=mv[:, 2:3], in0=mv[:, 0:1], scalar=mv[:, 0:1], in1=mv[:, 1:2], op0=ALU.mult, op1=ALU.add), reads=[mv], writes=[mv])
        k.op("dve", I("tensor_scalar", out=mv[:, 2:3], in0=mv[:, 2:3], scalar1=1e-6, scalar2=None, op0=ALU.add), reads=[mv], writes=[mv])
        k.op("act", I("activation", out=mv[:, 2:3], in_=mv[:, 2:3], func=AF.Sqrt), reads=[mv], writes=[mv])
        k.op("dve", I("reciprocal", out=mv[:, 3:4], in_=mv[:, 2:3]), reads=[mv], writes=[mv])
        k.op("dve", I("scalar_tensor_tensor", out=out_ap, in0=ps[:], scalar=mv[:, 3:4], in1=gtile[:], op0=ALU.mult, op1=ALU.mult), reads=[ps, mv, gtile], writes=[out_tile])

    def rope_ops(src3, cs, sn, dst3, nh, tmp_a, tmp_b, reads, dst_tile, scale=None):
        cb = sap(cs, 0, [[0, nh], [1, 32]])
        sb_ = sap(sn, 0, [[0, nh], [1, 32]])
        x1 = src3[:, :, 0:32]; x2 = src3[:, :, 32:64]
        ta = tmp_a[:, 0:nh * 32].rearrange("p (h r) -> p h r", r=32)
        tb = tmp_b[:, 0:nh * 32].rearrange("p (h r) -> p h r", r=32)
        k.op("dve", I("tensor_tensor", out=ta, in0=x1, in1=cb, op=ALU.mult), reads=reads + [cs], writes=[tmp_a])
        k.op("dve", I("tensor_tensor", out=tb, in0=x2, in1=sb_, op=ALU.mult), reads=reads + [sn], writes=[tmp_b])
        k.op("dve", I("tensor_tensor", out=dst3[:, :, 0:32], in0=ta, in1=tb, op=ALU.subtract), reads=[tmp_a, tmp_b], writes=[dst_tile])
        k.op("dve", I("tensor_tensor", out=ta, in0=x1, in1=sb_, op=ALU.mult), reads=reads + [sn], writes=[tmp_a])
        k.op("dve", I("tensor_tensor", out=tb, in0=x2, in1=cb, op=ALU.mult), reads=reads + [cs], writes=[tmp_b])
        k.op("dve", I("tensor_tensor", out=dst3[:, :, 32:64], in0=ta, in1=tb, op=ALU.add), reads=[tmp_a, tmp_b], writes=[dst_tile], nowaw=True)

    def phase_mla(Xsrc, Xdst, lnrow):
        stage('phase_mla_in')
        sc = 192 ** -0.5
        with k.phase():
            xT = k.sb([128, KC, T], BF16, "xT")
            load_xT(xT)
            winb = k.sb([128, KC, 1088], BF16, "winb")
            stg = Rot([k.sb([128, KC, 128], F32, "wstg") for _ in range(2)])
            for c in range(9):
                n = 128 if c < 8 else 64
                st = stg.next()
                k.op("sp", I("dma_start", out=st[:, :, 0:n], in_=b_win[:, c * 128:c * 128 + n].rearrange("(kc p) n -> p kc n", p=128)), reads=[b_win], writes=[st])
                k.op("pool", I("tensor_copy", out=winb[:, :, c * 128:c * 128 + n], in_=st[:, :, 0:n]), reads=[st], writes=[winb], nowaw=(c > 0))
            qn_t = k.sb([128, 512], F32, "qn_t"); kvn_t = k.sb([128, 512], F32, "kvn_t")
            k.op("sp", I("dma_start", out=qn_t[:], in_=dap(b_qn.t, 0, [[0, 128], [1, 512]])), reads=[b_qn], writes=[qn_t])
            k.op("sp", I("dma_start", out=kvn_t[:], in_=dap(b_kvn.t, 0, [[0, 128], [1, 512]])), reads=[b_kvn], writes=[kvn_t])
            st6 = k.sb([128, 1, 6], F32, "st6"); mv = k.sb([128, 4], F32, "mv")
            csr = Rot([k.sb([128, 32], F32, "cs") for _ in range(2)]); snr = Rot([k.sb([128, 32], F32, "sn") for _ in range(2)])
            p0r = Rot([k.ps([128, 512], F32, "p0") for _ in range(2)])
            p1r = Rot([k.ps([128, 512], F32, "p1") for _ in range(2)])
            p2r = Rot([k.ps([128, 64], F32, "p2") for _ in range(1)])
            pst = Rot([k.ps([128, 1024], BF16, "pst") for _ in range(2)])
            cqb = Rot([k.sb([128, 512], BF16, "cqb") for _ in range(2)])
            ckf = Rot([k.sb([128, 512], F32, "ckf") for _ in range(2)])
            ckb = Rot([k.sb([128, 512], BF16, "ckb") for _ in range(2)])
            krf = Rot([k.sb([128, 1, 64], F32, "krf") for _ in range(2)])
            krb = Rot([k.sb([128, 128], BF16, "krb") for _ in range(2)])
            ta = k.sb([128, 512], F32, "ta"); tb = k.sb([128, 512], F32, "tb")
            tsb = Rot([k.sb([128, 9, 128], BF16, "tsb") for _ in range(2)])

            def transposes_out(cq_bf, ck_bf, kr_bf, col0):
                ps = pst.next()
                ts = tsb.next()
                n = 0
                srcs = []
                if cq_bf is not None:
                    srcs += [(cq_bf, c) for c in range(4)]
                srcs += [(ck_bf, c) for c in range(4)]
                for (tl, c) in srcs:
                    k.op("pe", I("transpose", out=ps[:, n * 128:(n + 1) * 128], in_=tl[:, c * 128:(c + 1) * 128], identity=ident[:]), reads=[tl, ident], writes=[ps])
                    n += 1
                copy_op(evac_eng(), ts[:, 0:n, :], ps[:, 0:n * 128].rearrange("p (a b) -> p a b", b=128), [ps], [ts])
                ps2 = pst.next()
                k.op("pe", I("transpose", out=ps2[:, 0:128], in_=kr_bf[:], identity=ident[:]), reads=[kr_bf, ident], writes=[ps2])
                copy_op(evac_eng(), ts[:, 8, :], ps2[:, 0:128], [ps2], [ts])
                o = 0
                if cq_bf is not None:
                    k.op("sp", I("dma_start", out=CQT[:, :, col0:col0 + 128].rearrange("c p n -> p c n"), in_=ts[:, 0:4, :]), reads=[ts], writes=[CQT], nowaw=True)
                    o = 4
                k.op("sp", I("dma_start", out=CKVT[:, :, col0:col0 + 128].rearrange("c p n -> p c n"), in_=ts[:, o:o + 4, :]), reads=[ts], writes=[CKVT], nowaw=True)
                k.op("sp", I("dma_start", out=KR[:, col0:col0 + 128], in_=ts[:, 8, :]), reads=[ts], writes=[KR], nowaw=True)

            for tt in range(NT):
                cs = csr.next(); sn = snr.next()
                k.op("sp", I("dma_start", out=cs[:], in_=ropec[tt * 128:(tt + 1) * 128, :]), reads=[ropec], writes=[cs])
                k.op("sp", I("dma_start", out=sn[:], in_=ropes[tt * 128:(tt + 1) * 128, :]), reads=[ropes], writes=[sn])
                p0 = p0r.next(); p1 = p1r.next(); p2 = p2r.next()
                for (pt_, c0, n) in ((p0, 0, 512), (p1, 512, 512), (p2, 1024, 64)):
                    for kc in range(KC):
                        k.op("pe", I("matmul", pt_[:, 0:n], lhsT=xT[:, kc, tt * 128:(tt + 1) * 128], rhs=winb[:, kc, c0:c0 + n], start=(kc == 0), stop=(kc == KC - 1)),
                             reads=[xT, winb], writes=[pt_])
                cq = cqb.next()
                rmsnorm_psum(p0, qn_t, cq[:], st6, mv, [], cq)
                cf = ckf.next()
                rmsnorm_psum(p1, kvn_t, cf[:], st6, mv, [], cf)
                if tt < 16:
                    k.op("sp", I("dma_start", out=obc[tt * 128:(tt + 1) * 128, :], in_=cf[:]), reads=[cf], writes=[])
                else:
                    k.op("sp", I("dma_start", out=sbc[:, :], in_=cf[0:64, :]), reads=[cf], writes=[])
                cb_ = ckb.next()
                k.op("act", I("copy", out=cb_[:], in_=cf[:]), reads=[cf], writes=[cb_])
                kf = krf.next()
                rope_ops(p2[:].rearrange("p (h c) -> p h c", c=64), cs, sn, kf[:], 1, ta, tb, [p2], kf)
                if tt < 16:
                    k.op("sp", I("dma_start", out=obr[tt * 128:(tt + 1) * 128, :], in_=kf[:, 0, :]), reads=[kf], writes=[])
                else:
                    k.op("sp", I("dma_start", out=sbr[:, :], in_=kf[0:64, 0, :]), reads=[kf], writes=[])
                kb = krb.next()
                k.op("act", I("copy", out=kb[:, 0:64], in_=kf[:, 0, :]), reads=[kf], writes=[kb])
                k.op("act", I("copy", out=kb[:, 64:128], in_=kf[:, 0, :]), reads=[kf], writes=[kb], nowaw=True)
                transposes_out(cq, cb_, kb, tt * 128)
            for c in range(8):
                cf = ckf.next()
                k.op("sp", I("dma_start", out=cf[:], in_=cb_c[c * 128:(c + 1) * 128, :]), reads=[], writes=[cf])
                cb_ = ckb.next()
                k.op("act", I("copy", out=cb_[:], in_=cf[:]), reads=[cf], writes=[cb_])
                kf = krf.next()
                k.op("sp", I("dma_start", out=kf[:, 0, :], in_=cb_r[c * 128:(c + 1) * 128, :]), reads=[], writes=[kf])
                kb = krb.next()
                k.op("act", I("copy", out=kb[:, 0:64], in_=kf[:, 0, :]), reads=[kf], writes=[kb])
                k.op("act", I("copy", out=kb[:, 64:128], in_=kf[:, 0, :]), reads=[kf], writes=[kb], nowaw=True)
                transposes_out(None, cb_, kb, T + c * 128)
        stage('phase_mla_q')
        with k.phase():
            cqT = k.sb([128, 4, T], BF16, "cqT")
            k.op("sp", I("dma_start", out=cqT[:], in_=CQT[:, :, :].rearrange("c p n -> p c n")), reads=[CQT], writes=[cqT])
            wqn = k.sb([128, 4, 2048], BF16, "wqn"); wqr = k.sb([128, 4, 1024], BF16, "wqr")
            stg = Rot([k.sb([128, 4, 768], F32, "wstg") for _ in range(2)])
            for c in range(4):
                st = stg.next()
                k.op("sp", I("dma_start", out=st[:], in_=b_wqb[:, c * 768:(c + 1) * 768].rearrange("(kc p) n -> p kc n", p=128)), reads=[b_wqb], writes=[st])
                for kc in range(4):
                    sv = st[:, kc, :].rearrange("p (h c) -> p h c", c=192)
                    k.op("pool", I("tensor_copy", out=wqn[:, kc, c * 512:(c + 1) * 512].rearrange("p (h c) -> p h c", c=128), in_=sv[:, :, 0:128]), reads=[st], writes=[wqn], nowaw=True)
                    k.op("pool", I("tensor_copy", out=wqr[:, kc, c * 256:(c + 1) * 256].rearrange("p (h c) -> p h c", c=64), in_=sv[:, :, 128:192]), reads=[st], writes=[wqr], nowaw=True)
            pp = Rot([k.ps([128, 512], F32, "pp") for _ in range(3)])
            qts = Rot([k.sb([128, 512], BF16, "qts") for _ in range(3)])
            for h in range(16):
                for (t0, tn) in TOKB:
                    ps = pp.next()
                    for kc in range(4):
                        k.op("pe", I("matmul", ps[:, 0:tn], lhsT=wqn[:, kc, h * 128:(h + 1) * 128], rhs=cqT[:, kc, t0:t0 + tn], start=(kc == 0), stop=(kc == 3)),
                             reads=[wqn, cqT], writes=[ps])
                    qs = qts.next()
                    copy_op(evac_eng(), qs[:, 0:tn], ps[:, 0:tn], [ps], [qs], scale=sc)
                    k.op("sp", I("dma_start", out=QT[h, :, t0:t0 + tn], in_=qs[:, 0:tn]), reads=[qs], writes=[QT], nowaw=True)
            csr = Rot([k.sb([128, 32], F32, "cs") for _ in range(2)]); snr = Rot([k.sb([128, 32], F32, "sn") for _ in range(2)])
            ta = k.sb([128, 512], F32, "ta"); tb = k.sb([128, 512], F32, "tb")
            qrf = Rot([k.sb([128, 16, 64], F32, "qrf") for _ in range(2)])
            qrb = Rot([k.sb([128, 1024], BF16, "qrb") for _ in range(2)])
            pst = Rot([k.ps([128, 1024], BF16, "pst") for _ in range(2)])
            qrt = Rot([k.sb([128, 8, 128], BF16, "qrt") for _ in range(2)])
            for tt in range(NT):
                cs = csr.next(); sn = snr.next()
                k.op("sp", I("dma_start", out=cs[:], in_=ropec[tt * 128:(tt + 1) * 128, :]), reads=[ropec], writes=[cs])
                k.op("sp", I("dma_start", out=sn[:], in_=ropes[tt * 128:(tt + 1) * 128, :]), reads=[ropes], writes=[sn])
                qf = qrf.next()
                for hg in range(2):
                    ps = pp.next()
                    for kc in range(4):
                        k.op("pe", I("matmul", ps[:, :], lhsT=cqT[:, kc, tt * 128:(tt + 1) * 128], rhs=wqr[:, kc, hg * 512:(hg + 1) * 512], start=(kc == 0), stop=(kc == 3)),
                             reads=[wqr, cqT], writes=[ps])
                    rope_ops(ps[:].rearrange("p (h c) -> p h c", c=64), cs, sn, qf[:, hg * 8:(hg + 1) * 8, :], 8, ta, tb, [ps], qf)
                qb = qrb.next()
                k.op("act", I("activation", out=qb[:], in_=qf[:].rearrange("p h c -> p (h c)"), func=AF.Copy, scale=float(sc)), reads=[qf], writes=[qb])
                ps2 = pst.next()
                for j8 in range(8):
                    k.op("pe", I("transpose", out=ps2[:, j8 * 128:(j8 + 1) * 128], in_=qb[:, j8 * 128:(j8 + 1) * 128], identity=ident[:]), reads=[qb, ident], writes=[ps2])
                qt_ = qrt.next()
                copy_op(evac_eng(), qt_[:], ps2[:].rearrange("p (a b) -> p a b", a=8), [ps2], [qt_])
                k.op("sp", I("dma_start", out=QR[:, :, tt * 128:(tt + 1) * 128].rearrange("a p n -> p a n"), in_=qt_[:]), reads=[qt_], writes=[QR], nowaw=True)
        stage('phase_mla_kv')
        with k.phase():
            ckvT = k.sb([128, 4, KEXT], BF16, "ckvT")
            k.op("sp", I("dma_start", out=ckvT[:], in_=CKVT[:, :, :].rearrange("c p n -> p c n")), reads=[CKVT], writes=[ckvT])
            wkn = k.sb([128, 4, 2048], BF16, "wkn"); wkv = k.sb([128, 4, 2048], BF16, "wkv")
            stg = Rot([k.sb([128, 4, 1024], F32, "wstg") for _ in range(2)])
            for c in range(4):
                st = stg.next()
                k.op("sp", I("dma_start", out=st[:], in_=b_wkvb[:, c * 1024:(c + 1) * 1024].rearrange("(kc p) n -> p kc n", p=128)), reads=[b_wkvb], writes=[st])
                for kc in range(4):
                    sv = st[:, kc, :].rearrange("p (h c) -> p h c", c=256)
                    k.op("pool", I("tensor_copy", out=wkn[:, kc, c * 512:(c + 1) * 512].rearrange("p (h c) -> p h c", c=128), in_=sv[:, :, 0:128]), reads=[st], writes=[wkn], nowaw=True)
                    k.op("pool", I("tensor_copy", out=wkv[:, kc, c * 512:(c + 1) * 512].rearrange("p (h c) -> p h c", c=128), in_=sv[:, :, 128:256]), reads=[st], writes=[wkv], nowaw=True)
            pp = Rot([k.ps([128, 512], F32, "pp") for _ in range(4)])
            qts = Rot([k.sb([128, 512], BF16, "qts") for _ in range(3)])
            KB = [(i * 512, 512) for i in range(6)] + [(3072, 128)]
            for h in range(16):
                for (t0, tn) in KB:
                    ps = pp.next()
                    for kc in range(4):
                        k.op("pe", I("matmul", ps[:, 0:tn], lhsT=wkn[:, kc, h * 128:(h + 1) * 128], rhs=ckvT[:, kc, t0:t0 + tn], start=(kc == 0), stop=(kc == 3)),
                             reads=[wkn, ckvT], writes=[ps])
                    qs = qts.next()
                    copy_op(evac_eng(), qs[:, 0:tn], ps[:, 0:tn], [ps], [qs])
                    k.op("sp", I("dma_start", out=KTX[h, :, t0:t0 + tn], in_=qs[:, 0:tn]), reads=[qs], writes=[KTX], nowaw=True)
            for kt in range(25):
                for ns in range(4):
                    ps = pp.next()
                    for kc in range(4):
                        k.op("pe", I("matmul", ps[:, :], lhsT=ckvT[:, kc, kt * 128:(kt + 1) * 128], rhs=wkv[:, kc, ns * 512:(ns + 1) * 512], start=(kc == 0), stop=(kc == 3)),
                             reads=[wkv, ckvT], writes=[ps])
                    qs = qts.next()
                    copy_op(evac_eng(), qs[:], ps[:], [ps], [qs])
                    k.op("sp", I("dma_start", out=VX[kt * 128:(kt + 1) * 128, ns * 512:(ns + 1) * 512], in_=qs[:]), reads=[qs], writes=[VX], nowaw=True)
        stage('phase_mla_attn')
        with k.phase():
            aor = Rot([k.sb([128, 512], BF16, "ao") for _ in range(2)])
            qh = Rot([k.sb([128, T], BF16, "qh") for _ in range(2)])
            qrh = Rot([k.sb([128, T], BF16, "qrh") for _ in range(2)])
            kh = Rot([k.sb([128, KEXT], BF16, "kh") for _ in range(2)])
            vh = Rot([k.sb([128, 25, 128], BF16, "vh") for _ in range(2)])
            krs = k.sb([128, KEXT], BF16, "krs")
            k.op("sp", I("dma_start", out=krs[:], in_=KR[:, :]), reads=[KR], writes=[krs])
            ptr = Rot([k.sb([128, 512], BF16, "pt") for _ in range(3)])
            rz = Rot([k.sb([128, 512], F32, "rz") for _ in range(2)])
            ps_s = Rot([k.ps([128, 512], F32, "ps_s") for _ in range(3)])
            ps_o = Rot([k.ps([128, 512], F32, "ps_o") for _ in range(2)])
            ps_zz = Rot([k.ps([128, 512], F32, "ps_zz") for _ in range(2)])
            for h in range(16):
                q = qh.next(); kk = kh.next(); v = vh.next()
                hb = (h % 2) * 64
                k.op("sp", I("dma_start", out=q[:], in_=QT[h]), reads=[QT], writes=[q])
                if h % 2 == 0:
                    qr = qrh.next()
                    k.op("sp", I("dma_start", out=qr[:], in_=QR[h // 2]), reads=[QR], writes=[qr])
                k.op("sp", I("dma_start", out=kk[:], in_=KTX[h]), reads=[KTX], writes=[kk])
                k.op("sp", I("dma_start", out=v[:], in_=VX[:, h * 128:(h + 1) * 128].rearrange("(t p) d -> p t d", p=128)), reads=[VX], writes=[v])
                for sbk in range(5):
                    if sbk < 4:
                        qc0, nq = sbk * 512, 512
                        kts = [(128, kt * 128, v[:, kt, :], max(0, kt - 4 * sbk) * 128, kt >= 4 * sbk) for kt in range(0, 4 * sbk + 4)]
                    else:
                        qc0, nq = 2048, 64
                        kts = [(128, T + c * 128, v[:, 17 + c, :], 0, False) for c in range(8)] + [(64, 2048, v[0:64, 16, :], 0, False)]
                    po = ps_o.next(); pz = ps_zz.next()
                    prev = None
                    for si, (nk, kc0, va, c0, diag) in enumerate(kts):
                        last = (si == len(kts) - 1)
                        ps = ps_s.next()
                        k.op("pe", I("matmul", ps[0:nk, c0:nq], lhsT=kk[:, kc0:kc0 + nk], rhs=q[:, qc0 + c0:qc0 + nq], start=True, stop=False), reads=[kk, q], writes=[ps])
                        if diag:
                            k.op("pe", I("matmul", ps[0:nk, c0:c0 + 128], lhsT=ident[:], rhs=m0n[:], start=False, stop=False), reads=[ident, m0n], writes=[ps])
                        k.op("pe", I("matmul", ps[0:nk, c0:nq], lhsT=krs[hb:hb + 64, kc0:kc0 + nk], rhs=qr[hb:hb + 64, qc0 + c0:qc0 + nq], start=False, stop=True),
                             reads=[krs, qr], writes=[ps])
                        pt = ptr.next()
                        k.op("act", I("activation", out=pt[0:nk, c0:nq], in_=ps[0:nk, c0:nq], func=AF.Exp), reads=[ps], writes=[pt])
                        if prev is not None:
                            (psi, pnk, pva, pc0, ppt, plast) = prev
                            k.op("pe", I("matmul", po[:, pc0:nq], lhsT=pva, rhs=ppt[0:pnk, pc0:nq], start=(psi == 0), stop=plast), reads=[v, ppt], writes=[po])
                            k.op("pe", I("matmul", pz[:, pc0:nq], lhsT=ones_b[0:pnk, :], rhs=ppt[0:pnk, pc0:nq], start=(psi == 0), stop=plast), reads=[ones_b, ppt], writes=[pz])
                        prev = (si, nk, va, c0, pt, last)
                    (psi, pnk, pva, pc0, ppt, plast) = prev
                    k.op("pe", I("matmul", po[:, pc0:nq], lhsT=pva, rhs=ppt[0:pnk, pc0:nq], start=(psi == 0), stop=plast), reads=[v, ppt], writes=[po])
                    k.op("pe", I("matmul", pz[:, pc0:nq], lhsT=ones_b[0:pnk, :], rhs=ppt[0:pnk, pc0:nq], start=(psi == 0), stop=plast), reads=[ones_b, ppt], writes=[pz])
                    r = rz.next()
                    k.op("dve", I("reciprocal", out=r[:, 0:nq], in_=pz[:, 0:nq]), reads=[pz], writes=[r])
                    ao = aor.next()
                    if sbk == 4:
                        k.op("pool", I("memset", ao[:, 64:128], 0.0), writes=[ao])
                    k.op("dve", I("tensor_tensor", out=ao[:, 0:nq], in0=po[:, 0:nq], in1=r[:, 0:nq], op=ALU.mult), reads=[po, r], writes=[ao], nowaw=True)
                    nst = 512 if sbk < 4 else 128
                    k.op("sp", I("dma_start", out=ATd[h, :, sbk * 512:sbk * 512 + nst], in_=ao[:, 0:nst]), reads=[ao], writes=[ATd], nowaw=True)
        phase_wo_ln(b_wo, b_wo[:, :], Xsrc, Xdst, lnrow)


    def phase_peer(i, Xsrc, Xdst, lnrow, final_out=None):
        keys2d = p_keys[i]
        stage('peer_scores')
        with k.phase():
            xT = k.sb([128, KC, T], BF16, "xT")
            load_xT(xT)
            kin = Rot([k.sb([128, 128], F32, "kin") for _ in range(2)])
            kbf = Rot([k.sb([128, 128], BF16, "kbf") for _ in range(2)])
            keysT = k.sb([128, 16, 128], BF16, "keysT")
            pst = Rot([k.ps([128, 1024], BF16, "pst") for _ in range(1)])
            for half in range(2):
                ps = pst.next()
                for jx in range(8):
                    hc = half * 8 + jx
                    ki = kin.next(); kb = kbf.next()
                    k.op("sp", I("dma_start", out=ki[:], in_=keys2d[hc]), reads=[p_keys], writes=[ki])
                    k.op("act", I("copy", out=kb[:], in_=ki[:]), reads=[ki], writes=[kb])
                    k.op("pe", I("transpose", out=ps[:, jx * 128:(jx + 1) * 128], in_=kb[:], identity=ident[:]), reads=[kb, ident], writes=[ps])
                copy_op("act", keysT[:, half * 8:(half + 1) * 8, :], ps[:].rearrange("p (a b) -> p a b", a=8), [ps], [keysT])
            wl = WLoader(KC, 128, nbuf=2, cast_eng="act")
            pq = Rot([k.ps([128, 512], F32, "pq") for _ in range(3)])
            psc = Rot([k.ps([128, 512], F32, "psc") for _ in range(2)])
            qtb = Rot([k.sb([128, 512], BF16, "qtb") for _ in range(3)])
            Sr = Rot([k.sb([128, 16, 128], F32, "S") for _ in range(5)])
            T16 = k.sb([128, 16, 16], F32, "T16")
            tmpS = k.sb([128, 16, 128], F32, "tmpS")
            pen = k.sb([128, 16, 128], F32, "pen")
            Ar = Rot([k.sb([128, 2048 + 16], F32, "A12") for _ in range(2)])
            cand = k.sb([128, 8, 256], F32, "cand")
            ct1 = k.sb([128, 8, 256], F32, "ct1")
            ct2 = k.sb([128, 8, 256], F32, "ct2")
            C24 = k.sb([128, 8, 24], F32, "C24")
            dd = k.sb([128, 8, 16], F32, "dd")
            zz = k.sb([128, 16], F32, "zz")

            def topk_tile(S, tt):
                for hc in range(16):
                    k.op("dve", I("max", out=T16[:, hc, 0:8], in_=S[:, hc, :]), reads=[S], writes=[T16], nowaw=True)
                for hc in range(16):
                    k.op("dve", I("match_replace", out=tmpS[:, hc, :], in_to_replace=T16[:, hc, 0:8], in_values=S[:, hc, :], imm_value=-1e30),
                         reads=[T16, S], writes=[tmpS], nowaw=True)
                for hc in range(16):
                    k.op("dve", I("max", out=T16[:, hc, 8:16], in_=tmpS[:, hc, :]), reads=[tmpS], writes=[T16], nowaw=True)
                A = Ar.next()
                A3 = A[:, 0:2048].rearrange("p (a b) -> p a b", b=128)
                k.op("dve", I("tensor_tensor", out=pen[:], in0=S[:], in1=sap(T16, 15, [[16, 16], [0, 128]]), op=ALU.is_lt), reads=[S, T16], writes=[pen])
                k.op("dve", I("scalar_tensor_tensor", out=A3, in0=pen[:], scalar=-1e4, in1=S[:], op0=ALU.mult, op1=ALU.add), reads=[pen, S], writes=[A])
                k.op("dve", I("tensor_tensor", out=cand[:].rearrange("p h (i j) -> p h i j", j=16),
                              in0=sap(T16, 0, [[32, 8], [1, 16], [0, 16]]), in1=sap(T16, 16, [[32, 8], [0, 16], [1, 16]]), op=ALU.add),
                     reads=[T16], writes=[cand])
                for h in range(8):
                    k.op("dve", I("max", out=C24[:, h, 0:8], in_=cand[:, h, :]), reads=[cand], writes=[C24], nowaw=True)
                for h in range(8):
                    k.op("dve", I("match_replace", out=ct1[:, h, :], in_to_replace=C24[:, h, 0:8], in_values=cand[:, h, :], imm_value=-1e30), reads=[C24, cand], writes=[ct1], nowaw=True)
                for h in range(8):
                    k.op("dve", I("max", out=C24[:, h, 8:16], in_=ct1[:, h, :]), reads=[ct1], writes=[C24], nowaw=True)
                for h in range(8):
                    k.op("dve", I("match_replace", out=ct2[:, h, :], in_to_replace=C24[:, h, 8:16], in_values=ct1[:, h, :], imm_value=-1e30), reads=[C24, ct1], writes=[ct2], nowaw=True)
                for h in range(8):
                    k.op("dve", I("max", out=C24[:, h, 16:24], in_=ct2[:, h, :]), reads=[ct2], writes=[C24], nowaw=True)
                k.op("dve", I("tensor_tensor", out=A[:, 2048:2056], in0=C24[:, :, 15], in1=C24[:, :, 16], op=ALU.add), reads=[C24], writes=[A])
                k.op("dve", I("tensor_scalar", out=A[:, 2048:2056], in0=A[:, 2048:2056], scalar1=0.5, scalar2=None, op0=ALU.mult), reads=[A], writes=[A])
                k.op("dve", I("tensor_tensor", out=dd[:], in0=C24[:, :, 0:16], in1=sap(C24, 0, [[24, 8], [0, 16]]), op=ALU.subtract), reads=[C24], writes=[dd])
                k.op("act", I("activation", out=dd[:], in_=dd[:], func=AF.Exp), reads=[dd], writes=[dd])
                k.op("dve", I("tensor_reduce", out=zz[:, 0:8], in_=dd[:], axis=AX.X, op=ALU.add), reads=[dd], writes=[zz])
                k.op("act", I("activation", out=zz[:, 8:16], in_=zz[:, 0:8], func=AF.Ln), reads=[zz], writes=[zz])
                k.op("dve", I("tensor_tensor", out=zz[:, 8:16], in0=zz[:, 8:16], in1=C24[:, :, 0], op=ALU.add), reads=[zz, C24], writes=[zz])
                k.op("dve", I("tensor_scalar", out=A[:, 2056:2064], in0=zz[:, 8:16], scalar1=-1.0, scalar2=None, op0=ALU.mult), reads=[zz], writes=[A])
                k.op("sp", I("dma_start", out=AUX[tt], in_=A[:]), reads=[A], writes=[AUXb[tt]])

            for (t0, tn) in TOKB:
                nt_ = tn // 128
                tt0 = t0 // 128
                Sts = [Sr.next() for _ in range(nt_)]
                for hc in range(16):
                    wb = wl.load(p_wq, p_wq[i], [(hc * 128, 128)])
                    for hh in range(1):
                        ps = pq.next()
                        for kc in range(KC):
                            k.op("pe", I("matmul", ps[:, 0:tn], lhsT=wb[:, kc, 0:128], rhs=xT[:, kc, t0:t0 + tn],
                                         start=(kc == 0), stop=(kc == KC - 1)), reads=[wb, xT], writes=[ps])
                        qb = qtb.next()
                        copy_op("act", qb[:, 0:tn], ps[:, 0:tn], [ps], [qb])
                        p2 = psc.next()
                        for ti in range(nt_):
                            k.op("pe", I("matmul", p2[:, ti * 128:(ti + 1) * 128], lhsT=qb[:, ti * 128:(ti + 1) * 128], rhs=keysT[:, hc, :],
                                         start=True, stop=True), reads=[qb, keysT], writes=[p2])
                        for ti in range(nt_):
                            k.op("act", I("copy", out=Sts[ti][:, hc, :], in_=p2[:, ti * 128:(ti + 1) * 128]), reads=[p2], writes=[Sts[ti]], nowaw=(hc > 0))
                for ti in range(nt_):
                    topk_tile(Sts[ti], tt0 + ti)
        stage('peer_main')
        with k.phase():
            NG = 16
            ustg = Rot([k.sb([128, 2048], F32, "ustg") for _ in range(1)])
            ubf = Rot([k.sb([128, 2048], BF16, "ubf") for _ in range(2)])
            vstg = Rot([k.sb([128, 2048], F32, "vstg") for _ in range(1)])
            uT = Rot([k.sb([128, KC, 512], BF16, "uT") for _ in range(4)])
            vB = Rot([k.sb([128, 2048], BF16, "vB") for _ in range(10)])
            xts = Rot([k.sb([128, KC, 128], BF16, "xtl") for _ in range(2)])
            a2r = Rot([k.sb([128, 8, 128], F32, "a2") for _ in range(2)])
            a1r = Rot([k.sb([128, 8, 8], F32, "a1") for _ in range(2)])
            str_ = Rot([k.sb([128, 16], F32, "st") for _ in range(2)])
            tmpr = Rot([k.sb([128, 1024], F32, "tmp") for _ in range(3)])
            Ebr = Rot([k.sb([128, 1024], BF16, "Eb") for _ in range(3)])
            Ghr = Rot([k.sb([128, 1024], BF16, "Gh") for _ in range(4)])
            Gsr = Rot([k.sb([128, 1024], BF16, "Gs") for _ in range(2)])
            glr = Rot([k.sb([128, 1024], BF16, "gl") for _ in range(2)])
            Wbr = Rot([k.sb([128, 1024], BF16, "Wb") for _ in range(2)])
            WTr = Rot([k.sb([128, 8, 128], BF16, "WT") for _ in range(2)])
            ytr = Rot([k.sb([128, 2048], F32, "yt") for _ in range(2)])
            pH = k.ps([128, 1024], F32, "pH")
            pG = k.ps([128, 1024], F32, "pG")
            pUW = k.ps([128, 1024], BF16, "pUW")
            pYr = Rot([k.ps([128, 512], F32, "pY") for _ in range(3)])
            steps = [(g, tt) for g in range(NG) for tt in range(NT)]
            NS = len(steps)
            S = [dict() for _ in range(NS)]
            Wg = [dict(uts=[None, None], vbs=[None] * 8) for _ in range(NG)]

            def prep_u(g, half):
                ut = uT.next()
                Wg[g]["uts"][half] = ut
                for cc in range(4):
                    e0 = g * 1024 + (half * 4 + cc) * 128
                    us = ustg.next()
                    k.op("sp", I("dma_start", out=us[:], in_=p_u[i, e0:e0 + 128, :]), reads=[p_u], writes=[us])
                    ub = ubf.next()
                    k.op("act", I("copy", out=ub[:], in_=us[:]), reads=[us], writes=[ub])
                    for hf in range(2):
                        for jx in range(8):
                            kc = hf * 8 + jx
                            k.op("pe", I("transpose", out=pUW[:, jx * 128:(jx + 1) * 128], in_=ub[:, kc * 128:(kc + 1) * 128], identity=ident[:]),
                                 reads=[ub, ident], writes=[pUW])
                        copy_op("act", ut[:, hf * 8:(hf + 1) * 8, cc * 128:(cc + 1) * 128], pUW[:].rearrange("p (a b) -> p a b", a=8), [pUW], [ut])

            def prep_v(g, c):
                e0 = g * 1024 + c * 128
                vs = vstg.next()
                k.op("sp", I("dma_start", out=vs[:], in_=p_v[i, e0:e0 + 128, :]), reads=[p_v], writes=[vs])
                vb = vB.next()
                k.op("act", I("copy", out=vb[:], in_=vs[:]), reads=[vs], writes=[vb])
                Wg[g]["vbs"][c] = vb

            def emit_loads(s):
                g, tt = steps[s]
                d = S[s]
                d["xt"] = xts.next(); d["a2"] = a2r.next(); d["a1"] = a1r.next(); d["st"] = str_.next()
                k.op("sp", I("dma_start", out=d["xt"][:], in_=XT[tt]), reads=[XTb[tt]], writes=[d["xt"]])
                auxv = AUX[tt, :, 0:2048].rearrange("p (h c n) -> p h c n", h=8, c=2)
                k.op("sp", I("dma_start", out=d["a2"][:], in_=auxv[:, :, 1, :]), reads=[AUXb[tt]], writes=[d["a2"]])
                k.op("sp", I("dma_start", out=d["a1"][:], in_=auxv[:, :, 0, g * 8:(g + 1) * 8]), reads=[AUXb[tt]], writes=[d["a1"]])
                k.op("sp", I("dma_start", out=d["st"][:], in_=AUX[tt, :, 2048:2064]), reads=[AUXb[tt]], writes=[d["st"]])

            def emit_yload(s):
                g, tt = steps[s]
                d = S[s]
                d["y"] = ytr.next()
                if g > 0:
                    k.op("sp", I("dma_start", out=d["y"][:], in_=YAC[tt]), reads=[YACb[tt]], writes=[d["y"]])

            def emit_store(s):
                g, tt = steps[s]
                k.op("sp", I("dma_start", out=YAC[tt], in_=S[s]["y"][:]), reads=[S[s]["y"]], writes=[YACb[tt]])

            def g_head(s, h):
                d = S[s]
                a1, a2, st = d["a1"], d["a2"], d["st"]
                tm = tmpr.next()
                k.op(("pool" if h in TMP_POOL_HEADS else "dve"), I("tensor_tensor", out=tm[:].rearrange("p (r n) -> p r n", n=128),
                               in0=sap(a1, h * 8, [[1, 8], [0, 128]]), in1=sap(a2, h * 128, [[0, 8], [1, 128]]), op=ALU.add),
                     reads=[a1, a2], writes=[tm])
                eb = Ebr.next()
                k.op("act", I("activation", out=eb[:], in_=tm[:], func=AF.Exp, bias=st[:, 8 + h:9 + h], scale=1.0), reads=[tm, st], writes=[eb])
                gh = Ghr.next()
                k.op("dve", I("scalar_tensor_tensor", out=gh[:], in0=tm[:], scalar=st[:, h:h + 1], in1=eb[:], op0=ALU.is_ge, op1=ALU.mult),
                     reads=[tm, st, eb], writes=[gh])
                for half in range(2):
                    k.op("pe", I("matmul", pG[:, half * 512:(half + 1) * 512], lhsT=ident[:], rhs=gh[:, half * 512:(half + 1) * 512], start=(h == 0), stop=(h == 7)),
                         reads=[ident, gh], writes=[pG])

            def g_evac(s):
                gs = Gsr.next()
                S[s]["Gs"] = gs
                k.op("act", I("copy", out=gs[:], in_=pG[:]), reads=[pG], writes=[gs])

            prep_u(0, 0); prep_u(0, 1)
            for c in range(8):
                prep_v(0, c)
            emit_loads(0)
            for h in range(8):
                g_head(0, h)
            g_evac(0)
            def y_quarter(sp_, q4):
                gp, ttp = steps[sp_]
                dp = S[sp_]
                vbs_p = Wg[gp]["vbs"]
                py = pYr.next()
                for c in range(8):
                    k.op("pe", I("matmul", py[:], lhsT=dp["wt"][:, c, :], rhs=vbs_p[c][:, q4 * 512:(q4 + 1) * 512], start=(c == 0), stop=(c == 7)),
                         reads=[dp["wt"], vbs_p[c]], writes=[py])
                dp["py%d" % q4] = py

            def y_add(sp_, q4):
                gp, ttp = steps[sp_]
                dp = S[sp_]
                y = dp["y"]; py = dp["py%d" % q4]
                if gp == 0:
                    k.op("dve", I("tensor_copy", out=y[:, q4 * 512:(q4 + 1) * 512], in_=py[:]), reads=[py], writes=[y], nowaw=(q4 > 0))
                else:
                    k.op("dve", I("tensor_tensor", out=y[:, q4 * 512:(q4 + 1) * 512], in0=py[:], in1=y[:, q4 * 512:(q4 + 1) * 512], op=ALU.add), reads=[py, y], writes=[y], nowaw=(q4 > 0))

            for s in range(NS + 1):
                if s < NS:
                    g, tt = steps[s]
                    d = S[s]
                    uts = Wg[g]["uts"]
                    if s + 1 < NS:
                        emit_loads(s + 1)
                if s >= 2:
                    emit_store(s - 2)
                if s < NS:
                    emit_yload(s)
                    xt = d["xt"]
                    for half in range(2):
                        for kc in range(KC):
                            k.op("pe", I("matmul", pH[:, half * 512:(half + 1) * 512], lhsT=xt[:, kc, :], rhs=uts[half][:, kc, :], start=(kc == 0), stop=(kc == KC - 1)),
                                 reads=[xt, uts[half]], writes=[pH])
                for h in range(8):
                    if s + 1 < NS:
                        g_head(s + 1, h)
                    if s >= 1 and h <= 3:
                        y_quarter(s - 1, h)
                    if s >= 1 and 2 <= h <= 5:
                        y_add(s - 1, h - 2)
                    if s < NS:
                        if h == 3:
                            gg = glr.next()
                            k.op("act", I("activation", out=gg[:], in_=pH[:], func=AF.Gelu_apprx_tanh), reads=[pH], writes=[gg])
                        elif h == 5:
                            wb_ = Wbr.next()
                            k.op("pool", I("tensor_tensor", out=wb_[:], in0=gg[:], in1=d["Gs"][:], op=ALU.mult), reads=[gg, d["Gs"]], writes=[wb_])
                        elif h == 6:
                            for c in range(8):
                                k.op("pe", I("transpose", out=pUW[:, c * 128:(c + 1) * 128], in_=wb_[:, c * 128:(c + 1) * 128], identity=ident[:]), reads=[wb_, ident], writes=[pUW])
                        elif h == 7:
                            wt = WTr.next()
                            copy_op("act", wt[:], pUW[:].rearrange("p (a b) -> p a b", a=8), [pUW], [wt])
                            d["wt"] = wt
                if s + 1 < NS:
                    g_evac(s + 1)
                if s < NS:
                    if tt == 0 and g >= 1:
                        for c in range(2, 8):
                            prep_v(g, c)
                    if g + 1 < NG:
                        if tt == 3:
                            prep_u(g + 1, 0)
                        elif tt == 9:
                            prep_u(g + 1, 1)
                        elif tt in (6, 12):
                            prep_v(g + 1, (6, 12).index(tt))
            emit_store(NS - 1)
        phase_ln_from_dram([YAC[tt] for tt in range(NT)], YACb, Xsrc, Xdst, lnrow, final_out)

    try:
      phase_init()
      Xcur = x0
      for li in range(n_layers):
          kind, j = li % 3, li // 3
          if kinds is not None:
              kind, j = kinds[li], 0
          Xmid, Xnext = XA, XB
          if kind == 0:
              kro = [(tt, (128, oak[j, (tt - 12) * 128:(tt - 11) * 128, :])) for tt in range(12, 16)]
              vro = [(tt, (128, oav[j, (tt - 12) * 128:(tt - 11) * 128, :])) for tt in range(12, 16)]
              phase_qkv(a_wqkv, a_wqkv[j], ca_k[j], ca_v[j], 512, kro, vro, sak[j, 448:512, :], sav[j, 448:512, :], roll_k=sak[j], roll_v=sav[j])
              phase_attn_a(j, Xcur, Xmid, 2 * li)
          elif kind == 1:
              phase_mla(Xcur, Xmid, 2 * li)
          else:
              kro = [(tt, (128, ock[tt * 128:(tt + 1) * 128, :])) for tt in range(16)]
              vro = [(tt, (128, ocv[tt * 128:(tt + 1) * 128, :])) for tt in range(16)]
              phase_qkv(c_wqkv, c_wqkv[:, :], cc_k, cc_v, 1024, kro, vro, sck[:, :], scv[:, :])
              phase_attn_c(Xcur, Xmid, 2 * li)
          phase_peer(li, Xmid, Xnext, 2 * li + 1, final_out=(y_out if li == n_layers - 1 else None))
          Xcur = Xnext
    except _Stop as e:
        print('stopped before', e)
        if k.es is not None:
            k.P.barrier(); k.es.close(); k.es = None
    info = P.emit()
    return nc, info


def _prep_inputs(inputs, b, n_layers=4):
    f = np.float32
    xp = inputs["x_prompt"][b]
    xs = inputs["x_sample"][b]
    x0 = np.zeros((T, D), f)
    x0[0:2048] = xp
    x0[2048:2112] = xs
    half = 32
    inv = (10000.0 ** (-np.arange(half, dtype=np.float32) / half)).astype(np.float32)
    pos = np.zeros((T,), np.float32)
    pos[0:2048] = np.arange(2048)
    pos[2048:2112] = 1024 + np.arange(64)
    ang = (pos[:, None] * inv[None, :]).astype(np.float32)
    m = {
        "x0": x0,
        "ca_k": np.ascontiguousarray(inputs["cache_a_k"][:, b]).reshape(2, 512, 2048),
        "ca_v": np.ascontiguousarray(inputs["cache_a_v"][:, b]).reshape(2, 512, 2048),
        "cb_c": np.ascontiguousarray(inputs["cache_b_ckv"][0, b]),
        "cb_r": np.ascontiguousarray(inputs["cache_b_krope"][0, b]),
        "cc_k": np.ascontiguousarray(inputs["cache_c_k"][0, b]).reshape(1024, 2048),
        "cc_v": np.ascontiguousarray(inputs["cache_c_v"][0, b]).reshape(1024, 2048),
        "a_wqkv": inputs["a_wqkv"], "a_wo": inputs["a_wo"], "a_rel": inputs["a_relbias"],
        "b_win": inputs["b_win"][0], "b_qn": inputs["b_qnorm"], "b_kvn": inputs["b_kvnorm"],
        "b_wqb": inputs["b_wqb"][0], "b_wkvb": inputs["b_wkvb"][0], "b_wo": inputs["b_wo"][0],
        "c_wqkv": inputs["c_wqkv"][0], "c_wo": inputs["c_wo"][0],
        "p_wq": inputs["peer_wq"], "p_keys": inputs["peer_keys"].reshape(4, 16, 128, 128),
        "p_u": inputs["peer_u"][:n_layers], "p_v": inputs["peer_v"][:n_layers],
        "ln_g": inputs["ln_g"].reshape(8, 2048), "ln_b": inputs["ln_b"].reshape(8, 2048),
        "ropec": np.cos(ang).astype(f), "ropes": np.sin(ang).astype(f),
    }
    return {kk: np.ascontiguousarray(np.asarray(v, dtype=f)) for kk, v in m.items()}


_CACHE = {}


def kernel(**inputs):
    inputs = {kk: np.asarray(v) for kk, v in inputs.items()}
    if "nc" not in _CACHE:
        _CACHE["nc"] = build()[0]
    nc = _CACHE["nc"]
    in_maps = [_prep_inputs(inputs, b) for b in range(8)]
    res = run_bass_kernel_spmd(nc, in_maps, core_ids=list(range(8)))
    R = res.results
    f = np.float32

    def st(fn):
        return np.stack([fn(R[b]) for b in range(8)])

    y_prompt = st(lambda r: r["y"][0:2048])
    y_sample = st(lambda r: r["y"][2048:2112])
    oak = np.stack([R[b]["oak"].reshape(2, 512, 16, 128) for b in range(8)], axis=1)
    oav = np.stack([R[b]["oav"].reshape(2, 512, 16, 128) for b in range(8)], axis=1)
    obc = st(lambda r: r["obc"])[None]
    obr = st(lambda r: r["obr"])[None]
    ock = st(lambda r: r["ock"].reshape(2048, 16, 128))[None]
    ocv = st(lambda r: r["ocv"].reshape(2048, 16, 128))[None]
    sak = np.stack([R[b]["sak"].reshape(2, 512, 16, 128) for b in range(8)], axis=1)
    sav = np.stack([R[b]["sav"].reshape(2, 512, 16, 128) for b in range(8)], axis=1)
    sbc = st(lambda r: r["sbc"])[None]
    sbr = st(lambda r: r["sbr"])[None]
    sck = st(lambda r: r["sck"].reshape(64, 16, 128))[None]
    scv = st(lambda r: r["scv"].reshape(64, 16, 128))[None]
    outs = (y_prompt, y_sample, oak, oav, obc, obr, ock, ocv, sak, sav, sbc, sbr, sck, scv)
    return tuple(np.ascontiguousarray(o.astype(f)) for o in outs)
```

```python
import os
import numpy as np
from contextlib import ExitStack
import concourse.bass as bass
import concourse.mybir as mybir
from concourse.bass_utils import run_bass_kernel_spmd

F32 = mybir.dt.float32
BF16 = mybir.dt.bfloat16
AF = mybir.ActivationFunctionType
ALU = mybir.AluOpType
AX = mybir.AxisListType

NSLOT = 20
COMPUTE = ("pe", "act", "dve", "pool")
DMAQ = ("sp", "dq_pool")
ALLQ = COMPUTE + DMAQ


class Buf:
    __slots__ = ("name", "last_w", "readers", "war")

    def __init__(self, name):
        self.name = name
        self.last_w = []
        self.readers = []
        self.war = []


class Op:
    __slots__ = ("eng", "fn", "deps", "signal", "sigval", "eidx", "isdma", "slot", "idx")


class Prog:
    def __init__(self, nc):
        self.nc = nc
        self.ops = []
        self.ecount = {}
        self.last = {}
        self.recent_dma = {q: [] for q in DMAQ}

    def eng_obj(self, eng):
        nc = self.nc
        return {"pe": nc.tensor, "act": nc.scalar, "dve": nc.vector, "pool": nc.gpsimd,
                "sp": nc.sync, "dq_pool": nc.gpsimd}[eng]

    @staticmethod
    def phys(eng):
        return "pool" if eng == "dq_pool" else eng

    def op(self, eng, fn, reads=(), writes=(), nowaw=False):
        o = Op()
        o.eng = eng
        o.fn = fn
        o.isdma = eng in DMAQ
        o.signal = False
        o.sigval = 0
        o.slot = 0
        o.idx = len(self.ops)
        pe = self.phys(eng)
        o.eidx = self.ecount.get(pe, 0)
        self.ecount[pe] = o.eidx + 1
        deps = set()
        for b in reads:
            deps.update(b.last_w)
        for b in writes:
            if not nowaw:
                deps.update(b.last_w)
            else:
                deps.update(b.war)
            deps.update(b.readers)
        deps.discard(o)
        o.deps = deps
        for b in reads:
            b.readers.append(o)
        for b in writes:
            if nowaw:
                b.last_w.append(o)
            else:
                b.war = list(b.readers) + [x for x in b.last_w if x.fn is not None][-4:]
                b.last_w = [o]
                b.readers = []
        self.ops.append(o)
        if fn is not None:
            if o.isdma:
                r = self.recent_dma[eng]
                r.append(o)
                if len(r) > NSLOT:
                    r.pop(0)
            else:
                self.last[eng] = o
        return o

    def barrier(self):
        lasts = [o for o in self.last.values()]
        for q in DMAQ:
            lasts += self.recent_dma[q]
        for eng in ALLQ:
            o = self.op(eng, None)
            o.deps = set(lasts)

    def emit(self):
        nc = self.nc
        need = []
        for o in self.ops:
            ws = []
            for d in o.deps:
                if d.fn is None:
                    continue
                if (not d.isdma) and (not o.isdma) and d.eng == o.eng:
                    if o.eng == "pe" or o.fn is None:
                        continue
                    if o.eidx - d.eidx > 3:
                        continue
                ws.append(d)
                d.signal = True
            need.append(ws)
        sems = {e: nc.alloc_semaphore("s_" + e) for e in COMPUTE}
        dsems = {q: [nc.alloc_semaphore("d_%s_%d" % (q, i)) for i in range(NSLOT)] for q in DMAQ}
        sigcount = {e: 0 for e in COMPUTE}
        dcount = {q: 0 for q in DMAQ}
        waited = {}
        nw = [0]

        def do_wait(eng, semkey, sem, val):
            k = (self.phys(eng), semkey)
            if waited.get(k, 0) >= val:
                return
            waited[k] = val
            nw[0] += 1
            self.eng_obj(eng).wait_ge(sem, val)

        for o, ws in zip(self.ops, need):
            e = self.eng_obj(o.eng)
            for d in sorted(ws, key=lambda d: d.idx):
                if d.isdma:
                    do_wait(o.eng, ("d", d.eng, d.slot), dsems[d.eng][d.slot], d.sigval)
                else:
                    do_wait(o.eng, ("c", d.eng), sems[d.eng], d.sigval)
            if o.fn is None:
                continue
            if o.isdma:
                i = dcount[o.eng]
                dcount[o.eng] = i + 1
                o.slot = i % NSLOT
                o.sigval = 16 * (i // NSLOT + 1)
                if i >= NSLOT:
                    do_wait(o.eng, ("d", o.eng, o.slot), dsems[o.eng][o.slot], 16 * (i // NSLOT))
                ins = o.fn(e)
                ins.then_inc(dsems[o.eng][o.slot], 16)
            else:
                ins = o.fn(e)
                if o.signal:
                    sigcount[o.eng] += 1
                    o.sigval = sigcount[o.eng]
                    ins.then_inc(sems[o.eng], 1)
        for q in DMAQ:
            n = dcount[q]
            for s in range(min(n, NSLOT)):
                last_i = ((n - 1 - s) // NSLOT) * NSLOT + s
                nc.sync.wait_ge(dsems[q][s], 16 * (last_i // NSLOT + 1))
        for en in COMPUTE:
            if sigcount[en] > 0:
                nc.sync.wait_ge(sems[en], sigcount[en])
        return dict(n_ops=len(self.ops), sig=sigcount, dma=dcount, waits=nw[0])


def I(name, *a, **k):
    return lambda e: getattr(e, name)(*a, **k)


class Tile:
    def __init__(self, t, name):
        self.t = t
        self.b = Buf(name)

    def __getitem__(self, k):
        return self.t[k]


class Rot:
    def __init__(self, tiles):
        self.tiles = tiles
        self.i = 0

    def next(self):
        t = self.tiles[self.i % len(self.tiles)]
        self.i += 1
        return t


NT = 17
T = NT * 128
D = 2048
KC = 16
TOKB = [(0, 512), (512, 512), (1024, 512), (1536, 512), (2048, 128)]
ALPHA = (2.0 * 4) ** 0.25
NEG = -30000.0
TMP_POOL_HEADS = tuple(int(x) for x in os.environ.get('TMP_POOL_HEADS', '0,1,2,3,4,5,6,7').split(',') if x != '')
KEXT = T + 1024


class K:
    def __init__(self, nc):
        self.nc = nc
        self.P = Prog(nc)
        self.es = None
        self.uid = 0

    def sb(self, shape, dt, name=None):
        self.uid += 1
        name = "%s_%d" % (name or "t", self.uid)
        t = self.es.enter_context(self.nc.sbuf_tensor(name, list(shape), dt))
        return Tile(t, name)

    def ps(self, shape, dt, name=None):
        self.uid += 1
        name = "%s_%d" % (name or "p", self.uid)
        t = self.es.enter_context(self.nc.psum_tensor(name, list(shape), dt))
        return Tile(t, name)

    def dram(self, name, shape, dt, kind="Internal"):
        t = self.nc.dram_tensor(name, list(shape), dt, kind=kind)
        tl = Tile(t.ap(), name)
        return tl

    def phase(self):
        k = self

        class _Ph:
            def __enter__(s):
                k.es = ExitStack()
                k.es.__enter__()
                return s

            def __exit__(s, *a):
                k.P.barrier()
                k.es.__exit__(*a)
                k.es = None
                return False
        return _Ph()

    def op(self, eng, fn, reads=(), writes=(), nowaw=False):
        if eng == "dq_pool":
            eng = "sp"
        return self.P.op(eng, fn, [r.b if isinstance(r, Tile) else r for r in reads],
                         [w.b if isinstance(w, Tile) else w for w in writes], nowaw)


def sap(tile_or_t, offset, dims):
    t = tile_or_t.t if isinstance(tile_or_t, Tile) else tile_or_t
    full = t[:]
    pstride = full.ap[0][0]
    return bass.AP(t, offset, [[pstride, 128]] + [list(d) for d in dims])


def dap(ap, offset, dims):
    return bass.AP(ap.tensor, offset, [list(d) for d in dims])


class _Stop(Exception):
    pass


def build(n_layers=4, dbg=False, stop_after=None, kinds=None):
    nc = bass.Bass("TRN2", target_bir_lowering=False)
    k = K(nc)
    P = k.P
    stage_ctr = [0]

    def stage(name):
        stage_ctr[0] += 1
        if stop_after is not None and stage_ctr[0] > stop_after:
            raise _Stop(name)

    def din(name, shape):
        return Tile(nc.dram_tensor(name, list(shape), F32, kind="ExternalInput").ap(), name)

    def dout(name, shape):
        return Tile(nc.dram_tensor(name, list(shape), F32, kind="ExternalOutput").ap(), name)

    x0 = din("x0", [T, D])
    ca_k = din("ca_k", [2, 512, 2048]); ca_v = din("ca_v", [2, 512, 2048])
    cb_c = din("cb_c", [1024, 512]); cb_r = din("cb_r", [1024, 64])
    cc_k = din("cc_k", [1024, 2048]); cc_v = din("cc_v", [1024, 2048])
    a_wqkv = din("a_wqkv", [2, 2048, 6144]); a_wo = din("a_wo", [2, 2048, 2048]); a_rel = din("a_rel", [2, 16, 257])
    b_win = din("b_win", [2048, 1088]); b_qn = din("b_qn", [1, 512]); b_kvn = din("b_kvn", [1, 512])
    b_wqb = din("b_wqb", [512, 3072]); b_wkvb = din("b_wkvb", [512, 4096]); b_wo = din("b_wo", [2048, 2048])
    c_wqkv = din("c_wqkv", [2048, 6144]); c_wo = din("c_wo", [2048, 2048])
    p_wq = din("p_wq", [4, 2048, 2048]); p_keys = din("p_keys", [4, 16, 128, 128])
    p_u = din("p_u", [n_layers, 16384, 2048]); p_v = din("p_v", [n_layers, 16384, 2048])
    ln_g = din("ln_g", [8, 2048]); ln_b = din("ln_b", [8, 2048])
    ropec = din("ropec", [T, 32]); ropes = din("ropes", [T, 32])

    y_out = dout("y", [T, D])
    oak = dout("oak", [2, 512, 2048]); oav = dout("oav", [2, 512, 2048])
    obc = dout("obc", [2048, 512]); obr = dout("obr", [2048, 64])
    ock = dout("ock", [2048, 2048]); ocv = dout("ocv", [2048, 2048])
    sak = dout("sak", [2, 512, 2048]); sav = dout("sav", [2, 512, 2048])
    sbc = dout("sbc", [64, 512]); sbr = dout("sbr", [64, 64])
    sck = dout("sck", [64, 2048]); scv = dout("scv", [64, 2048])

    xkind = "ExternalOutput" if dbg else "Internal"
    XA = k.dram("XA", [T, D], F32, kind=xkind); XB = k.dram("XB", [T, D], F32, kind=xkind)
    XT = Tile(nc.dram_tensor("XT", [NT, 128, KC, 128], BF16, kind="Internal").ap(), "XT")
    QT = Tile(nc.dram_tensor("QT", [16, 128, T], BF16, kind="Internal").ap(), "QT")
    KTX = Tile(nc.dram_tensor("KTX", [16, 128, KEXT], BF16, kind="Internal").ap(), "KTX")
    QR = Tile(nc.dram_tensor("QR", [8, 128, T], BF16, kind="Internal").ap(), "QR")
    KR = Tile(nc.dram_tensor("KR", [128, KEXT], BF16, kind="Internal").ap(), "KR")
    VX = Tile(nc.dram_tensor("VX", [KEXT, 2048], BF16, kind="Internal").ap(), "VX")
    SS = Tile(nc.dram_tensor("SS", [NT, 128, 2048], F32, kind="Internal").ap(), "SS")
    AUX = Tile(nc.dram_tensor("AUX", [NT, 128, 2048 + 16], F32, kind="Internal").ap(), "AUX")
    YAC = Tile(nc.dram_tensor("YAC", [NT, 128, 2048], F32, kind="Internal").ap(), "YAC")
    EXT = Tile(nc.dram_tensor("EXT", [16, 384], F32, kind="Internal").ap(), "EXT")
    CQT = Tile(nc.dram_tensor("CQT", [4, 128, T], BF16, kind="Internal").ap(), "CQT")
    CKVT = Tile(nc.dram_tensor("CKVT", [4, 128, KEXT], BF16, kind="Internal").ap(), "CKVT")
    ATd = Tile(nc.dram_tensor("ATd", [16, 128, T], BF16, kind="Internal").ap(), "ATd")
    XTb = [Buf("XT%d" % i) for i in range(NT)]
    AUXb = [Buf("AUX%d" % i) for i in range(NT)]
    YACb = [Buf("YAC%d" % i) for i in range(NT)]
    SSb = [Buf("SS%d" % i) for i in range(NT)]

    def palloc(shape, dt, name):
        return Tile(nc.alloc_sbuf_tensor(name, list(shape), dt), name)

    identf = palloc([128, 128], F32, "identf")
    ident = palloc([128, 128], BF16, "ident")
    ones_b = palloc([128, 128], BF16, "ones_b")
    ones_f = palloc([128, 128], F32, "ones_f")
    zeros_b = palloc([128, 128], BF16, "zeros_b")
    Uf = palloc([128, 128], F32, "Uf")
    mcaus = palloc([128, 128], F32, "mcaus")
    m0b = palloc([128, 128], BF16, "m0b")
    m4b = palloc([128, 128], BF16, "m4b")
    k.op("pool", I("memset", identf[:], 0.0), writes=[identf])
    k.op("pool", I("affine_select", out=identf[:], in_=identf[:], pattern=[[-1, 128]], compare_op=ALU.not_equal,
                   fill=1.0, base=0, channel_multiplier=1), reads=[identf], writes=[identf])
    k.op("pool", I("tensor_copy", out=ident[:], in_=identf[:]), reads=[identf], writes=[ident])
    k.op("pool", I("memset", ones_b[:], 1.0), writes=[ones_b])
    k.op("pool", I("memset", ones_f[:], 1.0), writes=[ones_f])
    k.op("pool", I("memset", zeros_b[:], 0.0), writes=[zeros_b])
    k.op("pool", I("affine_select", out=Uf[:], in_=ones_f[:], pattern=[[-1, 128]], compare_op=ALU.is_gt,
                   fill=0.0, base=0, channel_multiplier=1), reads=[ones_f], writes=[Uf])
    k.op("pool", I("affine_select", out=mcaus[:], in_=ones_f[:], pattern=[[1, 128]], compare_op=ALU.is_gt,
                   fill=0.0, base=0, channel_multiplier=-1), reads=[ones_f], writes=[mcaus])
    k.op("pool", I("memset", m0b[:], 0.0), writes=[m0b])
    k.op("pool", I("memset", m0b[0:64, 0:64], NEG), writes=[m0b])
    k.op("pool", I("memset", m4b[:], 0.0), writes=[m4b])
    k.op("pool", I("memset", m4b[64:128, 64:128], NEG), writes=[m4b])
    m0n = palloc([128, 128], BF16, "m0n")
    k.op("pool", I("memset", m0n[:], 0.0), writes=[m0n])
    k.op("pool", I("memset", m0n[64:128, 0:64], NEG), writes=[m0n])
    mcausb = palloc([128, 128], BF16, "mcausb")
    k.op("pool", I("tensor_copy", out=mcausb[:], in_=mcaus[:]), reads=[mcaus], writes=[mcausb])
    Jf = palloc([128, 128], F32, "Jf")
    Jb = palloc([128, 128], BF16, "Jb")
    k.op("pool", I("memset", Jf[:], 0.0), writes=[Jf])
    k.op("pool", I("affine_select", out=Jf[:], in_=Jf[:], pattern=[[1, 128]], compare_op=ALU.not_equal,
                   fill=1.0, base=-127, channel_multiplier=1), reads=[Jf], writes=[Jf])
    k.op("pool", I("tensor_copy", out=Jb[:], in_=Jf[:]), reads=[Jf], writes=[Jb])

    alt = [0]

    def evac_eng():
        alt[0] += 1
        return "act" if alt[0] % 2 else "dve"

    def copy_op(eng, out, in_, reads, writes, scale=None):
        if eng == "act":
            if scale is None:
                k.op("act", I("copy", out=out, in_=in_), reads, writes)
            else:
                k.op("act", I("activation", out=out, in_=in_, func=AF.Copy, scale=float(scale)), reads, writes)
        else:
            if scale is None:
                k.op(eng, I("tensor_copy", out=out, in_=in_), reads, writes)
            else:
                k.op(eng, I("tensor_scalar", out=out, in0=in_, scalar1=float(scale), scalar2=None, op0=ALU.mult), reads, writes)

    def transpose_tile_to_XT(src_bf, tt, pst_rot, xts_rot):
        xts = xts_rot.next()
        for half in range(2):
            pst = pst_rot.next()
            for j in range(8):
                kc = half * 8 + j
                k.op("pe", I("transpose", out=pst[:, j * 128:(j + 1) * 128], in_=src_bf[:, kc * 128:(kc + 1) * 128],
                             identity=ident[:]), reads=[src_bf, ident], writes=[pst])
            copy_op(evac_eng(), xts[:, half * 8:(half + 1) * 8, :], pst[:].rearrange("p (a b) -> p a b", a=8),
                    [pst], [xts])
        k.op("dq_pool", I("dma_start", out=XT[tt], in_=xts[:]), reads=[xts], writes=[XTb[tt]])

    def load_xT(xT):
        for tt in range(NT):
            k.op("sp", I("dma_start", out=xT[:, :, tt * 128:(tt + 1) * 128], in_=XT[tt]), reads=[XTb[tt]], writes=[xT],
                 nowaw=(tt > 0))

    class WLoader:
        def __init__(self, kc, ncols, nbuf=2, cast_eng="pool"):
            self.kc = kc
            self.ncols = ncols
            self.stg = Rot([k.sb([128, kc, ncols], F32, "wstg") for _ in range(nbuf)])
            self.wb = Rot([k.sb([128, kc, ncols], BF16, "wbf") for _ in range(nbuf)])
            self.cast_eng = cast_eng

        def load(self, Wt, W2d, col_list):
            st = self.stg.next()
            wb = self.wb.next()
            pos = 0
            first = True
            for (c0, n) in col_list:
                src = W2d[:, c0:c0 + n].rearrange("(kc p) n -> p kc n", p=128)
                k.op("sp", I("dma_start", out=st[:, :, pos:pos + n], in_=src), reads=[Wt], writes=[st], nowaw=not first)
                first = False
                pos += n
            if self.cast_eng == "act":
                k.op("act", I("copy", out=wb[:, :, 0:pos], in_=st[:, :, 0:pos]), reads=[st], writes=[wb])
            else:
                k.op(self.cast_eng, I("tensor_copy", out=wb[:, :, 0:pos], in_=st[:, :, 0:pos]), reads=[st], writes=[wb])
            return wb

    def layernorm_tile(z, idx, gt, bt, st6, mv, xn):
        for c in range(4):
            k.op("dve", I("bn_stats", out=st6[:, c, :], in_=z[:, c * 512:(c + 1) * 512]), reads=[z], writes=[st6])
        k.op("dve", I("bn_aggr", out=mv[:, 0:2], in_=st6[:].rearrange("p a b -> p (a b)")), reads=[st6], writes=[mv])
        k.op("dve", I("tensor_scalar", out=mv[:, 3:4], in0=mv[:, 1:2], scalar1=1e-5, scalar2=None, op0=ALU.add), reads=[mv], writes=[mv])
        k.op("act", I("activation", out=mv[:, 3:4], in_=mv[:, 3:4], func=AF.Sqrt), reads=[mv], writes=[mv])
        k.op("dve", I("reciprocal", out=mv[:, 2:3], in_=mv[:, 3:4]), reads=[mv], writes=[mv])
        k.op("dve", I("tensor_scalar", out=xn[:], in0=z[:], scalar1=mv[:, 0:1], scalar2=mv[:, 2:3], op0=ALU.subtract,
                      op1=ALU.mult), reads=[z, mv], writes=[xn])
        k.op("pool", I("tensor_tensor", out=xn[:], in0=xn[:], in1=gt[:], op=ALU.mult), reads=[xn, gt], writes=[xn])
        k.op("pool", I("tensor_tensor", out=xn[:], in0=xn[:], in1=bt[:], op=ALU.add), reads=[xn, bt], writes=[xn])

    class LNState:
        def __init__(self, lnrow):
            self.gt = k.sb([128, 2048], F32, "lng")
            self.bt = k.sb([128, 2048], F32, "lnb")
            k.op("sp", I("dma_start", out=self.gt[:], in_=dap(ln_g.t, lnrow * 2048, [[0, 128], [1, 2048]])), reads=[ln_g], writes=[self.gt])
            k.op("sp", I("dma_start", out=self.bt[:], in_=dap(ln_b.t, lnrow * 2048, [[0, 128], [1, 2048]])), reads=[ln_b], writes=[self.bt])
            self.st6 = k.sb([128, 4, 6], F32, "st6")
            self.mv = k.sb([128, 4], F32, "mv")
            self.xin = Rot([k.sb([128, 2048], F32, "lnx") for _ in range(2)])
            self.z = Rot([k.sb([128, 2048], F32, "lnz") for _ in range(2)])
            self.xbf = Rot([k.sb([128, 2048], BF16, "lnxb") for _ in range(2)])
            self.xts = Rot([k.sb([128, 16, 128], BF16, "xts") for _ in range(2)])

        def prefetch(self, Xsrc, tt):
            xin = self.xin.next()
            k.op("sp", I("dma_start", out=xin[:], in_=Xsrc[tt * 128:(tt + 1) * 128, :]), reads=[Xsrc], writes=[xin])
            return xin

        def run(self, xin, sub_aps, sub_tiles, Xdst, tt, pst_rot, final_out=None):
            z = self.z.next()
            for c in range(4):
                k.op("dve", I("scalar_tensor_tensor", out=z[:, c * 512:(c + 1) * 512], in0=xin[:, c * 512:(c + 1) * 512],
                              scalar=float(ALPHA), in1=sub_aps[c], op0=ALU.mult, op1=ALU.add),
                     reads=[xin] + sub_tiles, writes=[z])
            layernorm_tile(z, 0, self.gt, self.bt, self.st6, self.mv, z)
            k.op("dq_pool", I("dma_start", out=Xdst[tt * 128:(tt + 1) * 128, :], in_=z[:]), reads=[z], writes=[Xdst], nowaw=True)
            if final_out is not None:
                k.op("dq_pool", I("dma_start", out=final_out[tt * 128:(tt + 1) * 128, :], in_=z[:]), reads=[z], writes=[])
            else:
                xbf = self.xbf.next()
                k.op("act", I("copy", out=xbf[:], in_=z[:]), reads=[z], writes=[xbf])
                transpose_tile_to_XT(xbf, tt, pst_rot, self.xts)

    def phase_init():
        stage('phase_init')
        with k.phase():
            xin = Rot([k.sb([128, 2048], F32, "ix") for _ in range(2)])
            xbf = Rot([k.sb([128, 2048], BF16, "ixb") for _ in range(2)])
            xts = Rot([k.sb([128, 16, 128], BF16, "xts") for _ in range(2)])
            pst = Rot([k.ps([128, 1024], BF16, "pst") for _ in range(2)])
            for tt in range(NT):
                xi = xin.next()
                k.op("sp", I("dma_start", out=xi[:], in_=x0[tt * 128:(tt + 1) * 128, :]), reads=[x0], writes=[xi])
                xb = xbf.next()
                k.op("pool", I("tensor_copy", out=xb[:], in_=xi[:]), reads=[xi], writes=[xb])
                transpose_tile_to_XT(xb, tt, pst, xts)

    def phase_qkv(Wt, W2d, cacheK, cacheV, ncache, k_out_rows, v_out_rows, ks_out, vs_out, roll_k=None, roll_v=None):
        stage('phase_qkv')
        scale = 128 ** -0.5
        with k.phase():
            xT = k.sb([128, KC, T], BF16, "xT")
            load_xT(xT)
            wl = WLoader(KC, 256, nbuf=2, cast_eng="act")
            pp = Rot([k.ps([128, 512], F32, "pp") for _ in range(4)])
            qts = Rot([k.sb([128, 512], BF16, "qts") for _ in range(3)])
            for which, dst in ((0, QT), (1, KTX)):
                for hp in range(8):
                    wb = wl.load(Wt, W2d, [(which * 2048 + hp * 256, 256)])
                    for hh in range(2):
                        h = hp * 2 + hh
                        for (t0, tn) in TOKB:
                            ps = pp.next()
                            for kc in range(KC):
                                k.op("pe", I("matmul", ps[:, 0:tn], lhsT=wb[:, kc, hh * 128:(hh + 1) * 128], rhs=xT[:, kc, t0:t0 + tn],
                                             start=(kc == 0), stop=(kc == KC - 1)), reads=[wb, xT], writes=[ps])
                            qs = qts.next()
                            copy_op(evac_eng(), qs[:, 0:tn], ps[:, 0:tn], [ps], [qs], scale=(scale if which == 0 else None))
                            k.op("dq_pool", I("dma_start", out=dst[h, :, t0:t0 + tn], in_=qs[:, 0:tn]), reads=[qs], writes=[dst])
            stage('qkv_tokmajor')
            vts = Rot([k.sb([128, 512], BF16, "vts") for _ in range(3)])
            vfs = Rot([k.sb([128, 512], F32, "vfs") for _ in range(3)])
            kdict = dict(k_out_rows)
            vdict = dict(v_out_rows)
            for which in (2, 1):
                for ns in range(4):
                    wb0 = wl.load(Wt, W2d, [(which * 2048 + ns * 512, 256)])
                    wb1 = wl.load(Wt, W2d, [(which * 2048 + ns * 512 + 256, 256)])
                    for tt in range(NT):
                        outd = (vdict if which == 2 else kdict)
                        need_f32 = (tt in outd) or tt == 16
                        if which == 1 and not need_f32:
                            continue
                        ps = pp.next()
                        for hf, wb in ((0, wb0), (1, wb1)):
                            for kc in range(KC):
                                k.op("pe", I("matmul", ps[:, hf * 256:(hf + 1) * 256], lhsT=xT[:, kc, tt * 128:(tt + 1) * 128], rhs=wb[:, kc, :],
                                             start=(kc == 0), stop=(kc == KC - 1)), reads=[wb, xT], writes=[ps])
                        src_t = ps
                        if need_f32:
                            vf = vfs.next()
                            copy_op("dve", vf[:], ps[:], [ps], [vf])
                            src_t = vf
                        if which == 2:
                            vt = vts.next()
                            copy_op("act", vt[:], src_t[:], [src_t], [vt])
                            k.op("dq_pool", I("dma_start", out=VX[tt * 128:(tt + 1) * 128, ns * 512:(ns + 1) * 512], in_=vt[:]), reads=[vt], writes=[VX], nowaw=True)
                        if need_f32:
                            if tt in outd:
                                dd = outd[tt]
                                k.op("dq_pool", I("dma_start", out=dd[1][:, ns * 512:(ns + 1) * 512], in_=vf[0:dd[0], :]), reads=[vf], writes=[])
                            if tt == 16:
                                so = vs_out if which == 2 else ks_out
                                k.op("dq_pool", I("dma_start", out=so[:, ns * 512:(ns + 1) * 512], in_=vf[0:64, :]), reads=[vf], writes=[])
        stage('qkv_caches')
        with k.phase():
            cin = Rot([k.sb([128, 2048], F32, "cin") for _ in range(2)])
            cbf = Rot([k.sb([128, 2048], BF16, "cbf") for _ in range(2)])
            kts = Rot([k.sb([128, 16, 128], BF16, "kts") for _ in range(2)])
            pst = Rot([k.ps([128, 1024], BF16, "pst") for _ in range(2)])
            for c in range(ncache // 128):
                ci = cin.next()
                k.op("sp", I("dma_start", out=ci[:], in_=cacheK[c * 128:(c + 1) * 128, :]), reads=[], writes=[ci])
                if roll_k is not None:
                    k.op("sp", I("dma_start", out=roll_k[c * 128:c * 128 + 64, :], in_=ci[64:128, :]), reads=[ci], writes=[])
                    if c > 0:
                        k.op("sp", I("dma_start", out=roll_k[c * 128 - 64:c * 128, :], in_=ci[0:64, :]), reads=[ci], writes=[])
                cb = cbf.next()
                k.op("pool", I("tensor_copy", out=cb[:], in_=ci[:]), reads=[ci], writes=[cb])
                kt = kts.next()
                for half in range(2):
                    ps = pst.next()
                    for j in range(8):
                        h = half * 8 + j
                        k.op("pe", I("transpose", out=ps[:, j * 128:(j + 1) * 128], in_=cb[:, h * 128:(h + 1) * 128], identity=ident[:]),
                             reads=[cb, ident], writes=[ps])
                    copy_op(evac_eng(), kt[:, half * 8:(half + 1) * 8, :], ps[:].rearrange("p (a b) -> p a b", a=8), [ps], [kt])
                k.op("dq_pool", I("dma_start", out=KTX[:, :, T + c * 128:T + (c + 1) * 128].rearrange("h p n -> p h n"), in_=kt[:]),
                     reads=[kt], writes=[KTX])
                ci = cin.next()
                k.op("sp", I("dma_start", out=ci[:], in_=cacheV[c * 128:(c + 1) * 128, :]), reads=[], writes=[ci])
                if roll_v is not None:
                    k.op("sp", I("dma_start", out=roll_v[c * 128:c * 128 + 64, :], in_=ci[64:128, :]), reads=[ci], writes=[])
                    if c > 0:
                        k.op("sp", I("dma_start", out=roll_v[c * 128 - 64:c * 128, :], in_=ci[0:64, :]), reads=[ci], writes=[])
                cb = cbf.next()
                k.op("pool", I("tensor_copy", out=cb[:], in_=ci[:]), reads=[ci], writes=[cb])
                k.op("dq_pool", I("dma_start", out=VX[T + c * 128:T + (c + 1) * 128, :], in_=cb[:]), reads=[cb], writes=[VX])

    def phase_wo_ln(Wt, W2d, Xsrc, Xdst, lnrow, final_out=None):
        stage('phase_wo_ln')
        with k.phase():
            wo = k.sb([128, KC, 2048], BF16, "wo")
            stg = Rot([k.sb([128, KC, 128], F32, "wostg") for _ in range(2)])
            for c in range(16):
                st = stg.next()
                k.op("sp", I("dma_start", out=st[:], in_=W2d[:, c * 128:(c + 1) * 128].rearrange("(kc p) n -> p kc n", p=128)), reads=[Wt], writes=[st])
                k.op("act", I("copy", out=wo[:, :, c * 128:(c + 1) * 128], in_=st[:]), reads=[st], writes=[wo], nowaw=(c > 0))
            ln = LNState(lnrow)
            att = Rot([k.sb([128, KC, 128], BF16, "att") for _ in range(2)])
            pp = Rot([k.ps([128, 2048], F32, "pwo") for _ in range(1)])
            pst = Rot([k.ps([128, 1024], BF16, "pst") for _ in range(2)])
            def _ld(tt_):
                xi = ln.prefetch(Xsrc, tt_)
                a_ = att.next()
                k.op("sp", I("dma_start", out=a_[:], in_=ATd[:, :, tt_ * 128:(tt_ + 1) * 128].rearrange("h p n -> p h n")), reads=[ATd], writes=[a_])
                return xi, a_
            nxt = _ld(0)
            for tt in range(NT):
                xin, at = nxt
                if tt + 1 < NT:
                    nxt = _ld(tt + 1)
                ps = pp.next()
                for ns in range(4):
                    for kc in range(KC):
                        k.op("pe", I("matmul", ps[:, ns * 512:(ns + 1) * 512], lhsT=at[:, kc, :], rhs=wo[:, kc, ns * 512:(ns + 1) * 512],
                                     start=(kc == 0), stop=(kc == KC - 1)), reads=[at, wo], writes=[ps])
                ln.run(xin, [ps[:, c * 512:(c + 1) * 512] for c in range(4)], [ps], Xdst, tt, pst, final_out)

    def phase_ln_from_dram(Ysrc_tiles, Ybufs, Xsrc, Xdst, lnrow, final_out=None):
        stage('phase_ln_from_dram')
        with k.phase():
            ln = LNState(lnrow)
            yr = Rot([k.sb([128, 2048], F32, "lny") for _ in range(2)])
            pst = Rot([k.ps([128, 1024], BF16, "pst") for _ in range(2)])
            def _ld(tt_):
                xi = ln.prefetch(Xsrc, tt_)
                y_ = yr.next()
                k.op("sp", I("dma_start", out=y_[:], in_=Ysrc_tiles[tt_]), reads=[Ybufs[tt_]], writes=[y_])
                return xi, y_
            nxt = _ld(0)
            for tt in range(NT):
                xin, y = nxt
                if tt + 1 < NT:
                    nxt = _ld(tt + 1)
                ln.run(xin, [y[:, c * 512:(c + 1) * 512] for c in range(4)], [y], Xdst, tt, pst, final_out)

    def phase_attn_a(j, Xsrc, Xdst, lnrow):
        stage('phase_attn_a')
        with k.phase():
            ext_s = k.sb([16, 384], F32, "ext_s")
            k.op("sp", I("dma_start", out=ext_s[:, 0:257], in_=a_rel[j]), reads=[a_rel], writes=[ext_s])
            k.op("dve", I("tensor_copy", out=ext_s[:, 257:384], in_=ext_s[:, 256:257].to_broadcast([16, 127])), reads=[ext_s], writes=[ext_s])
            k.op("sp", I("dma_start", out=EXT[:, :], in_=ext_s[:]), reads=[ext_s], writes=[EXT])
            cst = k.sb([128, 16], F32, "cst")
            k.op("sp", I("dma_start", out=cst[:], in_=dap(a_rel.t, j * 16 * 257 + 256, [[0, 128], [257, 16]]), allow_slow_non_contiguous=True), reads=[a_rel], writes=[cst])
            aor = Rot([k.sb([128, 512], BF16, "ao") for _ in range(3)])
            qh = Rot([k.sb([128, T], BF16, "qh") for _ in range(2)])
            kh = Rot([k.sb([128, KEXT], BF16, "kh") for _ in range(2)])
            vh = Rot([k.sb([128, 25, 128], BF16, "vh") for _ in range(2)])
            bf32 = Rot([k.sb([128, 2, 128], F32, "bf32") for _ in range(2)])
            bt = Rot([k.sb([128, 4, 128], BF16, "bt") for _ in range(2)])
            ptr = Rot([k.sb([128, 5, 128], BF16, "pt") for _ in range(3)])
            ps_s = Rot([k.ps([128, 1024], F32, "ps_s") for _ in range(2)])
            ps_o = Rot([k.ps([128, 512], F32, "ps_o") for _ in range(2)])
            ps_z = Rot([k.ps([128, 512], F32, "ps_z") for _ in range(2)])
            rz = Rot([k.sb([128, 512], F32, "rz") for _ in range(2)])
            for h in range(16):
                q = qh.next(); kk = kh.next(); v = vh.next()
                k.op("sp", I("dma_start", out=q[:], in_=QT[h]), reads=[QT], writes=[q])
                k.op("sp", I("dma_start", out=kk[:, 0:T + 512], in_=KTX[h, :, 0:T + 512]), reads=[KTX], writes=[kk])
                k.op("sp", I("dma_start", out=v[:, 0:21, :], in_=VX[0:21 * 128, h * 128:(h + 1) * 128].rearrange("(t p) d -> p t d", p=128)),
                     reads=[VX], writes=[v])
                bf = bf32.next()
                k.op("sp", I("dma_start", out=bf[:, 0, :], in_=dap(EXT.t, h * 384 + 1, [[1, 128], [1, 128]])), reads=[EXT], writes=[bf])
                k.op("sp", I("dma_start", out=bf[:, 1, :], in_=dap(EXT.t, h * 384 + 129, [[1, 128], [1, 128]])), reads=[EXT], writes=[bf], nowaw=True)
                b = bt.next()
                k.op("dve", I("tensor_tensor", out=b[:, 0, :], in0=bf[:, 0, :], in1=m0b[:], op=ALU.add), reads=[bf, m0b], writes=[b])
                k.op("dve", I("tensor_copy", out=b[:, 1, :], in_=bf[:, 1, :]), reads=[bf], writes=[b])
                k.op("dve", I("tensor_scalar", out=b[:, 2, :], in0=zeros_b[:], scalar1=cst[:, h:h + 1], scalar2=None, op0=ALU.add), reads=[zeros_b, cst], writes=[b])
                k.op("dve", I("tensor_scalar", out=b[:, 3, :], in0=m4b[:], scalar1=cst[:, h:h + 1], scalar2=None, op0=ALU.add), reads=[m4b, cst], writes=[b])
                btype = {0: 0, 1: 1, 2: 2, 3: 2, 4: 3}
                for sbk in range(5):
                    po = ps_o.next(); pz = ps_z.next()
                    if sbk < 4:
                        blocks = [(sbk * 4 + i, 128) for i in range(4)]
                    else:
                        blocks = [(16, 64)]
                    for bi, (bq, nq) in enumerate(blocks):
                        kts = []
                        if bq < 16:
                            for ty in range(4, -1, -1):
                                kt_ = bq - ty
                                if kt_ < 0:
                                    continue
                                kts.append((kk[:, kt_ * 128:(kt_ + 1) * 128], v[:, kt_, :], 128, b[:, btype[ty], 0:nq]))
                        else:
                            for c in range(4):
                                ty = 1 if c == 3 else 2
                                kts.append((kk[:, T + c * 128:T + (c + 1) * 128], v[:, 17 + c, :], 128, b[:, ty, 0:nq]))
                            kts.append((kk[:, 2048:2112], v[0:64, 16, :], 64, b[64:128, 0, 0:nq]))
                        pss = ps_s.next()
                        pt = ptr.next()
                        qa = q[:, bq * 128:bq * 128 + nq]
                        for s, (ka, va, nk, ba) in enumerate(kts):
                            k.op("pe", I("matmul", pss[0:nk, s * 128:s * 128 + nq], lhsT=ka, rhs=qa, start=True, stop=False), reads=[kk, q], writes=[pss])
                            k.op("pe", I("matmul", pss[0:nk, s * 128:s * 128 + nq], lhsT=(Jb[:] if nk == 128 else Jb[64:128, 0:64]), rhs=ba, start=False, stop=True), reads=[Jb, b], writes=[pss])
                        for s, (ka, va, nk, ba) in enumerate(kts):
                            k.op("act", I("activation", out=pt[0:nk, s, 0:nq], in_=pss[0:nk, s * 128:s * 128 + nq], func=AF.Exp), reads=[pss], writes=[pt])
                        for s, (ka, va, nk, ba) in enumerate(kts):
                            k.op("pe", I("matmul", po[:, bi * 128:bi * 128 + nq], lhsT=va, rhs=pt[0:nk, s, 0:nq], start=(s == 0), stop=(s == len(kts) - 1)),
                                 reads=[v, pt], writes=[po])
                        for s, (ka, va, nk, ba) in enumerate(kts):
                            k.op("pe", I("matmul", pz[:, bi * 128:bi * 128 + nq], lhsT=ones_b[0:nk, :], rhs=pt[0:nk, s, 0:nq], start=(s == 0), stop=(s == len(kts) - 1)),
                                 reads=[ones_b, pt], writes=[pz])
                    ncol = 512 if sbk < 4 else 64
                    r = rz.next()
                    k.op("dve", I("reciprocal", out=r[:, 0:ncol], in_=pz[:, 0:ncol]), reads=[pz], writes=[r])
                    ao = aor.next()
                    if sbk == 4:
                        k.op("pool", I("memset", ao[:, 64:128], 0.0), writes=[ao])
                    k.op("dve", I("tensor_tensor", out=ao[:, 0:ncol], in0=po[:, 0:ncol], in1=r[:, 0:ncol], op=ALU.mult),
                         reads=[po, r], writes=[ao], nowaw=True)
                    nst = 512 if sbk < 4 else 128
                    k.op("dq_pool", I("dma_start", out=ATd[h, :, sbk * 512:sbk * 512 + nst], in_=ao[:, 0:nst]), reads=[ao], writes=[ATd], nowaw=True)
        phase_wo_ln(a_wo, a_wo[j], Xsrc, Xdst, lnrow)

    def phase_attn_c(Xsrc, Xdst, lnrow):
        stage('phase_attn_c')
        with k.phase():
            aor = Rot([k.sb([128, 512], BF16, "ao") for _ in range(3)])
            qh = Rot([k.sb([128, T], BF16, "qh") for _ in range(2)])
            kh = Rot([k.sb([128, KEXT], BF16, "kh") for _ in range(2)])
            vh = Rot([k.sb([128, 25, 128], BF16, "vh") for _ in range(2)])
            e1r = Rot([k.sb([128, 512], F32, "e1") for _ in range(4)])
            lspr = Rot([k.sb([128, 512], F32, "lsp") for _ in range(4)])
            lsnr = Rot([k.sb([128, 512], F32, "lsn") for _ in range(4)])
            t1r = Rot([k.sb([128, 512], F32, "t1") for _ in range(4)])
            wr = Rot([k.sb([128, 512], BF16, "w") for _ in range(4)])
            carries = [k.sb([128, 512], F32, "carry") for _ in range(2)]
            ps_z = Rot([k.ps([128, 512], F32, "ps_z") for _ in range(2)])
            ps_a = Rot([k.ps([128, 512], F32, "ps_a") for _ in range(2)])
            ps_c = Rot([k.ps([128, 512], F32, "ps_c") for _ in range(2)])
            ps_o = Rot([k.ps([128, 512], F32, "ps_o") for _ in range(2)])

            def stream(h, sbk, q, kk, v, carry, po):
                if sbk < 4:
                    qc0, nq = sbk * 512, 512
                    kts = [(128, kk[:, kt * 128:(kt + 1) * 128], v[:, kt, :], max(0, kt - 4 * sbk) * 128, kt >= 4 * sbk)
                           for kt in range(4 * sbk + 3, -1, -1)]
                else:
                    qc0, nq = 2048, 64
                    kts = [(64, kk[:, 2048:2112], v[0:64, 16, :], 0, True)]
                    kts += [(128, kk[:, T + c * 128:T + (c + 1) * 128], v[:, 17 + c, :], 0, False) for c in range(7, -1, -1)]
                k.op("pe", I("matmul", po[:, 0:nq], lhsT=zeros_b[:], rhs=q[:, qc0:qc0 + nq], start=True, stop=False), reads=[zeros_b, q], writes=[po])
                k.op("pool", I("memset", carry[:], 0.0), writes=[carry])
                for si, (nk, ka, va, c0, diag) in enumerate(kts):
                    last = (si == len(kts) - 1)
                    dn = min(128, nq - c0)
                    pz = ps_z.next()
                    k.op("pe", I("matmul", pz[0:nk, c0:nq], lhsT=ka, rhs=q[:, qc0 + c0:qc0 + nq], start=True, stop=True), reads=[kk, q], writes=[pz])
                    yield
                    e1 = e1r.next()
                    k.op("act", I("activation", out=e1[0:nk, c0:nq], in_=pz[0:nk, c0:nq], func=AF.Exp, scale=-1.0), reads=[pz], writes=[e1])
                    lsp = lspr.next()
                    k.op("act", I("activation", out=lsp[0:nk, c0:nq], in_=e1[0:nk, c0:nq], func=AF.Ln, bias=ones_f[0:nk, 0:1], scale=1.0), reads=[e1, ones_f], writes=[lsp])
                    yield
                    lsn = lsnr.next()
                    k.op("dve", I("tensor_tensor", out=lsn[0:nk, c0:nq], in0=pz[0:nk, c0:nq], in1=lsp[0:nk, c0:nq], op=ALU.add), reads=[pz, lsp], writes=[lsn])
                    if diag:
                        k.op("pool", I("tensor_tensor", out=lsn[0:nk, c0:c0 + dn], in0=lsn[0:nk, c0:c0 + dn], in1=mcaus[0:nk, 0:dn], op=ALU.mult),
                             reads=[lsn, mcaus], writes=[lsn])
                    yield
                    pa = ps_a.next(); pc = ps_c.next()
                    k.op("pe", I("matmul", pa[0:nk, c0:nq], lhsT=Uf[0:nk, 0:nk], rhs=lsn[0:nk, c0:nq], start=True, stop=True), reads=[Uf, lsn], writes=[pa])
                    k.op("pe", I("matmul", pc[:, c0:nq], lhsT=ones_f[0:nk, :], rhs=lsn[0:nk, c0:nq], start=True, stop=True), reads=[ones_f, lsn], writes=[pc])
                    yield
                    t1 = t1r.next()
                    k.op("dve", I("tensor_tensor", out=t1[0:nk, c0:nq], in0=pa[0:nk, c0:nq], in1=lsp[0:nk, c0:nq], op=ALU.add), reads=[pa, lsp], writes=[t1])
                    k.op("pool", I("tensor_tensor", out=t1[0:nk, c0:nq], in0=t1[0:nk, c0:nq], in1=carry[0:nk, c0:nq], op=ALU.add), reads=[t1, carry], writes=[t1])
                    yield
                    w = wr.next()
                    k.op("act", I("activation", out=w[0:nk, c0:nq], in_=t1[0:nk, c0:nq], func=AF.Exp, scale=-1.0), reads=[t1], writes=[w])
                    if diag:
                        k.op("pool", I("tensor_tensor", out=w[0:nk, c0:c0 + dn], in0=w[0:nk, c0:c0 + dn], in1=mcausb[0:nk, 0:dn], op=ALU.mult),
                             reads=[w, mcausb], writes=[w])
                    yield
                    if not last:
                        k.op("dve", I("tensor_tensor", out=carry[:, c0:nq], in0=pc[:, c0:nq], in1=carry[:, c0:nq], op=ALU.add), reads=[pc, carry], writes=[carry])
                    k.op("pe", I("matmul", po[:, c0:nq], lhsT=va, rhs=w[0:nk, c0:nq], start=False, stop=last), reads=[v, w], writes=[po])
                    yield
                ao = aor.next()
                if sbk == 4:
                    k.op("pool", I("memset", ao[:, 64:128], 0.0), writes=[ao])
                copy_op("act", ao[:, 0:nq], po[:, 0:nq], [po], [ao])
                nst = 512 if sbk < 4 else 128
                k.op("sp", I("dma_start", out=ATd[h, :, sbk * 512:sbk * 512 + nst], in_=ao[:, 0:nst]), reads=[ao], writes=[ATd], nowaw=True)

            pending = []
            for h in range(16):
                q = qh.next(); kk = kh.next(); v = vh.next()
                k.op("sp", I("dma_start", out=q[:], in_=QT[h]), reads=[QT], writes=[q])
                k.op("sp", I("dma_start", out=kk[:], in_=KTX[h]), reads=[KTX], writes=[kk])
                k.op("sp", I("dma_start", out=v[:], in_=VX[:, h * 128:(h + 1) * 128].rearrange("(t p) d -> p t d", p=128)), reads=[VX], writes=[v])
                order = [3, 0, 2, 1, 4]
                slots = [None, None]
                todo = list(order)
                while todo or any(sl is not None for sl in slots):
                    for si_ in range(2):
                        if slots[si_] is None and todo:
                            slots[si_] = stream(h, todo.pop(0), q, kk, v, carries[si_], ps_o.tiles[si_])
                        if slots[si_] is not None:
                            try:
                                next(slots[si_])
                            except StopIteration:
                                slots[si_] = None
        phase_wo_ln(c_wo, c_wo[:, :], Xsrc, Xdst, lnrow)

    def rmsnorm_psum(ps, gtile, out_ap, st6, mv, reads_extra, out_tile):
        k.op("dve", I("bn_stats", out=st6[:, 0, :], in_=ps[:]), reads=[ps], writes=[st6])
        k.op("dve", I("bn_aggr", out=mv[:, 0:2], in_=st6[:, 0, :]), reads=[st6], writes=[mv])
        k.op("dve", I("scalar_tensor_tensor", out=mv[:, 2:3], in0=mv[:, 0:1], scalar=mv[:, 0:1], in1=mv[:, 1:2], op0=ALU.mult, op1=ALU.add), reads=[mv], writes=[mv])
        k.op("dve", I("tensor_scalar", out=mv[:, 2:3], in0=mv[:, 2:3], scalar1=1e-6, scalar2=None, op0=ALU.add), reads=[mv], writes=[mv])
        k.op("act", I("activation", out=mv[:, 2:3], in_=mv[:, 2:3], func=AF.Sqrt), reads=[mv], writes=[mv])
        k.op("dve", I("reciprocal", out=mv[:, 3:4], in_=mv[:, 2:3]), reads=[mv], writes=[mv])
        k.op("dve", I("scalar_tensor_tensor", out=out_ap, in0=ps[:], scalar=mv[:, 3:4], in1=gtile[:], op0=ALU.mult, op1=ALU.mult), reads=[ps, mv, gtile], writes=[out_tile])

    def rope_ops(src3, cs, sn, dst3, nh, tmp_a, tmp_b, reads, dst_tile, scale=None):
        cb = sap(cs, 0, [[0, nh], [1, 32]])
        sb_ = sap(sn, 0, [[0, nh], [1, 32]])
        x1 = src3[:, :, 0:32]; x2 = src3[:, :, 32:64]
        ta = tmp_a[:, 0:nh * 32].rearrange("p (h r) -> p h r", r=32)
        tb = tmp_b[:, 0:nh * 32].rearrange("p (h r) -> p h r", r=32)
        k.op("dve", I("tensor_tensor", out=ta, in0=x1, in1=cb, op=ALU.mult), reads=reads + [cs], writes=[tmp_a])
        k.op("dve", I("tensor_tensor", out=tb, in0=x2, in1=sb_, op=ALU.mult), reads=reads + [sn], writes=[tmp_b])
        k.op("dve", I("tensor_tensor", out=dst3[:, :, 0:32], in0=ta, in1=tb, op=ALU.subtract), reads=[tmp_a, tmp_b], writes=[dst_tile])
        k.op("dve", I("tensor_tensor", out=ta, in0=x1, in1=sb_, op=ALU.mult), reads=reads + [sn], writes=[tmp_a])
        k.op("dve", I("tensor_tensor", out=tb, in0=x2, in1=cb, op=ALU.mult), reads=reads + [cs], writes=[tmp_b])
        k.op("dve", I("tensor_tensor", out=dst3[:, :, 32:64], in0=ta, in1=tb, op=ALU.add), reads=[tmp_a, tmp_b], writes=[dst_tile], nowaw=True)

    def phase_mla(Xsrc, Xdst, lnrow):
        stage('phase_mla_in')
        sc = 192 ** -0.5
        with k.phase():
            xT = k.sb([128, KC, T], BF16, "xT")
            load_xT(xT)
            winb = k.sb([128, KC, 1088], BF16, "winb")
            stg = Rot([k.sb([128, KC, 128], F32, "wstg") for _ in range(2)])
            for c in range(9):
                n = 128 if c < 8 else 64
                st = stg.next()
                k.op("sp", I("dma_start", out=st[:, :, 0:n], in_=b_win[:, c * 128:c * 128 + n].rearrange("(kc p) n -> p kc n", p=128)), reads=[b_win], writes=[st])
                k.op("pool", I("tensor_copy", out=winb[:, :, c * 128:c * 128 + n], in_=st[:, :, 0:n]), reads=[st], writes=[winb], nowaw=(c > 0))
            qn_t = k.sb([128, 512], F32, "qn_t"); kvn_t = k.sb([128, 512], F32, "kvn_t")
            k.op("sp", I("dma_start", out=qn_t[:], in_=dap(b_qn.t, 0, [[0, 128], [1, 512]])), reads=[b_qn], writes=[qn_t])
            k.op("sp", I("dma_start", out=kvn_t[:], in_=dap(b_kvn.t, 0, [[0, 128], [1, 512]])), reads=[b_kvn], writes=[kvn_t])
            st6 = k.sb([128, 1, 6], F32, "st6"); mv = k.sb([128, 4], F32, "mv")
            csr = Rot([k.sb([128, 32], F32, "cs") for _ in range(2)]); snr = Rot([k.sb([128, 32], F32, "sn") for _ in range(2)])
            p0r = Rot([k.ps([128, 512], F32, "p0") for _ in range(2)])
            p1r = Rot([k.ps([128, 512], F32, "p1") for _ in range(2)])
            p2r = Rot([k.ps([128, 64], F32, "p2") for _ in range(1)])
            pst = Rot([k.ps([128, 1024], BF16, "pst") for _ in range(2)])
            cqb = Rot([k.sb([128, 512], BF16, "cqb") for _ in range(2)])
            ckf = Rot([k.sb([128, 512], F32, "ckf") for _ in range(2)])
            ckb = Rot([k.sb([128, 512], BF16, "ckb") for _ in range(2)])
            krf = Rot([k.sb([128, 1, 64], F32, "krf") for _ in range(2)])
            krb = Rot([k.sb([128, 128], BF16, "krb") for _ in range(2)])
            ta = k.sb([128, 512], F32, "ta"); tb = k.sb([128, 512], F32, "tb")
            tsb = Rot([k.sb([128, 9, 128], BF16, "tsb") for _ in range(2)])

            def transposes_out(cq_bf, ck_bf, kr_bf, col0):
                ps = pst.next()
                ts = tsb.next()
                n = 0
                srcs = []
                if cq_bf is not None:
                    srcs += [(cq_bf, c) for c in range(4)]
                srcs += [(ck_bf, c) for c in range(4)]
                for (tl, c) in srcs:
                    k.op("pe", I("transpose", out=ps[:, n * 128:(n + 1) * 128], in_=tl[:, c * 128:(c + 1) * 128], identity=ident[:]), reads=[tl, ident], writes=[ps])
                    n += 1
                copy_op(evac_eng(), ts[:, 0:n, :], ps[:, 0:n * 128].rearrange("p (a b) -> p a b", b=128), [ps], [ts])
                ps2 = pst.next()
                k.op("pe", I("transpose", out=ps2[:, 0:128], in_=kr_bf[:], identity=ident[:]), reads=[kr_bf, ident], writes=[ps2])
                copy_op(evac_eng(), ts[:, 8, :], ps2[:, 0:128], [ps2], [ts])
                o = 0
                if cq_bf is not None:
                    k.op("sp", I("dma_start", out=CQT[:, :, col0:col0 + 128].rearrange("c p n -> p c n"), in_=ts[:, 0:4, :]), reads=[ts], writes=[CQT], nowaw=True)
                    o = 4
                k.op("sp", I("dma_start", out=CKVT[:, :, col0:col0 + 128].rearrange("c p n -> p c n"), in_=ts[:, o:o + 4, :]), reads=[ts], writes=[CKVT], nowaw=True)
                k.op("sp", I("dma_start", out=KR[:, col0:col0 + 128], in_=ts[:, 8, :]), reads=[ts], writes=[KR], nowaw=True)

            for tt in range(NT):
                cs = csr.next(); sn = snr.next()
                k.op("sp", I("dma_start", out=cs[:], in_=ropec[tt * 128:(tt + 1) * 128, :]), reads=[ropec], writes=[cs])
                k.op("sp", I("dma_start", out=sn[:], in_=ropes[tt * 128:(tt + 1) * 128, :]), reads=[ropes], writes=[sn])
                p0 = p0r.next(); p1 = p1r.next(); p2 = p2r.next()
                for (pt_, c0, n) in ((p0, 0, 512), (p1, 512, 512), (p2, 1024, 64)):
                    for kc in range(KC):
                        k.op("pe", I("matmul", pt_[:, 0:n], lhsT=xT[:, kc, tt * 128:(tt + 1) * 128], rhs=winb[:, kc, c0:c0 + n], start=(kc == 0), stop=(kc == KC - 1)),
                             reads=[xT, winb], writes=[pt_])
                cq = cqb.next()
                rmsnorm_psum(p0, qn_t, cq[:], st6, mv, [], cq)
                cf = ckf.next()
                rmsnorm_psum(p1, kvn_t, cf[:], st6, mv, [], cf)
                if tt < 16:
                    k.op("sp", I("dma_start", out=obc[tt * 128:(tt + 1) * 128, :], in_=cf[:]), reads=[cf], writes=[])
                else:
                    k.op("sp", I("dma_start", out=sbc[:, :], in_=cf[0:64, :]), reads=[cf], writes=[])
                cb_ = ckb.next()
                k.op("act", I("copy", out=cb_[:], in_=cf[:]), reads=[cf], writes=[cb_])
                kf = krf.next()
                rope_ops(p2[:].rearrange("p (h c) -> p h c", c=64), cs, sn, kf[:], 1, ta, tb, [p2], kf)
                if tt < 16:
                    k.op("sp", I("dma_start", out=obr[tt * 128:(tt + 1) * 128, :], in_=kf[:, 0, :]), reads=[kf], writes=[])
                else:
                    k.op("sp", I("dma_start", out=sbr[:, :], in_=kf[0:64, 0, :]), reads=[kf], writes=[])
                kb = krb.next()
                k.op("act", I("copy", out=kb[:, 0:64], in_=kf[:, 0, :]), reads=[kf], writes=[kb])
                k.op("act", I("copy", out=kb[:, 64:128], in_=kf[:, 0, :]), reads=[kf], writes=[kb], nowaw=True)
                transposes_out(cq, cb_, kb, tt * 128)
            for c in range(8):
                cf = ckf.next()
                k.op("sp", I("dma_start", out=cf[:], in_=cb_c[c * 128:(c + 1) * 128, :]), reads=[], writes=[cf])
                cb_ = ckb.next()
                k.op("act", I("copy", out=cb_[:], in_=cf[:]), reads=[cf], writes=[cb_])
                kf = krf.next()
                k.op("sp", I("dma_start", out=kf[:, 0, :], in_=cb_r[c * 128:(c + 1) * 128, :]), reads=[], writes=[kf])
                kb = krb.next()
                k.op("act", I("copy", out=kb[:, 0:64], in_=kf[:, 0, :]), reads=[kf], writes=[kb])
                k.op("act", I("copy", out=kb[:, 64:128], in_=kf[:, 0, :]), reads=[kf], writes=[kb], nowaw=True)
                transposes_out(None, cb_, kb, T + c * 128)
        stage('phase_mla_q')
        with k.phase():
            cqT = k.sb([128, 4, T], BF16, "cqT")
            k.op("sp", I("dma_start", out=cqT[:], in_=CQT[:, :, :].rearrange("c p n -> p c n")), reads=[CQT], writes=[cqT])
            wqn = k.sb([128, 4, 2048], BF16, "wqn"); wqr = k.sb([128, 4, 1024], BF16, "wqr")
            stg = Rot([k.sb([128, 4, 768], F32, "wstg") for _ in range(2)])
            for c in range(4):
                st = stg.next()
                k.op("sp", I("dma_start", out=st[:], in_=b_wqb[:, c * 768:(c + 1) * 768].rearrange("(kc p) n -> p kc n", p=128)), reads=[b_wqb], writes=[st])
                for kc in range(4):
                    sv = st[:, kc, :].rearrange("p (h c) -> p h c", c=192)
                    k.op("pool", I("tensor_copy", out=wqn[:, kc, c * 512:(c + 1) * 512].rearrange("p (h c) -> p h c", c=128), in_=sv[:, :, 0:128]), reads=[st], writes=[wqn], nowaw=True)
                    k.op("pool", I("tensor_copy", out=wqr[:, kc, c * 256:(c + 1) * 256].rearrange("p (h c) -> p h c", c=64), in_=sv[:, :, 128:192]), reads=[st], writes=[wqr], nowaw=True)
            pp = Rot([k.ps([128, 512], F32, "pp") for _ in range(3)])
            qts = Rot([k.sb([128, 512], BF16, "qts") for _ in range(3)])
            for h in range(16):
                for (t0, tn) in TOKB:
                    ps = pp.next()
                    for kc in range(4):
                        k.op("pe", I("matmul", ps[:, 0:tn], lhsT=wqn[:, kc, h * 128:(h + 1) * 128], rhs=cqT[:, kc, t0:t0 + tn], start=(kc == 0), stop=(kc == 3)),
                             reads=[wqn, cqT], writes=[ps])
                    qs = qts.next()
                    copy_op(evac_eng(), qs[:, 0:tn], ps[:, 0:tn], [ps], [qs], scale=sc)
                    k.op("sp", I("dma_start", out=QT[h, :, t0:t0 + tn], in_=qs[:, 0:tn]), reads=[qs], writes=[QT], nowaw=True)
            csr = Rot([k.sb([128, 32], F32, "cs") for _ in range(2)]); snr = Rot([k.sb([128, 32], F32, "sn") for _ in range(2)])
            ta = k.sb([128, 512], F32, "ta"); tb = k.sb([128, 512], F32, "tb")
            qrf = Rot([k.sb([128, 16, 64], F32, "qrf") for _ in range(2)])
            qrb = Rot([k.sb([128, 1024], BF16, "qrb") for _ in range(2)])
            pst = Rot([k.ps([128, 1024], BF16, "pst") for _ in range(2)])
            qrt = Rot([k.sb([128, 8, 128], BF16, "qrt") for _ in range(2)])
            for tt in range(NT):
                cs = csr.next(); sn = snr.next()
                k.op("sp", I("dma_start", out=cs[:], in_=ropec[tt * 128:(tt + 1) * 128, :]), reads=[ropec], writes=[cs])
                k.op("sp", I("dma_start", out=sn[:], in_=ropes[tt * 128:(tt + 1) * 128, :]), reads=[ropes], writes=[sn])
                qf = qrf.next()
                for hg in range(2):
                    ps = pp.next()
                    for kc in range(4):
                        k.op("pe", I("matmul", ps[:, :], lhsT=cqT[:, kc, tt * 128:(tt + 1) * 128], rhs=wqr[:, kc, hg * 512:(hg + 1) * 512], start=(kc == 0), stop=(kc == 3)),
                             reads=[wqr, cqT], writes=[ps])
                    rope_ops(ps[:].rearrange("p (h c) -> p h c", c=64), cs, sn, qf[:, hg * 8:(hg + 1) * 8, :], 8, ta, tb, [ps], qf)
                qb = qrb.next()
                k.op("act", I("activation", out=qb[:], in_=qf[:].rearrange("p h c -> p (h c)"), func=AF.Copy, scale=float(sc)), reads=[qf], writes=[qb])
                ps2 = pst.next()
                for j8 in range(8):
                    k.op("pe", I("transpose", out=ps2[:, j8 * 128:(j8 + 1) * 128], in_=qb[:, j8 * 128:(j8 + 1) * 128], identity=ident[:]), reads=[qb, ident], writes=[ps2])
                qt_ = qrt.next()
                copy_op(evac_eng(), qt_[:], ps2[:].rearrange("p (a b) -> p a b", a=8), [ps2], [qt_])
                k.op("sp", I("dma_start", out=QR[:, :, tt * 128:(tt + 1) * 128].rearrange("a p n -> p a n"), in_=qt_[:]), reads=[qt_], writes=[QR], nowaw=True)
        stage('phase_mla_kv')
        with k.phase():
            ckvT = k.sb([128, 4, KEXT], BF16, "ckvT")
            k.op("sp", I("dma_start", out=ckvT[:], in_=CKVT[:, :, :].rearrange("c p n -> p c n")), reads=[CKVT], writes=[ckvT])
            wkn = k.sb([128, 4, 2048], BF16, "wkn"); wkv = k.sb([128, 4, 2048], BF16, "wkv")
            stg = Rot([k.sb([128, 4, 1024], F32, "wstg") for _ in range(2)])
            for c in range(4):
                st = stg.next()
                k.op("sp", I("dma_start", out=st[:], in_=b_wkvb[:, c * 1024:(c + 1) * 1024].rearrange("(kc p) n -> p kc n", p=128)), reads=[b_wkvb], writes=[st])
                for kc in range(4):
                    sv = st[:, kc, :].rearrange("p (h c) -> p h c", c=256)
                    k.op("pool", I("tensor_copy", out=wkn[:, kc, c * 512:(c + 1) * 512].rearrange("p (h c) -> p h c", c=128), in_=sv[:, :, 0:128]), reads=[st], writes=[wkn], nowaw=True)
                    k.op("pool", I("tensor_copy", out=wkv[:, kc, c * 512:(c + 1) * 512].rearrange("p (h c) -> p h c", c=128), in_=sv[:, :, 128:256]), reads=[st], writes=[wkv], nowaw=True)
            pp = Rot([k.ps([128, 512], F32, "pp") for _ in range(4)])
            qts = Rot([k.sb([128, 512], BF16, "qts") for _ in range(3)])
            KB = [(i * 512, 512) for i in range(6)] + [(3072, 128)]
            for h in range(16):
                for (t0, tn) in KB:
                    ps = pp.next()
                    for kc in range(4):
                        k.op("pe", I("matmul", ps[:, 0:tn], lhsT=wkn[:, kc, h * 128:(h + 1) * 128], rhs=ckvT[:, kc, t0:t0 + tn], start=(kc == 0), stop=(kc == 3)),
                             reads=[wkn, ckvT], writes=[ps])
                    qs = qts.next()
                    copy_op(evac_eng(), qs[:, 0:tn], ps[:, 0:tn], [ps], [qs])
                    k.op("sp", I("dma_start", out=KTX[h, :, t0:t0 + tn], in_=qs[:, 0:tn]), reads=[qs], writes=[KTX], nowaw=True)
            for kt in range(25):
                for ns in range(4):
                    ps = pp.next()
                    for kc in range(4):
                        k.op("pe", I("matmul", ps[:, :], lhsT=ckvT[:, kc, kt * 128:(kt + 1) * 128], rhs=wkv[:, kc, ns * 512:(ns + 1) * 512], start=(kc == 0), stop=(kc == 3)),
                             reads=[wkv, ckvT], writes=[ps])
                    qs = qts.next()
                    copy_op(evac_eng(), qs[:], ps[:], [ps], [qs])
                    k.op("sp", I("dma_start", out=VX[kt * 128:(kt + 1) * 128, ns * 512:(ns + 1) * 512], in_=qs[:]), reads=[qs], writes=[VX], nowaw=True)
        stage('phase_mla_attn')
        with k.phase():
            aor = Rot([k.sb([128, 512], BF16, "ao") for _ in range(2)])
            qh = Rot([k.sb([128, T], BF16, "qh") for _ in range(2)])
            qrh = Rot([k.sb([128, T], BF16, "qrh") for _ in range(2)])
            kh = Rot([k.sb([128, KEXT], BF16, "kh") for _ in range(2)])
            vh = Rot([k.sb([128, 25, 128], BF16, "vh") for _ in range(2)])
            krs = k.sb([128, KEXT], BF16, "krs")
            k.op("sp", I("dma_start", out=krs[:], in_=KR[:, :]), reads=[KR], writes=[krs])
            ptr = Rot([k.sb([128, 512], BF16, "pt") for _ in range(3)])
            rz = Rot([k.sb([128, 512], F32, "rz") for _ in range(2)])
            ps_s = Rot([k.ps([128, 512], F32, "ps_s") for _ in range(3)])
            ps_o = Rot([k.ps([128, 512], F32, "ps_o") for _ in range(2)])
            ps_zz = Rot([k.ps([128, 512], F32, "ps_zz") for _ in range(2)])
            for h in range(16):
                q = qh.next(); kk = kh.next(); v = vh.next()
                hb = (h % 2) * 64
                k.op("sp", I("dma_start", out=q[:], in_=QT[h]), reads=[QT], writes=[q])
                if h % 2 == 0:
                    qr = qrh.next()
                    k.op("sp", I("dma_start", out=qr[:], in_=QR[h // 2]), reads=[QR], writes=[qr])
                k.op("sp", I("dma_start", out=kk[:], in_=KTX[h]), reads=[KTX], writes=[kk])
                k.op("sp", I("dma_start", out=v[:], in_=VX[:, h * 128:(h + 1) * 128].rearrange("(t p) d -> p t d", p=128)), reads=[VX], writes=[v])
                for sbk in range(5):
                    if sbk < 4:
                        qc0, nq = sbk * 512, 512
                        kts = [(128, kt * 128, v[:, kt, :], max(0, kt - 4 * sbk) * 128, kt >= 4 * sbk) for kt in range(0, 4 * sbk + 4)]
                    else:
                        qc0, nq = 2048, 64
                        kts = [(128, T + c * 128, v[:, 17 + c, :], 0, False) for c in range(8)] + [(64, 2048, v[0:64, 16, :], 0, False)]
                    po = ps_o.next(); pz = ps_zz.next()
                    prev = None
                    for si, (nk, kc0, va, c0, diag) in enumerate(kts):
                        last = (si == len(kts) - 1)
                        ps = ps_s.next()
                        k.op("pe", I("matmul", ps[0:nk, c0:nq], lhsT=kk[:, kc0:kc0 + nk], rhs=q[:, qc0 + c0:qc0 + nq], start=True, stop=False), reads=[kk, q], writes=[ps])
                        if diag:
                            k.op("pe", I("matmul", ps[0:nk, c0:c0 + 128], lhsT=ident[:], rhs=m0n[:], start=False, stop=False), reads=[ident, m0n], writes=[ps])
                        k.op("pe", I("matmul", ps[0:nk, c0:nq], lhsT=krs[hb:hb + 64, kc0:kc0 + nk], rhs=qr[hb:hb + 64, qc0 + c0:qc0 + nq], start=False, stop=True),
                             reads=[krs, qr], writes=[ps])
                        pt = ptr.next()
                        k.op("act", I("activation", out=pt[0:nk, c0:nq], in_=ps[0:nk, c0:nq], func=AF.Exp), reads=[ps], writes=[pt])
                        if prev is not None:
                            (psi, pnk, pva, pc0, ppt, plast) = prev
                            k.op("pe", I("matmul", po[:, pc0:nq], lhsT=pva, rhs=ppt[0:pnk, pc0:nq], start=(psi == 0), stop=plast), reads=[v, ppt], writes=[po])
                            k.op("pe", I("matmul", pz[:, pc0:nq], lhsT=ones_b[0:pnk, :], rhs=ppt[0:pnk, pc0:nq], start=(psi == 0), stop=plast), reads=[ones_b, ppt], writes=[pz])
                        prev = (si, nk, va, c0, pt, last)
                    (psi, pnk, pva, pc0, ppt, plast) = prev
                    k.op("pe", I("matmul", po[:, pc0:nq], lhsT=pva, rhs=ppt[0:pnk, pc0:nq], start=(psi == 0), stop=plast), reads=[v, ppt], writes=[po])
                    k.op("pe", I("matmul", pz[:, pc0:nq], lhsT=ones_b[0:pnk, :], rhs=ppt[0:pnk, pc0:nq], start=(psi == 0), stop=plast), reads=[ones_b, ppt], writes=[pz])
                    r = rz.next()
                    k.op("dve", I("reciprocal", out=r[:, 0:nq], in_=pz[:, 0:nq]), reads=[pz], writes=[r])
                    ao = aor.next()
                    if sbk == 4:
                        k.op("pool", I("memset", ao[:, 64:128], 0.0), writes=[ao])
                    k.op("dve", I("tensor_tensor", out=ao[:, 0:nq], in0=po[:, 0:nq], in1=r[:, 0:nq], op=ALU.mult), reads=[po, r], writes=[ao], nowaw=True)
                    nst = 512 if sbk < 4 else 128
                    k.op("sp", I("dma_start", out=ATd[h, :, sbk * 512:sbk * 512 + nst], in_=ao[:, 0:nst]), reads=[ao], writes=[ATd], nowaw=True)
        phase_wo_ln(b_wo, b_wo[:, :], Xsrc, Xdst, lnrow)


    def phase_peer(i, Xsrc, Xdst, lnrow, final_out=None):
        keys2d = p_keys[i]
        stage('peer_scores')
        with k.phase():
            xT = k.sb([128, KC, T], BF16, "xT")
            load_xT(xT)
            kin = Rot([k.sb([128, 128], F32, "kin") for _ in range(2)])
            kbf = Rot([k.sb([128, 128], BF16, "kbf") for _ in range(2)])
            keysT = k.sb([128, 16, 128], BF16, "keysT")
            pst = Rot([k.ps([128, 1024], BF16, "pst") for _ in range(1)])
            for half in range(2):
                ps = pst.next()
                for jx in range(8):
                    hc = half * 8 + jx
                    ki = kin.next(); kb = kbf.next()
                    k.op("sp", I("dma_start", out=ki[:], in_=keys2d[hc]), reads=[p_keys], writes=[ki])
                    k.op("act", I("copy", out=kb[:], in_=ki[:]), reads=[ki], writes=[kb])
                    k.op("pe", I("transpose", out=ps[:, jx * 128:(jx + 1) * 128], in_=kb[:], identity=ident[:]), reads=[kb, ident], writes=[ps])
                copy_op("act", keysT[:, half * 8:(half + 1) * 8, :], ps[:].rearrange("p (a b) -> p a b", a=8), [ps], [keysT])
            wl = WLoader(KC, 128, nbuf=2, cast_eng="act")
            pq = Rot([k.ps([128, 512], F32, "pq") for _ in range(3)])
            psc = Rot([k.ps([128, 512], F32, "psc") for _ in range(2)])
            qtb = Rot([k.sb([128, 512], BF16, "qtb") for _ in range(3)])
            Sr = Rot([k.sb([128, 16, 128], F32, "S") for _ in range(5)])
            T16 = k.sb([128, 16, 16], F32, "T16")
            tmpS = k.sb([128, 16, 128], F32, "tmpS")
            pen = k.sb([128, 16, 128], F32, "pen")
            Ar = Rot([k.sb([128, 2048 + 16], F32, "A12") for _ in range(2)])
            cand = k.sb([128, 8, 256], F32, "cand")
            ct1 = k.sb([128, 8, 256], F32, "ct1")
            ct2 = k.sb([128, 8, 256], F32, "ct2")
            C24 = k.sb([128, 8, 24], F32, "C24")
            dd = k.sb([128, 8, 16], F32, "dd")
            zz = k.sb([128, 16], F32, "zz")

            def topk_tile(S, tt):
                for hc in range(16):
                    k.op("dve", I("max", out=T16[:, hc, 0:8], in_=S[:, hc, :]), reads=[S], writes=[T16], nowaw=True)
                for hc in range(16):
                    k.op("dve", I("match_replace", out=tmpS[:, hc, :], in_to_replace=T16[:, hc, 0:8], in_values=S[:, hc, :], imm_value=-1e30),
                         reads=[T16, S], writes=[tmpS], nowaw=True)
                for hc in range(16):
                    k.op("dve", I("max", out=T16[:, hc, 8:16], in_=tmpS[:, hc, :]), reads=[tmpS], writes=[T16], nowaw=True)
                A = Ar.next()
                A3 = A[:, 0:2048].rearrange("p (a b) -> p a b", b=128)
                k.op("dve", I("tensor_tensor", out=pen[:], in0=S[:], in1=sap(T16, 15, [[16, 16], [0, 128]]), op=ALU.is_lt), reads=[S, T16], writes=[pen])
                k.op("dve", I("scalar_tensor_tensor", out=A3, in0=pen[:], scalar=-1e4, in1=S[:], op0=ALU.mult, op1=ALU.add), reads=[pen, S], writes=[A])
                k.op("dve", I("tensor_tensor", out=cand[:].rearrange("p h (i j) -> p h i j", j=16),
                              in0=sap(T16, 0, [[32, 8], [1, 16], [0, 16]]), in1=sap(T16, 16, [[32, 8], [0, 16], [1, 16]]), op=ALU.add),
                     reads=[T16], writes=[cand])
                for h in range(8):
                    k.op("dve", I("max", out=C24[:, h, 0:8], in_=cand[:, h, :]), reads=[cand], writes=[C24], nowaw=True)
                for h in range(8):
                    k.op("dve", I("match_replace", out=ct1[:, h, :], in_to_replace=C24[:, h, 0:8], in_values=cand[:, h, :], imm_value=-1e30), reads=[C24, cand], writes=[ct1], nowaw=True)
                for h in range(8):
                    k.op("dve", I("max", out=C24[:, h, 8:16], in_=ct1[:, h, :]), reads=[ct1], writes=[C24], nowaw=True)
                for h in range(8):
                    k.op("dve", I("match_replace", out=ct2[:, h, :], in_to_replace=C24[:, h, 8:16], in_values=ct1[:, h, :], imm_value=-1e30), reads=[C24, ct1], writes=[ct2], nowaw=True)
                for h in range(8):
                    k.op("dve", I("max", out=C24[:, h, 16:24], in_=ct2[:, h, :]), reads=[ct2], writes=[C24], nowaw=True)
                k.op("dve", I("tensor_tensor", out=A[:, 2048:2056], in0=C24[:, :, 15], in1=C24[:, :, 16], op=ALU.add), reads=[C24], writes=[A])
                k.op("dve", I("tensor_scalar", out=A[:, 2048:2056], in0=A[:, 2048:2056], scalar1=0.5, scalar2=None, op0=ALU.mult), reads=[A], writes=[A])
                k.op("dve", I("tensor_tensor", out=dd[:], in0=C24[:, :, 0:16], in1=sap(C24, 0, [[24, 8], [0, 16]]), op=ALU.subtract), reads=[C24], writes=[dd])
                k.op("act", I("activation", out=dd[:], in_=dd[:], func=AF.Exp), reads=[dd], writes=[dd])
                k.op("dve", I("tensor_reduce", out=zz[:, 0:8], in_=dd[:], axis=AX.X, op=ALU.add), reads=[dd], writes=[zz])
                k.op("act", I("activation", out=zz[:, 8:16], in_=zz[:, 0:8], func=AF.Ln), reads=[zz], writes=[zz])
                k.op("dve", I("tensor_tensor", out=zz[:, 8:16], in0=zz[:, 8:16], in1=C24[:, :, 0], op=ALU.add), reads=[zz, C24], writes=[zz])
                k.op("dve", I("tensor_scalar", out=A[:, 2056:2064], in0=zz[:, 8:16], scalar1=-1.0, scalar2=None, op0=ALU.mult), reads=[zz], writes=[A])
                k.op("sp", I("dma_start", out=AUX[tt], in_=A[:]), reads=[A], writes=[AUXb[tt]])

            for (t0, tn) in TOKB:
                nt_ = tn // 128
                tt0 = t0 // 128
                Sts = [Sr.next() for _ in range(nt_)]
                for hc in range(16):
                    wb = wl.load(p_wq, p_wq[i], [(hc * 128, 128)])
                    for hh in range(1):
                        ps = pq.next()
                        for kc in range(KC):
                            k.op("pe", I("matmul", ps[:, 0:tn], lhsT=wb[:, kc, 0:128], rhs=xT[:, kc, t0:t0 + tn],
                                         start=(kc == 0), stop=(kc == KC - 1)), reads=[wb, xT], writes=[ps])
                        qb = qtb.next()
                        copy_op("act", qb[:, 0:tn], ps[:, 0:tn], [ps], [qb])
                        p2 = psc.next()
                        for ti in range(nt_):
                            k.op("pe", I("matmul", p2[:, ti * 128:(ti + 1) * 128], lhsT=qb[:, ti * 128:(ti + 1) * 128], rhs=keysT[:, hc, :],
                                         start=True, stop=True), reads=[qb, keysT], writes=[p2])
                        for ti in range(nt_):
                            k.op("act", I("copy", out=Sts[ti][:, hc, :], in_=p2[:, ti * 128:(ti + 1) * 128]), reads=[p2], writes=[Sts[ti]], nowaw=(hc > 0))
                for ti in range(nt_):
                    topk_tile(Sts[ti], tt0 + ti)
        stage('peer_main')
        with k.phase():
            NG = 16
            ustg = Rot([k.sb([128, 2048], F32, "ustg") for _ in range(1)])
            ubf = Rot([k.sb([128, 2048], BF16, "ubf") for _ in range(2)])
            vstg = Rot([k.sb([128, 2048], F32, "vstg") for _ in range(1)])
            uT = Rot([k.sb([128, KC, 512], BF16, "uT") for _ in range(4)])
            vB = Rot([k.sb([128, 2048], BF16, "vB") for _ in range(10)])
            xts = Rot([k.sb([128, KC, 128], BF16, "xtl") for _ in range(2)])
            a2r = Rot([k.sb([128, 8, 128], F32, "a2") for _ in range(2)])
            a1r = Rot([k.sb([128, 8, 8], F32, "a1") for _ in range(2)])
            str_ = Rot([k.sb([128, 16], F32, "st") for _ in range(2)])
            tmpr = Rot([k.sb([128, 1024], F32, "tmp") for _ in range(3)])
            Ebr = Rot([k.sb([128, 1024], BF16, "Eb") for _ in range(3)])
            Ghr = Rot([k.sb([128, 1024], BF16, "Gh") for _ in range(4)])
            Gsr = Rot([k.sb([128, 1024], BF16, "Gs") for _ in range(2)])
            glr = Rot([k.sb([128, 1024], BF16, "gl") for _ in range(2)])
            Wbr = Rot([k.sb([128, 1024], BF16, "Wb") for _ in range(2)])
            WTr = Rot([k.sb([128, 8, 128], BF16, "WT") for _ in range(2)])
            ytr = Rot([k.sb([128, 2048], F32, "yt") for _ in range(2)])
            pH = k.ps([128, 1024], F32, "pH")
            pG = k.ps([128, 1024], F32, "pG")
            pUW = k.ps([128, 1024], BF16, "pUW")
            pYr = Rot([k.ps([128, 512], F32, "pY") for _ in range(3)])
            steps = [(g, tt) for g in range(NG) for tt in range(NT)]
            NS = len(steps)
            S = [dict() for _ in range(NS)]
            Wg = [dict(uts=[None, None], vbs=[None] * 8) for _ in range(NG)]

            def prep_u(g, half):
                ut = uT.next()
                Wg[g]["uts"][half] = ut
                for cc in range(4):
                    e0 = g * 1024 + (half * 4 + cc) * 128
                    us = ustg.next()
                    k.op("sp", I("dma_start", out=us[:], in_=p_u[i, e0:e0 + 128, :]), reads=[p_u], writes=[us])
                    ub = ubf.next()
                    k.op("act", I("copy", out=ub[:], in_=us[:]), reads=[us], writes=[ub])
                    for hf in range(2):
                        for jx in range(8):
                            kc = hf * 8 + jx
                            k.op("pe", I("transpose", out=pUW[:, jx * 128:(jx + 1) * 128], in_=ub[:, kc * 128:(kc + 1) * 128], identity=ident[:]),
                                 reads=[ub, ident], writes=[pUW])
                        copy_op("act", ut[:, hf * 8:(hf + 1) * 8, cc * 128:(cc + 1) * 128], pUW[:].rearrange("p (a b) -> p a b", a=8), [pUW], [ut])

            def prep_v(g, c):
                e0 = g * 1024 + c * 128
                vs = vstg.next()
                k.op("sp", I("dma_start", out=vs[:], in_=p_v[i, e0:e0 + 128, :]), reads=[p_v], writes=[vs])
                vb = vB.next()
                k.op("act", I("copy", out=vb[:], in_=vs[:]), reads=[vs], writes=[vb])
                Wg[g]["vbs"][c] = vb

            def emit_loads(s):
                g, tt = steps[s]
                d = S[s]
                d["xt"] = xts.next(); d["a2"] = a2r.next(); d["a1"] = a1r.next(); d["st"] = str_.next()
                k.op("sp", I("dma_start", out=d["xt"][:], in_=XT[tt]), reads=[XTb[tt]], writes=[d["xt"]])
                auxv = AUX[tt, :, 0:2048].rearrange("p (h c n) -> p h c n", h=8, c=2)
                k.op("sp", I("dma_start", out=d["a2"][:], in_=auxv[:, :, 1, :]), reads=[AUXb[tt]], writes=[d["a2"]])
                k.op("sp", I("dma_start", out=d["a1"][:], in_=auxv[:, :, 0, g * 8:(g + 1) * 8]), reads=[AUXb[tt]], writes=[d["a1"]])
                k.op("sp", I("dma_start", out=d["st"][:], in_=AUX[tt, :, 2048:2064]), reads=[AUXb[tt]], writes=[d["st"]])

            def emit_yload(s):
                g, tt = steps[s]
                d = S[s]
                d["y"] = ytr.next()
                if g > 0:
                    k.op("sp", I("dma_start", out=d["y"][:], in_=YAC[tt]), reads=[YACb[tt]], writes=[d["y"]])

            def emit_store(s):
                g, tt = steps[s]
                k.op("sp", I("dma_start", out=YAC[tt], in_=S[s]["y"][:]), reads=[S[s]["y"]], writes=[YACb[tt]])

            def g_head(s, h):
                d = S[s]
                a1, a2, st = d["a1"], d["a2"], d["st"]
                tm = tmpr.next()
                k.op(("pool" if h in TMP_POOL_HEADS else "dve"), I("tensor_tensor", out=tm[:].rearrange("p (r n) -> p r n", n=128),
                               in0=sap(a1, h * 8, [[1, 8], [0, 128]]), in1=sap(a2, h * 128, [[0, 8], [1, 128]]), op=ALU.add),
                     reads=[a1, a2], writes=[tm])
                eb = Ebr.next()
                k.op("act", I("activation", out=eb[:], in_=tm[:], func=AF.Exp, bias=st[:, 8 + h:9 + h], scale=1.0), reads=[tm, st], writes=[eb])
                gh = Ghr.next()
                k.op("dve", I("scalar_tensor_tensor", out=gh[:], in0=tm[:], scalar=st[:, h:h + 1], in1=eb[:], op0=ALU.is_ge, op1=ALU.mult),
                     reads=[tm, st, eb], writes=[gh])
                for half in range(2):
                    k.op("pe", I("matmul", pG[:, half * 512:(half + 1) * 512], lhsT=ident[:], rhs=gh[:, half * 512:(half + 1) * 512], start=(h == 0), stop=(h == 7)),
                         reads=[ident, gh], writes=[pG])

            def g_evac(s):
                gs = Gsr.next()
                S[s]["Gs"] = gs
                k.op("act", I("copy", out=gs[:], in_=pG[:]), reads=[pG], writes=[gs])

            prep_u(0, 0); prep_u(0, 1)
            for c in range(8):
                prep_v(0, c)
            emit_loads(0)
            for h in range(8):
                g_head(0, h)
            g_evac(0)
            def y_quarter(sp_, q4):
                gp, ttp = steps[sp_]
                dp = S[sp_]
                vbs_p = Wg[gp]["vbs"]
                py = pYr.next()
                for c in range(8):
                    k.op("pe", I("matmul", py[:], lhsT=dp["wt"][:, c, :], rhs=vbs_p[c][:, q4 * 512:(q4 + 1) * 512], start=(c == 0), stop=(c == 7)),
                         reads=[dp["wt"], vbs_p[c]], writes=[py])
                dp["py%d" % q4] = py

            def y_add(sp_, q4):
                gp, ttp = steps[sp_]
                dp = S[sp_]
                y = dp["y"]; py = dp["py%d" % q4]
                if gp == 0:
                    k.op("dve", I("tensor_copy", out=y[:, q4 * 512:(q4 + 1) * 512], in_=py[:]), reads=[py], writes=[y], nowaw=(q4 > 0))
                else:
                    k.op("dve", I("tensor_tensor", out=y[:, q4 * 512:(q4 + 1) * 512], in0=py[:], in1=y[:, q4 * 512:(q4 + 1) * 512], op=ALU.add), reads=[py, y], writes=[y], nowaw=(q4 > 0))

            for s in range(NS + 1):
                if s < NS:
                    g, tt = steps[s]
                    d = S[s]
                    uts = Wg[g]["uts"]
                    if s + 1 < NS:
                        emit_loads(s + 1)
                if s >= 2:
                    emit_store(s - 2)
                if s < NS:
                    emit_yload(s)
                    xt = d["xt"]
                    for half in range(2):
                        for kc in range(KC):
                            k.op("pe", I("matmul", pH[:, half * 512:(half + 1) * 512], lhsT=xt[:, kc, :], rhs=uts[half][:, kc, :], start=(kc == 0), stop=(kc == KC - 1)),
                                 reads=[xt, uts[half]], writes=[pH])
                for h in range(8):
                    if s + 1 < NS:
                        g_head(s + 1, h)
                    if s >= 1 and h <= 3:
                        y_quarter(s - 1, h)
                    if s >= 1 and 2 <= h <= 5:
                        y_add(s - 1, h - 2)
                    if s < NS:
                        if h == 3:
                            gg = glr.next()
                            k.op("act", I("activation", out=gg[:], in_=pH[:], func=AF.Gelu_apprx_tanh), reads=[pH], writes=[gg])
                        elif h == 5:
                            wb_ = Wbr.next()
                            k.op("pool", I("tensor_tensor", out=wb_[:], in0=gg[:], in1=d["Gs"][:], op=ALU.mult), reads=[gg, d["Gs"]], writes=[wb_])
                        elif h == 6:
                            for c in range(8):
                                k.op("pe", I("transpose", out=pUW[:, c * 128:(c + 1) * 128], in_=wb_[:, c * 128:(c + 1) * 128], identity=ident[:]), reads=[wb_, ident], writes=[pUW])
                        elif h == 7:
                            wt = WTr.next()
                            copy_op("act", wt[:], pUW[:].rearrange("p (a b) -> p a b", a=8), [pUW], [wt])
                            d["wt"] = wt
                if s + 1 < NS:
                    g_evac(s + 1)
                if s < NS:
                    if tt == 0 and g >= 1:
                        for c in range(2, 8):
                            prep_v(g, c)
                    if g + 1 < NG:
                        if tt == 3:
                            prep_u(g + 1, 0)
                        elif tt == 9:
                            prep_u(g + 1, 1)
                        elif tt in (6, 12):
                            prep_v(g + 1, (6, 12).index(tt))
            emit_store(NS - 1)
        phase_ln_from_dram([YAC[tt] for tt in range(NT)], YACb, Xsrc, Xdst, lnrow, final_out)

    try:
      phase_init()
      Xcur = x0
      for li in range(n_layers):
          kind, j = li % 3, li // 3
          if kinds is not None:
              kind, j = kinds[li], 0
          Xmid, Xnext = XA, XB
          if kind == 0:
              kro = [(tt, (128, oak[j, (tt - 12) * 128:(tt - 11) * 128, :])) for tt in range(12, 16)]
              vro = [(tt, (128, oav[j, (tt - 12) * 128:(tt - 11) * 128, :])) for tt in range(12, 16)]
              phase_qkv(a_wqkv, a_wqkv[j], ca_k[j], ca_v[j], 512, kro, vro, sak[j, 448:512, :], sav[j, 448:512, :], roll_k=sak[j], roll_v=sav[j])
              phase_attn_a(j, Xcur, Xmid, 2 * li)
          elif kind == 1:
              phase_mla(Xcur, Xmid, 2 * li)
          else:
              kro = [(tt, (128, ock[tt * 128:(tt + 1) * 128, :])) for tt in range(16)]
              vro = [(tt, (128, ocv[tt * 128:(tt + 1) * 128, :])) for tt in range(16)]
              phase_qkv(c_wqkv, c_wqkv[:, :], cc_k, cc_v, 1024, kro, vro, sck[:, :], scv[:, :])
              phase_attn_c(Xcur, Xmid, 2 * li)
          phase_peer(li, Xmid, Xnext, 2 * li + 1, final_out=(y_out if li == n_layers - 1 else None))
          Xcur = Xnext
    except _Stop as e:
        print('stopped before', e)
        if k.es is not None:
            k.P.barrier(); k.es.close(); k.es = None
    info = P.emit()
    return nc, info


def _prep_inputs(inputs, b, n_layers=4):
    f = np.float32
    xp = inputs["x_prompt"][b]
    xs = inputs["x_sample"][b]
    x0 = np.zeros((T, D), f)
    x0[0:2048] = xp
    x0[2048:2112] = xs
    half = 32
    inv = (10000.0 ** (-np.arange(half, dtype=np.float32) / half)).astype(np.float32)
    pos = np.zeros((T,), np.float32)
    pos[0:2048] = np.arange(2048)
    pos[2048:2112] = 1024 + np.arange(64)
    ang = (pos[:, None] * inv[None, :]).astype(np.float32)
    m = {
        "x0": x0,
        "ca_k": np.ascontiguousarray(inputs["cache_a_k"][:, b]).reshape(2, 512, 2048),
        "ca_v": np.ascontiguousarray(inputs["cache_a_v"][:, b]).reshape(2, 512, 2048),
        "cb_c": np.ascontiguousarray(inputs["cache_b_ckv"][0, b]),
        "cb_r": np.ascontiguousarray(inputs["cache_b_krope"][0, b]),
        "cc_k": np.ascontiguousarray(inputs["cache_c_k"][0, b]).reshape(1024, 2048),
        "cc_v": np.ascontiguousarray(inputs["cache_c_v"][0, b]).reshape(1024, 2048),
        "a_wqkv": inputs["a_wqkv"], "a_wo": inputs["a_wo"], "a_rel": inputs["a_relbias"],
        "b_win": inputs["b_win"][0], "b_qn": inputs["b_qnorm"], "b_kvn": inputs["b_kvnorm"],
        "b_wqb": inputs["b_wqb"][0], "b_wkvb": inputs["b_wkvb"][0], "b_wo": inputs["b_wo"][0],
        "c_wqkv": inputs["c_wqkv"][0], "c_wo": inputs["c_wo"][0],
        "p_wq": inputs["peer_wq"], "p_keys": inputs["peer_keys"].reshape(4, 16, 128, 128),
        "p_u": inputs["peer_u"][:n_layers], "p_v": inputs["peer_v"][:n_layers],
        "ln_g": inputs["ln_g"].reshape(8, 2048), "ln_b": inputs["ln_b"].reshape(8, 2048),
        "ropec": np.cos(ang).astype(f), "ropes": np.sin(ang).astype(f),
    }
    return {kk: np.ascontiguousarray(np.asarray(v, dtype=f)) for kk, v in m.items()}


_CACHE = {}


def kernel(**inputs):
    inputs = {kk: np.asarray(v) for kk, v in inputs.items()}
    if "nc" not in _CACHE:
        _CACHE["nc"] = build()[0]
    nc = _CACHE["nc"]
    in_maps = [_prep_inputs(inputs, b) for b in range(8)]
    res = run_bass_kernel_spmd(nc, in_maps, core_ids=list(range(8)))
    R = res.results
    f = np.float32

    def st(fn):
        return np.stack([fn(R[b]) for b in range(8)])

    y_prompt = st(lambda r: r["y"][0:2048])
    y_sample = st(lambda r: r["y"][2048:2112])
    oak = np.stack([R[b]["oak"].reshape(2, 512, 16, 128) for b in range(8)], axis=1)
    oav = np.stack([R[b]["oav"].reshape(2, 512, 16, 128) for b in range(8)], axis=1)
    obc = st(lambda r: r["obc"])[None]
    obr = st(lambda r: r["obr"])[None]
    ock = st(lambda r: r["ock"].reshape(2048, 16, 128))[None]
    ocv = st(lambda r: r["ocv"].reshape(2048, 16, 128))[None]
    sak = np.stack([R[b]["sak"].reshape(2, 512, 16, 128) for b in range(8)], axis=1)
    sav = np.stack([R[b]["sav"].reshape(2, 512, 16, 128) for b in range(8)], axis=1)
    sbc = st(lambda r: r["sbc"])[None]
    sbr = st(lambda r: r["sbr"])[None]
    sck = st(lambda r: r["sck"].reshape(64, 16, 128))[None]
    scv = st(lambda r: r["scv"].reshape(64, 16, 128))[None]
    outs = (y_prompt, y_sample, oak, oav, obc, obr, ock, ocv, sak, sav, sbc, sbr, sck, scv)
    return tuple(np.ascontiguousarray(o.astype(f)) for o in outs)
```
